# Optimizing a Trainium2 kernel written in Bass

```python
import math
import jax, jax.numpy as jnp
from jax import lax
import numpy as np

D_MODEL = 1024
BATCH = 4
SEQ = 8192
DEPTH = 4

GRID_W = 64
CTX_LEN = 256
N_MIXERS = 4
N_SSD_LAYERS = (DEPTH + 3) // 4
N_LRU_LAYERS = (DEPTH + 2) // 4
N_S5_LAYERS = (DEPTH + 1) // 4
N_MLA_LAYERS = DEPTH // 4
NORM_EPS = 1e-6
CONV_WIDTH = 4
CONV_PAD_LEFT = 2
FFN_HIDDEN = ((8 * D_MODEL // 3 + 255) // 256) * 256
M2_INNER = 2 * D_MODEL
M2_HEADDIM = 64
M2_HEADS = M2_INNER // M2_HEADDIM
M2_GROUPS = 4
M2_HPG = M2_HEADS // M2_GROUPS
M2_STATE = 128
M2_CONV_CH = M2_INNER + 2 * M2_GROUPS * M2_STATE
M2_PROJ = M2_INNER + M2_CONV_CH + 2 * M2_HEADS
SSD_CHUNK = 128
LRU_WIDTH = 5 * D_MODEL // 4
LRU_BLOCK = 128
LRU_BLOCKS = LRU_WIDTH // LRU_BLOCK
LRU_C = 8.0
S5_GROUP = 16
S5_GROUPS = D_MODEL // S5_GROUP
S5_STATE = 64
MLA_HEADS = 16
MLA_NOPE = 64
MLA_ROPE = 32
MLA_V = 64
MLA_Q_RANK = 384
MLA_KV_RANK = 256
MLA_IN = MLA_Q_RANK + MLA_KV_RANK + MLA_ROPE
ROPE_FREQ = MLA_ROPE // 4
ROPE_BASE = 10000.0
Q_BLOCK = 128

kernel_name = 'hybrid_interleaved_flow_trunk'


def rms_norm(x, w):
    xf = x.astype(jnp.float32)
    y = xf * lax.rsqrt(jnp.mean(xf * xf, axis=-1, keepdims=True) + NORM_EPS)
    return (y * w.astype(jnp.float32)).astype(x.dtype)


def dwconv(x, w, b):
    y = lax.conv_general_dilated(x, w[:, None, :], window_strides=(1,),
                                 padding=[(CONV_PAD_LEFT, CONV_WIDTH - 1 - CONV_PAD_LEFT)],
                                 dimension_numbers=('NWC', 'WIO', 'NWC'),
                                 feature_group_count=x.shape[-1])
    return y + b


def swiglu(h, w13, w2):
    a, g = jnp.split(h @ w13, 2, axis=-1)
    return (jax.nn.silu(a) * g) @ w2


def rope2d(x, cos, sin):
    xs = x.reshape(x.shape[:-1] + (2, 2, ROPE_FREQ))
    x1, x2 = xs[..., 0, :], xs[..., 1, :]
    out = jnp.stack([x1 * cos - x2 * sin, x2 * cos + x1 * sin], axis=-2)
    return out.reshape(x.shape)


def attend_blocks(q, k, v):
    b, lq, h, dq = q.shape
    nb = lq // Q_BLOCK
    qb = jnp.moveaxis(q.reshape(b, nb, Q_BLOCK, h, dq), 1, 0)
    scale = dq ** -0.5

    def one(qi):
        s = jnp.einsum('bqhd,bkhd->bhqk', qi, k).astype(jnp.float32) * scale
        p = jax.nn.softmax(s, axis=-1).astype(v.dtype)
        return jnp.einsum('bhqk,bkhd->bqhd', p, v)

    o = lax.map(one, qb)
    return jnp.moveaxis(o, 0, 1).reshape(b, lq, h, v.shape[-1])


def prefix_bidir(scan_f, scan_b, ctx_args, lat_args, h0):
    flip = lambda args: tuple(jnp.flip(a, axis=1) for a in args)
    yc_f, sc_f = scan_f(ctx_args, h0)
    yl_f, _ = scan_f(lat_args, sc_f)
    yc_b, sc_b = scan_b(flip(ctx_args), h0)
    yl_b, _ = scan_b(flip(lat_args), sc_b)
    return yc_f + jnp.flip(yc_b, axis=1), yl_f + jnp.flip(yl_b, axis=1)


def linear_scan(a, u, h0):
    def comb(l, r):
        return l[0] * r[0], r[0] * l[1] + r[1]
    a_cum, h = lax.associative_scan(comb, (a, u), axis=1)
    h = h + a_cum * h0[:, None]
    return h, h[:, -1]


def complex_linear_scan(ar, ai, ur, ui, h0):
    def comb(l, r):
        lar, lai, lur, lui = l
        rar, rai, rur, rui = r
        return (rar * lar - rai * lai, rar * lai + rai * lar,
                rar * lur - rai * lui + rur, rar * lui + rai * lur + rui)
    a_r, a_i, h_r, h_i = lax.associative_scan(comb, (ar, ai, ur, ui), axis=1)
    h0r, h0i = h0
    hr = h_r + a_r * h0r[:, None] - a_i * h0i[:, None]
    hi = h_i + a_r * h0i[:, None] + a_i * h0r[:, None]
    return (hr, hi), (hr[:, -1], hi[:, -1])


def ssd_chunked(xs, da, bm, cm, h0):
    b, n = xs.shape[:2]
    t, q = n // SSD_CHUNK, SSD_CHUNK
    xs = xs.reshape(b, t, q, M2_GROUPS, M2_HPG, M2_HEADDIM)
    da = da.reshape(b, t, q, M2_GROUPS, M2_HPG)
    bm = bm.reshape(b, t, q, M2_GROUPS, M2_STATE)
    cm = cm.reshape(b, t, q, M2_GROUPS, M2_STATE)
    acs = jnp.cumsum(da, axis=2)
    tri = jnp.tril(jnp.ones((q, q), dtype=bool))[:, :, None, None]
    decay_in = jnp.exp(jnp.where(tri, acs[:, :, :, None] - acs[:, :, None, :], -jnp.inf))
    scores = jnp.einsum('btlgn,btsgn->btlsg', cm, bm)
    y_diag = jnp.einsum('btlsg,btlsgk,btsgkp->btlgkp', scores, decay_in, xs)
    decay_out = jnp.exp(acs[:, :, -1:] - acs)
    chunk_states = jnp.einsum('btsgn,btsgk,btsgkp->btgkpn', bm, decay_out, xs)
    chunk_decay = jnp.exp(acs[:, :, -1])

    def step(h, inp):
        s, d = inp
        return h * d[..., None, None] + s, h

    h_last, h_in = lax.scan(step, h0, (jnp.moveaxis(chunk_states, 1, 0), jnp.moveaxis(chunk_decay, 1, 0)))
    y_off = jnp.einsum('btlgn,tbgkpn,btlgk->btlgkp', cm, h_in, jnp.exp(acs))
    return (y_diag + y_off).reshape(b, n, M2_GROUPS, M2_HPG, M2_HEADDIM), h_last


def mamba2_mixer(hc, hl, in_w, conv_w, conv_b, dt_bias, a_log, d_skip, norm_w, out_w, ctx_out):
    f32 = jnp.float32
    gn = M2_GROUPS * M2_STATE

    def project(h):
        b, n, _ = h.shape
        z, xbc, dt = jnp.split(h @ in_w, [M2_INNER, M2_INNER + M2_CONV_CH], axis=-1)
        xbc = jax.nn.silu(dwconv(xbc, conv_w, conv_b))
        xs, bm, cm = jnp.split(xbc, [M2_INNER, M2_INNER + gn], axis=-1)
        return (z,
                xs.reshape(b, n, M2_GROUPS, M2_HPG, M2_HEADDIM).astype(f32),
                bm.reshape(b, n, M2_GROUPS, M2_STATE).astype(f32),
                cm.reshape(b, n, M2_GROUPS, M2_STATE).astype(f32),
                dt.reshape(b, n, 2, M2_GROUPS, M2_HPG).astype(f32))

    pc = project(hc)
    pl = project(hl)

    def make_scan(dirn):
        a = -jnp.exp(a_log[dirn].astype(f32)).reshape(M2_GROUPS, M2_HPG)
        bias = dt_bias[dirn].astype(f32).reshape(M2_GROUPS, M2_HPG)

        def scan(args, h0):
            xs, bm, cm, dt = args
            dt = jax.nn.softplus(dt[:, :, dirn] + bias)
            return ssd_chunked(xs * dt[..., None], dt * a, bm, cm, h0)
        return scan

    h0 = jnp.zeros((hc.shape[0], M2_GROUPS, M2_HPG, M2_HEADDIM, M2_STATE), f32)
    yc, yl = prefix_bidir(make_scan(0), make_scan(1), pc[1:], pl[1:], h0)
    dsk = d_skip.astype(f32).reshape(M2_GROUPS, M2_HPG, 1)

    def finish(y, xs, z):
        b, n = z.shape[:2]
        y = (y + dsk * xs).reshape(b, n, M2_INNER).astype(z.dtype)
        return rms_norm(y * jax.nn.silu(z), norm_w) @ out_w

    y_c = finish(yc, pc[1], pc[0]) if ctx_out else None
    return y_c, finish(yl, pl[1], pl[0])


def rglru_mixer(hc, hl, in_w, conv_w, conv_b, gate_w, gate_b, a_param, out_w, ctx_out):
    f32 = jnp.float32

    def project(h):
        y, xr = jnp.split(h @ in_w, 2, axis=-1)
        return jax.nn.gelu(y), dwconv(xr, conv_w, conv_b)

    gy_c, xc = project(hc)
    gy_l, xl = project(hl)

    def make_scan(dirn):
        gw, gb = gate_w[dirn], gate_b[dirn]
        log_base = -LRU_C * jax.nn.softplus(-a_param[dirn].astype(f32))

        def scan(args, h0):
            (xs,) = args
            b, n, _ = xs.shape
            g = jnp.einsum('blnj,nji->blni', xs.reshape(b, n, LRU_BLOCKS, LRU_BLOCK), gw) + gb
            g = jax.nn.sigmoid(g.astype(f32))
            r = g[..., :LRU_BLOCK].reshape(b, n, LRU_WIDTH)
            i_g = g[..., LRU_BLOCK:].reshape(b, n, LRU_WIDTH)
            log_a = r * log_base
            mult = jnp.sqrt(jnp.maximum(-jnp.expm1(2.0 * log_a), 0.0))
            return linear_scan(jnp.exp(log_a), xs.astype(f32) * i_g * mult, h0)
        return scan

    h0 = jnp.zeros((hc.shape[0], LRU_WIDTH), f32)
    rc, rl = prefix_bidir(make_scan(0), make_scan(1), (xc,), (xl,), h0)
    y_c = (gy_c * rc.astype(gy_c.dtype)) @ out_w if ctx_out else None
    return y_c, (gy_l * rl.astype(gy_l.dtype)) @ out_w


def s5_mixer(hc, hl, lam_re, lam_im, log_step, b_re, b_im, c_re, c_im, d_skip, glu_w, glu_b, ctx_out):
    f32 = jnp.float32
    br, bi = b_re.astype(f32), b_im.astype(f32)

    def make_scan(dirn):
        lr = jnp.minimum(lam_re[dirn].astype(f32), -1e-4)
        li = lam_im[dirn].astype(f32)
        step = jnp.exp(log_step[dirn].astype(f32))[:, None]
        mag = jnp.exp(lr * step)
        abr, abi = mag * jnp.cos(li * step), mag * jnp.sin(li * step)
        den = lr * lr + li * li
        zr = ((abr - 1.0) * lr + abi * li) / den
        zi = (abi * lr - (abr - 1.0) * li) / den
        bbr = zr[..., None] * br - zi[..., None] * bi
        bbi = zr[..., None] * bi + zi[..., None] * br
        cr, ci = c_re[dirn].astype(f32), c_im[dirn].astype(f32)

        def scan(args, h0):
            (u,) = args
            n = u.shape[1]
            ur = jnp.einsum('blgj,gpj->blgp', u, bbr)
            ui = jnp.einsum('blgj,gpj->blgp', u, bbi)
            ar = jnp.broadcast_to(abr, (1, n) + abr.shape)
            ai = jnp.broadcast_to(abi, (1, n) + abi.shape)
            (hr, hi), state = complex_linear_scan(ar, ai, ur, ui, h0)
            y = jnp.einsum('blgp,gjp->blgj', hr, cr) - jnp.einsum('blgp,gjp->blgj', hi, ci)
            return y, state
        return scan

    b = hc.shape[0]
    uc = hc.astype(f32).reshape(b, hc.shape[1], S5_GROUPS, S5_GROUP)
    ul = hl.astype(f32).reshape(b, hl.shape[1], S5_GROUPS, S5_GROUP)
    h0 = (jnp.zeros((b, S5_GROUPS, S5_STATE), f32), jnp.zeros((b, S5_GROUPS, S5_STATE), f32))
    yc, yl = prefix_bidir(make_scan(0), make_scan(1), (uc,), (ul,), h0)
    dsk = d_skip.astype(f32).reshape(S5_GROUPS, S5_GROUP)

    def finish(y, u):
        bb, n = u.shape[:2]
        g = jax.nn.gelu((y + dsk * u).reshape(bb, n, D_MODEL)).astype(hc.dtype)
        a, gate = jnp.split(g @ glu_w + glu_b, 2, axis=-1)
        return a * jax.nn.sigmoid(gate)

    y_c = finish(yc, uc) if ctx_out else None
    return y_c, finish(yl, ul)


def mla_mixer(hc, hl, cos, sin, in_w, q_norm_w, kv_norm_w, qb_w, kvb_w, out_w, ctx_out):
    def project(h):
        b, n, _ = h.shape
        q_lat, kv_lat, k_rope = jnp.split(h @ in_w, [MLA_Q_RANK, MLA_Q_RANK + MLA_KV_RANK], axis=-1)
        q = (rms_norm(q_lat, q_norm_w) @ qb_w).reshape(b, n, MLA_HEADS, MLA_NOPE + MLA_ROPE)
        kv = (rms_norm(kv_lat, kv_norm_w) @ kvb_w).reshape(b, n, MLA_HEADS, MLA_NOPE + MLA_V)
        return q[..., :MLA_NOPE], q[..., MLA_NOPE:], kv[..., :MLA_NOPE], kv[..., MLA_NOPE:], k_rope

    def keys(k_nope, k_rope):
        shared = jnp.broadcast_to(k_rope[:, :, None, :], k_nope.shape[:3] + (MLA_ROPE,))
        return jnp.concatenate([k_nope, shared], axis=-1)

    def merge(o):
        b, n = o.shape[:2]
        return o.reshape(b, n, MLA_HEADS * MLA_V) @ out_w

    qn_c, qr_c, kn_c, v_c, kr_c = project(hc)
    qn_l, qr_l, kn_l, v_l, kr_l = project(hl)
    k_c = keys(kn_c, kr_c)
    k_l = keys(kn_l, rope2d(kr_l, cos, sin))
    q_l = jnp.concatenate([qn_l, rope2d(qr_l, cos[:, None], sin[:, None])], axis=-1)
    o_l = attend_blocks(q_l, jnp.concatenate([k_c, k_l], axis=1), jnp.concatenate([v_c, v_l], axis=1))
    y_c = merge(attend_blocks(jnp.concatenate([qn_c, qr_c], axis=-1), k_c, v_c)) if ctx_out else None
    return y_c, merge(o_l)


def setup_inputs(seed: int = 0) -> dict:
    key = jax.random.key(seed)
    ks = iter(jax.random.split(key, 64))
    f32 = jnp.float32

    def nrm(shape, scale):
        return jax.random.normal(next(ks), shape, f32) * scale

    def gain(shape):
        return 1.0 + nrm(shape, 0.02)

    def unif(shape, lo, hi):
        return jax.random.uniform(next(ks), shape, f32, lo, hi)

    na, nb, nc, nd = N_SSD_LAYERS, N_LRU_LAYERS, N_S5_LAYERS, N_MLA_LAYERS
    dt0 = jnp.exp(unif((na, 2, M2_HEADS), math.log(1e-3), math.log(1e-1)))
    lru_base = unif((nb, 2, LRU_WIDTH), 0.9, 0.999) ** (1.0 / LRU_C)
    return {
        'x': nrm((BATCH, SEQ, D_MODEL), 1.0),
        'c': nrm((BATCH, D_MODEL), 1.0),
        'ctx': nrm((BATCH, CTX_LEN, D_MODEL), 1.0),
        'c_ctx': nrm((D_MODEL,), 1.0),
        'ada_w': nrm((DEPTH, D_MODEL, 6 * D_MODEL), 0.5 * D_MODEL ** -0.5),
        'ada_b': nrm((DEPTH, 6 * D_MODEL), 0.02),
        'norm1_w': gain((DEPTH, D_MODEL)),
        'norm2_w': gain((DEPTH, D_MODEL)),
        'ffn_w13': nrm((DEPTH, D_MODEL, 2 * FFN_HIDDEN), D_MODEL ** -0.5),
        'ffn_w2': nrm((DEPTH, FFN_HIDDEN, D_MODEL), FFN_HIDDEN ** -0.5),
        'm2_in_w': nrm((na, D_MODEL, M2_PROJ), D_MODEL ** -0.5),
        'm2_conv_w': nrm((na, CONV_WIDTH, M2_CONV_CH), CONV_WIDTH ** -0.5),
        'm2_conv_b': nrm((na, M2_CONV_CH), 0.02),
        'm2_dt_bias': dt0 + jnp.log(-jnp.expm1(-dt0)),
        'm2_a_log': jnp.log(unif((na, 2, M2_HEADS), 1.0, 16.0)),
        'm2_d': gain((na, M2_HEADS)),
        'm2_norm_w': gain((na, M2_INNER)),
        'm2_out_w': nrm((na, M2_INNER, D_MODEL), M2_INNER ** -0.5),
        'lru_in_w': nrm((nb, D_MODEL, 2 * LRU_WIDTH), D_MODEL ** -0.5),
        'lru_conv_w': nrm((nb, CONV_WIDTH, LRU_WIDTH), CONV_WIDTH ** -0.5),
        'lru_conv_b': nrm((nb, LRU_WIDTH), 0.02),
        'lru_gate_w': nrm((nb, 2, LRU_BLOCKS, LRU_BLOCK, 2 * LRU_BLOCK), LRU_BLOCK ** -0.5),
        'lru_gate_b': nrm((nb, 2, LRU_BLOCKS, 2 * LRU_BLOCK), 0.02),
        'lru_a_param': jnp.log(lru_base) - jnp.log1p(-lru_base),
        'lru_out_w': nrm((nb, LRU_WIDTH, D_MODEL), LRU_WIDTH ** -0.5),
        's5_lambda_re': -0.5 + nrm((nc, 2, S5_GROUPS, S5_STATE), 0.01),
        's5_lambda_im': jnp.pi * jnp.arange(S5_STATE, dtype=f32) + nrm((nc, 2, S5_GROUPS, S5_STATE), 0.01),
        's5_log_step': unif((nc, 2, S5_GROUPS), math.log(1e-3), math.log(1e-1)),
        's5_b_re': nrm((nc, S5_GROUPS, S5_STATE, S5_GROUP), (2 * S5_GROUP) ** -0.5),
        's5_b_im': nrm((nc, S5_GROUPS, S5_STATE, S5_GROUP), (2 * S5_GROUP) ** -0.5),
        's5_c_re': nrm((nc, 2, S5_GROUPS, S5_GROUP, S5_STATE), (2 * S5_STATE) ** -0.5),
        's5_c_im': nrm((nc, 2, S5_GROUPS, S5_GROUP, S5_STATE), (2 * S5_STATE) ** -0.5),
        's5_d': nrm((nc, D_MODEL), 1.0),
        's5_glu_w': nrm((nc, D_MODEL, 2 * D_MODEL), D_MODEL ** -0.5),
        's5_glu_b': nrm((nc, 2 * D_MODEL), 0.02),
        'mla_in_w': nrm((nd, D_MODEL, MLA_IN), D_MODEL ** -0.5),
        'mla_q_norm_w': gain((nd, MLA_Q_RANK)),
        'mla_kv_norm_w': gain((nd, MLA_KV_RANK)),
        'mla_qb_w': nrm((nd, MLA_Q_RANK, MLA_HEADS * (MLA_NOPE + MLA_ROPE)), MLA_Q_RANK ** -0.5),
        'mla_kvb_w': nrm((nd, MLA_KV_RANK, MLA_HEADS * (MLA_NOPE + MLA_V)), MLA_KV_RANK ** -0.5),
        'mla_out_w': nrm((nd, MLA_HEADS * MLA_V, D_MODEL), (MLA_HEADS * MLA_V) ** -0.5),
        'final_norm_w': gain((D_MODEL,)),
    }


def reference(x, c, ctx, c_ctx, ada_w, ada_b, norm1_w, norm2_w, ffn_w13, ffn_w2,
              m2_in_w, m2_conv_w, m2_conv_b, m2_dt_bias, m2_a_log, m2_d, m2_norm_w, m2_out_w,
              lru_in_w, lru_conv_w, lru_conv_b, lru_gate_w, lru_gate_b, lru_a_param, lru_out_w,
              s5_lambda_re, s5_lambda_im, s5_log_step, s5_b_re, s5_b_im, s5_c_re, s5_c_im,
              s5_d, s5_glu_w, s5_glu_b,
              mla_in_w, mla_q_norm_w, mla_kv_norm_w, mla_qb_w, mla_kvb_w, mla_out_w,
              final_norm_w):
    n = x.shape[1]
    rows = n // GRID_W
    row = jnp.repeat(jnp.arange(rows, dtype=jnp.float32), GRID_W)
    col = jnp.tile(jnp.arange(GRID_W, dtype=jnp.float32), rows)
    inv_freq = ROPE_BASE ** (-jnp.arange(ROPE_FREQ, dtype=jnp.float32) / ROPE_FREQ)
    ang = jnp.stack([row[:, None] * inv_freq, col[:, None] * inv_freq], axis=1)
    cos, sin = jnp.cos(ang).astype(x.dtype), jnp.sin(ang).astype(x.dtype)

    cx = ctx
    for i in range(DEPTH):
        kind, j = i % N_MIXERS, i // N_MIXERS
        ctx_out = i < DEPTH - 1
        mod_l = jnp.split(jax.nn.silu(c) @ ada_w[i] + ada_b[i], 6, axis=-1)
        sh1, sc1, g1, sh2, sc2, g2 = [m[:, None, :] for m in mod_l]
        csh1, csc1, cg1, csh2, csc2, cg2 = jnp.split(jax.nn.silu(c_ctx) @ ada_w[i] + ada_b[i], 6, axis=-1)
        hl = rms_norm(x, norm1_w[i]) * (1.0 + sc1) + sh1
        hc = rms_norm(cx, norm1_w[i]) * (1.0 + csc1) + csh1
        if kind == 0:
            yc, yl = mamba2_mixer(hc, hl, m2_in_w[j], m2_conv_w[j], m2_conv_b[j], m2_dt_bias[j],
                                  m2_a_log[j], m2_d[j], m2_norm_w[j], m2_out_w[j], ctx_out)
        elif kind == 1:
            yc, yl = rglru_mixer(hc, hl, lru_in_w[j], lru_conv_w[j], lru_conv_b[j], lru_gate_w[j],
                                 lru_gate_b[j], lru_a_param[j], lru_out_w[j], ctx_out)
        elif kind == 2:
            yc, yl = s5_mixer(hc, hl, s5_lambda_re[j], s5_lambda_im[j], s5_log_step[j], s5_b_re[j],
                              s5_b_im[j], s5_c_re[j], s5_c_im[j], s5_d[j], s5_glu_w[j], s5_glu_b[j], ctx_out)
        else:
            yc, yl = mla_mixer(hc, hl, cos, sin, mla_in_w[j], mla_q_norm_w[j], mla_kv_norm_w[j],
                               mla_qb_w[j], mla_kvb_w[j], mla_out_w[j], ctx_out)
        x = x + g1 * yl
        x = x + g2 * swiglu(rms_norm(x, norm2_w[i]) * (1.0 + sc2) + sh2, ffn_w13[i], ffn_w2[i])
        if ctx_out:
            cx = cx + cg1 * yc
            cx = cx + cg2 * swiglu(rms_norm(cx, norm2_w[i]) * (1.0 + csc2) + csh2, ffn_w13[i], ffn_w2[i])
    return rms_norm(x, final_norm_w)
```

```python
import numpy as np
from contextlib import ExitStack
import concourse.bass as bass
import concourse.mybir as mybir
from concourse.bass_utils import run_bass_kernel_spmd

F32 = mybir.dt.float32
BF16 = mybir.dt.bfloat16
I32 = mybir.dt.int32
AF = mybir.ActivationFunctionType
ALU = mybir.AluOpType
AX = mybir.AxisListType

D = 1024
NCTX = 256
NLAT = 8192
NTOK = NCTX + NLAT
FH = 2816
EPS = 1e-6


class Buf:
    __slots__ = ("name", "lw", "rd")

    def __init__(self, name=""):
        self.name = name
        self.lw = None
        self.rd = []


ATTACH_WAITS = True


class K:
    NDMA = 8

    def __init__(self, nc, es):
        self.nc = nc
        self.es = es
        self.eng = {"pe": nc.tensor, "dve": nc.vector, "act": nc.scalar, "pool": nc.gpsimd, "sp": nc.sync}
        self.sem = {}
        self.cnt = {}
        for e in self.eng:
            self.sem[e] = es.enter_context(nc.semaphore("s_" + e))
            self.cnt[e] = 0
        self.dslots = {}
        for q in ("sp", "pool", "act"):
            self.dslots[q] = []
            for j in range(self.NDMA):
                key = "d_%s_%d" % (q, j)
                self.sem[key] = es.enter_context(nc.semaphore(key))
                self.cnt[key] = 0
                self.dslots[q].append(key)
        self.dnext = {q: 0 for q in self.dslots}
        self.seen = {e: {} for e in self.eng}
        self.ninst = 0

    def _need(self, e, deps, attach=False):
        best = {}
        for (k, v) in deps:
            if k == e and e == "pe":
                continue
            if best.get(k, 0) < v:
                best[k] = v
        todo = [(k, v) for k, v in best.items() if self.seen[e].get(k, 0) < v]
        held = None
        if attach and ATTACH_WAITS and todo:
            held = todo.pop()
        for k, v in todo:
            self.eng[e].wait_ge(self.sem[k], v)
            self.seen[e][k] = v
            self.ninst += 1
        if held is not None:
            self.seen[e][held[0]] = held[1]
        return held

    def _deps(self, reads, writes):
        deps = []
        for b in reads:
            if b.lw is not None:
                deps.append(b.lw)
        for b in writes:
            if b.lw is not None:
                deps.append(b.lw)
            deps.extend(b.rd)
        return deps

    def op(self, e, fn, reads=(), writes=(), inc=True):
        held = self._need(e, self._deps(reads, writes), attach=True)
        ins = fn(self.eng[e])
        if held is not None:
            ins._wait_ge(self.sem[held[0]], held[1])
        self.ninst += 1
        if inc:
            self.cnt[e] += 1
            ins.then_inc(self.sem[e], 1)
            v = self.cnt[e]
        else:
            v = self.cnt[e] + 1
        for b in reads:
            b.rd.append((e, v))
            if len(b.rd) > 24:
                b.rd = self._compact(b.rd)
        for b in writes:
            b.lw = (e, v)
            b.rd = []
        return ins

    @staticmethod
    def _compact(rd):
        best = {}
        for (k, v) in rd:
            if best.get(k, 0) < v:
                best[k] = v
        return list(best.items())

    def dma(self, q, out, in_, reads=(), writes=(), **kw):
        if out.dtype != in_.dtype:
            q = "pool"
        slot = self.dslots[q][self.dnext[q] % self.NDMA]
        self.dnext[q] += 1
        deps = self._deps(reads, writes)
        if self.cnt[slot] > 0:
            deps.append((slot, self.cnt[slot]))
        self._need(q, deps)
        ins = self.eng[q].dma_start(out=out, in_=in_, **kw)
        self.ninst += 1
        self.cnt[slot] += 16
        ins.then_inc(self.sem[slot], 16)
        v = self.cnt[slot]
        for b in reads:
            b.rd.append((slot, v))
            if len(b.rd) > 24:
                b.rd = self._compact(b.rd)
        for b in writes:
            b.lw = (slot, v)
            b.rd = []
        return ins

    def barrier(self):
        deps = [(k, v) for k, v in self.cnt.items() if v > 0]
        for e in self.eng:
            self._need(e, deps)


def build(layers=(0, 1, 2, 3), do_final=True, debug=None, debug_sub=None):
    nc = bass.Bass("TRN2", target_bir_lowering=False)
    di = {}

    def inp(name, shape):
        di[name] = nc.dram_tensor(name, list(shape), F32, kind="ExternalInput").ap()
        return di[name]

    xT_in = inp("xT", [D, NTOK])
    cc_in = inp("cc", [D, 2])
    ident_in = inp("ident", [128, 128])
    ada_w = inp("ada_w", [4, D, 6 * D]); ada_b = inp("ada_b", [4, 6 * D])
    norm1_w = inp("norm1_w", [4, D]); norm2_w = inp("norm2_w", [4, D])
    ffn_w13 = inp("ffn_w13", [4, D, 2 * FH]); ffn_w2 = inp("ffn_w2", [4, FH, D])
    lru_in_w = inp("lru_in_w", [1, D, 2560]); lru_conv_w = inp("lru_conv_w", [1, 4, 1280]); lru_conv_b = inp("lru_conv_b", [1, 1280])
    lru_gate_w = inp("lru_gate_w", [1, 2, 10, 128, 256]); lru_gate_b = inp("lru_gate_b", [1, 2, 10, 256])
    lru_a_param = inp("lru_a_param", [1, 2, 1280]); lru_out_w = inp("lru_out_w", [1, 1280, D])
    masks_in = inp("masks", [4, 128, 128])
    ramp_in = inp("ramp", [128, 128]); sel_in = inp("sel", [128, 8])
    s5_lre = inp("s5_lambda_re", [1, 2, 64, 64]); s5_lim = inp("s5_lambda_im", [1, 2, 64, 64]); s5_ls = inp("s5_log_step", [1, 2, 64])
    s5_bre = inp("s5_b_re", [1, 64, 64, 16]); s5_bim = inp("s5_b_im", [1, 64, 64, 16])
    s5_cre = inp("s5_c_re", [1, 2, 64, 16, 64]); s5_cim = inp("s5_c_im", [1, 2, 64, 16, 64])
    s5_d = inp("s5_d", [1, D]); s5_glu_w = inp("s5_glu_w", [1, D, 2 * D]); s5_glu_b = inp("s5_glu_b", [1, 2 * D])
    m2_in_w = inp("m2_in_w", [1, D, 5184]); m2_conv_w = inp("m2_conv_w", [1, 4, 3072]); m2_conv_b = inp("m2_conv_b", [1, 3072])
    m2_dt_bias = inp("m2_dt_bias", [1, 2, 32]); m2_a_log = inp("m2_a_log", [1, 2, 32]); m2_d = inp("m2_d", [1, 32])
    m2_norm_w = inp("m2_norm_w", [1, 2048]); m2_out_w = inp("m2_out_w", [1, 2048, D])
    mla_in_w = inp("mla_in_w", [1, D, 672]); mla_q_norm_w = inp("mla_q_norm_w", [1, 384]); mla_kv_norm_w = inp("mla_kv_norm_w", [1, 256])
    mla_qb_w = inp("mla_qb_w", [1, 384, 1536]); mla_kvb_w = inp("mla_kvb_w", [1, 256, 2048]); mla_out_w = inp("mla_out_w", [1, D, D])
    mla_inw_sw = inp("mla_inw_sw", [D, 32]); mla_qbw_sw = inp("mla_qbw_sw", [384, 512])
    ropeC_in = inp("ropeC", [32, NLAT]); ropeS_in = inp("ropeS", [32, NLAT])
    e96_in = inp("e96", [128, 128]); odm_in = inp("odm", [2, 128, 128]); swp_in = inp("swp", [128, 128])
    final_norm_w = inp("final_norm_w", [D])
    outT = nc.dram_tensor("outT", [D, NLAT], F32, kind="ExternalOutput").ap()
    xres = nc.dram_tensor("xres", [D, NTOK], F32).ap()
    scrA = nc.dram_tensor("scrA", [2048, NTOK], F32).ap()
    scrB = nc.dram_tensor("scrB", [3072, NTOK], F32).ap()
    scrM = nc.dram_tensor("scrM", [2048, NTOK], BF16).ap()
    NCH = NTOK // 128
    scrZ = nc.dram_tensor("scrZ", [NTOK, 2048], F32).ap()
    scrDT = nc.dram_tensor("scrDT", [NTOK, 64], F32).ap()
    scrS = nc.dram_tensor("scrS", [2, NCH, 128, 2048], BF16).ap()
    scrXB = nc.dram_tensor("scrXB", [3072, NTOK], BF16).ap()
    scrH = nc.dram_tensor("scrH", [2, NCH, 128, 2048], BF16).ap()
    scrDec = nc.dram_tensor("scrDec", [NCH, 128, 64], F32).ap()
    scrY = nc.dram_tensor("scrY", [NCH, 128, 2048], F32).ap()
    scrC = nc.dram_tensor("scrC", [NCH, 128, 512], BF16).ap()
    scrE = nc.dram_tensor("scrE", [NCH, 128, 64], F32).ap()

    es = ExitStack()
    with es:
        k = K(nc, es)
        uniq = [0]

        def sb(name, shape, dt, st=es):
            uniq[0] += 1
            return st.enter_context(nc.sbuf_tensor("%s_%d" % (name, uniq[0]), shape, dt))

        ps = [es.enter_context(nc.psum_tensor("ps%d" % i, [128, 512], F32)) for i in range(8)]
        psb = [Buf("ps%d" % i) for i in range(8)]
        psn = [0]

        pspool = [list(range(8))]

        def next_ps():
            pl = pspool[0]
            i = pl[psn[0] % len(pl)]
            psn[0] += 1
            return ps[i], psb[i]

        ones_f = sb("ones_f", [128, 128], F32); b_ones = Buf()
        k.op("pool", lambda e: e.memset(ones_f[:], 1.0), writes=[b_ones])
        ident_f = sb("ident_f", [128, 128], F32); b_ident = Buf()
        k.dma("sp", ident_f[:], ident_in, writes=[b_ident])
        ident_b = sb("ident_b", [128, 128], BF16); b_identb = Buf()
        k.dma("sp", ident_b[:], ident_in, writes=[b_identb])
        ccT = sb("ccT", [128, 8, 2], F32); b_cc = Buf()
        k.dma("sp", ccT[:], cc_in.rearrange("(c p) v -> p c v", p=128), writes=[b_cc])
        scT = sb("scT", [128, 8, 2], BF16); b_sc = Buf()
        k.op("act", lambda e: e.activation(scT[:], ccT[:], AF.Silu), reads=[b_cc], writes=[b_sc])
        mod = sb("mod", [128, 48, 2], F32); b_mod = Buf()
        adab = sb("adab", [128, 48], F32); b_adab = Buf()
        nw1 = sb("nw1", [128, 8], F32); nw2 = sb("nw2", [128, 8], F32); b_nw = Buf()
        A1 = sb("A1", [128, 8, 2], F32); A2 = sb("A2", [128, 8, 2], F32); b_A = Buf()

        def seg_tiles(tt):
            res = []
            t = 0
            while t < NCTX:
                n = min(tt, NCTX - t)
                res.append((t, n, 1))
                t += n
            while t < NTOK:
                n = min(tt, NTOK - t)
                res.append((t, n, 0))
                t += n
            return res

        def load_w(st, name, w_ap, kc, ncols, q="sp"):
            t = sb(name, [128, kc, ncols], BF16, st)
            b = Buf(name)
            src = w_ap.rearrange("(c p) n -> p c n", p=128)
            for c in range(kc):
                k.dma(q if c % 2 == 0 else "pool", t[:, c, :], src[:, c, :], writes=[b])
            return t, b

        def compute_mod(li):
            with ExitStack() as st:
                k.dma("sp", adab[:], ada_b[li].rearrange("(j p) -> p j", p=128), writes=[b_adab], allow_slow_non_contiguous=True)
                k.dma("sp", nw1[:], norm1_w[li].rearrange("(c p) -> p c", p=128), writes=[b_nw], allow_slow_non_contiguous=True)
                k.dma("sp", nw2[:], norm2_w[li].rearrange("(c p) -> p c", p=128), writes=[b_nw], allow_slow_non_contiguous=True)
                wt = [sb("adaw%d" % i, [128, 8, 1024], BF16, st) for i in range(2)]
                wb = [Buf(), Buf()]
                for j in range(6):
                    w, b = wt[j % 2], wb[j % 2]
                    src = ada_w[li][:, j * 1024:(j + 1) * 1024].rearrange("(c p) n -> p c n", p=128)
                    for c in range(8):
                        k.dma("sp" if c % 2 == 0 else "pool", w[:, c, :], src[:, c, :], writes=[b])
                    for m in range(8):
                        p, pb = next_ps()
                        for c in range(8):
                            k.op("pe", lambda e, p=p, w=w, c=c, m=m: e.matmul(p[:, 0:2], w[:, c, m * 128:(m + 1) * 128], scT[:, c, :],
                                                                          start=(c == 0), stop=(c == 7)),
                                 reads=[b, b_sc], writes=[pb], inc=(c == 7))
                        jj = j * 8 + m
                        k.op("act", lambda e, p=p, jj=jj: e.activation(mod[:, jj, :], p[:, 0:2], AF.Identity, bias=adab[:, jj:jj + 1], scale=1.0),
                             reads=[pb, b_adab], writes=[b_mod])
                for v in range(2):
                    k.op("dve", lambda e, v=v: e.scalar_tensor_tensor(A1[:, :, v], mod[:, 8:16, v], 1.0, nw1[:], ALU.add, ALU.mult),
                         reads=[b_mod, b_nw], writes=[b_A])
                    k.op("dve", lambda e, v=v: e.scalar_tensor_tensor(A2[:, :, v], mod[:, 32:40, v], 1.0, nw2[:], ALU.add, ALU.mult),
                         reads=[b_mod, b_nw], writes=[b_A])
            k.barrier()

        def norm_mod(st_tiles, xt, bx, n, vsel, A, shoff, hT, bh):
            sq, bsq, rstd, brs = st_tiles["sq"], st_tiles["bsq"], st_tiles["rstd"], st_tiles["brs"]
            p, pb = next_ps()
            for c in range(8):
                k.op("act", lambda e, c=c: e.activation(sq[:, :n], xt[:, c, :n], AF.Square), reads=[bx], writes=[bsq])
                k.op("pe", lambda e, c=c: e.matmul(p[:, :n], ones_f[:], sq[:, :n], start=(c == 0), stop=(c == 7)),
                     reads=[b_ones, bsq], writes=[pb])
            k.op("act", lambda e: e.activation(rstd[:, :n], p[:, :n], AF.Sqrt, bias=EPS, scale=1.0 / D), reads=[pb], writes=[brs])
            k.op("dve", lambda e: e.reciprocal(rstd[:, :n], rstd[:, :n]), reads=[brs], writes=[brs])
            for c in range(8):
                k.op("dve", lambda e, c=c: e.tensor_tensor(sq[:, :n], xt[:, c, :n], rstd[:, :n], ALU.mult), reads=[bx, brs], writes=[bsq])
                k.op("act", lambda e, c=c: e.activation(hT[:, c, :n], sq[:, :n], AF.Identity, bias=mod[:, shoff + c, vsel:vsel + 1],
                                                        scale=A[:, c, vsel:vsel + 1]),
                     reads=[bsq, b_mod, b_A], writes=[bh])

        def norm_scratch(st, tt):
            return {"sq": sb("n_sq", [128, tt], F32, st), "bsq": Buf(), "rstd": sb("n_rstd", [128, tt], F32, st), "brs": Buf()}

        def outproj_phase(w_ap, kc, src_scr, ctx_out):
            TT = 512
            with ExitStack() as st:
                w, bw = load_w(st, "opw", w_ap, kc, D)
                mt = [sb("op_m%d" % i, [128, kc, TT], BF16, st) for i in range(2)]; bm = [Buf(), Buf()]
                xt = [sb("op_x%d" % i, [128, 8, TT], F32, st) for i in range(2)]; bx = [Buf(), Buf()]
                for ti, (t0, n, vsel) in enumerate(seg_tiles(TT)):
                    if vsel == 1 and not ctx_out:
                        continue
                    m, b_m, x, b_x = mt[ti % 2], bm[ti % 2], xt[ti % 2], bx[ti % 2]
                    k.dma("sp", m[:, :, :n], src_scr[0:kc * 128, t0:t0 + n].rearrange("(c p) t -> p c t", p=128), writes=[b_m])
                    k.dma("pool", x[:, :, :n], xres[:, t0:t0 + n].rearrange("(c p) t -> p c t", p=128), writes=[b_x])
                    for c in range(8):
                        p, pb = next_ps()
                        for j in range(kc):
                            k.op("pe", lambda e, p=p, j=j, c=c, m=m: e.matmul(p[:, :n], w[:, j, c * 128:(c + 1) * 128], m[:, j, :n],
                                                                          start=(j == 0), stop=(j == kc - 1)),
                                 reads=[bw, b_m], writes=[pb], inc=(j == kc - 1))
                        k.op("dve", lambda e, p=p, c=c, x=x: e.scalar_tensor_tensor(x[:, c, :n], p[:, :n], mod[:, 16 + c, vsel:vsel + 1], x[:, c, :n],
                                                                                 ALU.mult, ALU.add),
                             reads=[pb, b_mod, b_x], writes=[b_x])
                    k.dma("pool", xres[:, t0:t0 + n].rearrange("(c p) t -> p c t", p=128), x[:, :, :n], reads=[b_x], writes=[Buf()])
            k.barrier()

        def resid_phase(src_scr, ctx_out):
            TT = 512
            with ExitStack() as st:
                yt = [sb("rp_y%d" % i, [128, 8, TT], F32, st) for i in range(2)]; by = [Buf(), Buf()]
                xt = [sb("rp_x%d" % i, [128, 8, TT], F32, st) for i in range(2)]; bx = [Buf(), Buf()]
                for ti, (t0, n, vsel) in enumerate(seg_tiles(TT)):
                    if vsel == 1 and not ctx_out:
                        continue
                    y, b_y, x, b_x = yt[ti % 2], by[ti % 2], xt[ti % 2], bx[ti % 2]
                    k.dma("sp", y[:, :, :n], src_scr[0:D, t0:t0 + n].rearrange("(c p) t -> p c t", p=128), writes=[b_y])
                    k.dma("pool", x[:, :, :n], xres[:, t0:t0 + n].rearrange("(c p) t -> p c t", p=128), writes=[b_x])
                    for c in range(8):
                        k.op("dve", lambda e, c=c, x=x, y=y: e.scalar_tensor_tensor(x[:, c, :n], y[:, c, :n], mod[:, 16 + c, vsel:vsel + 1], x[:, c, :n],
                                                                                 ALU.mult, ALU.add),
                             reads=[b_y, b_mod, b_x], writes=[b_x])
                    k.dma("pool", xres[:, t0:t0 + n].rearrange("(c p) t -> p c t", p=128), x[:, :, :n], reads=[b_x], writes=[Buf()])
            k.barrier()

        def ffn_phase(li, ctx_out):
            TT = 256
            NJ = FH // 128
            with ExitStack() as st:
                w13, b13 = load_w(st, "w13", ffn_w13[li], 8, 2 * FH)
                w2, b2 = load_w(st, "w2", ffn_w2[li], NJ, D)
                nscs = [norm_scratch(st, TT) for _ in range(2)]
                xt = [sb("f_x%d" % i, [128, 8, TT], F32, st) for i in range(2)]; bx = [Buf(), Buf()]
                hTs = [sb("f_h%d" % i, [128, 8, TT], BF16, st) for i in range(2)]; bhs = [Buf(), Buf()]
                sT = sb("f_s", [128, NJ, TT], BF16, st); bs = Buf()
                sa = [sb("f_sa%d" % i, [128, TT], F32, st) for i in range(2)]; bsa = [Buf(), Buf()]
                tl = [t for t in seg_tiles(TT) if not (t[2] == 1 and not ctx_out)]

                def prep(i):
                    t0, n, vsel = tl[i]
                    k.dma("sp", xt[i % 2][:, :, :n], xres[:, t0:t0 + n].rearrange("(c p) t -> p c t", p=128), writes=[bx[i % 2]])
                    norm_mod(nscs[i % 2], xt[i % 2], bx[i % 2], n, vsel, A2, 24, hTs[i % 2], bhs[i % 2])

                prep(0)
                for i, (t0, n, vsel) in enumerate(tl):
                    x, b_x = xt[i % 2], bx[i % 2]
                    hT, bh = hTs[i % 2], bhs[i % 2]
                    for j in range(NJ):
                        pa, pab = next_ps()
                        pg, pgb = next_ps()
                        for c in range(8):
                            k.op("pe", lambda e, pa=pa, c=c, j=j: e.matmul(pa[:, :n], w13[:, c, j * 128:(j + 1) * 128], hT[:, c, :n],
                                                                       start=(c == 0), stop=(c == 7)),
                                 reads=[b13, bh], writes=[pab], inc=(c == 7))
                        for c in range(8):
                            k.op("pe", lambda e, pg=pg, c=c, j=j: e.matmul(pg[:, :n], w13[:, c, FH + j * 128:FH + (j + 1) * 128], hT[:, c, :n],
                                                                       start=(c == 0), stop=(c == 7)),
                                 reads=[b13, bh], writes=[pgb], inc=(c == 7))
                        s_a, b_sa = sa[j % 2], bsa[j % 2]
                        k.op("act", lambda e, pa=pa, s_a=s_a: e.activation(s_a[:, :n], pa[:, :n], AF.Silu), reads=[pab], writes=[b_sa])
                        k.op("dve", lambda e, pg=pg, s_a=s_a, j=j: e.tensor_tensor(sT[:, j, :n], s_a[:, :n], pg[:, :n], ALU.mult),
                             reads=[b_sa, pgb], writes=[bs])
                    if i + 1 < len(tl):
                        prep(i + 1)
                    for c in range(8):
                        p, pb = next_ps()
                        for j in range(NJ):
                            k.op("pe", lambda e, p=p, c=c, j=j: e.matmul(p[:, :n], w2[:, j, c * 128:(c + 1) * 128], sT[:, j, :n],
                                                                     start=(j == 0), stop=(j == NJ - 1)),
                                 reads=[b2, bs], writes=[pb], inc=(j == NJ - 1))
                        k.op("dve", lambda e, p=p, c=c, x=x: e.scalar_tensor_tensor(x[:, c, :n], p[:, :n], mod[:, 40 + c, vsel:vsel + 1], x[:, c, :n],
                                                                                 ALU.mult, ALU.add),
                             reads=[pb, b_mod, b_x], writes=[b_x])
                    k.dma("sp", xres[:, t0:t0 + n].rearrange("(c p) t -> p c t", p=128), x[:, :, :n], reads=[b_x], writes=[Buf()])
            k.barrier()

        def final_phase():
            TT = 512
            with ExitStack() as st:
                fw = sb("fnw", [128, 8], F32, st); bfw = Buf()
                k.dma("sp", fw[:], final_norm_w.rearrange("(c p) -> p c", p=128), writes=[bfw], allow_slow_non_contiguous=True)
                nsc = norm_scratch(st, TT)
                xt = [sb("fn_x%d" % i, [128, 8, TT], F32, st) for i in range(2)]; bx = [Buf(), Buf()]
                for ti, (t0, n, vsel) in enumerate(seg_tiles(TT)):
                    if vsel == 1:
                        continue
                    x, b_x = xt[ti % 2], bx[ti % 2]
                    k.dma("sp", x[:, :, :n], xres[:, t0:t0 + n].rearrange("(c p) t -> p c t", p=128), writes=[b_x])
                    sq, bsq, rstd, brs = nsc["sq"], nsc["bsq"], nsc["rstd"], nsc["brs"]
                    p, pb = next_ps()
                    for c in range(8):
                        k.op("act", lambda e, c=c, x=x: e.activation(sq[:, :n], x[:, c, :n], AF.Square), reads=[b_x], writes=[bsq])
                        k.op("pe", lambda e, c=c, p=p: e.matmul(p[:, :n], ones_f[:], sq[:, :n], start=(c == 0), stop=(c == 7)),
                             reads=[b_ones, bsq], writes=[pb])
                    k.op("act", lambda e, p=p: e.activation(rstd[:, :n], p[:, :n], AF.Sqrt, bias=EPS, scale=1.0 / D), reads=[pb], writes=[brs])
                    k.op("dve", lambda e: e.reciprocal(rstd[:, :n], rstd[:, :n]), reads=[brs], writes=[brs])
                    for c in range(8):
                        k.op("dve", lambda e, c=c, x=x: e.scalar_tensor_tensor(x[:, c, :n], x[:, c, :n], fw[:, c:c + 1], rstd[:, :n], ALU.mult, ALU.mult),
                             reads=[b_x, brs, bfw], writes=[b_x])
                    k.dma("pool", outT[:, t0 - NCTX:t0 - NCTX + n].rearrange("(c p) t -> p c t", p=128), x[:, :, :n], reads=[b_x], writes=[Buf()])
            k.barrier()

        def lru_layer(li, ctx_out):
            TT = 512
            gyT = scrA
            xrT = scrB
            hfT = scrB[1280:2560]
            with ExitStack() as st:
                w, bw = load_w(st, "lin", lru_in_w[0], 8, 2560)
                nsc = norm_scratch(st, TT)
                xt = [sb("la_x%d" % i, [128, 8, TT], F32, st) for i in range(2)]; bx = [Buf(), Buf()]
                hT = sb("la_h", [128, 8, TT], BF16, st); bh = Buf()
                og = [sb("la_o%d" % i, [128, TT], F32, st) for i in range(4)]; bog = [Buf() for _ in range(4)]
                on = 0
                for ti, (t0, n, vsel) in enumerate(seg_tiles(TT)):
                    x, b_x = xt[ti % 2], bx[ti % 2]
                    k.dma("sp", x[:, :, :n], xres[:, t0:t0 + n].rearrange("(c p) t -> p c t", p=128), writes=[b_x])
                    norm_mod(nsc, x, b_x, n, vsel, A1, 0, hT, bh)
                    for m in range(20):
                        p, pb = next_ps()
                        for c in range(8):
                            k.op("pe", lambda e, p=p, c=c, m=m: e.matmul(p[:, :n], w[:, c, m * 128:(m + 1) * 128], hT[:, c, :n],
                                                                     start=(c == 0), stop=(c == 7)),
                                 reads=[bw, bh], writes=[pb], inc=(c == 7))
                        o, bo = og[on % 4], bog[on % 4]
                        on += 1
                        if m < 10:
                            k.op("act", lambda e, p=p, o=o: e.activation(o[:, :n], p[:, :n], AF.Gelu_apprx_tanh), reads=[pb], writes=[bo])
                            k.dma("sp", gyT[m * 128:(m + 1) * 128, t0:t0 + n], o[:, :n], reads=[bo], writes=[Buf()])
                        else:
                            k.op("dve", lambda e, p=p, o=o: e.tensor_copy(o[:, :n], p[:, :n]), reads=[pb], writes=[bo])
                            k.dma("pool", xrT[(m - 10) * 128:(m - 9) * 128, t0:t0 + n], o[:, :n], reads=[bo], writes=[Buf()])
            k.barrier()
            with ExitStack() as st:
                gw = sb("lgw", [128, 2, 10, 256], BF16, st); bgw = Buf()
                for d in range(2):
                    for blk in range(10):
                        k.dma("sp" if blk % 2 else "pool", gw[:, d, blk, :], lru_gate_w[0, d, blk], writes=[bgw])
                gb = sb("lgb", [128, 2, 10, 2], F32, st); bgb = Buf()
                k.dma("sp", gb[:], lru_gate_b[0].rearrange("d b (h p) -> p d b h", p=128), writes=[bgb], allow_slow_non_contiguous=True)
                cw = sb("lcw", [128, 4, 10], F32, st); cb = sb("lcb", [128, 10], F32, st); bcw = Buf()
                k.dma("sp", cw[:], lru_conv_w[0].rearrange("k (b p) -> p k b", p=128), writes=[bcw], allow_slow_non_contiguous=True)
                k.dma("sp", cb[:], lru_conv_b[0].rearrange("(b p) -> p b", p=128), writes=[bcw], allow_slow_non_contiguous=True)
                lb = sb("llb", [128, 2, 10], F32, st); lb2 = sb("llb2", [128, 2, 10], F32, st); blb = Buf()
                k.dma("sp", lb[:], lru_a_param[0].rearrange("d (b p) -> p d b", p=128), writes=[blb], allow_slow_non_contiguous=True)
                k.op("act", lambda e: e.activation(lb[:], lb[:], AF.Exp, scale=-1.0), reads=[blb], writes=[blb])
                k.op("act", lambda e: e.activation(lb[:], lb[:], AF.Ln, bias=1.0, scale=1.0), reads=[blb], writes=[blb])
                k.op("dve", lambda e: e.tensor_scalar(lb[:], lb[:], -8.0, None, ALU.mult), reads=[blb], writes=[blb])
                k.op("dve", lambda e: e.tensor_scalar(lb2[:], lb[:], 2.0, None, ALU.mult), reads=[blb], writes=[blb])
                HB = TT + 3
                xr = [sb("lb_xr%d" % i, [128, HB], F32, st) for i in range(4)]; bxr = [Buf() for _ in range(4)]
                xc = [sb("lb_xc%d" % i, [128, TT], F32, st) for i in range(4)]; bxc = [Buf() for _ in range(4)]
                xcb = [sb("lb_xcb%d" % i, [128, TT], BF16, st) for i in range(4)]; bxcb = [Buf() for _ in range(4)]
                ta = [sb("lb_a%d" % i, [128, TT], F32, st) for i in range(4)]; bta = [Buf() for _ in range(4)]
                tm = [sb("lb_m%d" % i, [128, TT], F32, st) for i in range(4)]; btm = [Buf() for _ in range(4)]
                tu = [sb("lb_u%d" % i, [128, TT], F32, st) for i in range(4)]; btu = [Buf() for _ in range(4)]
                th = [sb("lb_h%d" % i, [128, TT], F32, st) for i in range(4)]; bth = [Buf() for _ in range(4)]
                stc = sb("lb_stc", [128, 2, 10], F32, st); bstc = [[Buf() for _ in range(10)] for _ in range(2)]
                k.op("pool", lambda e: e.memset(stc[:], 0.0), writes=[b for r in bstc for b in r])
                tg = [sb("lb_g%d" % i, [128, TT], F32, st) for i in range(4)]; btg = [Buf() for _ in range(4)]
                to = [sb("lb_o%d" % i, [128, TT], BF16, st) for i in range(4)]; bto = [Buf() for _ in range(4)]
                zero = sb("lb_z", [128, 1], F32, st); bz = Buf()
                k.op("pool", lambda e: e.memset(zero[:], 0.0), writes=[bz])
                tiles = seg_tiles(TT)
                segs = {1: (0, NCTX), 0: (NCTX, NTOK)}
                it = 0
                hn = 0
                hfbufs = {}
                for d in range(2):
                    order = tiles if d == 0 else ([tiles[0]] + tiles[:0:-1])
                    for (t0, n, vsel) in order:
                      for blk in range(10):
                        if True:
                            if vsel == 1 and not ctx_out and False:
                                continue
                            s0, s1 = segs[vsel]
                            i2 = it % 4
                            it += 1
                            x_r, b_xr = xr[i2], bxr[i2]
                            lo = max(t0 - 2, s0); hi = min(t0 + n + 1, s1)
                            if lo > t0 - 2 or hi < t0 + n + 1:
                                k.op("pool", lambda e, x_r=x_r: e.memset(x_r[:], 0.0), writes=[b_xr])
                            k.dma("sp", x_r[:, lo - (t0 - 2):hi - (t0 - 2)], xrT[blk * 128:(blk + 1) * 128, lo:hi], writes=[b_xr])
                            x_c, b_xc = xc[i2], bxc[i2]
                            k.op("act", lambda e, x_c=x_c, x_r=x_r: e.activation(x_c[:, :n], x_r[:, 0:n], AF.Identity, bias=cb[:, blk:blk + 1],
                                                                                scale=cw[:, 0, blk:blk + 1]),
                                 reads=[b_xr, bcw], writes=[b_xc])
                            for kk in range(1, 4):
                                k.op("dve", lambda e, kk=kk, x_c=x_c, x_r=x_r: e.scalar_tensor_tensor(x_c[:, :n], x_r[:, kk:kk + n], cw[:, kk, blk:blk + 1],
                                                                                                    x_c[:, :n], ALU.mult, ALU.add),
                                     reads=[b_xr, bcw, b_xc], writes=[b_xc])
                            x_cb, b_xcb = xcb[i2], bxcb[i2]
                            k.op("pool", lambda e, x_cb=x_cb, x_c=x_c: e.tensor_copy(x_cb[:, :n], x_c[:, :n]), reads=[b_xc], writes=[b_xcb])
                            pr, prb = next_ps()
                            pi, pib = next_ps()
                            k.op("pe", lambda e, pr=pr, x_cb=x_cb: e.matmul(pr[:, :n], gw[:, d, blk, 0:128], x_cb[:, :n], start=True, stop=True),
                                 reads=[bgw, b_xcb], writes=[prb])
                            k.op("pe", lambda e, pi=pi, x_cb=x_cb: e.matmul(pi[:, :n], gw[:, d, blk, 128:256], x_cb[:, :n], start=True, stop=True),
                                 reads=[bgw, b_xcb], writes=[pib])
                            a_, b_a = ta[i2], bta[i2]
                            m_, b_m = tm[i2], btm[i2]
                            u_, b_u = tu[i2], btu[i2]
                            k.op("act", lambda e, a_=a_, pr=pr: e.activation(a_[:, :n], pr[:, :n], AF.Sigmoid, bias=gb[:, d, blk, 0:1], scale=1.0),
                                 reads=[prb, bgb], writes=[b_a])
                            k.op("act", lambda e, u_=u_, pi=pi: e.activation(u_[:, :n], pi[:, :n], AF.Sigmoid, bias=gb[:, d, blk, 1:2], scale=1.0),
                                 reads=[pib, bgb], writes=[b_u])
                            k.op("act", lambda e, m_=m_, a_=a_: e.activation(m_[:, :n], a_[:, :n], AF.Exp, scale=lb2[:, d, blk:blk + 1]),
                                 reads=[b_a, blb], writes=[b_m])
                            k.op("act", lambda e, m_=m_: e.activation(m_[:, :n], m_[:, :n], AF.Sqrt, bias=1.0, scale=-1.0), reads=[b_m], writes=[b_m])
                            k.op("act", lambda e, a_=a_: e.activation(a_[:, :n], a_[:, :n], AF.Exp, scale=lb[:, d, blk:blk + 1]),
                                 reads=[b_a, blb], writes=[b_a])
                            k.op("dve", lambda e, u_=u_, x_c=x_c: e.tensor_tensor(u_[:, :n], u_[:, :n], x_c[:, :n], ALU.mult), reads=[b_u, b_xc], writes=[b_u])
                            k.op("pool", lambda e, u_=u_, m_=m_: e.tensor_tensor(u_[:, :n], u_[:, :n], m_[:, :n], ALU.mult), reads=[b_u, b_m], writes=[b_u])
                            h_, b_h = th[hn % 4], bth[hn % 4]
                            hn += 1
                            init = stc[:, d, blk:blk + 1]
                            if d == 0:
                                k.op("dve", lambda e, h_=h_, a_=a_, u_=u_, init=init: e.tensor_tensor_scan(h_[:, :n], a_[:, :n], u_[:, :n], init, ALU.mult, ALU.add),
                                     reads=[b_a, b_u, bstc[d][blk]], writes=[b_h])
                                k.op("act", lambda e, h_=h_: e.copy(stc[:, d, blk:blk + 1], h_[:, n - 1:n]), reads=[b_h], writes=[bstc[d][blk]])
                                hb = Buf()
                                hfbufs[(blk, t0)] = hb
                                k.dma("pool", hfT[blk * 128:(blk + 1) * 128, t0:t0 + n], h_[:, :n], reads=[b_h], writes=[hb])
                            else:
                                k.op("dve", lambda e, h_=h_, a_=a_, u_=u_, init=init: e.tensor_tensor_scan(h_[:, n - 1::-1] if n == h_.shape[1] else h_[:, n - 1::-1],
                                                                                                         a_[:, n - 1::-1], u_[:, n - 1::-1], init, ALU.mult, ALU.add),
                                     reads=[b_a, b_u, bstc[d][blk]], writes=[b_h])
                                k.op("act", lambda e, h_=h_: e.copy(stc[:, d, blk:blk + 1], h_[:, 0:1]), reads=[b_h], writes=[bstc[d][blk]])
                                if vsel == 1 and not ctx_out:
                                    continue
                                g_, b_g = tg[i2], btg[i2]
                                hf_, b_hf = xc[i2], bxc[i2]
                                k.dma("sp", hf_[:, :n], hfT[blk * 128:(blk + 1) * 128, t0:t0 + n], reads=[hfbufs[(blk, t0)]], writes=[b_xc])
                                k.dma("sp", g_[:, :n], gyT[blk * 128:(blk + 1) * 128, t0:t0 + n], writes=[b_g])
                                k.op("pool", lambda e, hf_=hf_, h_=h_: e.tensor_tensor(hf_[:, :n], hf_[:, :n], h_[:, :n], ALU.add), reads=[b_xc, b_h], writes=[b_xc])
                                o_, b_o = to[i2], bto[i2]
                                k.op("dve", lambda e, o_=o_, hf_=hf_, g_=g_: e.tensor_tensor(o_[:, :n], hf_[:, :n], g_[:, :n], ALU.mult), reads=[b_xc, b_g], writes=[b_o])
                                k.dma("pool", scrM[blk * 128:(blk + 1) * 128, t0:t0 + n], o_[:, :n], reads=[b_o], writes=[Buf()])
            k.barrier()
            outproj_phase(lru_out_w[0], 10, scrM, ctx_out)


        def ssd_layer(li, ctx_out):
            TT = 512
            xbcT = scrXB
            with ExitStack() as st:
                w, bw = load_w(st, "m2in", m2_in_w[0], 8, 5184)
                nsc = norm_scratch(st, TT)
                xt = [sb("sa_x%d" % i, [128, 8, TT], F32, st) for i in range(2)]; bx = [Buf(), Buf()]
                hT = sb("sa_h", [128, 8, TT], BF16, st); bh = Buf()
                og = [sb("sa_o%d" % i, [128, TT], F32, st) for i in range(4)]; bog = [Buf() for _ in range(4)]
                ogb = [sb("sa_ob%d" % i, [128, TT], BF16, st) for i in range(4)]; bogb = [Buf() for _ in range(4)]
                on = 0
                for ti, (t0, n, vsel) in enumerate(seg_tiles(TT)):
                    x, b_x = xt[ti % 2], bx[ti % 2]
                    k.dma("sp", x[:, :, :n], xres[:, t0:t0 + n].rearrange("(c p) t -> p c t", p=128), writes=[b_x])
                    norm_mod(nsc, x, b_x, n, vsel, A1, 0, hT, bh)
                    for m in range(24):
                        p, pb = next_ps()
                        for c in range(8):
                            k.op("pe", lambda e, p=p, c=c, m=m: e.matmul(p[:, :n], w[:, c, 2048 + m * 128:2048 + (m + 1) * 128], hT[:, c, :n],
                                                                     start=(c == 0), stop=(c == 7)),
                                 reads=[bw, bh], writes=[pb], inc=(c == 7))
                        o, bo = ogb[on % 4], bogb[on % 4]
                        on += 1
                        if m % 2 == 0:
                            k.op("act", lambda e, p=p, o=o: e.copy(o[:, :n], p[:, :n]), reads=[pb], writes=[bo])
                        else:
                            k.op("dve", lambda e, p=p, o=o: e.tensor_copy(o[:, :n], p[:, :n]), reads=[pb], writes=[bo])
                        k.dma("sp", xbcT[m * 128:(m + 1) * 128, t0:t0 + n], o[:, :n], reads=[bo], writes=[Buf()])
                    for tb in range(n // 128):
                        for cb_ in range(4):
                            p, pb = next_ps()
                            for c in range(8):
                                k.op("pe", lambda e, p=p, c=c, tb=tb, cb_=cb_: e.matmul(p[:, :], hT[:, c, tb * 128:(tb + 1) * 128], w[:, c, cb_ * 512:(cb_ + 1) * 512],
                                                                                    start=(c == 0), stop=(c == 7)),
                                     reads=[bw, bh], writes=[pb], inc=(c == 7))
                            o, bo = og[on % 4], bog[on % 4]
                            on += 1
                            if cb_ % 2 == 0:
                                k.op("act", lambda e, p=p, o=o: e.copy(o[:, :], p[:, :]), reads=[pb], writes=[bo])
                            else:
                                k.op("dve", lambda e, p=p, o=o: e.tensor_copy(o[:, :], p[:, :]), reads=[pb], writes=[bo])
                            k.dma("sp", scrZ[t0 + tb * 128:t0 + (tb + 1) * 128, cb_ * 512:(cb_ + 1) * 512], o[:, :], reads=[bo], writes=[Buf()])
                        p, pb = next_ps()
                        for c in range(8):
                            k.op("pe", lambda e, p=p, c=c, tb=tb: e.matmul(p[:, 0:64], hT[:, c, tb * 128:(tb + 1) * 128], w[:, c, 5120:5184],
                                                                      start=(c == 0), stop=(c == 7)),
                                 reads=[bw, bh], writes=[pb], inc=(c == 7))
                        o, bo = og[on % 4], bog[on % 4]
                        on += 1
                        k.op("dve", lambda e, p=p, o=o: e.tensor_copy(o[:, 0:64], p[:, 0:64]), reads=[pb], writes=[bo])
                        k.dma("sp", scrDT[t0 + tb * 128:t0 + (tb + 1) * 128, :], o[:, 0:64], reads=[bo], writes=[Buf()])
            k.barrier()
            if debug == "ssdA":
                return
            with ExitStack() as st:
                def small(name, shape, src, **kw):
                    t = sb(name, shape, F32, st); b = Buf()
                    k.dma("sp", t[:], src, writes=[b], allow_slow_non_contiguous=True)
                    return t, b
                cw, bcw = small("scw", [128, 4, 24], m2_conv_w[0].rearrange("k (m p) -> p k m", p=128))
                cb, bcb = small("scb", [128, 24], m2_conv_b[0].rearrange("(m p) -> p m", p=128))
                dtb, bdtb = small("sdtb", [128, 64], m2_dt_bias[0].rearrange("d h -> (d h)").partition_broadcast(128))
                abc, babc = small("sabc", [128, 64], m2_a_log[0].rearrange("d h -> (d h)").partition_broadcast(128))
                dsk, bdsk = small("sdsk", [128, 32], m2_d[0].partition_broadcast(128))
                k.op("act", lambda e: e.activation(abc[:], abc[:], AF.Exp), reads=[babc], writes=[babc])
                k.op("dve", lambda e: e.tensor_scalar(abc[:], abc[:], -1.0, None, ALU.mult), reads=[babc], writes=[babc])
                msk = sb("smask", [128, 4, 128], F32, st); bmsk = Buf()
                k.dma("sp", msk[:], masks_in.rearrange("m p l -> p m l"), writes=[bmsk])
                MU, ML, MSL, MSU = (msk[:, i, :] for i in range(4))
                xr = [sb("s1_xr%d" % i, [128, 24, 132], BF16, st) for i in range(2)]; bxr = [Buf(), Buf()]
                xc = sb("s1_xc", [128, 128], F32, st); bxc = Buf()
                xsT = sb("s1_xsT", [128, 24, 128], BF16, st); bxsT = Buf()
                xs_tok = [sb("s1_xst%d" % i, [128, 2048], BF16, st) for i in range(2)]; bxst = [Buf(), Buf()]
                b_tok = sb("s1_btok", [128, 512], BF16, st); bbtok = Buf()
                dtr = sb("s1_dtr", [128, 64], F32, st); bdtr = Buf()
                dtt = sb("s1_dt", [128, 64], F32, st); bdtt = Buf()
                da = sb("s1_da", [128, 64], F32, st); bda = Buf()
                tmp64 = sb("s1_t64", [128, 64], F32, st); bt64 = Buf()
                wgt = sb("s1_wgt", [128, 64], F32, st); bwgt = Buf()
                Et = [sb("s1_E%d" % i, [128, 64], F32, st) for i in range(2)]; bE = [Buf(), Buf()]
                dec = [sb("s1_dec%d" % i, [128, 64], F32, st) for i in range(2)]; bdec = [Buf(), Buf()]
                Gf = sb("s1_Gf", [128, 4, 128], F32, st); Gb = sb("s1_Gb", [128, 4, 128], F32, st); bG = Buf()
                NBH = 4
                Lf = [sb("s1_Lf%d" % i, [128, 128], F32, st) for i in range(NBH)]; bLf = [Buf() for _ in range(NBH)]
                Lb = [sb("s1_Lb%d" % i, [128, 128], F32, st) for i in range(NBH)]; bLb = [Buf() for _ in range(NBH)]
                Df = [sb("s1_Df%d" % i, [128, 128], F32, st) for i in range(NBH)]; bDf = [Buf() for _ in range(NBH)]
                Db = [sb("s1_Db%d" % i, [128, 128], F32, st) for i in range(NBH)]; bDb = [Buf() for _ in range(NBH)]
                Mt = [sb("s1_M%d" % i, [128, 128], BF16, st) for i in range(NBH)]; bMt = [Buf() for _ in range(NBH)]
                xw = sb("s1_xw", [128, 2048], BF16, st); bxw = Buf()
                yo = [sb("s1_yo%d" % i, [128, 2048], F32, st) for i in range(2)]; byo = [Buf(), Buf()]
                so = [sb("s1_so%d" % i, [128, 512], BF16, st) for i in range(4)]; bso = [Buf() for _ in range(4)]
                dtt2 = [dtt, sb("s1_dt2", [128, 64], F32, st)]; bdtt2 = [bdtt, Buf()]
                da2 = [da, sb("s1_da2", [128, 64], F32, st)]; bda2 = [bda, Buf()]
                Gf2 = [Gf, sb("s1_Gf2", [128, 4, 128], F32, st)]; Gb2 = [Gb, sb("s1_Gb2", [128, 4, 128], F32, st)]; bG2 = [bG, Buf()]
                pspool[0] = [4]

                xc4 = [xc] + [sb("s1_xc%d" % i, [128, 128], F32, st) for i in range(3)]; bxc4 = [bxc, Buf(), Buf(), Buf()]

                def pieces(ch):
                    dtt_, bdtt_ = dtt2[ch % 2], bdtt2[ch % 2]
                    da_, bda_ = da2[ch % 2], bda2[ch % 2]
                    Gf_, Gb_, bG_ = Gf2[ch % 2], Gb2[ch % 2], bG2[ch % 2]
                    t0 = ch * 128
                    s0, s1 = (0, NCTX) if t0 < NCTX else (NCTX, NTOK)
                    x_r, b_xr = xr[ch % 2], bxr[ch % 2]
                    xst, b_xst = xs_tok[ch % 2], bxst[ch % 2]
                    E_, b_E = Et[ch % 2], bE[ch % 2]
                    d_, b_d = dec[ch % 2], bdec[ch % 2]
                    lo = max(t0 - 2, s0); hi = min(t0 + 129, s1)
                    pa, pab = ps[5], psb[5]
                    P = []

                    def convA(m):
                        if m == 0:
                            if lo > t0 - 2 or hi < t0 + 129:
                                k.op("pool", lambda e: e.memset(x_r[:], 0.0), writes=[b_xr])
                            k.dma("sp", x_r[:, :, lo - (t0 - 2):hi - (t0 - 2)], xbcT[:, lo:hi].rearrange("(m p) t -> p m t", p=128), writes=[b_xr])
                        xc_, bxc_ = xc4[m % 4], bxc4[m % 4]
                        k.op("act", lambda e: e.activation(xc_[:], x_r[:, m, 0:128], AF.Identity, bias=cb[:, m:m + 1], scale=cw[:, 0, m:m + 1]),
                             reads=[b_xr, bcw, bcb], writes=[bxc_])

                    def convT(m):
                        xc_, bxc_ = xc4[m % 4], bxc4[m % 4]
                        for kk in range(1, 4):
                            k.op("dve", lambda e, kk=kk: e.scalar_tensor_tensor(xc_[:], x_r[:, m, kk:kk + 128], cw[:, kk, m:m + 1], xc_[:], ALU.mult, ALU.add),
                                 reads=[b_xr, bcw, bxc_], writes=[bxc_])

                    def convS(m):
                        xc_, bxc_ = xc4[m % 4], bxc4[m % 4]
                        k.op("act", lambda e: e.activation(xsT[:, m, :], xc_[:], AF.Silu), reads=[bxc_], writes=[bxsT])

                    def dtp():
                        k.dma("sp", dtr[:], scrDT[t0:t0 + 128, :], writes=[bdtr])
                        k.op("dve", lambda e: e.tensor_tensor(dtr[:], dtr[:], dtb[:], ALU.add), reads=[bdtr, bdtb], writes=[bdtr])
                        k.op("act", lambda e: e.activation(tmp64[:], dtr[:], AF.Abs), reads=[bdtr], writes=[bt64])
                        k.op("act", lambda e: e.activation(tmp64[:], tmp64[:], AF.Exp, scale=-1.0), reads=[bt64], writes=[bt64])
                        k.op("act", lambda e: e.activation(tmp64[:], tmp64[:], AF.Ln, bias=1.0, scale=1.0), reads=[bt64], writes=[bt64])
                        k.op("dve", lambda e: e.scalar_tensor_tensor(dtt_[:], dtr[:], 0.0, tmp64[:], ALU.max, ALU.add), reads=[bdtr, bt64], writes=[bdtt_])
                        k.op("dve", lambda e: e.tensor_tensor(da_[:], dtt_[:], abc[:], ALU.mult), reads=[bdtt_, babc], writes=[bda_])

                    def convpiece(i):
                        if i - 2 >= 0 and i - 2 < 24:
                            convS(i - 2)
                        if i - 1 >= 0 and i - 1 < 24:
                            convT(i - 1)
                        if i < 24:
                            convA(i)
                        if i == 8:
                            dtp()
                        if i == 12:
                            cums()
                    for i in range(26):
                        P.append(lambda i=i: convpiece(i))

                    def tr(qs, last):
                        for q in qs:
                            p, pb = next_ps()
                            pbf = p[:].bitcast(BF16)
                            for j in range(4):
                                m = q * 4 + j
                                k.op("pe", lambda e, pbf=pbf, j=j, m=m: e.transpose(pbf[:, j * 128:(j + 1) * 128], xsT[:, m, :], ident_b[:]),
                                     reads=[bxsT, b_identb], writes=[pb], inc=(j == 3))
                            if q < 4:
                                k.op("dve", lambda e, pbf=pbf, q=q: e.tensor_copy(xst[:, q * 512:(q + 1) * 512], pbf[:, 0:512]), reads=[pb], writes=[b_xst])
                            else:
                                k.op("dve", lambda e, pbf=pbf: e.tensor_copy(b_tok[:], pbf[:, 0:512]), reads=[pb], writes=[bbtok])
                        if last:
                            k.dma("sp", scrC[ch].rearrange("p (g t) -> p g t", t=128), xsT[:, 20:24, :], reads=[bxsT], writes=[Buf()])
                    P.append(lambda: tr((0, 1), False))
                    P.append(lambda: tr((2, 3), False))
                    P.append(lambda: tr((4,), True))

                    def cums():
                        k.op("pe", lambda e: e.matmul(pa[:, 0:32], MU, da_[:, 0:32], start=True, stop=True), reads=[bmsk, bda_], writes=[pab], inc=False)
                        k.op("pe", lambda e: e.matmul(pa[:, 32:64], ML, da_[:, 32:64], start=True, stop=True), reads=[bmsk, bda_], writes=[pab], inc=False)
                        k.op("pe", lambda e: e.matmul(pa[:, 64:128], ones_f[:], da_[:, 0:64], start=True, stop=True), reads=[b_ones, bda_], writes=[pab])
                        k.op("act", lambda e: e.activation(E_[:], pa[:, 0:64], AF.Exp), reads=[pab], writes=[b_E])
                        k.op("act", lambda e: e.activation(d_[:], pa[:, 64:128], AF.Exp), reads=[pab], writes=[b_d])
                        k.dma("sp", scrE[ch], E_[:], reads=[b_E], writes=[Buf()])
                        k.dma("sp", scrDec[ch], d_[:], reads=[b_d], writes=[Buf()])
                        k.op("act", lambda e: e.copy(tmp64[:], pa[:, 0:64]), reads=[pab], writes=[bt64])
                        k.op("dve", lambda e: e.tensor_tensor(wgt[:], pa[:, 64:128], tmp64[:], ALU.subtract), reads=[pab, bt64], writes=[bwgt])
                        k.op("act", lambda e: e.activation(wgt[:], wgt[:], AF.Exp), reads=[bwgt], writes=[bwgt])
                        k.op("dve", lambda e: e.tensor_tensor(wgt[:], wgt[:], dtt_[:], ALU.mult), reads=[bwgt, bdtt_], writes=[bwgt])

                    def states(d):
                        k.op("dve", lambda e: e.tensor_tensor(xw[:].rearrange("p (h q) -> p h q", q=64), xst[:].rearrange("p (h q) -> p h q", q=64),
                                                              wgt[:, d * 32:(d + 1) * 32].unsqueeze(2).to_broadcast([128, 32, 64]), ALU.mult),
                             reads=[b_xst, bwgt], writes=[bxw])
                        for g in range(4):
                            p, pb = next_ps()
                            k.op("pe", lambda e, p=p, g=g: e.matmul(p[:, :], b_tok[:, g * 128:(g + 1) * 128], xw[:, g * 512:(g + 1) * 512], start=True, stop=True),
                                 reads=[bbtok, bxw], writes=[pb])
                            s_, b_s = so[g], bso[g]
                            k.op("act", lambda e, p=p, s_=s_: e.copy(s_[:], p[:]), reads=[pb], writes=[b_s])
                            k.dma("sp", scrS[d, ch, :, g * 512:(g + 1) * 512], s_[:], reads=[b_s], writes=[Buf()])
                    P.append(lambda: states(0))
                    P.append(lambda: states(1))

                    def gmat():
                        for g in range(4):
                            p, pb = next_ps()
                            k.op("pe", lambda e, p=p, g=g: e.matmul(p[:, 0:128], xsT[:, 16 + g, :], xsT[:, 20 + g, :], start=True, stop=True),
                                 reads=[bxsT], writes=[pb])
                            k.op("dve", lambda e, p=p, g=g: e.tensor_tensor(Gf_[:, g, :], p[:, 0:128], MU, ALU.mult), reads=[pb, bmsk], writes=[bG_])
                            k.op("dve", lambda e, p=p, g=g: e.tensor_tensor(Gb_[:, g, :], p[:, 0:128], ML, ALU.mult), reads=[pb, bmsk], writes=[bG_])
                    P.append(gmat)
                    assert len(P) == 32
                    return P

                pieces_cache = {}

                def piece(ch, i):
                    if ch >= NCH:
                        return
                    if ch not in pieces_cache:
                        pieces_cache.clear()
                        pieces_cache[ch] = pieces(ch)
                    pieces_cache[ch][i]()

                def prologue(ch):
                    if ch == 0:
                        for i in range(32):
                            piece(0, i)

                def make_head(ch, h, hc):
                    dtt_, bdtt_ = dtt2[ch % 2], bdtt2[ch % 2]
                    da_, bda_ = da2[ch % 2], bda2[ch % 2]
                    Gf_, Gb_, bG_ = Gf2[ch % 2], Gb2[ch % 2], bG2[ch % 2]
                    xst, b_xst = xs_tok[ch % 2], bxst[ch % 2]
                    g = h // 8
                    i2 = hc % NBH
                    p, pb = ps[6 + hc % 2], psb[6 + hc % 2]
                    yb = h // 8

                    def h0():
                        if h == 0:
                            prologue(ch)
                        piece(ch + 1, h)
                        k.op("pool", lambda e: e.tensor_scalar(Lf[i2][:], MSL, da_[:, h:h + 1], None, ALU.mult), reads=[bmsk, bda_], writes=[bLf[i2]])
                        k.op("pool", lambda e: e.tensor_scalar(Lb[i2][:], MSU, da_[:, 32 + h:33 + h], None, ALU.mult), reads=[bmsk, bda_], writes=[bLb[i2]])

                    def h1():
                        k.op("pe", lambda e: e.matmul(p[:, 0:128], Lf[i2][:], MU, start=True, stop=True), reads=[bLf[i2], bmsk], writes=[pb], inc=False)
                        k.op("pe", lambda e: e.matmul(p[:, 128:256], Lb[i2][:], ML, start=True, stop=True), reads=[bLb[i2], bmsk], writes=[pb])

                    def h2():
                        k.op("act", lambda e: e.activation(Df[i2][:], p[:, 0:128], AF.Exp), reads=[pb], writes=[bDf[i2]])
                        k.op("act", lambda e: e.activation(Db[i2][:], p[:, 128:256], AF.Exp), reads=[pb], writes=[bDb[i2]])

                    def h3():
                        k.op("dve", lambda e: e.scalar_tensor_tensor(Df[i2][:], Df[i2][:], dtt_[:, h:h + 1], Gf_[:, g, :], ALU.mult, ALU.mult),
                             reads=[bDf[i2], bdtt_, bG_], writes=[bDf[i2]])
                        k.op("dve", lambda e: e.scalar_tensor_tensor(Db[i2][:], Db[i2][:], dtt_[:, 32 + h:33 + h], Gb_[:, g, :], ALU.mult, ALU.mult),
                             reads=[bDb[i2], bdtt_, bG_], writes=[bDb[i2]])
                        k.op("dve", lambda e: e.scalar_tensor_tensor(Df[i2][:], ident_f[:], dsk[:, h:h + 1], Df[i2][:], ALU.mult, ALU.add),
                             reads=[bDf[i2], bdsk, b_ident], writes=[bDf[i2]])
                        k.op("dve", lambda e: e.tensor_tensor(Mt[i2][:], Df[i2][:], Db[i2][:], ALU.add), reads=[bDf[i2], bDb[i2]], writes=[bMt[i2]])

                    def h4():
                        k.op("pe", lambda e: e.matmul(ps[yb][:, (h % 8) * 64:(h % 8 + 1) * 64], Mt[i2][:], xst[:, h * 64:(h + 1) * 64], start=True, stop=True),
                             reads=[bMt[i2], b_xst], writes=[psb[yb]])
                        if h != 31:
                            return
                        y_, b_y = yo[ch % 2], byo[ch % 2]
                        for yb2 in range(4):
                            if yb2 % 2 == 0:
                                k.op("act", lambda e, yb2=yb2: e.copy(y_[:, yb2 * 512:(yb2 + 1) * 512], ps[yb2][:]), reads=[psb[yb2]], writes=[b_y])
                            else:
                                k.op("dve", lambda e, yb2=yb2: e.tensor_copy(y_[:, yb2 * 512:(yb2 + 1) * 512], ps[yb2][:]), reads=[psb[yb2]], writes=[b_y])
                        k.dma("sp", scrY[ch], y_[:], reads=[b_y], writes=[Buf()])

                    return [h0, h1, h2, h3, h4]

                hunits = []
                for ch in range(NCH):
                    for h in range(32):
                        hunits.append(make_head(ch, h, len(hunits)))
                NSH = 5
                for step in range(len(hunits) + NSH - 1):
                    for j in range(NSH - 1, -1, -1):
                        u = step - j
                        if 0 <= u < len(hunits):
                            hunits[u][j]()
                pspool[0] = list(range(8))
            k.barrier()
            if debug == "ssd1":
                return
            with ExitStack() as st:
                Hs = [sb("s2_H%d" % d, [128, 2048], F32, st) for d in range(2)]; bH = [Buf(), Buf()]
                Hb = [sb("s2_Hb%d" % i, [128, 2048], BF16, st) for i in range(2)]; bHb = [Buf(), Buf()]
                St = [sb("s2_S%d" % i, [128, 2048], BF16, st) for i in range(2)]; bSt = [Buf(), Buf()]
                dc = [sb("s2_d%d" % i, [128, 64], F32, st) for i in range(2)]; bdc = [Buf(), Buf()]
                it = 0
                for d in range(2):
                    order = list(range(NCH)) if d == 0 else [1, 0] + list(range(NCH - 1, 1, -1))
                    k.op("pool", lambda e, d=d: e.memset(Hs[d][:], 0.0), writes=[bH[d]])
                    for ch in order:
                        i2 = it % 2
                        it += 1
                        k.op("act", lambda e, d=d, i2=i2: e.copy(Hb[i2][:], Hs[d][:]), reads=[bH[d]], writes=[bHb[i2]])
                        k.dma("pool", scrH[d, ch], Hb[i2][:], reads=[bHb[i2]], writes=[Buf()])
                        k.dma("sp", St[i2][:], scrS[d, ch], writes=[bSt[i2]])
                        k.dma("sp", dc[i2][:], scrDec[ch], writes=[bdc[i2]])
                        k.op("dve", lambda e, d=d, i2=i2: e.tensor_tensor(Hs[d][:].rearrange("p (h q) -> p h q", q=64), Hs[d][:].rearrange("p (h q) -> p h q", q=64),
                                                                      dc[i2][:, d * 32:(d + 1) * 32].unsqueeze(2).to_broadcast([128, 32, 64]), ALU.mult),
                             reads=[bH[d], bdc[i2]], writes=[bH[d]])
                        k.op("dve", lambda e, d=d, i2=i2: e.tensor_tensor(Hs[d][:], Hs[d][:], St[i2][:], ALU.add), reads=[bH[d], bSt[i2]], writes=[bH[d]])
            k.barrier()
            if debug == "ssd2":
                return
            with ExitStack() as st:
                ow, bow = load_w(st, "m2ow", m2_out_w[0], 16, D)
                nwb = sb("s3_nw", [128, 2048], F32, st); bnwb = Buf()
                k.dma("sp", nwb[:], m2_norm_w[0].partition_broadcast(128), writes=[bnwb])
                yt = [sb("s3_y%d" % i, [128, 2048], F32, st) for i in range(2)]; byt = [Buf(), Buf()]
                zt = [sb("s3_z%d" % i, [128, 2048], F32, st) for i in range(2)]; bzt = [Buf(), Buf()]
                Hf = [sb("s3_Hf%d" % i, [128, 2048], BF16, st) for i in range(2)]; bHf = [Buf(), Buf()]
                Hbk = [sb("s3_Hb%d" % i, [128, 2048], BF16, st) for i in range(2)]; bHbk = [Buf(), Buf()]
                Ct = [sb("s3_C%d" % i, [128, 512], BF16, st) for i in range(2)]; bCt = [Buf(), Buf()]
                Ee = [sb("s3_E%d" % i, [128, 64], F32, st) for i in range(2)]; bEe = [Buf(), Buf()]
                tmp = sb("s3_tmp", [128, 512], F32, st); btmp = Buf()
                ss = sb("s3_ss", [128, 2], F32, st); bss = Buf()
                sqj = sb("s3_sq", [128, 2048], F32, st); bsqj = Buf()
                yn = sb("s3_yn", [128, 2048], BF16, st); byn = Buf()
                ynT = sb("s3_ynT", [128, 16, 128], BF16, st); bynT = Buf()
                xt = [sb("s3_x%d" % i, [128, 8, 128], F32, st) for i in range(2)]; bx = [Buf(), Buf()]
                for ch in range(NCH):
                    t0 = ch * 128
                    vsel = 1 if t0 < NCTX else 0
                    if vsel == 1 and not ctx_out:
                        continue
                    i2 = ch % 2
                    y_, z_, b_y, b_z = yt[i2], zt[i2], byt[i2], bzt[i2]
                    k.dma("sp", y_[:], scrY[ch], writes=[b_y])
                    k.dma("sp", z_[:], scrZ[t0:t0 + 128, :], writes=[b_z])
                    k.dma("pool", Hf[i2][:], scrH[0, ch], writes=[bHf[i2]])
                    k.dma("pool", Hbk[i2][:], scrH[1, ch], writes=[bHbk[i2]])
                    k.dma("sp", Ct[i2][:], scrC[ch], writes=[bCt[i2]])
                    k.dma("sp", Ee[i2][:], scrE[ch], writes=[bEe[i2]])
                    k.dma("pool", xt[i2][:], xres[:, t0:t0 + 128].rearrange("(c p) t -> p c t", p=128), writes=[bx[i2]])
                    for d in range(2):
                        Hd, bHd = (Hf[i2], bHf[i2]) if d == 0 else (Hbk[i2], bHbk[i2])
                        for g in range(4):
                            p, pb = next_ps()
                            k.op("pe", lambda e, p=p, g=g, Hd=Hd, i2=i2: e.matmul(p[:], Ct[i2][:, g * 128:(g + 1) * 128], Hd[:, g * 512:(g + 1) * 512], start=True, stop=True),
                                 reads=[bCt[i2], bHd], writes=[pb])
                            k.op("dve", lambda e, p=p, g=g, d=d, i2=i2: e.tensor_tensor(tmp[:].rearrange("p (h q) -> p h q", q=64), p[:].rearrange("p (h q) -> p h q", q=64),
                                                                                   Ee[i2][:, d * 32 + g * 8:d * 32 + (g + 1) * 8].unsqueeze(2).to_broadcast([128, 8, 64]), ALU.mult),
                                 reads=[pb, bEe[i2]], writes=[btmp])
                            k.op("pool", lambda e, g=g, y_=y_: e.tensor_tensor(y_[:, g * 512:(g + 1) * 512], y_[:, g * 512:(g + 1) * 512], tmp[:], ALU.add),
                                 reads=[btmp, b_y], writes=[b_y])
                    k.op("act", lambda e, z_=z_: e.activation(z_[:], z_[:], AF.Silu), reads=[b_z], writes=[b_z])
                    k.op("dve", lambda e, y_=y_, z_=z_: e.tensor_tensor(y_[:], y_[:], z_[:], ALU.mult), reads=[b_y, b_z], writes=[b_y])
                    k.op("pool", lambda e: e.memset(ss[:], 0.0), writes=[bss])
                    k.op("act", lambda e, y_=y_: e.activation(sqj[:], y_[:], AF.Square, accum_out=ss[:, 0:1]), reads=[b_y], writes=[bsqj, bss])
                    k.op("act", lambda e: e.activation(ss[:, 1:2], ss[:, 0:1], AF.Sqrt, bias=EPS, scale=1.0 / 2048), reads=[bss], writes=[bss])
                    k.op("dve", lambda e: e.reciprocal(ss[:, 1:2], ss[:, 1:2]), reads=[bss], writes=[bss])
                    k.op("dve", lambda e, y_=y_: e.scalar_tensor_tensor(yn[:], y_[:], ss[:, 1:2], nwb[:], ALU.mult, ALU.mult), reads=[b_y, bss, bnwb], writes=[byn])
                    for q in range(4):
                        p, pb = next_ps()
                        pbf = p[:].bitcast(BF16)
                        for j in range(4):
                            m = q * 4 + j
                            k.op("pe", lambda e, pbf=pbf, j=j, m=m: e.transpose(pbf[:, j * 128:(j + 1) * 128], yn[:, m * 128:(m + 1) * 128], ident_b[:]),
                                 reads=[byn, b_identb], writes=[pb], inc=(j == 3))
                        k.op("act" if q % 2 else "dve", (lambda e, pbf=pbf, q=q: e.copy(ynT[:, q * 4:(q + 1) * 4, :], pbf[:, 0:512].rearrange("p (j t) -> p j t", t=128))) if q % 2 else
                             (lambda e, pbf=pbf, q=q: e.tensor_copy(ynT[:, q * 4:(q + 1) * 4, :], pbf[:, 0:512].rearrange("p (j t) -> p j t", t=128))),
                             reads=[pb], writes=[bynT])
                    for c in range(8):
                        p, pb = next_ps()
                        for j in range(16):
                            k.op("pe", lambda e, p=p, j=j, c=c: e.matmul(p[:, 0:128], ow[:, j, c * 128:(c + 1) * 128], ynT[:, j, :], start=(j == 0), stop=(j == 15)),
                                 reads=[bow, bynT], writes=[pb], inc=(j == 15))
                        k.op("dve", lambda e, p=p, c=c, i2=i2: e.scalar_tensor_tensor(xt[i2][:, c, :], p[:, 0:128], mod[:, 16 + c, vsel:vsel + 1], xt[i2][:, c, :], ALU.mult, ALU.add),
                             reads=[pb, b_mod, bx[i2]], writes=[bx[i2]])
                    k.dma("pool", xres[:, t0:t0 + 128].rearrange("(c p) t -> p c t", p=128), xt[i2][:], reads=[bx[i2]], writes=[Buf()])
            k.barrier()

        TWO_PI = 6.283185307179586

        def sincos(st, tag, phi, bphi, shape, out_sin, out_cos, bout):
            t = sb("sc_t" + tag, shape, F32, st); ki = sb("sc_k" + tag, shape, I32, st); bt = Buf()
            for (o, off) in ((out_sin, 0.0), (out_cos, 0.5 * np.pi)):
                k.op("dve", lambda e, off=off: e.tensor_scalar(t[:], phi, 1.0 / TWO_PI, off / TWO_PI, ALU.mult, ALU.add), reads=[bphi], writes=[bt])
                k.op("dve", lambda e: e.tensor_copy(ki[:], t[:]), reads=[bt], writes=[bt])
                k.op("dve", lambda e: e.tensor_copy(t[:], ki[:]), reads=[bt], writes=[bt])
                k.op("dve", lambda e: e.scalar_tensor_tensor(t[:], t[:], -TWO_PI, phi, ALU.mult, ALU.add), reads=[bt, bphi], writes=[bt])
                k.op("dve", lambda e, off=off: e.tensor_scalar(t[:], t[:], off, 3.1415925, ALU.add, ALU.min), reads=[bt], writes=[bt])
                k.op("dve", lambda e: e.tensor_scalar(t[:], t[:], -3.1415925, None, ALU.max), reads=[bt], writes=[bt])
                k.op("act", lambda e, o=o: e.activation(o, t[:], AF.Sin), reads=[bt], writes=[bout])

        def s5_layer(li, ctx_out):
            T = 128
            yfT = scrA
            tiles = seg_tiles(T)
            for d in range(2):
                with ExitStack() as st:
                    def small(name, shape, src, q="sp"):
                        t = sb(name, shape, F32, st); b = Buf()
                        k.dma(q, t[:], src, writes=[b], allow_slow_non_contiguous=True)
                        return t, b
                    ramp, bramp = small("ramp", [128, 128], ramp_in)
                    sel, bsel = small("sel", [128, 8], sel_in)
                    lrs, b1 = small("lrs", [128, 32], s5_lre[0, d].rearrange("g p -> (g p)").rearrange("(b q) -> q b", q=128))
                    lis, b2 = small("lis", [128, 32], s5_lim[0, d].rearrange("g p -> (g p)").rearrange("(b q) -> q b", q=128))
                    sts = sb("sts", [128, 32], F32, st); b3 = Buf()
                    for gl in range(2):
                        k.dma("sp", sts[gl * 64:(gl + 1) * 64, :], s5_ls[0, d].rearrange("(b gl) -> gl b", gl=2)[gl].partition_broadcast(64), writes=[b3],
                              allow_slow_non_contiguous=True)
                    rho = sb("rho", [128, 32], F32, st); theta = sb("theta", [128, 32], F32, st); bpar = Buf()
                    k.op("dve", lambda e: e.tensor_scalar(lrs[:], lrs[:], -1e-4, None, ALU.min), reads=[b1], writes=[b1])
                    k.op("act", lambda e: e.activation(sts[:], sts[:], AF.Exp), reads=[b3], writes=[b3])
                    k.op("dve", lambda e: e.tensor_tensor(rho[:], lrs[:], sts[:], ALU.mult), reads=[b1, b3], writes=[bpar])
                    k.op("act", lambda e: e.activation(rho[:], rho[:], AF.Exp), reads=[bpar], writes=[bpar])
                    k.op("dve", lambda e: e.tensor_tensor(theta[:], lis[:], sts[:], ALU.mult), reads=[b2, b3], writes=[bpar])
                    rhoT = sb("rhoT", [128, 32, 128], F32, st)
                    for b in range(32):
                        k.op("pool", lambda e, b=b: e.tensor_scalar(rhoT[:, b, :], ramp[:], 0.0, rho[:, b:b + 1], ALU.mult, ALU.add), reads=[bramp, bpar], writes=[bpar])
                    cosT = sb("cosT", [128, 32, 128], F32, st); sinT = sb("sinT", [128, 32, 128], F32, st); btab = Buf()
                    phi = sb("phi", [128, 128], F32, st); bphi = Buf()
                    with ExitStack() as st2:
                        for b in range(32):
                            k.op("dve", lambda e, b=b: e.tensor_scalar(phi[:], ramp[:], theta[:, b:b + 1], None, ALU.mult), reads=[bramp, bpar], writes=[bphi])
                            if b == 0:
                                tt_ = sb("sc_t", [128, 128], F32, st2); ki_ = sb("sc_k", [128, 128], I32, st2); bt_ = Buf()
                            for (o, off) in ((sinT[:, b, :], 0.0), (cosT[:, b, :], 0.5 * np.pi)):
                                k.op("dve", lambda e, off=off: e.tensor_scalar(tt_[:], phi[:], 1.0 / TWO_PI, off / TWO_PI, ALU.mult, ALU.add), reads=[bphi], writes=[bt_])
                                k.op("dve", lambda e: e.tensor_copy(ki_[:], tt_[:]), reads=[bt_], writes=[bt_])
                                k.op("dve", lambda e: e.tensor_copy(tt_[:], ki_[:]), reads=[bt_], writes=[bt_])
                                k.op("dve", lambda e: e.scalar_tensor_tensor(tt_[:], tt_[:], -TWO_PI, phi[:], ALU.mult, ALU.add), reads=[bt_, bphi], writes=[bt_])
                                k.op("dve", lambda e, off=off: e.tensor_scalar(tt_[:], tt_[:], off, 3.1415925, ALU.add, ALU.min), reads=[bt_], writes=[bt_])
                                k.op("dve", lambda e: e.tensor_scalar(tt_[:], tt_[:], -3.1415925, None, ALU.max), reads=[bt_], writes=[bt_])
                                k.op("act", lambda e, o=o: e.activation(o, tt_[:], AF.Sin), reads=[bt_], writes=[btab])
                    k.barrier()
                    Wre = sb("Wre", [128, 8, 4, 128], BF16, st); Wim = sb("Wim", [128, 8, 4, 128], BF16, st); bW = Buf()
                    with ExitStack() as st2:
                        def wl(name, src3):
                            t = sb(name, [128, 8, 64], F32, st2); b = Buf()
                            v = src3.rearrange("(c gl) p -> gl c p", gl=8)
                            for gl in range(8):
                                k.dma("sp" if gl % 2 else "pool", t[gl * 16:(gl + 1) * 16, :, :], v[gl].partition_broadcast(16), writes=[b], allow_slow_non_contiguous=True)
                            return t, b
                        lrw, blr = wl("lrw", s5_lre[0, d]); liw, bli = wl("liw", s5_lim[0, d])
                        stw = sb("stw", [128, 8], F32, st2); bst = Buf()
                        v = s5_ls[0, d].rearrange("(c gl) -> gl c", gl=8)
                        for gl in range(8):
                            k.dma("sp", stw[gl * 16:(gl + 1) * 16, :], v[gl].partition_broadcast(16), writes=[bst], allow_slow_non_contiguous=True)
                        brw = sb("brw", [128, 8, 64], F32, st2); biw = sb("biw", [128, 8, 64], F32, st2); bbw = Buf()
                        for (t_, src) in ((brw, s5_bre[0]), (biw, s5_bim[0])):
                            v = src.rearrange("(c gl) p j -> gl j c p", gl=8)
                            for gl in range(8):
                                for c_ in range(8):
                                    k.dma("sp" if c_ % 2 else "pool", t_[gl * 16:(gl + 1) * 16, c_, :], v[gl][:, c_, :], writes=[bbw], allow_slow_non_contiguous=True)
                        k.op("dve", lambda e: e.tensor_scalar(lrw[:], lrw[:], -1e-4, None, ALU.min), reads=[blr], writes=[blr])
                        k.op("act", lambda e: e.activation(stw[:], stw[:], AF.Exp), reads=[bst], writes=[bst])
                        stb = stw[:].unsqueeze(2).to_broadcast([128, 8, 64])
                        mag = sb("mag", [128, 8, 64], F32, st2); th = sb("thw", [128, 8, 64], F32, st2); bm_ = Buf(); bth = Buf()
                        k.op("dve", lambda e: e.tensor_tensor(mag[:], lrw[:], stb, ALU.mult), reads=[blr, bst], writes=[bm_])
                        k.op("act", lambda e: e.activation(mag[:], mag[:], AF.Exp), reads=[bm_], writes=[bm_])
                        k.op("dve", lambda e: e.tensor_tensor(th[:], liw[:], stb, ALU.mult), reads=[bli, bst], writes=[bth])
                        sn = sb("snw", [128, 8, 64], F32, st2); cs = sb("csw", [128, 8, 64], F32, st2); bsc_ = Buf()
                        sincos(st2, "w", th[:], bth, [128, 8, 64], sn[:], cs[:], bsc_)
                        k.op("dve", lambda e: e.tensor_tensor(cs[:], cs[:], mag[:], ALU.mult), reads=[bsc_, bm_], writes=[bsc_])
                        k.op("dve", lambda e: e.tensor_scalar(cs[:], cs[:], -1.0, None, ALU.add), reads=[bsc_], writes=[bsc_])
                        k.op("dve", lambda e: e.tensor_tensor(sn[:], sn[:], mag[:], ALU.mult), reads=[bsc_, bm_], writes=[bsc_])
                        den = sb("den", [128, 8, 64], F32, st2); t1 = sb("t1w", [128, 8, 64], F32, st2); bden = Buf(); bt1 = Buf()
                        zr = sb("zr", [128, 8, 64], F32, st2); zi = sb("zi", [128, 8, 64], F32, st2); bz_ = Buf()
                        k.op("dve", lambda e: e.tensor_tensor(den[:], lrw[:], lrw[:], ALU.mult), reads=[blr], writes=[bden])
                        k.op("dve", lambda e: e.tensor_tensor(t1[:], liw[:], liw[:], ALU.mult), reads=[bli], writes=[bt1])
                        k.op("dve", lambda e: e.tensor_tensor(den[:], den[:], t1[:], ALU.add), reads=[bden, bt1], writes=[bden])
                        k.op("dve", lambda e: e.reciprocal(den[:], den[:]), reads=[bden], writes=[bden])
                        k.op("dve", lambda e: e.tensor_tensor(zr[:], cs[:], lrw[:], ALU.mult), reads=[bsc_, blr], writes=[bz_])
                        k.op("dve", lambda e: e.tensor_tensor(t1[:], sn[:], liw[:], ALU.mult), reads=[bsc_, bli], writes=[bt1])
                        k.op("dve", lambda e: e.tensor_tensor(zr[:], zr[:], t1[:], ALU.add), reads=[bz_, bt1], writes=[bz_])
                        k.op("dve", lambda e: e.tensor_tensor(zr[:], zr[:], den[:], ALU.mult), reads=[bz_, bden], writes=[bz_])
                        k.op("dve", lambda e: e.tensor_tensor(zi[:], sn[:], lrw[:], ALU.mult), reads=[bsc_, blr], writes=[bz_])
                        k.op("dve", lambda e: e.tensor_tensor(t1[:], cs[:], liw[:], ALU.mult), reads=[bsc_, bli], writes=[bt1])
                        k.op("dve", lambda e: e.tensor_tensor(zi[:], zi[:], t1[:], ALU.subtract), reads=[bz_, bt1], writes=[bz_])
                        k.op("dve", lambda e: e.tensor_tensor(zi[:], zi[:], den[:], ALU.mult), reads=[bz_, bden], writes=[bz_])
                        k.op("dve", lambda e: e.tensor_tensor(mag[:], zr[:], brw[:], ALU.mult), reads=[bz_, bbw], writes=[bm_])
                        k.op("dve", lambda e: e.tensor_tensor(t1[:], zi[:], biw[:], ALU.mult), reads=[bz_, bbw], writes=[bt1])
                        k.op("dve", lambda e: e.tensor_tensor(mag[:], mag[:], t1[:], ALU.subtract), reads=[bm_, bt1], writes=[bm_])
                        k.op("dve", lambda e: e.tensor_tensor(th[:], zr[:], biw[:], ALU.mult), reads=[bz_, bbw], writes=[bth])
                        k.op("dve", lambda e: e.tensor_tensor(t1[:], zi[:], brw[:], ALU.mult), reads=[bz_, bbw], writes=[bt1])
                        k.op("dve", lambda e: e.tensor_tensor(th[:], th[:], t1[:], ALU.add), reads=[bth, bt1], writes=[bth])
                        if debug_sub == "pre":
                            for i_, (t_, b_) in enumerate(((brw, bbw), (mag, bm_), (zr, bz_), (den, bden), (sn, bsc_), (lrw, blr), (liw, bli))):
                                k.dma("sp", outT[i_ * 128:(i_ + 1) * 128, 0:512], t_[:].rearrange("p c q -> p (c q)"), reads=[b_], writes=[Buf()])
                            k.dma("sp", outT[896:1024, 0:8], stw[:], reads=[bst], writes=[Buf()])
                            k.dma("sp", outT[896:1024, 8:16], sel[:], reads=[bsel], writes=[Buf()])
                            k.barrier()
                            return
                        for q in range(4):
                            for gl2 in range(2):
                                k.op("dve", lambda e, q=q, gl2=gl2: e.tensor_scalar(Wre[:, :, q, gl2 * 64:(gl2 + 1) * 64], mag[:], sel[:, q * 2 + gl2:q * 2 + gl2 + 1], None, ALU.mult),
                                     reads=[bm_, bsel], writes=[bW])
                                k.op("dve", lambda e, q=q, gl2=gl2: e.tensor_scalar(Wim[:, :, q, gl2 * 64:(gl2 + 1) * 64], th[:], sel[:, q * 2 + gl2:q * 2 + gl2 + 1], None, ALU.mult),
                                     reads=[bth, bsel], writes=[bW])
                    k.barrier()
                    if debug_sub == "pre2":
                        for c_ in range(8):
                            k.dma("sp", outT[c_ * 128:(c_ + 1) * 128, 0:512], Wre[:, c_, :, :].rearrange("p q m -> p (q m)"), reads=[bW], writes=[Buf()])
                        k.barrier()
                        return
                    WcR = sb("WcR", [128, 32, 128], BF16, st); WcI = sb("WcI", [128, 32, 128], BF16, st); bWc = Buf()
                    for b_ in range(32):
                        k.op("pool", lambda e, b_=b_: e.memset(WcR[:, b_, :], 0.0), writes=[bWc])
                        k.op("pool", lambda e, b_=b_: e.memset(WcI[:, b_, :], 0.0), writes=[bWc])
                    if debug_sub == "p_a":
                        for c_ in range(8):
                            k.dma("sp", outT[c_ * 128:(c_ + 1) * 128, 0:512], Wre[:, c_, :, :].rearrange("p q m -> p (q m)"), reads=[bW], writes=[Buf()])
                        k.barrier()
                        return
                    with ExitStack() as st2:
                        for (Wc_, src, sgn) in ((WcR, s5_cre[0, d], 1.0), (WcI, s5_cim[0, d], -1.0)):
                            t2 = sb("t2c", [128, 32, 16], F32, st2); bt2 = Buf()
                            v = src.rearrange("(b gl) j p -> gl p b j", gl=2)
                            for gl2 in range(2):
                                for hb in range(32):
                                    k.dma("sp" if hb % 2 else "pool", t2[gl2 * 64:(gl2 + 1) * 64, hb, :], v[gl2][:, hb, :], writes=[bt2],
                                          allow_slow_non_contiguous=True)
                            for gl2 in range(2):
                                for q in range(4):
                                    col = (2 * q + gl2) * 16
                                    k.op("dve", lambda e, gl2=gl2, q=q, col=col, Wc_=Wc_, t2=t2, sgn=sgn: e.tensor_scalar(
                                        Wc_[gl2 * 64:(gl2 + 1) * 64, q::4, col:col + 16], t2[gl2 * 64:(gl2 + 1) * 64, q::4, :], sgn, None, ALU.mult),
                                        reads=[bt2], writes=[bWc])
                    k.barrier()
                    if debug_sub == "pre3":
                        for c_ in range(8):
                            k.dma("sp", outT[c_ * 128:(c_ + 1) * 128, 0:512], Wre[:, c_, :, :].rearrange("p q m -> p (q m)"), reads=[bW], writes=[Buf()])
                        k.barrier()
                        return
                    if d == 1:
                        gw_, bgw_ = load_w(st, "s5glu", s5_glu_w[0], 8, 2 * D)
                        gbias, bgb_ = small("s5gb", [128, 16], s5_glu_b[0].rearrange("(m p) -> p m", p=128))
                        dskT, bdsk_ = small("s5d", [128, 8], s5_d[0].rearrange("(c p) -> p c", p=128))
                    nscs = [norm_scratch(st, T) for _ in range(2)]
                    xt = [sb("s5_x%d" % i, [128, 8, T], F32, st) for i in range(2)]; bx = [Buf(), Buf()]
                    hTs = [sb("s5_h%d" % i, [128, 8, T], BF16, st) for i in range(2)]; bhs = [Buf(), Buf()]
                    stR = sb("s5_stR", [128, 32], F32, st); stI = sb("s5_stI", [128, 32], F32, st); bstate = [Buf() for _ in range(32)]
                    k.op("pool", lambda e: e.memset(stR[:], 0.0), writes=bstate)
                    k.op("pool", lambda e: e.memset(stI[:], 0.0), writes=bstate)
                    NB = 6
                    def mk(name, dt=F32):
                        return [sb("%s%d" % (name, i), [128, T], dt, st) for i in range(NB)], [Buf() for _ in range(NB)]
                    ur, bur = mk("s5_ur"); ui, bui = mk("s5_ui")
                    ta, bta = mk("s5_ta"); tb_, btb = mk("s5_tb"); tc, btc = mk("s5_tc"); td, btd = mk("s5_td")
                    vr, bvr = mk("s5_vr"); vi, bvi = mk("s5_vi")
                    q1, bq1 = mk("s5_q1", BF16); q2, bq2 = mk("s5_q2", BF16); q3, bq3 = mk("s5_q3", BF16); q4, bq4 = mk("s5_q4", BF16)
                    stt_ = [sb("s5_stt%d" % i, [128, 2], F32, st) for i in range(NB)]; bstt = [Buf() for _ in range(NB)]
                    yo = [sb("s5_yo%d" % i, [128, T], F32, st) for i in range(2)]; byo = [Buf(), Buf()]
                    if d == 1:
                        yf = [sb("s5_yf%d" % i, [128, T], F32, st) for i in range(2)]; byf = [Buf(), Buf()]
                        gT = sb("s5_g", [128, 8, T], BF16, st); bg = Buf()
                        ga = [sb("s5_ga%d" % i, [128, T], F32, st) for i in range(2)]; bga = [Buf(), Buf()]
                    order = tiles if d == 0 else [tiles[1], tiles[0]] + tiles[:1:-1]
                    rv = (lambda ap: ap) if d == 0 else (lambda ap: ap[:, ::-1])
                    pspool[0] = [6, 7]
                    units = []

                    def make_unit(ti, t0, n, vsel, c, q, un):
                        x, b_x = xt[ti % 2], bx[ti % 2]
                        hT, bh = hTs[ti % 2], bhs[ti % 2]
                        b = c * 4 + q
                        i3 = un % NB
                        pu, pub = ps[un % 3], psb[un % 3]
                        py, pyb = ps[3 + (un // 4) % 3], psb[3 + (un // 4) % 3]
                        cs_ = rv(cosT[:, b, :]); sn_ = rv(sinT[:, b, :])
                        rb = rhoT[:, b, :]
                        last = (n - 1) if d == 0 else 0

                        def s0():
                            if c == 0 and q == 0:
                                k.dma("sp", x[:, :, :n], xres[:, t0:t0 + n].rearrange("(c p) t -> p c t", p=128), writes=[b_x])
                                norm_mod(nscs[ti % 2], x, b_x, n, vsel, A1, 0, hT, bh)
                            k.op("pe", lambda e: e.matmul(pu[:, 0:128], Wre[:, c, q, :], hT[:, c, :], start=True, stop=True), reads=[bW, bh], writes=[pub], inc=False)
                            k.op("pe", lambda e: e.matmul(pu[:, 128:256], Wim[:, c, q, :], hT[:, c, :], start=True, stop=True), reads=[bW, bh], writes=[pub])

                        def s1():
                            k.op("act", lambda e: e.copy(ur[i3][:], pu[:, 0:128]), reads=[pub], writes=[bur[i3]])
                            k.op("act", lambda e: e.copy(ui[i3][:], pu[:, 128:256]), reads=[pub], writes=[bui[i3]])

                        def s2():
                            k.op("dve", lambda e: e.tensor_tensor(ta[i3][:], ur[i3][:], cs_, ALU.mult), reads=[bur[i3], btab], writes=[bta[i3]])
                            k.op("pool", lambda e: e.tensor_tensor(tb_[i3][:], ui[i3][:], sn_, ALU.mult), reads=[bui[i3], btab], writes=[btb[i3]])
                            k.op("dve", lambda e: e.tensor_tensor(tc[i3][:], ui[i3][:], cs_, ALU.mult), reads=[bui[i3], btab], writes=[btc[i3]])
                            k.op("pool", lambda e: e.tensor_tensor(td[i3][:], ur[i3][:], sn_, ALU.mult), reads=[bur[i3], btab], writes=[btd[i3]])

                        def s3():
                            k.op("pool", lambda e: e.tensor_tensor(ta[i3][:], ta[i3][:], tb_[i3][:], ALU.add), reads=[bta[i3], btb[i3]], writes=[bta[i3]])
                            k.op("pool", lambda e: e.tensor_tensor(tc[i3][:], tc[i3][:], td[i3][:], ALU.subtract), reads=[btc[i3], btd[i3]], writes=[btc[i3]])

                        def s4():
                            k.op("dve", lambda e: e.tensor_tensor_scan(rv(vr[i3][:]), rb, rv(ta[i3][:]), stR[:, b:b + 1], ALU.mult, ALU.add),
                                 reads=[bta[i3], bpar, bstate[b]], writes=[bvr[i3]])
                            k.op("dve", lambda e: e.tensor_tensor_scan(rv(vi[i3][:]), rb, rv(tc[i3][:]), stI[:, b:b + 1], ALU.mult, ALU.add),
                                 reads=[btc[i3], bpar, bstate[b]], writes=[bvi[i3]])

                        def s5():
                            k.op("pool", lambda e: e.tensor_tensor(q1[i3][:], vr[i3][:], cs_, ALU.mult), reads=[bvr[i3], btab], writes=[bq1[i3]])
                            k.op("dve", lambda e: e.scalar_tensor_tensor(q2[i3][:], vi[i3][:], -1.0, sn_, ALU.mult, ALU.mult), reads=[bvi[i3], btab], writes=[bq2[i3]])
                            k.op("pool", lambda e: e.tensor_tensor(q3[i3][:], vi[i3][:], cs_, ALU.mult), reads=[bvi[i3], btab], writes=[bq3[i3]])
                            k.op("dve", lambda e: e.tensor_tensor(q4[i3][:], vr[i3][:], sn_, ALU.mult), reads=[bvr[i3], btab], writes=[bq4[i3]])
                            k.op("act", lambda e: e.activation(stt_[i3][:, 0:1], vi[i3][:, last:last + 1], AF.Identity, scale=sn_[:, last:last + 1]),
                                 reads=[bvi[i3], btab], writes=[bstt[i3]])
                            k.op("act", lambda e: e.activation(stt_[i3][:, 0:1], stt_[i3][:, 0:1], AF.Identity, scale=-1.0), reads=[bstt[i3]], writes=[bstt[i3]])
                            k.op("act", lambda e: e.activation(stt_[i3][:, 1:2], vr[i3][:, last:last + 1], AF.Identity, scale=sn_[:, last:last + 1]),
                                 reads=[bvr[i3], btab], writes=[bstt[i3]])

                        def s6():
                            k.op("act", lambda e: e.activation(stR[:, b:b + 1], vr[i3][:, last:last + 1], AF.Identity, bias=stt_[i3][:, 0:1], scale=cs_[:, last:last + 1]),
                                 reads=[bvr[i3], btab, bstt[i3]], writes=[bstate[b]])
                            k.op("act", lambda e: e.activation(stI[:, b:b + 1], vi[i3][:, last:last + 1], AF.Identity, bias=stt_[i3][:, 1:2], scale=cs_[:, last:last + 1]),
                                 reads=[bvi[i3], btab, bstt[i3]], writes=[bstate[b]])
                            k.op("pe", lambda e: e.matmul(py[:, 0:128], WcR[:, b, :], q1[i3][:], start=(q == 0), stop=False), reads=[bWc, bq1[i3]], writes=[pyb], inc=False)
                            k.op("pe", lambda e: e.matmul(py[:, 0:128], WcR[:, b, :], q2[i3][:], start=False, stop=False), reads=[bWc, bq2[i3]], writes=[pyb], inc=False)
                            k.op("pe", lambda e: e.matmul(py[:, 0:128], WcI[:, b, :], q3[i3][:], start=False, stop=False), reads=[bWc, bq3[i3]], writes=[pyb], inc=False)
                            k.op("pe", lambda e: e.matmul(py[:, 0:128], WcI[:, b, :], q4[i3][:], start=False, stop=(q == 3)), reads=[bWc, bq4[i3]], writes=[pyb], inc=(q == 3))
                            if q != 3:
                                return
                            if d == 0:
                                o, bo = yo[c % 2], byo[c % 2]
                                k.op("act", lambda e: e.copy(o[:], py[:, 0:128]), reads=[pyb], writes=[bo])
                                k.dma("pool", yfT[c * 128:(c + 1) * 128, t0:t0 + n], o[:], reads=[bo], writes=[Buf()])
                                return
                            if vsel == 1 and not ctx_out:
                                return
                            f_, bf_ = yf[c % 2], byf[c % 2]
                            k.dma("sp", f_[:], yfT[c * 128:(c + 1) * 128, t0:t0 + n], writes=[bf_])
                            k.op("dve", lambda e: e.tensor_tensor(f_[:], f_[:], py[:, 0:128], ALU.add), reads=[pyb, bf_], writes=[bf_])
                            k.op("dve", lambda e: e.scalar_tensor_tensor(f_[:], hT[:, c, :], dskT[:, c:c + 1], f_[:], ALU.mult, ALU.add), reads=[bh, bdsk_, bf_], writes=[bf_])
                            k.op("act", lambda e: e.activation(gT[:, c, :], f_[:], AF.Gelu_apprx_tanh), reads=[bf_], writes=[bg])
                            if c != 7:
                                return
                            for c2 in range(8):
                                pa_, pab_ = next_ps()
                                pg_, pgb_ = next_ps()
                                for kc in range(8):
                                    k.op("pe", lambda e, kc=kc, c2=c2, pa_=pa_: e.matmul(pa_[:, 0:128], gw_[:, kc, c2 * 128:(c2 + 1) * 128], gT[:, kc, :], start=(kc == 0), stop=(kc == 7)),
                                         reads=[bgw_, bg], writes=[pab_], inc=(kc == 7))
                                for kc in range(8):
                                    k.op("pe", lambda e, kc=kc, c2=c2, pg_=pg_: e.matmul(pg_[:, 0:128], gw_[:, kc, D + c2 * 128:D + (c2 + 1) * 128], gT[:, kc, :], start=(kc == 0), stop=(kc == 7)),
                                         reads=[bgw_, bg], writes=[pgb_], inc=(kc == 7))
                                a_, ba_ = ga[c2 % 2], bga[c2 % 2]
                                k.op("act", lambda e, pg_=pg_, a_=a_, c2=c2: e.activation(a_[:], pg_[:, 0:128], AF.Sigmoid, bias=gbias[:, 8 + c2:9 + c2], scale=1.0), reads=[pgb_, bgb_], writes=[ba_])
                                k.op("dve", lambda e, pa_=pa_, a_=a_, c2=c2: e.scalar_tensor_tensor(a_[:], pa_[:, 0:128], gbias[:, c2:c2 + 1], a_[:], ALU.add, ALU.mult), reads=[pab_, bgb_, ba_], writes=[ba_])
                                k.op("dve", lambda e, a_=a_, c2=c2: e.scalar_tensor_tensor(x[:, c2, :], a_[:], mod[:, 16 + c2, vsel:vsel + 1], x[:, c2, :], ALU.mult, ALU.add),
                                     reads=[ba_, b_mod, b_x], writes=[b_x])
                            k.dma("pool", xres[:, t0:t0 + n].rearrange("(c p) t -> p c t", p=128), x[:, :, :n], reads=[b_x], writes=[Buf()])

                        return [s0, s1, s2, s3, s4, s5, s6]

                    un = 0
                    for ti, (t0, n, vsel) in enumerate(order):
                        for c in range(8):
                            for q in range(4):
                                units.append(make_unit(ti, t0, n, vsel, c, q, un))
                                un += 1
                    NS = 7
                    for step in range(len(units) + NS - 1):
                        for j in range(NS - 1, -1, -1):
                            u = step - j
                            if 0 <= u < len(units):
                                units[u][j]()
                    pspool[0] = list(range(8))
                k.barrier()
                if debug == "s5f" and d == 0:
                    return

        def mla_layer(li, ctx_out):
            TT = 512
            SCALE = 96.0 ** -0.5
            with ExitStack() as L:
                qnT = sb("qnT", [128, 3, NLAT], BF16, L); bqn = Buf()
                kvnT = sb("kvnT", [128, 2, NTOK], BF16, L); bkvn = Buf()
                KT = sb("KT", [128, NTOK], BF16, L); bKr = Buf(); bKn = Buf()
                k.op("pool", lambda e: e.memset(KT[96:97, :], 1.0), writes=[bKr])
                with ExitStack() as st:
                    w, bw = load_w(st, "mlin", mla_in_w[0], 8, 672)
                    wkr = sb("wkr", [128, 8, 96], BF16, st); wkrs = sb("wkrs", [128, 8, 96], BF16, st); bwk = Buf()
                    k.op("pool", lambda e: e.memset(wkr[:], 0.0), writes=[bwk])
                    k.op("pool", lambda e: e.memset(wkrs[:], 0.0), writes=[bwk])
                    k.dma("pool", wkr[:, :, 64:96], mla_in_w[0][:, 640:672].rearrange("(c p) n -> p c n", p=128), writes=[bwk], allow_slow_non_contiguous=True)
                    k.dma("pool", wkrs[:, :, 64:96], mla_inw_sw.rearrange("(c p) n -> p c n", p=128), writes=[bwk], allow_slow_non_contiguous=True)
                    qnw = sb("qnw", [128, 3], F32, st); kvnw = sb("kvnw", [128, 2], F32, st); bnw_ = Buf()
                    k.dma("sp", qnw[:], mla_q_norm_w[0].rearrange("(c p) -> p c", p=128), writes=[bnw_], allow_slow_non_contiguous=True)
                    k.dma("sp", kvnw[:], mla_kv_norm_w[0].rearrange("(c p) -> p c", p=128), writes=[bnw_], allow_slow_non_contiguous=True)
                    nsc = norm_scratch(st, TT)
                    xt = [sb("ma_x%d" % i, [128, 8, TT], F32, st) for i in range(2)]; bx = [Buf(), Buf()]
                    hT = sb("ma_h", [128, 8, TT], BF16, st); bh = Buf()
                    ql = sb("ma_ql", [128, 3, TT], F32, st); bql = Buf()
                    sq = sb("ma_sq", [128, TT], F32, st); bsq = Buf()
                    rs = sb("ma_rs", [128, TT], F32, st); brs = Buf()
                    rc = [sb("ma_rc%d" % i, [128, TT], F32, st) for i in range(2)]; rsn = [sb("ma_rsn%d" % i, [128, TT], F32, st) for i in range(2)]; brc = [Buf(), Buf()]
                    t1 = sb("ma_t1", [128, TT], F32, st); t2 = sb("ma_t2", [128, TT], F32, st); bt1 = Buf(); bt2 = Buf()

                    def lat_norm(ncl, col0, nw_t, dim, out_t, out_b, tsl):
                        pss, pssb = next_ps()
                        for c in range(ncl):
                            p, pb = next_ps()
                            for kc in range(8):
                                k.op("pe", lambda e, p=p, kc=kc, c=c: e.matmul(p[:, :n], w[:, kc, col0 + c * 128:col0 + (c + 1) * 128], hT[:, kc, :n], start=(kc == 0), stop=(kc == 7)),
                                     reads=[bw, bh], writes=[pb], inc=(kc == 7))
                            k.op("act", lambda e, p=p, c=c: e.copy(ql[:, c, :n], p[:, :n]), reads=[pb], writes=[bql])
                            k.op("act", lambda e, p=p: e.activation(sq[:, :n], p[:, :n], AF.Square), reads=[pb], writes=[bsq])
                            k.op("pe", lambda e, pss=pss, c=c: e.matmul(pss[:, :n], ones_f[:], sq[:, :n], start=(c == 0), stop=(c == ncl - 1)), reads=[b_ones, bsq], writes=[pssb])
                        k.op("act", lambda e, pss=pss: e.activation(rs[:, :n], pss[:, :n], AF.Sqrt, bias=EPS, scale=1.0 / dim), reads=[pssb], writes=[brs])
                        k.op("dve", lambda e: e.reciprocal(rs[:, :n], rs[:, :n]), reads=[brs], writes=[brs])
                        for c in range(ncl):
                            k.op("dve", lambda e, c=c: e.scalar_tensor_tensor(out_t[:, c, tsl], ql[:, c, :n], nw_t[:, c:c + 1], rs[:, :n], ALU.mult, ALU.mult),
                                 reads=[bql, bnw_, brs], writes=[out_b])

                    for ti, (t0, n, vsel) in enumerate(seg_tiles(TT)):
                        x, b_x = xt[ti % 2], bx[ti % 2]
                        k.dma("sp", x[:, :, :n], xres[:, t0:t0 + n].rearrange("(c p) t -> p c t", p=128), writes=[b_x])
                        norm_mod(nsc, x, b_x, n, vsel, A1, 0, hT, bh)
                        if vsel == 0:
                            lat_norm(3, 0, qnw, 384.0, qnT, bqn, slice(t0 - NCTX, t0 - NCTX + n))
                        lat_norm(2, 384, kvnw, 256.0, kvnT, bkvn, slice(t0, t0 + n))
                        pA, pAb = next_ps()
                        for kc in range(8):
                            k.op("pe", lambda e, pA=pA, kc=kc: e.matmul(pA[0:96, :n], wkr[:, kc, :], hT[:, kc, :n], start=(kc == 0), stop=(kc == 7)), reads=[bwk, bh], writes=[pAb], inc=(kc == 7))
                        if vsel == 1:
                            k.op("act", lambda e, pA=pA: e.copy(KT[64:96, t0:t0 + n], pA[64:96, :n]), reads=[pAb], writes=[bKr])
                        else:
                            pB, pBb = next_ps()
                            for kc in range(8):
                                k.op("pe", lambda e, pB=pB, kc=kc: e.matmul(pB[0:96, :n], wkrs[:, kc, :], hT[:, kc, :n], start=(kc == 0), stop=(kc == 7)), reads=[bwk, bh], writes=[pBb], inc=(kc == 7))
                            i2 = ti % 2
                            k.dma("sp", rc[i2][64:96, :n], ropeC_in[:, t0 - NCTX:t0 - NCTX + n], writes=[brc[i2]])
                            k.dma("sp", rsn[i2][64:96, :n], ropeS_in[:, t0 - NCTX:t0 - NCTX + n], writes=[brc[i2]])
                            k.op("dve", lambda e, pA=pA, i2=i2: e.tensor_tensor(t1[64:96, :n], pA[64:96, :n], rc[i2][64:96, :n], ALU.mult), reads=[pAb, brc[i2]], writes=[bt1])
                            k.op("dve", lambda e, pB=pB, i2=i2: e.tensor_tensor(t2[64:96, :n], pB[64:96, :n], rsn[i2][64:96, :n], ALU.mult), reads=[pBb, brc[i2]], writes=[bt2])
                            k.op("pool", lambda e: e.tensor_tensor(KT[64:96, t0:t0 + n], t1[64:96, :n], t2[64:96, :n], ALU.add), reads=[bt1, bt2], writes=[bKr])
                k.barrier()
                with ExitStack() as st:
                    kvbw, bkvbw = load_w(st, "kvbw", mla_kvb_w[0], 2, 2048)
                    qbw, bqbw = load_w(st, "qbw", mla_qb_w[0], 3, 1536)
                    qbws, bqbws = load_w(st, "qbws", mla_qbw_sw, 3, 512)
                    e96 = sb("e96", [128, 128], BF16, st); be96 = Buf()
                    k.dma("pool", e96[:], e96_in, writes=[be96])
                    odm = sb("odm", [128, 2, 128], BF16, st); bodm = Buf()
                    k.dma("pool", odm[:], odm_in.rearrange("a p m -> p a m"), writes=[bodm])
                    wv = [sb("wv%d" % i, [128, 2, 128], BF16, st) for i in range(2)]; bwv = [Buf(), Buf()]
                    wqr = sb("wqr", [128, 3, 96], BF16, st); wqrs = sb("wqrs", [128, 3, 96], BF16, st); bwq = Buf()
                    for t_ in (wv[0], wv[1], wqr, wqrs):
                        k.op("pool", lambda e, t_=t_: e.memset(t_[:], 0.0), writes=[bwv[0], bwv[1], bwq])
                    QT = sb("QT", [128, NLAT], BF16, st); bQ = Buf()
                    Vt = sb("Vt", [128, NCH, 128], BF16, st); bV = Buf()
                    sqb = sb("mb_sq", [128, TT], BF16, st); bsqb = Buf()
                    kk = sb("mb_kk", [128, TT], F32, st); bkk = Buf()
                    kmx = sb("mb_kmx", [128, 2], F32, st); bkmx = Buf()
                    rc = [sb("mb_rc%d" % i, [128, TT], F32, st) for i in range(2)]; rsn = [sb("mb_rsn%d" % i, [128, TT], F32, st) for i in range(2)]; brc = [Buf(), Buf()]
                    t1 = sb("mb_t1", [128, TT], F32, st); t2 = sb("mb_t2", [128, TT], F32, st); bt1 = Buf(); bt2 = Buf()
                    Pt = [sb("mb_P%d" % i, [128, TT], BF16, st) for i in range(3)]; bP = [Buf() for _ in range(3)]
                    rd = sb("mb_rd", [128, TT], F32, st); brd = Buf()
                    xs_ = sb("mb_xs", [128, TT], F32, st); bxs_ = Buf()
                    swp = sb("mb_swp", [128, 128], F32, st); bswp = Buf()
                    k.dma("sp", swp[:], swp_in, writes=[bswp])
                    ot = [sb("mb_o%d" % i, [128, TT], BF16, st) for i in range(2)]; bot = [Buf(), Buf()]
                    pspool[0] = [4, 5, 6, 7]
                    ktiles = seg_tiles(TT)
                    pn = 0
                    for h in range(16):
                        par = h % 2
                        off = par * 64
                        k.op("pool", lambda e, h=h, par=par, off=off: e.tensor_copy(wv[par][:, :, off:off + 64], kvbw[:, :, h * 128 + 64:h * 128 + 128]), reads=[bkvbw], writes=[bwv[par]])
                        k.op("pool", lambda e, h=h: e.tensor_copy(wqr[:, :, 64:96], qbw[:, :, h * 96 + 64:h * 96 + 96]), reads=[bqbw], writes=[bwq])
                        k.op("pool", lambda e, h=h: e.tensor_copy(wqrs[:, :, 64:96], qbws[:, :, h * 32:h * 32 + 32]), reads=[bqbws], writes=[bwq])
                        k.op("pool", lambda e: e.memset(kmx[:], 0.0), writes=[bkmx])
                        for (t0, n, vsel) in ktiles:
                            p, pb = next_ps()
                            for c in range(2):
                                k.op("pe", lambda e, p=p, c=c, h=h: e.matmul(p[0:64, :n], kvbw[:, c, h * 128:h * 128 + 64], kvnT[:, c, t0:t0 + n], start=(c == 0), stop=(c == 1)),
                                     reads=[bkvbw, bkvn], writes=[pb], inc=(c == 1))
                            k.op("act", lambda e, p=p: e.copy(KT[0:64, t0:t0 + n], p[0:64, :n]), reads=[pb], writes=[bKn])
                            k.op("act", lambda e: e.activation(sqb[0:96, :n], KT[0:96, t0:t0 + n], AF.Square), reads=[bKn, bKr], writes=[bsqb])
                            p2, p2b = next_ps()
                            k.op("pe", lambda e, p2=p2: e.matmul(p2[0:97, :n], e96[0:96, 0:97], sqb[0:96, :n], start=True, stop=True), reads=[be96, bsqb], writes=[p2b])
                            k.op("dve", lambda e, p2=p2: e.reduce_max(kmx[96:97, 1:2], p2[96:97, :n], AX.X), reads=[p2b], writes=[bkmx])
                            k.op("dve", lambda e: e.tensor_tensor(kmx[96:97, 0:1], kmx[96:97, 0:1], kmx[96:97, 1:2], ALU.max), reads=[bkmx], writes=[bkmx])
                        k.op("act", lambda e: e.activation(kmx[96:97, 0:1], kmx[96:97, 0:1], AF.Sqrt), reads=[bkmx], writes=[bkmx])
                        k.op("dve", lambda e: e.tensor_scalar(kmx[96:97, 0:1], kmx[96:97, 0:1], -1.0, None, ALU.mult), reads=[bkmx], writes=[bkmx])
                        for b4 in range(0, NCH, 4):
                            p, pb = next_ps()
                            nb = min(4, NCH - b4)
                            for j in range(nb):
                                blk = b4 + j
                                for c in range(2):
                                    k.op("pe", lambda e, p=p, j=j, blk=blk, c=c, par=par: e.matmul(p[:, j * 128:(j + 1) * 128], kvnT[:, c, blk * 128:(blk + 1) * 128], wv[par][:, c, :],
                                                                                                start=(c == 0), stop=(c == 1)),
                                         reads=[bkvn, bwv[par]], writes=[pb], inc=(c == 1 and j == nb - 1))
                            k.op("dve", lambda e, p=p, b4=b4, nb=nb, off=off: e.tensor_copy(Vt[:, b4:b4 + nb, off:off + 64], p[:, 0:nb * 128].rearrange("p (j m) -> p j m", m=128)[:, :, off:off + 64]),
                                 reads=[pb], writes=[bV])
                        k.op("pool", lambda e, off=off: e.memset(Vt[:, :, 64 - off:128 - off], 1.0), writes=[bV])
                        for qi in range(NLAT // TT):
                            q0 = qi * TT
                            p, pb = next_ps()
                            for c in range(3):
                                k.op("pe", lambda e, p=p, c=c, h=h: e.matmul(p[0:64, :], qbw[:, c, h * 96:h * 96 + 64], qnT[:, c, q0:q0 + TT], start=(c == 0), stop=(c == 2)),
                                     reads=[bqbw, bqn], writes=[pb], inc=(c == 2))
                            k.op("act", lambda e, p=p: e.copy(QT[0:64, q0:q0 + TT], p[0:64, :]), reads=[pb], writes=[bQ])
                            pA, pAb = next_ps()
                            pB, pBb = next_ps()
                            for c in range(3):
                                k.op("pe", lambda e, pA=pA, c=c: e.matmul(pA[0:96, :], wqr[:, c, :], qnT[:, c, q0:q0 + TT], start=(c == 0), stop=(c == 2)), reads=[bwq, bqn], writes=[pAb], inc=(c == 2))
                            for c in range(3):
                                k.op("pe", lambda e, pB=pB, c=c: e.matmul(pB[0:96, :], wqrs[:, c, :], qnT[:, c, q0:q0 + TT], start=(c == 0), stop=(c == 2)), reads=[bwq, bqn], writes=[pBb], inc=(c == 2))
                            i2 = qi % 2
                            k.dma("sp", rc[i2][64:96, :], ropeC_in[:, q0:q0 + TT], writes=[brc[i2]])
                            k.dma("sp", rsn[i2][64:96, :], ropeS_in[:, q0:q0 + TT], writes=[brc[i2]])
                            k.op("dve", lambda e, pA=pA, i2=i2: e.tensor_tensor(t1[64:96, :], pA[64:96, :], rc[i2][64:96, :], ALU.mult), reads=[pAb, brc[i2]], writes=[bt1])
                            k.op("dve", lambda e, pB=pB, i2=i2: e.tensor_tensor(t2[64:96, :], pB[64:96, :], rsn[i2][64:96, :], ALU.mult), reads=[pBb, brc[i2]], writes=[bt2])
                            k.op("pool", lambda e: e.tensor_tensor(QT[64:96, q0:q0 + TT], t1[64:96, :], t2[64:96, :], ALU.add), reads=[bt1, bt2], writes=[bQ])
                            k.op("act", lambda e: e.activation(sqb[0:96, :], QT[0:96, q0:q0 + TT], AF.Square), reads=[bQ], writes=[bsqb])
                            p2, p2b = next_ps()
                            k.op("pe", lambda e, p2=p2: e.matmul(p2[0:97, :], e96[0:96, 0:97], sqb[0:96, :], start=True, stop=True), reads=[be96, bsqb], writes=[p2b])
                            k.op("act", lambda e, p2=p2: e.activation(kk[96:97, :], p2[96:97, :], AF.Sqrt), reads=[p2b], writes=[bkk])
                            k.op("dve", lambda e: e.tensor_scalar(QT[96:97, q0:q0 + TT], kk[96:97, :], kmx[96:97, 0:1], None, ALU.mult), reads=[bkk, bkmx], writes=[bQ])
                        for qi in range(NLAT // TT):
                            q0 = qi * TT
                            pnum, pnb = ps[qi % 2], psb[qi % 2]
                            sbank = {}

                            def emit_S(blk):
                                nonlocal pn
                                p, pb = next_ps()
                                k.op("pe", lambda e, p=p, blk=blk: e.matmul(p[:, :], KT[0:97, blk * 128:(blk + 1) * 128], QT[0:97, q0:q0 + TT], start=True, stop=True),
                                     reads=[bKn, bKr, bQ], writes=[pb])
                                i3 = pn % 3
                                pn += 1
                                k.op("act", lambda e, p=p, i3=i3: e.activation(Pt[i3][:], p[:, :], AF.Exp, scale=SCALE), reads=[pb], writes=[bP[i3]])
                                sbank[blk] = i3
                            emit_S(0)
                            emit_S(1)
                            for blk in range(NCH):
                                i3 = sbank.pop(blk)
                                k.op("pe", lambda e, blk=blk, i3=i3, pnum=pnum: e.matmul(pnum[:, :], Vt[:, blk, :], Pt[i3][:], start=(blk == 0), stop=(blk == NCH - 1)),
                                     reads=[bV, bP[i3]], writes=[pnb])
                                if blk + 2 < NCH:
                                    emit_S(blk + 2)
                            k.op("act", lambda e, pnum=pnum: e.copy(xs_[:], pnum[:, :]), reads=[pnb], writes=[bxs_])
                            psw, pswb = ps[2 + qi % 2], psb[2 + qi % 2]
                            k.op("pe", lambda e, psw=psw: e.matmul(psw[:, :], swp[:], xs_[:], start=True, stop=True), reads=[bswp, bxs_], writes=[pswb])
                            k.op("dve", lambda e, psw=psw, off=off: e.reciprocal(rd[off:off + 64, :], psw[off:off + 64, :]), reads=[pswb], writes=[brd])
                            o_, b_o = ot[qi % 2], bot[qi % 2]
                            k.op("dve", lambda e, off=off, o_=o_: e.tensor_tensor(o_[off:off + 64, :], xs_[off:off + 64, :], rd[off:off + 64, :], ALU.mult),
                                 reads=[bxs_, brd], writes=[b_o])
                            k.dma("sp", scrM[h * 64:(h + 1) * 64, NCTX + q0:NCTX + q0 + TT], o_[off:off + 64, :], reads=[b_o], writes=[Buf()])
                    pspool[0] = list(range(8))
                k.barrier()
            k.barrier()
            outproj_phase(mla_out_w[0], 8, scrM, ctx_out)

        with ExitStack() as st:
            cp = [sb("cp%d" % i, [128, 8, 512], F32, st) for i in range(2)]; bcp = [Buf(), Buf()]
            for ti, (t0, n, vsel) in enumerate(seg_tiles(512)):
                k.dma("sp", cp[ti % 2][:, :, :n], xT_in[:, t0:t0 + n].rearrange("(c p) t -> p c t", p=128), writes=[bcp[ti % 2]])
                k.dma("pool", xres[:, t0:t0 + n].rearrange("(c p) t -> p c t", p=128), cp[ti % 2][:, :, :n], reads=[bcp[ti % 2]], writes=[Buf()])
        k.barrier()
        for li in layers:
            ctx_out = li < 3
            compute_mod(li)
            if debug == "mod_only":
                continue
            if debug == "ffn_only":
                ffn_phase(li, ctx_out)
                continue
            if li == 0:
                ssd_layer(li, ctx_out)
                if debug in ("ssdA", "ssd1", "ssd2"):
                    continue
            if li == 1:
                lru_layer(li, ctx_out)
            if li == 3:
                mla_layer(li, ctx_out)
            if li == 2:
                s5_layer(li, ctx_out)
                if debug == "s5f" and debug_sub in ("pre", "pre2", "pre3", "p_a"):
                    k.barrier()
                    return nc, list(di.keys())
                if debug == "s5f":
                    with ExitStack() as st:
                        cp = [sb("dq%d" % i, [128, 8, 512], F32, st) for i in range(2)]; bcp = [Buf(), Buf()]
                        for ti, (t0, n, vsel) in enumerate(seg_tiles(512)):
                            if vsel == 1:
                                continue
                            k.dma("sp", cp[ti % 2][:, :, :n], scrA[0:D, t0:t0 + n].rearrange("(c p) t -> p c t", p=128), writes=[bcp[ti % 2]])
                            k.dma("pool", outT[:, t0 - NCTX:t0 - NCTX + n].rearrange("(c p) t -> p c t", p=128), cp[ti % 2][:, :, :n], reads=[bcp[ti % 2]], writes=[Buf()])
                    k.barrier()
                    print("ninst", k.ninst)
                    return nc, list(di.keys())
            ffn_phase(li, ctx_out)
        if do_final:
            final_phase()
        else:
            with ExitStack() as st:
                cp = [sb("dp%d" % i, [128, 8, 512], F32, st) for i in range(2)]; bcp = [Buf(), Buf()]
                for ti, (t0, n, vsel) in enumerate(seg_tiles(512)):
                    if vsel == 1:
                        continue
                    k.dma("sp", cp[ti % 2][:, :, :n], xres[:, t0:t0 + n].rearrange("(c p) t -> p c t", p=128), writes=[bcp[ti % 2]])
                    k.dma("pool", outT[:, t0 - NCTX:t0 - NCTX + n].rearrange("(c p) t -> p c t", p=128), cp[ti % 2][:, :, :n], reads=[bcp[ti % 2]], writes=[Buf()])
            k.barrier()
        k.barrier()
        print("ninst", k.ninst)
    return nc, list(di.keys())


def make_in_map(inputs, b, names, x_override=None, ctx_override=None):
    xb = inputs["x"][b] if x_override is None else x_override
    cb = inputs["ctx"][b] if ctx_override is None else ctx_override
    xT = np.ascontiguousarray(np.concatenate([cb, xb], axis=0).T.astype(np.float32))
    cc = np.ascontiguousarray(np.stack([inputs["c"][b], inputs["c_ctx"]], axis=1).astype(np.float32))
    ii = np.arange(128)
    U = (ii[:, None] <= ii[None, :]).astype(np.float32)
    masks = np.stack([U, U.T.copy(), (ii[:, None] > ii[None, :]).astype(np.float32), (ii[:, None] < ii[None, :]).astype(np.float32)], axis=0)
    m = {"xT": xT, "cc": cc, "ident": np.eye(128, dtype=np.float32), "masks": np.ascontiguousarray(masks)}
    m["ramp"] = np.ascontiguousarray(np.tile(np.arange(1, 129, dtype=np.float32)[None, :], (128, 1)))
    sel = np.zeros((128, 8), np.float32)
    for gl8 in range(8):
        sel[gl8 * 16:(gl8 + 1) * 16, gl8] = 1.0
    m["sel"] = sel
    perm = np.concatenate([np.arange(8, 16), np.arange(0, 8), np.arange(24, 32), np.arange(16, 24)])
    m["mla_inw_sw"] = np.ascontiguousarray(np.asarray(inputs["mla_in_w"], np.float32)[0][:, 640:672][:, perm])
    qb = np.asarray(inputs["mla_qb_w"], np.float32)[0].reshape(384, 16, 96)
    m["mla_qbw_sw"] = np.ascontiguousarray(qb[:, :, 64:96][:, :, perm].reshape(384, 512))
    rows = NLAT // 64
    row = np.repeat(np.arange(rows, dtype=np.float32), 64)
    col = np.tile(np.arange(64, dtype=np.float32), rows)
    inv_freq = (np.float32(10000.0) ** (-np.arange(8, dtype=np.float32) / np.float32(8))).astype(np.float32)
    ang = np.stack([row[:, None] * inv_freq, col[:, None] * inv_freq], axis=1).astype(np.float32)
    cs, sn = np.cos(ang).astype(np.float32), np.sin(ang).astype(np.float32)
    C = np.zeros((32, NLAT), np.float32); S = np.zeros((32, NLAT), np.float32)
    for half in range(2):
        for part in range(2):
            r0 = half * 16 + part * 8
            C[r0:r0 + 8] = cs[:, half, :].T
            S[r0:r0 + 8] = (-sn[:, half, :].T) if part == 0 else sn[:, half, :].T
    m["ropeC"] = C; m["ropeS"] = S
    e96 = np.zeros((128, 128), np.float32); e96[0:96, 96] = 1.0
    m["e96"] = e96
    odm = np.zeros((2, 128, 128), np.float32); odm[0, :, 0:64] = 1.0; odm[1, :, 64:128] = 1.0
    m["odm"] = odm
    swp = np.zeros((128, 128), np.float32)
    swp[(np.arange(128) + 64) % 128, np.arange(128)] = 1.0
    m["swp"] = swp
    for n in names:
        if n not in m:
            m[n] = np.ascontiguousarray(np.asarray(inputs[n], dtype=np.float32))
    return m


def kernel(**inputs):
    inputs = {k_: np.asarray(v) for k_, v in inputs.items()}
    nc, names = build()
    nb = inputs["x"].shape[0]
    in_maps = [make_in_map(inputs, b, names) for b in range(nb)]
    res = run_bass_kernel_spmd(nc, in_maps, core_ids=list(range(nb)))
    out = np.stack([np.ascontiguousarray(r["outT"].T) for r in res.results], axis=0)
    return out.astype(np.float32)
```

```python
import numpy as np
from contextlib import ExitStack
import concourse.bass as bass
import concourse.mybir as mybir
from concourse.bass_utils import run_bass_kernel_spmd

F32 = mybir.dt.float32
BF16 = mybir.dt.bfloat16
I32 = mybir.dt.int32
AF = mybir.ActivationFunctionType
ALU = mybir.AluOpType
AX = mybir.AxisListType

D = 1024
NCTX = 256
NLAT = 8192
NTOK = NCTX + NLAT
FH = 2816
EPS = 1e-6


class Buf:
    __slots__ = ("name", "lw", "rd")

    def __init__(self, name=""):
        self.name = name
        self.lw = None
        self.rd = []


ATTACH_WAITS = True


class K:
    NDMA = 8

    def __init__(self, nc, es):
        self.nc = nc
        self.es = es
        self.eng = {"pe": nc.tensor, "dve": nc.vector, "act": nc.scalar, "pool": nc.gpsimd, "sp": nc.sync}
        self.sem = {}
        self.cnt = {}
        for e in self.eng:
            self.sem[e] = es.enter_context(nc.semaphore("s_" + e))
            self.cnt[e] = 0
        self.dslots = {}
        for q in ("sp", "pool", "act"):
            self.dslots[q] = []
            for j in range(self.NDMA):
                key = "d_%s_%d" % (q, j)
                self.sem[key] = es.enter_context(nc.semaphore(key))
                self.cnt[key] = 0
                self.dslots[q].append(key)
        self.dnext = {q: 0 for q in self.dslots}
        self.seen = {e: {} for e in self.eng}
        self.ninst = 0

    def _need(self, e, deps, attach=False):
        best = {}
        for (k, v) in deps:
            if k == e and e == "pe":
                continue
            if best.get(k, 0) < v:
                best[k] = v
        todo = [(k, v) for k, v in best.items() if self.seen[e].get(k, 0) < v]
        held = None
        if attach and ATTACH_WAITS and todo:
            held = todo.pop()
        for k, v in todo:
            self.eng[e].wait_ge(self.sem[k], v)
            self.seen[e][k] = v
            self.ninst += 1
        if held is not None:
            self.seen[e][held[0]] = held[1]
        return held

    def _deps(self, reads, writes):
        deps = []
        for b in reads:
            if b.lw is not None:
                deps.append(b.lw)
        for b in writes:
            if b.lw is not None:
                deps.append(b.lw)
            deps.extend(b.rd)
        return deps

    def op(self, e, fn, reads=(), writes=(), inc=True):
        held = self._need(e, self._deps(reads, writes), attach=True)
        ins = fn(self.eng[e])
        if held is not None:
            ins._wait_ge(self.sem[held[0]], held[1])
        self.ninst += 1
        if inc:
            self.cnt[e] += 1
            ins.then_inc(self.sem[e], 1)
            v = self.cnt[e]
        else:
            v = self.cnt[e] + 1
        for b in reads:
            b.rd.append((e, v))
            if len(b.rd) > 24:
                b.rd = self._compact(b.rd)
        for b in writes:
            b.lw = (e, v)
            b.rd = []
        return ins

    @staticmethod
    def _compact(rd):
        best = {}
        for (k, v) in rd:
            if best.get(k, 0) < v:
                best[k] = v
        return list(best.items())

    def dma(self, q, out, in_, reads=(), writes=(), **kw):
        if out.dtype != in_.dtype:
            q = "pool"
        slot = self.dslots[q][self.dnext[q] % self.NDMA]
        self.dnext[q] += 1
        deps = self._deps(reads, writes)
        if self.cnt[slot] > 0:
            deps.append((slot, self.cnt[slot]))
        self._need(q, deps)
        ins = self.eng[q].dma_start(out=out, in_=in_, **kw)
        self.ninst += 1
        self.cnt[slot] += 16
        ins.then_inc(self.sem[slot], 16)
        v = self.cnt[slot]
        for b in reads:
            b.rd.append((slot, v))
            if len(b.rd) > 24:
                b.rd = self._compact(b.rd)
        for b in writes:
            b.lw = (slot, v)
            b.rd = []
        return ins

    def barrier(self):
        deps = [(k, v) for k, v in self.cnt.items() if v > 0]
        for e in self.eng:
            self._need(e, deps)


def build(layers=(0, 1, 2, 3), do_final=True, debug=None, debug_sub=None):
    nc = bass.Bass("TRN2", target_bir_lowering=False)
    di = {}

    def inp(name, shape):
        di[name] = nc.dram_tensor(name, list(shape), F32, kind="ExternalInput").ap()
        return di[name]

    xT_in = inp("xT", [D, NTOK])
    cc_in = inp("cc", [D, 2])
    ident_in = inp("ident", [128, 128])
    ada_w = inp("ada_w", [4, D, 6 * D]); ada_b = inp("ada_b", [4, 6 * D])
    norm1_w = inp("norm1_w", [4, D]); norm2_w = inp("norm2_w", [4, D])
    ffn_w13 = inp("ffn_w13", [4, D, 2 * FH]); ffn_w2 = inp("ffn_w2", [4, FH, D])
    lru_in_w = inp("lru_in_w", [1, D, 2560]); lru_conv_w = inp("lru_conv_w", [1, 4, 1280]); lru_conv_b = inp("lru_conv_b", [1, 1280])
    lru_gate_w = inp("lru_gate_w", [1, 2, 10, 128, 256]); lru_gate_b = inp("lru_gate_b", [1, 2, 10, 256])
    lru_a_param = inp("lru_a_param", [1, 2, 1280]); lru_out_w = inp("lru_out_w", [1, 1280, D])
    masks_in = inp("masks", [4, 128, 128])
    ramp_in = inp("ramp", [128, 128]); sel_in = inp("sel", [128, 8])
    s5_lre = inp("s5_lambda_re", [1, 2, 64, 64]); s5_lim = inp("s5_lambda_im", [1, 2, 64, 64]); s5_ls = inp("s5_log_step", [1, 2, 64])
    s5_bre = inp("s5_b_re", [1, 64, 64, 16]); s5_bim = inp("s5_b_im", [1, 64, 64, 16])
    s5_cre = inp("s5_c_re", [1, 2, 64, 16, 64]); s5_cim = inp("s5_c_im", [1, 2, 64, 16, 64])
    s5_d = inp("s5_d", [1, D]); s5_glu_w = inp("s5_glu_w", [1, D, 2 * D]); s5_glu_b = inp("s5_glu_b", [1, 2 * D])
    m2_in_w = inp("m2_in_w", [1, D, 5184]); m2_conv_w = inp("m2_conv_w", [1, 4, 3072]); m2_conv_b = inp("m2_conv_b", [1, 3072])
    m2_dt_bias = inp("m2_dt_bias", [1, 2, 32]); m2_a_log = inp("m2_a_log", [1, 2, 32]); m2_d = inp("m2_d", [1, 32])
    m2_norm_w = inp("m2_norm_w", [1, 2048]); m2_out_w = inp("m2_out_w", [1, 2048, D])
    mla_in_w = inp("mla_in_w", [1, D, 672]); mla_q_norm_w = inp("mla_q_norm_w", [1, 384]); mla_kv_norm_w = inp("mla_kv_norm_w", [1, 256])
    mla_qb_w = inp("mla_qb_w", [1, 384, 1536]); mla_kvb_w = inp("mla_kvb_w", [1, 256, 2048]); mla_out_w = inp("mla_out_w", [1, D, D])
    mla_inw_sw = inp("mla_inw_sw", [D, 32]); mla_qbw_sw = inp("mla_qbw_sw", [384, 512])
    ropeC_in = inp("ropeC", [32, NLAT]); ropeS_in = inp("ropeS", [32, NLAT])
    e96_in = inp("e96", [128, 128]); odm_in = inp("odm", [2, 128, 128]); swp_in = inp("swp", [128, 128])
    final_norm_w = inp("final_norm_w", [D])
    outT = nc.dram_tensor("outT", [D, NLAT], F32, kind="ExternalOutput").ap()
    xres = nc.dram_tensor("xres", [D, NTOK], F32).ap()
    scrA = nc.dram_tensor("scrA", [2048, NTOK], F32).ap()
    scrB = nc.dram_tensor("scrB", [3072, NTOK], F32).ap()
    scrM = nc.dram_tensor("scrM", [2048, NTOK], BF16).ap()
    NCH = NTOK // 128
    scrZ = nc.dram_tensor("scrZ", [NTOK, 2048], F32).ap()
    scrDT = nc.dram_tensor("scrDT", [NTOK, 64], F32).ap()
    scrS = nc.dram_tensor("scrS", [2, NCH, 128, 2048], BF16).ap()
    scrXB = nc.dram_tensor("scrXB", [3072, NTOK], BF16).ap()
    scrH = nc.dram_tensor("scrH", [2, NCH, 128, 2048], BF16).ap()
    scrDec = nc.dram_tensor("scrDec", [NCH, 128, 64], F32).ap()
    scrY = nc.dram_tensor("scrY", [NCH, 128, 2048], F32).ap()
    scrC = nc.dram_tensor("scrC", [NCH, 128, 512], BF16).ap()
    scrE = nc.dram_tensor("scrE", [NCH, 128, 64], F32).ap()

    es = ExitStack()
    with es:
        k = K(nc, es)
        uniq = [0]

        def sb(name, shape, dt, st=es):
            uniq[0] += 1
            return st.enter_context(nc.sbuf_tensor("%s_%d" % (name, uniq[0]), shape, dt))

        ps = [es.enter_context(nc.psum_tensor("ps%d" % i, [128, 512], F32)) for i in range(8)]
        psb = [Buf("ps%d" % i) for i in range(8)]
        psn = [0]

        pspool = [list(range(8))]

        def next_ps():
            pl = pspool[0]
            i = pl[psn[0] % len(pl)]
            psn[0] += 1
            return ps[i], psb[i]

        ones_f = sb("ones_f", [128, 128], F32); b_ones = Buf()
        k.op("pool", lambda e: e.memset(ones_f[:], 1.0), writes=[b_ones])
        ident_f = sb("ident_f", [128, 128], F32); b_ident = Buf()
        k.dma("sp", ident_f[:], ident_in, writes=[b_ident])
        ident_b = sb("ident_b", [128, 128], BF16); b_identb = Buf()
        k.dma("sp", ident_b[:], ident_in, writes=[b_identb])
        ccT = sb("ccT", [128, 8, 2], F32); b_cc = Buf()
        k.dma("sp", ccT[:], cc_in.rearrange("(c p) v -> p c v", p=128), writes=[b_cc])
        scT = sb("scT", [128, 8, 2], BF16); b_sc = Buf()
        k.op("act", lambda e: e.activation(scT[:], ccT[:], AF.Silu), reads=[b_cc], writes=[b_sc])
        mod = sb("mod", [128, 48, 2], F32); b_mod = Buf()
        adab = sb("adab", [128, 48], F32); b_adab = Buf()
        nw1 = sb("nw1", [128, 8], F32); nw2 = sb("nw2", [128, 8], F32); b_nw = Buf()
        A1 = sb("A1", [128, 8, 2], F32); A2 = sb("A2", [128, 8, 2], F32); b_A = Buf()

        def seg_tiles(tt):
            res = []
            t = 0
            while t < NCTX:
                n = min(tt, NCTX - t)
                res.append((t, n, 1))
                t += n
            while t < NTOK:
                n = min(tt, NTOK - t)
                res.append((t, n, 0))
                t += n
            return res

        def load_w(st, name, w_ap, kc, ncols, q="sp"):
            t = sb(name, [128, kc, ncols], BF16, st)
            b = Buf(name)
            src = w_ap.rearrange("(c p) n -> p c n", p=128)
            for c in range(kc):
                k.dma(q if c % 2 == 0 else "pool", t[:, c, :], src[:, c, :], writes=[b])
            return t, b

        def compute_mod(li):
            with ExitStack() as st:
                k.dma("sp", adab[:], ada_b[li].rearrange("(j p) -> p j", p=128), writes=[b_adab], allow_slow_non_contiguous=True)
                k.dma("sp", nw1[:], norm1_w[li].rearrange("(c p) -> p c", p=128), writes=[b_nw], allow_slow_non_contiguous=True)
                k.dma("sp", nw2[:], norm2_w[li].rearrange("(c p) -> p c", p=128), writes=[b_nw], allow_slow_non_contiguous=True)
                wt = [sb("adaw%d" % i, [128, 8, 1024], BF16, st) for i in range(2)]
                wb = [Buf(), Buf()]
                for j in range(6):
                    w, b = wt[j % 2], wb[j % 2]
                    src = ada_w[li][:, j * 1024:(j + 1) * 1024].rearrange("(c p) n -> p c n", p=128)
                    for c in range(8):
                        k.dma("sp" if c % 2 == 0 else "pool", w[:, c, :], src[:, c, :], writes=[b])
                    for m in range(8):
                        p, pb = next_ps()
                        for c in range(8):
                            k.op("pe", lambda e, p=p, w=w, c=c, m=m: e.matmul(p[:, 0:2], w[:, c, m * 128:(m + 1) * 128], scT[:, c, :],
                                                                          start=(c == 0), stop=(c == 7)),
                                 reads=[b, b_sc], writes=[pb], inc=(c == 7))
                        jj = j * 8 + m
                        k.op("act", lambda e, p=p, jj=jj: e.activation(mod[:, jj, :], p[:, 0:2], AF.Identity, bias=adab[:, jj:jj + 1], scale=1.0),
                             reads=[pb, b_adab], writes=[b_mod])
                for v in range(2):
                    k.op("dve", lambda e, v=v: e.scalar_tensor_tensor(A1[:, :, v], mod[:, 8:16, v], 1.0, nw1[:], ALU.add, ALU.mult),
                         reads=[b_mod, b_nw], writes=[b_A])
                    k.op("dve", lambda e, v=v: e.scalar_tensor_tensor(A2[:, :, v], mod[:, 32:40, v], 1.0, nw2[:], ALU.add, ALU.mult),
                         reads=[b_mod, b_nw], writes=[b_A])
            k.barrier()

        def norm_mod(st_tiles, xt, bx, n, vsel, A, shoff, hT, bh):
            sq, bsq, rstd, brs = st_tiles["sq"], st_tiles["bsq"], st_tiles["rstd"], st_tiles["brs"]
            p, pb = next_ps()
            for c in range(8):
                k.op("act", lambda e, c=c: e.activation(sq[:, :n], xt[:, c, :n], AF.Square), reads=[bx], writes=[bsq])
                k.op("pe", lambda e, c=c: e.matmul(p[:, :n], ones_f[:], sq[:, :n], start=(c == 0), stop=(c == 7)),
                     reads=[b_ones, bsq], writes=[pb])
            k.op("act", lambda e: e.activation(rstd[:, :n], p[:, :n], AF.Sqrt, bias=EPS, scale=1.0 / D), reads=[pb], writes=[brs])
            k.op("dve", lambda e: e.reciprocal(rstd[:, :n], rstd[:, :n]), reads=[brs], writes=[brs])
            for c in range(8):
                k.op("dve", lambda e, c=c: e.tensor_tensor(sq[:, :n], xt[:, c, :n], rstd[:, :n], ALU.mult), reads=[bx, brs], writes=[bsq])
                k.op("act", lambda e, c=c: e.activation(hT[:, c, :n], sq[:, :n], AF.Identity, bias=mod[:, shoff + c, vsel:vsel + 1],
                                                        scale=A[:, c, vsel:vsel + 1]),
                     reads=[bsq, b_mod, b_A], writes=[bh])

        def norm_scratch(st, tt):
            return {"sq": sb("n_sq", [128, tt], F32, st), "bsq": Buf(), "rstd": sb("n_rstd", [128, tt], F32, st), "brs": Buf()}

        def outproj_phase(w_ap, kc, src_scr, ctx_out):
            TT = 512
            with ExitStack() as st:
                w, bw = load_w(st, "opw", w_ap, kc, D)
                mt = [sb("op_m%d" % i, [128, kc, TT], BF16, st) for i in range(2)]; bm = [Buf(), Buf()]
                xt = [sb("op_x%d" % i, [128, 8, TT], F32, st) for i in range(2)]; bx = [Buf(), Buf()]
                for ti, (t0, n, vsel) in enumerate(seg_tiles(TT)):
                    if vsel == 1 and not ctx_out:
                        continue
                    m, b_m, x, b_x = mt[ti % 2], bm[ti % 2], xt[ti % 2], bx[ti % 2]
                    k.dma("sp", m[:, :, :n], src_scr[0:kc * 128, t0:t0 + n].rearrange("(c p) t -> p c t", p=128), writes=[b_m])
                    k.dma("pool", x[:, :, :n], xres[:, t0:t0 + n].rearrange("(c p) t -> p c t", p=128), writes=[b_x])
                    for c in range(8):
                        p, pb = next_ps()
                        for j in range(kc):
                            k.op("pe", lambda e, p=p, j=j, c=c, m=m: e.matmul(p[:, :n], w[:, j, c * 128:(c + 1) * 128], m[:, j, :n],
                                                                          start=(j == 0), stop=(j == kc - 1)),
                                 reads=[bw, b_m], writes=[pb], inc=(j == kc - 1))
                        k.op("dve", lambda e, p=p, c=c, x=x: e.scalar_tensor_tensor(x[:, c, :n], p[:, :n], mod[:, 16 + c, vsel:vsel + 1], x[:, c, :n],
                                                                                 ALU.mult, ALU.add),
                             reads=[pb, b_mod, b_x], writes=[b_x])
                    k.dma("pool", xres[:, t0:t0 + n].rearrange("(c p) t -> p c t", p=128), x[:, :, :n], reads=[b_x], writes=[Buf()])
            k.barrier()

        def resid_phase(src_scr, ctx_out):
            TT = 512
            with ExitStack() as st:
                yt = [sb("rp_y%d" % i, [128, 8, TT], F32, st) for i in range(2)]; by = [Buf(), Buf()]
                xt = [sb("rp_x%d" % i, [128, 8, TT], F32, st) for i in range(2)]; bx = [Buf(), Buf()]
                for ti, (t0, n, vsel) in enumerate(seg_tiles(TT)):
                    if vsel == 1 and not ctx_out:
                        continue
                    y, b_y, x, b_x = yt[ti % 2], by[ti % 2], xt[ti % 2], bx[ti % 2]
                    k.dma("sp", y[:, :, :n], src_scr[0:D, t0:t0 + n].rearrange("(c p) t -> p c t", p=128), writes=[b_y])
                    k.dma("pool", x[:, :, :n], xres[:, t0:t0 + n].rearrange("(c p) t -> p c t", p=128), writes=[b_x])
                    for c in range(8):
                        k.op("dve", lambda e, c=c, x=x, y=y: e.scalar_tensor_tensor(x[:, c, :n], y[:, c, :n], mod[:, 16 + c, vsel:vsel + 1], x[:, c, :n],
                                                                                 ALU.mult, ALU.add),
                             reads=[b_y, b_mod, b_x], writes=[b_x])
                    k.dma("pool", xres[:, t0:t0 + n].rearrange("(c p) t -> p c t", p=128), x[:, :, :n], reads=[b_x], writes=[Buf()])
            k.barrier()

        def ffn_phase(li, ctx_out):
            TT = 256
            NJ = FH // 128
            with ExitStack() as st:
                w13, b13 = load_w(st, "w13", ffn_w13[li], 8, 2 * FH)
                w2, b2 = load_w(st, "w2", ffn_w2[li], NJ, D)
                nscs = [norm_scratch(st, TT) for _ in range(2)]
                xt = [sb("f_x%d" % i, [128, 8, TT], F32, st) for i in range(2)]; bx = [Buf(), Buf()]
                hTs = [sb("f_h%d" % i, [128, 8, TT], BF16, st) for i in range(2)]; bhs = [Buf(), Buf()]
                sT = sb("f_s", [128, NJ, TT], BF16, st); bs = Buf()
                sa = [sb("f_sa%d" % i, [128, TT], F32, st) for i in range(2)]; bsa = [Buf(), Buf()]
                tl = [t for t in seg_tiles(TT) if not (t[2] == 1 and not ctx_out)]

                def prep(i):
                    t0, n, vsel = tl[i]
                    k.dma("sp", xt[i % 2][:, :, :n], xres[:, t0:t0 + n].rearrange("(c p) t -> p c t", p=128), writes=[bx[i % 2]])
                    norm_mod(nscs[i % 2], xt[i % 2], bx[i % 2], n, vsel, A2, 24, hTs[i % 2], bhs[i % 2])

                prep(0)
                for i, (t0, n, vsel) in enumerate(tl):
                    x, b_x = xt[i % 2], bx[i % 2]
                    hT, bh = hTs[i % 2], bhs[i % 2]
                    for j in range(NJ):
                        pa, pab = next_ps()
                        pg, pgb = next_ps()
                        for c in range(8):
                            k.op("pe", lambda e, pa=pa, c=c, j=j: e.matmul(pa[:, :n], w13[:, c, j * 128:(j + 1) * 128], hT[:, c, :n],
                                                                       start=(c == 0), stop=(c == 7)),
                                 reads=[b13, bh], writes=[pab], inc=(c == 7))
                        for c in range(8):
                            k.op("pe", lambda e, pg=pg, c=c, j=j: e.matmul(pg[:, :n], w13[:, c, FH + j * 128:FH + (j + 1) * 128], hT[:, c, :n],
                                                                       start=(c == 0), stop=(c == 7)),
                                 reads=[b13, bh], writes=[pgb], inc=(c == 7))
                        s_a, b_sa = sa[j % 2], bsa[j % 2]
                        k.op("act", lambda e, pa=pa, s_a=s_a: e.activation(s_a[:, :n], pa[:, :n], AF.Silu), reads=[pab], writes=[b_sa])
                        k.op("dve", lambda e, pg=pg, s_a=s_a, j=j: e.tensor_tensor(sT[:, j, :n], s_a[:, :n], pg[:, :n], ALU.mult),
                             reads=[b_sa, pgb], writes=[bs])
                    if i + 1 < len(tl):
                        prep(i + 1)
                    for c in range(8):
                        p, pb = next_ps()
                        for j in range(NJ):
                            k.op("pe", lambda e, p=p, c=c, j=j: e.matmul(p[:, :n], w2[:, j, c * 128:(c + 1) * 128], sT[:, j, :n],
                                                                     start=(j == 0), stop=(j == NJ - 1)),
                                 reads=[b2, bs], writes=[pb], inc=(j == NJ - 1))
                        k.op("dve", lambda e, p=p, c=c, x=x: e.scalar_tensor_tensor(x[:, c, :n], p[:, :n], mod[:, 40 + c, vsel:vsel + 1], x[:, c, :n],
                                                                                 ALU.mult, ALU.add),
                             reads=[pb, b_mod, b_x], writes=[b_x])
                    k.dma("sp", xres[:, t0:t0 + n].rearrange("(c p) t -> p c t", p=128), x[:, :, :n], reads=[b_x], writes=[Buf()])
            k.barrier()

        def final_phase():
            TT = 512
            with ExitStack() as st:
                fw = sb("fnw", [128, 8], F32, st); bfw = Buf()
                k.dma("sp", fw[:], final_norm_w.rearrange("(c p) -> p c", p=128), writes=[bfw], allow_slow_non_contiguous=True)
                nsc = norm_scratch(st, TT)
                xt = [sb("fn_x%d" % i, [128, 8, TT], F32, st) for i in range(2)]; bx = [Buf(), Buf()]
                for ti, (t0, n, vsel) in enumerate(seg_tiles(TT)):
                    if vsel == 1:
                        continue
                    x, b_x = xt[ti % 2], bx[ti % 2]
                    k.dma("sp", x[:, :, :n], xres[:, t0:t0 + n].rearrange("(c p) t -> p c t", p=128), writes=[b_x])
                    sq, bsq, rstd, brs = nsc["sq"], nsc["bsq"], nsc["rstd"], nsc["brs"]
                    p, pb = next_ps()
                    for c in range(8):
                        k.op("act", lambda e, c=c, x=x: e.activation(sq[:, :n], x[:, c, :n], AF.Square), reads=[b_x], writes=[bsq])
                        k.op("pe", lambda e, c=c, p=p: e.matmul(p[:, :n], ones_f[:], sq[:, :n], start=(c == 0), stop=(c == 7)),
                             reads=[b_ones, bsq], writes=[pb])
                    k.op("act", lambda e, p=p: e.activation(rstd[:, :n], p[:, :n], AF.Sqrt, bias=EPS, scale=1.0 / D), reads=[pb], writes=[brs])
                    k.op("dve", lambda e: e.reciprocal(rstd[:, :n], rstd[:, :n]), reads=[brs], writes=[brs])
                    for c in range(8):
                        k.op("dve", lambda e, c=c, x=x: e.scalar_tensor_tensor(x[:, c, :n], x[:, c, :n], fw[:, c:c + 1], rstd[:, :n], ALU.mult, ALU.mult),
                             reads=[b_x, brs, bfw], writes=[b_x])
                    k.dma("pool", outT[:, t0 - NCTX:t0 - NCTX + n].rearrange("(c p) t -> p c t", p=128), x[:, :, :n], reads=[b_x], writes=[Buf()])
            k.barrier()

        def lru_layer(li, ctx_out):
            TT = 512
            gyT = scrA
            xrT = scrB
            hfT = scrB[1280:2560]
            with ExitStack() as st:
                w, bw = load_w(st, "lin", lru_in_w[0], 8, 2560)
                nsc = norm_scratch(st, TT)
                xt = [sb("la_x%d" % i, [128, 8, TT], F32, st) for i in range(2)]; bx = [Buf(), Buf()]
                hT = sb("la_h", [128, 8, TT], BF16, st); bh = Buf()
                og = [sb("la_o%d" % i, [128, TT], F32, st) for i in range(4)]; bog = [Buf() for _ in range(4)]
                on = 0
                for ti, (t0, n, vsel) in enumerate(seg_tiles(TT)):
                    x, b_x = xt[ti % 2], bx[ti % 2]
                    k.dma("sp", x[:, :, :n], xres[:, t0:t0 + n].rearrange("(c p) t -> p c t", p=128), writes=[b_x])
                    norm_mod(nsc, x, b_x, n, vsel, A1, 0, hT, bh)
                    for m in range(20):
                        p, pb = next_ps()
                        for c in range(8):
                            k.op("pe", lambda e, p=p, c=c, m=m: e.matmul(p[:, :n], w[:, c, m * 128:(m + 1) * 128], hT[:, c, :n],
                                                                     start=(c == 0), stop=(c == 7)),
                                 reads=[bw, bh], writes=[pb], inc=(c == 7))
                        o, bo = og[on % 4], bog[on % 4]
                        on += 1
                        if m < 10:
                            k.op("act", lambda e, p=p, o=o: e.activation(o[:, :n], p[:, :n], AF.Gelu_apprx_tanh), reads=[pb], writes=[bo])
                            k.dma("sp", gyT[m * 128:(m + 1) * 128, t0:t0 + n], o[:, :n], reads=[bo], writes=[Buf()])
                        else:
                            k.op("dve", lambda e, p=p, o=o: e.tensor_copy(o[:, :n], p[:, :n]), reads=[pb], writes=[bo])
                            k.dma("pool", xrT[(m - 10) * 128:(m - 9) * 128, t0:t0 + n], o[:, :n], reads=[bo], writes=[Buf()])
            k.barrier()
            with ExitStack() as st:
                gw = sb("lgw", [128, 2, 10, 256], BF16, st); bgw = Buf()
                for d in range(2):
                    for blk in range(10):
                        k.dma("sp" if blk % 2 else "pool", gw[:, d, blk, :], lru_gate_w[0, d, blk], writes=[bgw])
                gb = sb("lgb", [128, 2, 10, 2], F32, st); bgb = Buf()
                k.dma("sp", gb[:], lru_gate_b[0].rearrange("d b (h p) -> p d b h", p=128), writes=[bgb], allow_slow_non_contiguous=True)
                cw = sb("lcw", [128, 4, 10], F32, st); cb = sb("lcb", [128, 10], F32, st); bcw = Buf()
                k.dma("sp", cw[:], lru_conv_w[0].rearrange("k (b p) -> p k b", p=128), writes=[bcw], allow_slow_non_contiguous=True)
                k.dma("sp", cb[:], lru_conv_b[0].rearrange("(b p) -> p b", p=128), writes=[bcw], allow_slow_non_contiguous=True)
                lb = sb("llb", [128, 2, 10], F32, st); lb2 = sb("llb2", [128, 2, 10], F32, st); blb = Buf()
                k.dma("sp", lb[:], lru_a_param[0].rearrange("d (b p) -> p d b", p=128), writes=[blb], allow_slow_non_contiguous=True)
                k.op("act", lambda e: e.activation(lb[:], lb[:], AF.Exp, scale=-1.0), reads=[blb], writes=[blb])
                k.op("act", lambda e: e.activation(lb[:], lb[:], AF.Ln, bias=1.0, scale=1.0), reads=[blb], writes=[blb])
                k.op("dve", lambda e: e.tensor_scalar(lb[:], lb[:], -8.0, None, ALU.mult), reads=[blb], writes=[blb])
                k.op("dve", lambda e: e.tensor_scalar(lb2[:], lb[:], 2.0, None, ALU.mult), reads=[blb], writes=[blb])
                HB = TT + 3
                xr = [sb("lb_xr%d" % i, [128, HB], F32, st) for i in range(4)]; bxr = [Buf() for _ in range(4)]
                xc = [sb("lb_xc%d" % i, [128, TT], F32, st) for i in range(4)]; bxc = [Buf() for _ in range(4)]
                xcb = [sb("lb_xcb%d" % i, [128, TT], BF16, st) for i in range(4)]; bxcb = [Buf() for _ in range(4)]
                ta = [sb("lb_a%d" % i, [128, TT], F32, st) for i in range(4)]; bta = [Buf() for _ in range(4)]
                tm = [sb("lb_m%d" % i, [128, TT], F32, st) for i in range(4)]; btm = [Buf() for _ in range(4)]
                tu = [sb("lb_u%d" % i, [128, TT], F32, st) for i in range(4)]; btu = [Buf() for _ in range(4)]
                th = [sb("lb_h%d" % i, [128, TT], F32, st) for i in range(4)]; bth = [Buf() for _ in range(4)]
                stc = sb("lb_stc", [128, 2, 10], F32, st); bstc = [[Buf() for _ in range(10)] for _ in range(2)]
                k.op("pool", lambda e: e.memset(stc[:], 0.0), writes=[b for r in bstc for b in r])
                tg = [sb("lb_g%d" % i, [128, TT], F32, st) for i in range(4)]; btg = [Buf() for _ in range(4)]
                to = [sb("lb_o%d" % i, [128, TT], BF16, st) for i in range(4)]; bto = [Buf() for _ in range(4)]
                zero = sb("lb_z", [128, 1], F32, st); bz = Buf()
                k.op("pool", lambda e: e.memset(zero[:], 0.0), writes=[bz])
                tiles = seg_tiles(TT)
                segs = {1: (0, NCTX), 0: (NCTX, NTOK)}
                it = 0
                hn = 0
                hfbufs = {}
                for d in range(2):
                    order = tiles if d == 0 else ([tiles[0]] + tiles[:0:-1])
                    for (t0, n, vsel) in order:
                      for blk in range(10):
                        if True:
                            if vsel == 1 and not ctx_out and False:
                                continue
                            s0, s1 = segs[vsel]
                            i2 = it % 4
                            it += 1
                            x_r, b_xr = xr[i2], bxr[i2]
                            lo = max(t0 - 2, s0); hi = min(t0 + n + 1, s1)
                            if lo > t0 - 2 or hi < t0 + n + 1:
                                k.op("pool", lambda e, x_r=x_r: e.memset(x_r[:], 0.0), writes=[b_xr])
                            k.dma("sp", x_r[:, lo - (t0 - 2):hi - (t0 - 2)], xrT[blk * 128:(blk + 1) * 128, lo:hi], writes=[b_xr])
                            x_c, b_xc = xc[i2], bxc[i2]
                            k.op("act", lambda e, x_c=x_c, x_r=x_r: e.activation(x_c[:, :n], x_r[:, 0:n], AF.Identity, bias=cb[:, blk:blk + 1],
                                                                                scale=cw[:, 0, blk:blk + 1]),
                                 reads=[b_xr, bcw], writes=[b_xc])
                            for kk in range(1, 4):
                                k.op("dve", lambda e, kk=kk, x_c=x_c, x_r=x_r: e.scalar_tensor_tensor(x_c[:, :n], x_r[:, kk:kk + n], cw[:, kk, blk:blk + 1],
                                                                                                    x_c[:, :n], ALU.mult, ALU.add),
                                     reads=[b_xr, bcw, b_xc], writes=[b_xc])
                            x_cb, b_xcb = xcb[i2], bxcb[i2]
                            k.op("pool", lambda e, x_cb=x_cb, x_c=x_c: e.tensor_copy(x_cb[:, :n], x_c[:, :n]), reads=[b_xc], writes=[b_xcb])
                            pr, prb = next_ps()
                            pi, pib = next_ps()
                            k.op("pe", lambda e, pr=pr, x_cb=x_cb: e.matmul(pr[:, :n], gw[:, d, blk, 0:128], x_cb[:, :n], start=True, stop=True),
                                 reads=[bgw, b_xcb], writes=[prb])
                            k.op("pe", lambda e, pi=pi, x_cb=x_cb: e.matmul(pi[:, :n], gw[:, d, blk, 128:256], x_cb[:, :n], start=True, stop=True),
                                 reads=[bgw, b_xcb], writes=[pib])
                            a_, b_a = ta[i2], bta[i2]
                            m_, b_m = tm[i2], btm[i2]
                            u_, b_u = tu[i2], btu[i2]
                            k.op("act", lambda e, a_=a_, pr=pr: e.activation(a_[:, :n], pr[:, :n], AF.Sigmoid, bias=gb[:, d, blk, 0:1], scale=1.0),
                                 reads=[prb, bgb], writes=[b_a])
                            k.op("act", lambda e, u_=u_, pi=pi: e.activation(u_[:, :n], pi[:, :n], AF.Sigmoid, bias=gb[:, d, blk, 1:2], scale=1.0),
                                 reads=[pib, bgb], writes=[b_u])
                            k.op("act", lambda e, m_=m_, a_=a_: e.activation(m_[:, :n], a_[:, :n], AF.Exp, scale=lb2[:, d, blk:blk + 1]),
                                 reads=[b_a, blb], writes=[b_m])
                            k.op("act", lambda e, m_=m_: e.activation(m_[:, :n], m_[:, :n], AF.Sqrt, bias=1.0, scale=-1.0), reads=[b_m], writes=[b_m])
                            k.op("act", lambda e, a_=a_: e.activation(a_[:, :n], a_[:, :n], AF.Exp, scale=lb[:, d, blk:blk + 1]),
                                 reads=[b_a, blb], writes=[b_a])
                            k.op("dve", lambda e, u_=u_, x_c=x_c: e.tensor_tensor(u_[:, :n], u_[:, :n], x_c[:, :n], ALU.mult), reads=[b_u, b_xc], writes=[b_u])
                            k.op("pool", lambda e, u_=u_, m_=m_: e.tensor_tensor(u_[:, :n], u_[:, :n], m_[:, :n], ALU.mult), reads=[b_u, b_m], writes=[b_u])
                            h_, b_h = th[hn % 4], bth[hn % 4]
                            hn += 1
                            init = stc[:, d, blk:blk + 1]
                            if d == 0:
                                k.op("dve", lambda e, h_=h_, a_=a_, u_=u_, init=init: e.tensor_tensor_scan(h_[:, :n], a_[:, :n], u_[:, :n], init, ALU.mult, ALU.add),
                                     reads=[b_a, b_u, bstc[d][blk]], writes=[b_h])
                                k.op("act", lambda e, h_=h_: e.copy(stc[:, d, blk:blk + 1], h_[:, n - 1:n]), reads=[b_h], writes=[bstc[d][blk]])
                                hb = Buf()
                                hfbufs[(blk, t0)] = hb
                                k.dma("pool", hfT[blk * 128:(blk + 1) * 128, t0:t0 + n], h_[:, :n], reads=[b_h], writes=[hb])
                            else:
                                k.op("dve", lambda e, h_=h_, a_=a_, u_=u_, init=init: e.tensor_tensor_scan(h_[:, n - 1::-1] if n == h_.shape[1] else h_[:, n - 1::-1],
                                                                                                         a_[:, n - 1::-1], u_[:, n - 1::-1], init, ALU.mult, ALU.add),
                                     reads=[b_a, b_u, bstc[d][blk]], writes=[b_h])
                                k.op("act", lambda e, h_=h_: e.copy(stc[:, d, blk:blk + 1], h_[:, 0:1]), reads=[b_h], writes=[bstc[d][blk]])
                                if vsel == 1 and not ctx_out:
                                    continue
                                g_, b_g = tg[i2], btg[i2]
                                hf_, b_hf = xc[i2], bxc[i2]
                                k.dma("sp", hf_[:, :n], hfT[blk * 128:(blk + 1) * 128, t0:t0 + n], reads=[hfbufs[(blk, t0)]], writes=[b_xc])
                                k.dma("sp", g_[:, :n], gyT[blk * 128:(blk + 1) * 128, t0:t0 + n], writes=[b_g])
                                k.op("pool", lambda e, hf_=hf_, h_=h_: e.tensor_tensor(hf_[:, :n], hf_[:, :n], h_[:, :n], ALU.add), reads=[b_xc, b_h], writes=[b_xc])
                                o_, b_o = to[i2], bto[i2]
                                k.op("dve", lambda e, o_=o_, hf_=hf_, g_=g_: e.tensor_tensor(o_[:, :n], hf_[:, :n], g_[:, :n], ALU.mult), reads=[b_xc, b_g], writes=[b_o])
                                k.dma("pool", scrM[blk * 128:(blk + 1) * 128, t0:t0 + n], o_[:, :n], reads=[b_o], writes=[Buf()])
            k.barrier()
            outproj_phase(lru_out_w[0], 10, scrM, ctx_out)


        def ssd_layer(li, ctx_out):
            TT = 512
            xbcT = scrXB
            with ExitStack() as st:
                w, bw = load_w(st, "m2in", m2_in_w[0], 8, 5184)
                nsc = norm_scratch(st, TT)
                xt = [sb("sa_x%d" % i, [128, 8, TT], F32, st) for i in range(2)]; bx = [Buf(), Buf()]
                hT = sb("sa_h", [128, 8, TT], BF16, st); bh = Buf()
                og = [sb("sa_o%d" % i, [128, TT], F32, st) for i in range(4)]; bog = [Buf() for _ in range(4)]
                ogb = [sb("sa_ob%d" % i, [128, TT], BF16, st) for i in range(4)]; bogb = [Buf() for _ in range(4)]
                on = 0
                for ti, (t0, n, vsel) in enumerate(seg_tiles(TT)):
                    x, b_x = xt[ti % 2], bx[ti % 2]
                    k.dma("sp", x[:, :, :n], xres[:, t0:t0 + n].rearrange("(c p) t -> p c t", p=128), writes=[b_x])
                    norm_mod(nsc, x, b_x, n, vsel, A1, 0, hT, bh)
                    for m in range(24):
                        p, pb = next_ps()
                        for c in range(8):
                            k.op("pe", lambda e, p=p, c=c, m=m: e.matmul(p[:, :n], w[:, c, 2048 + m * 128:2048 + (m + 1) * 128], hT[:, c, :n],
                                                                     start=(c == 0), stop=(c == 7)),
                                 reads=[bw, bh], writes=[pb], inc=(c == 7))
                        o, bo = ogb[on % 4], bogb[on % 4]
                        on += 1
                        if m % 2 == 0:
                            k.op("act", lambda e, p=p, o=o: e.copy(o[:, :n], p[:, :n]), reads=[pb], writes=[bo])
                        else:
                            k.op("dve", lambda e, p=p, o=o: e.tensor_copy(o[:, :n], p[:, :n]), reads=[pb], writes=[bo])
                        k.dma("sp", xbcT[m * 128:(m + 1) * 128, t0:t0 + n], o[:, :n], reads=[bo], writes=[Buf()])
                    for tb in range(n // 128):
                        for cb_ in range(4):
                            p, pb = next_ps()
                            for c in range(8):
                                k.op("pe", lambda e, p=p, c=c, tb=tb, cb_=cb_: e.matmul(p[:, :], hT[:, c, tb * 128:(tb + 1) * 128], w[:, c, cb_ * 512:(cb_ + 1) * 512],
                                                                                    start=(c == 0), stop=(c == 7)),
                                     reads=[bw, bh], writes=[pb], inc=(c == 7))
                            o, bo = og[on % 4], bog[on % 4]
                            on += 1
                            if cb_ % 2 == 0:
                                k.op("act", lambda e, p=p, o=o: e.copy(o[:, :], p[:, :]), reads=[pb], writes=[bo])
                            else:
                                k.op("dve", lambda e, p=p, o=o: e.tensor_copy(o[:, :], p[:, :]), reads=[pb], writes=[bo])
                            k.dma("sp", scrZ[t0 + tb * 128:t0 + (tb + 1) * 128, cb_ * 512:(cb_ + 1) * 512], o[:, :], reads=[bo], writes=[Buf()])
                        p, pb = next_ps()
                        for c in range(8):
                            k.op("pe", lambda e, p=p, c=c, tb=tb: e.matmul(p[:, 0:64], hT[:, c, tb * 128:(tb + 1) * 128], w[:, c, 5120:5184],
                                                                      start=(c == 0), stop=(c == 7)),
                                 reads=[bw, bh], writes=[pb], inc=(c == 7))
                        o, bo = og[on % 4], bog[on % 4]
                        on += 1
                        k.op("dve", lambda e, p=p, o=o: e.tensor_copy(o[:, 0:64], p[:, 0:64]), reads=[pb], writes=[bo])
                        k.dma("sp", scrDT[t0 + tb * 128:t0 + (tb + 1) * 128, :], o[:, 0:64], reads=[bo], writes=[Buf()])
            k.barrier()
            if debug == "ssdA":
                return
            with ExitStack() as st:
                def small(name, shape, src, **kw):
                    t = sb(name, shape, F32, st); b = Buf()
                    k.dma("sp", t[:], src, writes=[b], allow_slow_non_contiguous=True)
                    return t, b
                cw, bcw = small("scw", [128, 4, 24], m2_conv_w[0].rearrange("k (m p) -> p k m", p=128))
                cb, bcb = small("scb", [128, 24], m2_conv_b[0].rearrange("(m p) -> p m", p=128))
                dtb, bdtb = small("sdtb", [128, 64], m2_dt_bias[0].rearrange("d h -> (d h)").partition_broadcast(128))
                abc, babc = small("sabc", [128, 64], m2_a_log[0].rearrange("d h -> (d h)").partition_broadcast(128))
                dsk, bdsk = small("sdsk", [128, 32], m2_d[0].partition_broadcast(128))
                k.op("dve", lambda e: e.tensor_scalar(cw[:], cw[:], 0.5, None, ALU.mult), reads=[bcw], writes=[bcw])
                k.op("dve", lambda e: e.tensor_scalar(cb[:], cb[:], 0.5, None, ALU.mult), reads=[bcb], writes=[bcb])
                k.op("act", lambda e: e.activation(abc[:], abc[:], AF.Exp), reads=[babc], writes=[babc])
                k.op("dve", lambda e: e.tensor_scalar(abc[:], abc[:], -1.0, None, ALU.mult), reads=[babc], writes=[babc])
                msk = sb("smask", [128, 4, 128], F32, st); bmsk = Buf()
                k.dma("sp", msk[:], masks_in.rearrange("m p l -> p m l"), writes=[bmsk])
                MU, ML, MSL, MSU = (msk[:, i, :] for i in range(4))
                xr = [sb("s1_xr%d" % i, [128, 24, 516], BF16, st) for i in range(2)]; bxr = [Buf(), Buf()]
                xc = sb("s1_xc", [128, 128], F32, st); bxc = Buf()
                xsT = sb("s1_xsT", [128, 24, 128], BF16, st); bxsT = Buf()
                xs_tok = [sb("s1_xst%d" % i, [128, 2048], BF16, st) for i in range(2)]; bxst = [Buf(), Buf()]
                b_tok = sb("s1_btok", [128, 512], BF16, st); bbtok = Buf()
                dtr = sb("s1_dtr", [128, 64], F32, st); bdtr = Buf()
                dtt = sb("s1_dt", [128, 64], F32, st); bdtt = Buf()
                da = sb("s1_da", [128, 64], F32, st); bda = Buf()
                tmp64 = sb("s1_t64", [128, 64], F32, st); bt64 = Buf()
                wgt = sb("s1_wgt", [128, 64], F32, st); bwgt = Buf()
                Et = [sb("s1_E%d" % i, [128, 64], F32, st) for i in range(2)]; bE = [Buf(), Buf()]
                dec = [sb("s1_dec%d" % i, [128, 64], F32, st) for i in range(2)]; bdec = [Buf(), Buf()]
                Gf = sb("s1_Gf", [128, 4, 128], F32, st); Gb = sb("s1_Gb", [128, 4, 128], F32, st); bG = Buf()
                NBH = 4
                Lf = [sb("s1_Lf%d" % i, [128, 128], F32, st) for i in range(NBH)]; bLf = [Buf() for _ in range(NBH)]
                Lb = [sb("s1_Lb%d" % i, [128, 128], F32, st) for i in range(NBH)]; bLb = [Buf() for _ in range(NBH)]
                Df = [sb("s1_Df%d" % i, [128, 128], F32, st) for i in range(NBH)]; bDf = [Buf() for _ in range(NBH)]
                Db = [sb("s1_Db%d" % i, [128, 128], F32, st) for i in range(NBH)]; bDb = [Buf() for _ in range(NBH)]
                Mt = [sb("s1_M%d" % i, [128, 128], BF16, st) for i in range(NBH)]; bMt = [Buf() for _ in range(NBH)]
                Mb = [sb("s1_Mb%d" % i, [128, 128], BF16, st) for i in range(NBH)]; bMb = [Buf() for _ in range(NBH)]
                dskI = sb("s1_dskI", [128, 32, 128], BF16, st); bdskI = Buf()
                for h_ in range(32):
                    k.op("dve", lambda e, h_=h_: e.tensor_scalar(dskI[:, h_, :], ident_f[:], dsk[:, h_:h_ + 1], None, ALU.mult), reads=[b_ident, bdsk], writes=[bdskI])
                xw = sb("s1_xw", [128, 2048], BF16, st); bxw = Buf()
                yo = [sb("s1_yo%d" % i, [128, 2048], F32, st) for i in range(2)]; byo = [Buf(), Buf()]
                so = [sb("s1_so%d" % i, [128, 512], BF16, st) for i in range(4)]; bso = [Buf() for _ in range(4)]
                dtt2 = [dtt, sb("s1_dt2", [128, 64], F32, st)]; bdtt2 = [bdtt, Buf()]
                da2 = [da, sb("s1_da2", [128, 64], F32, st)]; bda2 = [bda, Buf()]
                Gf2 = [Gf, sb("s1_Gf2", [128, 4, 128], F32, st)]; Gb2 = [Gb, sb("s1_Gb2", [128, 4, 128], F32, st)]; bG2 = [bG, Buf()]
                pspool[0] = [4]

                xc4 = [xc] + [sb("s1_xc%d" % i, [128, 128], F32, st) for i in range(3)]; bxc4 = [bxc, Buf(), Buf(), Buf()]
                th4 = [sb("s1_th%d" % i, [128, 128], F32, st) for i in range(4)]; bth4 = [Buf() for _ in range(4)]

                def pieces(ch):
                    dtt_, bdtt_ = dtt2[ch % 2], bdtt2[ch % 2]
                    da_, bda_ = da2[ch % 2], bda2[ch % 2]
                    Gf_, Gb_, bG_ = Gf2[ch % 2], Gb2[ch % 2], bG2[ch % 2]
                    t0 = ch * 128
                    s0, s1 = (0, NCTX) if t0 < NCTX else (NCTX, NTOK)
                    if t0 < NCTX:
                        big_i, big_t0, big_n = 0, 0, NCTX
                    else:
                        big_i = 1 + (t0 - NCTX) // 512
                        big_t0 = NCTX + (big_i - 1) * 512
                        big_n = 512
                    xbig, b_xr = xr[big_i % 2], bxr[big_i % 2]
                    boff = t0 - big_t0
                    x_r = xbig[:, :, boff:boff + 131]
                    blo = max(big_t0 - 2, s0); bhi = min(big_t0 + big_n + 1, s1)
                    xst, b_xst = xs_tok[ch % 2], bxst[ch % 2]
                    E_, b_E = Et[ch % 2], bE[ch % 2]
                    d_, b_d = dec[ch % 2], bdec[ch % 2]
                    lo = max(t0 - 2, s0); hi = min(t0 + 129, s1)
                    pa, pab = ps[5], psb[5]
                    P = []

                    def convA(m):
                        if m == 0 and boff == 0:
                            if blo > big_t0 - 2 or bhi < big_t0 + big_n + 1:
                                k.op("pool", lambda e: e.memset(xbig[:], 0.0), writes=[b_xr])
                            k.dma("sp", xbig[:, :, blo - (big_t0 - 2):bhi - (big_t0 - 2)], xbcT[:, blo:bhi].rearrange("(m p) t -> p m t", p=128), writes=[b_xr])
                        xc_, bxc_ = xc4[m % 4], bxc4[m % 4]
                        k.op("act", lambda e: e.activation(xc_[:], x_r[:, m, 0:128], AF.Identity, bias=cb[:, m:m + 1], scale=cw[:, 0, m:m + 1]),
                             reads=[b_xr, bcw, bcb], writes=[bxc_])

                    def convT(m):
                        xc_, bxc_ = xc4[m % 4], bxc4[m % 4]
                        for kk in range(1, 4):
                            k.op("dve", lambda e, kk=kk: e.scalar_tensor_tensor(xc_[:], x_r[:, m, kk:kk + 128], cw[:, kk, m:m + 1], xc_[:], ALU.mult, ALU.add),
                                 reads=[b_xr, bcw, bxc_], writes=[bxc_])

                    def convS(m):
                        xc_, bxc_ = xc4[m % 4], bxc4[m % 4]
                        th_, bth_ = th4[m % 4], bth4[m % 4]
                        k.op("act", lambda e: e.activation(th_[:], xc_[:], AF.Tanh), reads=[bxc_], writes=[bth_])
                        k.op("dve", lambda e: e.scalar_tensor_tensor(xsT[:, m, :], th_[:], 1.0, xc_[:], ALU.add, ALU.mult), reads=[bth_, bxc_], writes=[bxsT])

                    def dtp():
                        k.dma("sp", dtr[:], scrDT[t0:t0 + 128, :], writes=[bdtr])
                        k.op("dve", lambda e: e.tensor_tensor(dtr[:], dtr[:], dtb[:], ALU.add), reads=[bdtr, bdtb], writes=[bdtr])
                        k.op("act", lambda e: e.activation(tmp64[:], dtr[:], AF.Abs), reads=[bdtr], writes=[bt64])
                        k.op("act", lambda e: e.activation(tmp64[:], tmp64[:], AF.Exp, scale=-1.0), reads=[bt64], writes=[bt64])
                        k.op("act", lambda e: e.activation(tmp64[:], tmp64[:], AF.Ln, bias=1.0, scale=1.0), reads=[bt64], writes=[bt64])
                        k.op("dve", lambda e: e.scalar_tensor_tensor(dtt_[:], dtr[:], 0.0, tmp64[:], ALU.max, ALU.add), reads=[bdtr, bt64], writes=[bdtt_])
                        k.op("dve", lambda e: e.tensor_tensor(da_[:], dtt_[:], abc[:], ALU.mult), reads=[bdtt_, babc], writes=[bda_])

                    def convpiece(i):
                        if i - 2 >= 0 and i - 2 < 24:
                            convS(i - 2)
                        if i - 1 >= 0 and i - 1 < 24:
                            convT(i - 1)
                        if i < 24:
                            convA(i)
                        if i == 8:
                            dtp()
                        if i == 12:
                            cums()
                    for i in range(26):
                        P.append(lambda i=i: convpiece(i))

                    def tr(qs, last):
                        for q in qs:
                            p, pb = next_ps()
                            pbf = p[:].bitcast(BF16)
                            for j in range(4):
                                m = q * 4 + j
                                k.op("pe", lambda e, pbf=pbf, j=j, m=m: e.transpose(pbf[:, j * 128:(j + 1) * 128], xsT[:, m, :], ident_b[:]),
                                     reads=[bxsT, b_identb], writes=[pb], inc=(j == 3))
                            if q < 4:
                                k.op("dve", lambda e, pbf=pbf, q=q: e.tensor_copy(xst[:, q * 512:(q + 1) * 512], pbf[:, 0:512]), reads=[pb], writes=[b_xst])
                            else:
                                k.op("dve", lambda e, pbf=pbf: e.tensor_copy(b_tok[:], pbf[:, 0:512]), reads=[pb], writes=[bbtok])
                        if last:
                            k.dma("sp", scrC[ch].rearrange("p (g t) -> p g t", t=128), xsT[:, 20:24, :], reads=[bxsT], writes=[Buf()])
                    P.append(lambda: tr((0, 1), False))
                    P.append(lambda: tr((2, 3), False))
                    P.append(lambda: tr((4,), True))

                    def cums():
                        k.op("pe", lambda e: e.matmul(pa[:, 0:32], MU, da_[:, 0:32], start=True, stop=True), reads=[bmsk, bda_], writes=[pab], inc=False)
                        k.op("pe", lambda e: e.matmul(pa[:, 32:64], ML, da_[:, 32:64], start=True, stop=True), reads=[bmsk, bda_], writes=[pab], inc=False)
                        k.op("pe", lambda e: e.matmul(pa[:, 64:128], ones_f[:], da_[:, 0:64], start=True, stop=True), reads=[b_ones, bda_], writes=[pab])
                        k.op("act", lambda e: e.activation(E_[:], pa[:, 0:64], AF.Exp), reads=[pab], writes=[b_E])
                        k.op("act", lambda e: e.activation(d_[:], pa[:, 64:128], AF.Exp), reads=[pab], writes=[b_d])
                        k.dma("sp", scrE[ch], E_[:], reads=[b_E], writes=[Buf()])
                        k.dma("sp", scrDec[ch], d_[:], reads=[b_d], writes=[Buf()])
                        k.op("act", lambda e: e.copy(tmp64[:], pa[:, 0:64]), reads=[pab], writes=[bt64])
                        k.op("dve", lambda e: e.tensor_tensor(wgt[:], pa[:, 64:128], tmp64[:], ALU.subtract), reads=[pab, bt64], writes=[bwgt])
                        k.op("act", lambda e: e.activation(wgt[:], wgt[:], AF.Exp), reads=[bwgt], writes=[bwgt])
                        k.op("dve", lambda e: e.tensor_tensor(wgt[:], wgt[:], dtt_[:], ALU.mult), reads=[bwgt, bdtt_], writes=[bwgt])

                    def states(d):
                        k.op("dve", lambda e: e.tensor_tensor(xw[:].rearrange("p (h q) -> p h q", q=64), xst[:].rearrange("p (h q) -> p h q", q=64),
                                                              wgt[:, d * 32:(d + 1) * 32].unsqueeze(2).to_broadcast([128, 32, 64]), ALU.mult),
                             reads=[b_xst, bwgt], writes=[bxw])
                        for g in range(4):
                            p, pb = next_ps()
                            k.op("pe", lambda e, p=p, g=g: e.matmul(p[:, :], b_tok[:, g * 128:(g + 1) * 128], xw[:, g * 512:(g + 1) * 512], start=True, stop=True),
                                 reads=[bbtok, bxw], writes=[pb])
                            s_, b_s = so[g], bso[g]
                            k.op("act", lambda e, p=p, s_=s_: e.copy(s_[:], p[:]), reads=[pb], writes=[b_s])
                            k.dma("sp", scrS[d, ch, :, g * 512:(g + 1) * 512], s_[:], reads=[b_s], writes=[Buf()])
                    P.append(lambda: states(0))
                    P.append(lambda: states(1))

                    def gmat():
                        for g in range(4):
                            p, pb = next_ps()
                            k.op("pe", lambda e, p=p, g=g: e.matmul(p[:, 0:128], xsT[:, 16 + g, :], xsT[:, 20 + g, :], start=True, stop=True),
                                 reads=[bxsT], writes=[pb])
                            k.op("dve", lambda e, p=p, g=g: e.tensor_tensor(Gf_[:, g, :], p[:, 0:128], MU, ALU.mult), reads=[pb, bmsk], writes=[bG_])
                            k.op("dve", lambda e, p=p, g=g: e.tensor_tensor(Gb_[:, g, :], p[:, 0:128], ML, ALU.mult), reads=[pb, bmsk], writes=[bG_])
                    P.append(gmat)
                    assert len(P) == 32
                    return P

                pieces_cache = {}

                def piece(ch, i):
                    if ch >= NCH:
                        return
                    if ch not in pieces_cache:
                        pieces_cache.clear()
                        pieces_cache[ch] = pieces(ch)
                    pieces_cache[ch][i]()

                def prologue(ch):
                    if ch == 0:
                        for i in range(32):
                            piece(0, i)

                def make_head(ch, h, hc):
                    dtt_, bdtt_ = dtt2[ch % 2], bdtt2[ch % 2]
                    da_, bda_ = da2[ch % 2], bda2[ch % 2]
                    Gf_, Gb_, bG_ = Gf2[ch % 2], Gb2[ch % 2], bG2[ch % 2]
                    xst, b_xst = xs_tok[ch % 2], bxst[ch % 2]
                    g = h // 8
                    i2 = hc % NBH
                    p, pb = ps[6 + hc % 2], psb[6 + hc % 2]
                    yb = h // 8

                    def h0():
                        if h == 0:
                            prologue(ch)
                        piece(ch + 1, h)
                        k.op("act", lambda e: e.activation(Lf[i2][:], MSL, AF.Identity, scale=da_[:, h:h + 1]), reads=[bmsk, bda_], writes=[bLf[i2]])
                        k.op("act", lambda e: e.activation(Lb[i2][:], MSU, AF.Identity, scale=da_[:, 32 + h:33 + h]), reads=[bmsk, bda_], writes=[bLb[i2]])

                    def h1():
                        k.op("pe", lambda e: e.matmul(p[:, 0:128], Lf[i2][:], MU, start=True, stop=True), reads=[bLf[i2], bmsk], writes=[pb], inc=False)
                        k.op("pe", lambda e: e.matmul(p[:, 128:256], Lb[i2][:], ML, start=True, stop=True), reads=[bLb[i2], bmsk], writes=[pb])

                    def h2():
                        k.op("act", lambda e: e.activation(Df[i2][:], p[:, 0:128], AF.Exp), reads=[pb], writes=[bDf[i2]])
                        k.op("act", lambda e: e.activation(Db[i2][:], p[:, 128:256], AF.Exp), reads=[pb], writes=[bDb[i2]])

                    def h3():
                        k.op("dve", lambda e: e.scalar_tensor_tensor(Mt[i2][:], Df[i2][:], dtt_[:, h:h + 1], Gf_[:, g, :], ALU.mult, ALU.mult),
                             reads=[bDf[i2], bdtt_, bG_], writes=[bMt[i2]])
                        k.op("dve", lambda e: e.scalar_tensor_tensor(Mb[i2][:], Db[i2][:], dtt_[:, 32 + h:33 + h], Gb_[:, g, :], ALU.mult, ALU.mult),
                             reads=[bDb[i2], bdtt_, bG_], writes=[bMb[i2]])

                    def h4():
                        yout = ps[yb][:, (h % 8) * 64:(h % 8 + 1) * 64]
                        k.op("pe", lambda e: e.matmul(yout, Mt[i2][:], xst[:, h * 64:(h + 1) * 64], start=True, stop=False),
                             reads=[bMt[i2], b_xst], writes=[psb[yb]], inc=False)
                        k.op("pe", lambda e: e.matmul(yout, Mb[i2][:], xst[:, h * 64:(h + 1) * 64], start=False, stop=False),
                             reads=[bMb[i2], b_xst], writes=[psb[yb]], inc=False)
                        k.op("pe", lambda e: e.matmul(yout, dskI[:, h, :], xst[:, h * 64:(h + 1) * 64], start=False, stop=True),
                             reads=[bdskI, b_xst], writes=[psb[yb]])
                        if h != 31:
                            return
                        y_, b_y = yo[ch % 2], byo[ch % 2]
                        for yb2 in range(4):
                            if yb2 % 2 == 0:
                                k.op("act", lambda e, yb2=yb2: e.copy(y_[:, yb2 * 512:(yb2 + 1) * 512], ps[yb2][:]), reads=[psb[yb2]], writes=[b_y])
                            else:
                                k.op("dve", lambda e, yb2=yb2: e.tensor_copy(y_[:, yb2 * 512:(yb2 + 1) * 512], ps[yb2][:]), reads=[psb[yb2]], writes=[b_y])
                        k.dma("sp", scrY[ch], y_[:], reads=[b_y], writes=[Buf()])

                    return [h0, h1, h2, h3, h4]

                hunits = []
                for ch in range(NCH):
                    for h in range(32):
                        hunits.append(make_head(ch, h, len(hunits)))
                NSH = 5
                for step in range(len(hunits) + NSH - 1):
                    for j in range(NSH - 1, -1, -1):
                        u = step - j
                        if 0 <= u < len(hunits):
                            hunits[u][j]()
                pspool[0] = list(range(8))
            k.barrier()
            if debug == "ssd1":
                return
            with ExitStack() as st:
                Hs = [sb("s2_H%d" % d, [128, 2048], F32, st) for d in range(2)]; bH = [Buf(), Buf()]
                Hb = [sb("s2_Hb%d" % i, [128, 2048], BF16, st) for i in range(2)]; bHb = [Buf(), Buf()]
                St = [sb("s2_S%d" % i, [128, 2048], BF16, st) for i in range(2)]; bSt = [Buf(), Buf()]
                dc = [sb("s2_d%d" % i, [128, 64], F32, st) for i in range(2)]; bdc = [Buf(), Buf()]
                it = 0
                for d in range(2):
                    order = list(range(NCH)) if d == 0 else [1, 0] + list(range(NCH - 1, 1, -1))
                    k.op("pool", lambda e, d=d: e.memset(Hs[d][:], 0.0), writes=[bH[d]])
                    for ch in order:
                        i2 = it % 2
                        it += 1
                        k.op("act", lambda e, d=d, i2=i2: e.copy(Hb[i2][:], Hs[d][:]), reads=[bH[d]], writes=[bHb[i2]])
                        k.dma("pool", scrH[d, ch], Hb[i2][:], reads=[bHb[i2]], writes=[Buf()])
                        k.dma("sp", St[i2][:], scrS[d, ch], writes=[bSt[i2]])
                        k.dma("sp", dc[i2][:], scrDec[ch], writes=[bdc[i2]])
                        k.op("dve", lambda e, d=d, i2=i2: e.tensor_tensor(Hs[d][:].rearrange("p (h q) -> p h q", q=64), Hs[d][:].rearrange("p (h q) -> p h q", q=64),
                                                                      dc[i2][:, d * 32:(d + 1) * 32].unsqueeze(2).to_broadcast([128, 32, 64]), ALU.mult),
                             reads=[bH[d], bdc[i2]], writes=[bH[d]])
                        k.op("dve", lambda e, d=d, i2=i2: e.tensor_tensor(Hs[d][:], Hs[d][:], St[i2][:], ALU.add), reads=[bH[d], bSt[i2]], writes=[bH[d]])
            k.barrier()
            if debug == "ssd2":
                return
            with ExitStack() as st:
                ow, bow = load_w(st, "m2ow", m2_out_w[0], 16, D)
                nwb = sb("s3_nw", [128, 2048], F32, st); bnwb = Buf()
                k.dma("sp", nwb[:], m2_norm_w[0].partition_broadcast(128), writes=[bnwb])
                yt = [sb("s3_y%d" % i, [128, 2048], F32, st) for i in range(2)]; byt = [Buf(), Buf()]
                zt = [sb("s3_z%d" % i, [128, 2048], F32, st) for i in range(2)]; bzt = [Buf(), Buf()]
                Hf = [sb("s3_Hf%d" % i, [128, 2048], BF16, st) for i in range(2)]; bHf = [Buf(), Buf()]
                Hbk = [sb("s3_Hb%d" % i, [128, 2048], BF16, st) for i in range(2)]; bHbk = [Buf(), Buf()]
                Ct = [sb("s3_C%d" % i, [128, 512], BF16, st) for i in range(2)]; bCt = [Buf(), Buf()]
                Ee = [sb("s3_E%d" % i, [128, 64], F32, st) for i in range(2)]; bEe = [Buf(), Buf()]
                tmp = sb("s3_tmp", [128, 512], F32, st); btmp = Buf()
                ss = sb("s3_ss", [128, 2], F32, st); bss = Buf()
                sqj = sb("s3_sq", [128, 2048], F32, st); bsqj = Buf()
                yn = sb("s3_yn", [128, 2048], BF16, st); byn = Buf()
                ynT = sb("s3_ynT", [128, 16, 128], BF16, st); bynT = Buf()
                xt = [sb("s3_x%d" % i, [128, 8, 128], F32, st) for i in range(2)]; bx = [Buf(), Buf()]
                for ch in range(NCH):
                    t0 = ch * 128
                    vsel = 1 if t0 < NCTX else 0
                    if vsel == 1 and not ctx_out:
                        continue
                    i2 = ch % 2
                    y_, z_, b_y, b_z = yt[i2], zt[i2], byt[i2], bzt[i2]
                    k.dma("sp", y_[:], scrY[ch], writes=[b_y])
                    k.dma("sp", z_[:], scrZ[t0:t0 + 128, :], writes=[b_z])
                    k.dma("pool", Hf[i2][:], scrH[0, ch], writes=[bHf[i2]])
                    k.dma("pool", Hbk[i2][:], scrH[1, ch], writes=[bHbk[i2]])
                    k.dma("sp", Ct[i2][:], scrC[ch], writes=[bCt[i2]])
                    k.dma("sp", Ee[i2][:], scrE[ch], writes=[bEe[i2]])
                    k.dma("pool", xt[i2][:], xres[:, t0:t0 + 128].rearrange("(c p) t -> p c t", p=128), writes=[bx[i2]])
                    for d in range(2):
                        Hd, bHd = (Hf[i2], bHf[i2]) if d == 0 else (Hbk[i2], bHbk[i2])
                        for g in range(4):
                            p, pb = next_ps()
                            k.op("pe", lambda e, p=p, g=g, Hd=Hd, i2=i2: e.matmul(p[:], Ct[i2][:, g * 128:(g + 1) * 128], Hd[:, g * 512:(g + 1) * 512], start=True, stop=True),
                                 reads=[bCt[i2], bHd], writes=[pb])
                            k.op("dve", lambda e, p=p, g=g, d=d, i2=i2: e.tensor_tensor(tmp[:].rearrange("p (h q) -> p h q", q=64), p[:].rearrange("p (h q) -> p h q", q=64),
                                                                                   Ee[i2][:, d * 32 + g * 8:d * 32 + (g + 1) * 8].unsqueeze(2).to_broadcast([128, 8, 64]), ALU.mult),
                                 reads=[pb, bEe[i2]], writes=[btmp])
                            k.op("pool", lambda e, g=g, y_=y_: e.tensor_tensor(y_[:, g * 512:(g + 1) * 512], y_[:, g * 512:(g + 1) * 512], tmp[:], ALU.add),
                                 reads=[btmp, b_y], writes=[b_y])
                    k.op("act", lambda e, z_=z_: e.activation(z_[:], z_[:], AF.Silu), reads=[b_z], writes=[b_z])
                    k.op("dve", lambda e, y_=y_, z_=z_: e.tensor_tensor(y_[:], y_[:], z_[:], ALU.mult), reads=[b_y, b_z], writes=[b_y])
                    k.op("pool", lambda e: e.memset(ss[:], 0.0), writes=[bss])
                    k.op("act", lambda e, y_=y_: e.activation(sqj[:], y_[:], AF.Square, accum_out=ss[:, 0:1]), reads=[b_y], writes=[bsqj, bss])
                    k.op("act", lambda e: e.activation(ss[:, 1:2], ss[:, 0:1], AF.Sqrt, bias=EPS, scale=1.0 / 2048), reads=[bss], writes=[bss])
                    k.op("dve", lambda e: e.reciprocal(ss[:, 1:2], ss[:, 1:2]), reads=[bss], writes=[bss])
                    k.op("dve", lambda e, y_=y_: e.scalar_tensor_tensor(yn[:], y_[:], ss[:, 1:2], nwb[:], ALU.mult, ALU.mult), reads=[b_y, bss, bnwb], writes=[byn])
                    for q in range(4):
                        p, pb = next_ps()
                        pbf = p[:].bitcast(BF16)
                        for j in range(4):
                            m = q * 4 + j
                            k.op("pe", lambda e, pbf=pbf, j=j, m=m: e.transpose(pbf[:, j * 128:(j + 1) * 128], yn[:, m * 128:(m + 1) * 128], ident_b[:]),
                                 reads=[byn, b_identb], writes=[pb], inc=(j == 3))
                        k.op("act" if q % 2 else "dve", (lambda e, pbf=pbf, q=q: e.copy(ynT[:, q * 4:(q + 1) * 4, :], pbf[:, 0:512].rearrange("p (j t) -> p j t", t=128))) if q % 2 else
                             (lambda e, pbf=pbf, q=q: e.tensor_copy(ynT[:, q * 4:(q + 1) * 4, :], pbf[:, 0:512].rearrange("p (j t) -> p j t", t=128))),
                             reads=[pb], writes=[bynT])
                    for c in range(8):
                        p, pb = next_ps()
                        for j in range(16):
                            k.op("pe", lambda e, p=p, j=j, c=c: e.matmul(p[:, 0:128], ow[:, j, c * 128:(c + 1) * 128], ynT[:, j, :], start=(j == 0), stop=(j == 15)),
                                 reads=[bow, bynT], writes=[pb], inc=(j == 15))
                        k.op("dve", lambda e, p=p, c=c, i2=i2: e.scalar_tensor_tensor(xt[i2][:, c, :], p[:, 0:128], mod[:, 16 + c, vsel:vsel + 1], xt[i2][:, c, :], ALU.mult, ALU.add),
                             reads=[pb, b_mod, bx[i2]], writes=[bx[i2]])
                    k.dma("pool", xres[:, t0:t0 + 128].rearrange("(c p) t -> p c t", p=128), xt[i2][:], reads=[bx[i2]], writes=[Buf()])
            k.barrier()

        TWO_PI = 6.283185307179586

        def sincos(st, tag, phi, bphi, shape, out_sin, out_cos, bout):
            t = sb("sc_t" + tag, shape, F32, st); ki = sb("sc_k" + tag, shape, I32, st); bt = Buf()
            for (o, off) in ((out_sin, 0.0), (out_cos, 0.5 * np.pi)):
                k.op("dve", lambda e, off=off: e.tensor_scalar(t[:], phi, 1.0 / TWO_PI, off / TWO_PI, ALU.mult, ALU.add), reads=[bphi], writes=[bt])
                k.op("dve", lambda e: e.tensor_copy(ki[:], t[:]), reads=[bt], writes=[bt])
                k.op("dve", lambda e: e.tensor_copy(t[:], ki[:]), reads=[bt], writes=[bt])
                k.op("dve", lambda e: e.scalar_tensor_tensor(t[:], t[:], -TWO_PI, phi, ALU.mult, ALU.add), reads=[bt, bphi], writes=[bt])
                k.op("dve", lambda e, off=off: e.tensor_scalar(t[:], t[:], off, 3.1415925, ALU.add, ALU.min), reads=[bt], writes=[bt])
                k.op("dve", lambda e: e.tensor_scalar(t[:], t[:], -3.1415925, None, ALU.max), reads=[bt], writes=[bt])
                k.op("act", lambda e, o=o: e.activation(o, t[:], AF.Sin), reads=[bt], writes=[bout])

        def s5_layer(li, ctx_out):
            T = 128
            yfT = scrA
            tiles = seg_tiles(T)
            for d in range(2):
                with ExitStack() as st:
                    def small(name, shape, src, q="sp"):
                        t = sb(name, shape, F32, st); b = Buf()
                        k.dma(q, t[:], src, writes=[b], allow_slow_non_contiguous=True)
                        return t, b
                    ramp, bramp = small("ramp", [128, 128], ramp_in)
                    sel, bsel = small("sel", [128, 8], sel_in)
                    lrs, b1 = small("lrs", [128, 32], s5_lre[0, d].rearrange("g p -> (g p)").rearrange("(b q) -> q b", q=128))
                    lis, b2 = small("lis", [128, 32], s5_lim[0, d].rearrange("g p -> (g p)").rearrange("(b q) -> q b", q=128))
                    sts = sb("sts", [128, 32], F32, st); b3 = Buf()
                    for gl in range(2):
                        k.dma("sp", sts[gl * 64:(gl + 1) * 64, :], s5_ls[0, d].rearrange("(b gl) -> gl b", gl=2)[gl].partition_broadcast(64), writes=[b3],
                              allow_slow_non_contiguous=True)
                    rho = sb("rho", [128, 32], F32, st); theta = sb("theta", [128, 32], F32, st); bpar = Buf()
                    k.op("dve", lambda e: e.tensor_scalar(lrs[:], lrs[:], -1e-4, None, ALU.min), reads=[b1], writes=[b1])
                    k.op("act", lambda e: e.activation(sts[:], sts[:], AF.Exp), reads=[b3], writes=[b3])
                    k.op("dve", lambda e: e.tensor_tensor(rho[:], lrs[:], sts[:], ALU.mult), reads=[b1, b3], writes=[bpar])
                    k.op("act", lambda e: e.activation(rho[:], rho[:], AF.Exp), reads=[bpar], writes=[bpar])
                    k.op("dve", lambda e: e.tensor_tensor(theta[:], lis[:], sts[:], ALU.mult), reads=[b2, b3], writes=[bpar])
                    rhoT = sb("rhoT", [128, 32, 128], F32, st)
                    for b in range(32):
                        k.op("pool", lambda e, b=b: e.tensor_scalar(rhoT[:, b, :], ramp[:], 0.0, rho[:, b:b + 1], ALU.mult, ALU.add), reads=[bramp, bpar], writes=[bpar])
                    cosT = sb("cosT", [128, 32, 128], F32, st); sinT = sb("sinT", [128, 32, 128], F32, st); btab = Buf()
                    phi = sb("phi", [128, 128], F32, st); bphi = Buf()
                    with ExitStack() as st2:
                        for b in range(32):
                            k.op("dve", lambda e, b=b: e.tensor_scalar(phi[:], ramp[:], theta[:, b:b + 1], None, ALU.mult), reads=[bramp, bpar], writes=[bphi])
                            if b == 0:
                                tt_ = sb("sc_t", [128, 128], F32, st2); ki_ = sb("sc_k", [128, 128], I32, st2); bt_ = Buf()
                            for (o, off) in ((sinT[:, b, :], 0.0), (cosT[:, b, :], 0.5 * np.pi)):
                                k.op("dve", lambda e, off=off: e.tensor_scalar(tt_[:], phi[:], 1.0 / TWO_PI, off / TWO_PI, ALU.mult, ALU.add), reads=[bphi], writes=[bt_])
                                k.op("dve", lambda e: e.tensor_copy(ki_[:], tt_[:]), reads=[bt_], writes=[bt_])
                                k.op("dve", lambda e: e.tensor_copy(tt_[:], ki_[:]), reads=[bt_], writes=[bt_])
                                k.op("dve", lambda e: e.scalar_tensor_tensor(tt_[:], tt_[:], -TWO_PI, phi[:], ALU.mult, ALU.add), reads=[bt_, bphi], writes=[bt_])
                                k.op("dve", lambda e, off=off: e.tensor_scalar(tt_[:], tt_[:], off, 3.1415925, ALU.add, ALU.min), reads=[bt_], writes=[bt_])
                                k.op("dve", lambda e: e.tensor_scalar(tt_[:], tt_[:], -3.1415925, None, ALU.max), reads=[bt_], writes=[bt_])
                                k.op("act", lambda e, o=o: e.activation(o, tt_[:], AF.Sin), reads=[bt_], writes=[btab])
                    k.barrier()
                    Wre = sb("Wre", [128, 8, 4, 128], BF16, st); Wim = sb("Wim", [128, 8, 4, 128], BF16, st); bW = Buf()
                    with ExitStack() as st2:
                        def wl(name, src3):
                            t = sb(name, [128, 8, 64], F32, st2); b = Buf()
                            v = src3.rearrange("(c gl) p -> gl c p", gl=8)
                            for gl in range(8):
                                k.dma("sp" if gl % 2 else "pool", t[gl * 16:(gl + 1) * 16, :, :], v[gl].partition_broadcast(16), writes=[b], allow_slow_non_contiguous=True)
                            return t, b
                        lrw, blr = wl("lrw", s5_lre[0, d]); liw, bli = wl("liw", s5_lim[0, d])
                        stw = sb("stw", [128, 8], F32, st2); bst = Buf()
                        v = s5_ls[0, d].rearrange("(c gl) -> gl c", gl=8)
                        for gl in range(8):
                            k.dma("sp", stw[gl * 16:(gl + 1) * 16, :], v[gl].partition_broadcast(16), writes=[bst], allow_slow_non_contiguous=True)
                        brw = sb("brw", [128, 8, 64], F32, st2); biw = sb("biw", [128, 8, 64], F32, st2); bbw = Buf()
                        for (t_, src) in ((brw, s5_bre[0]), (biw, s5_bim[0])):
                            v = src.rearrange("(c gl) p j -> gl j c p", gl=8)
                            for gl in range(8):
                                for c_ in range(8):
                                    k.dma("sp" if c_ % 2 else "pool", t_[gl * 16:(gl + 1) * 16, c_, :], v[gl][:, c_, :], writes=[bbw], allow_slow_non_contiguous=True)
                        k.op("dve", lambda e: e.tensor_scalar(lrw[:], lrw[:], -1e-4, None, ALU.min), reads=[blr], writes=[blr])
                        k.op("act", lambda e: e.activation(stw[:], stw[:], AF.Exp), reads=[bst], writes=[bst])
                        stb = stw[:].unsqueeze(2).to_broadcast([128, 8, 64])
                        mag = sb("mag", [128, 8, 64], F32, st2); th = sb("thw", [128, 8, 64], F32, st2); bm_ = Buf(); bth = Buf()
                        k.op("dve", lambda e: e.tensor_tensor(mag[:], lrw[:], stb, ALU.mult), reads=[blr, bst], writes=[bm_])
                        k.op("act", lambda e: e.activation(mag[:], mag[:], AF.Exp), reads=[bm_], writes=[bm_])
                        k.op("dve", lambda e: e.tensor_tensor(th[:], liw[:], stb, ALU.mult), reads=[bli, bst], writes=[bth])
                        sn = sb("snw", [128, 8, 64], F32, st2); cs = sb("csw", [128, 8, 64], F32, st2); bsc_ = Buf()
                        sincos(st2, "w", th[:], bth, [128, 8, 64], sn[:], cs[:], bsc_)
                        k.op("dve", lambda e: e.tensor_tensor(cs[:], cs[:], mag[:], ALU.mult), reads=[bsc_, bm_], writes=[bsc_])
                        k.op("dve", lambda e: e.tensor_scalar(cs[:], cs[:], -1.0, None, ALU.add), reads=[bsc_], writes=[bsc_])
                        k.op("dve", lambda e: e.tensor_tensor(sn[:], sn[:], mag[:], ALU.mult), reads=[bsc_, bm_], writes=[bsc_])
                        den = sb("den", [128, 8, 64], F32, st2); t1 = sb("t1w", [128, 8, 64], F32, st2); bden = Buf(); bt1 = Buf()
                        zr = sb("zr", [128, 8, 64], F32, st2); zi = sb("zi", [128, 8, 64], F32, st2); bz_ = Buf()
                        k.op("dve", lambda e: e.tensor_tensor(den[:], lrw[:], lrw[:], ALU.mult), reads=[blr], writes=[bden])
                        k.op("dve", lambda e: e.tensor_tensor(t1[:], liw[:], liw[:], ALU.mult), reads=[bli], writes=[bt1])
                        k.op("dve", lambda e: e.tensor_tensor(den[:], den[:], t1[:], ALU.add), reads=[bden, bt1], writes=[bden])
                        k.op("dve", lambda e: e.reciprocal(den[:], den[:]), reads=[bden], writes=[bden])
                        k.op("dve", lambda e: e.tensor_tensor(zr[:], cs[:], lrw[:], ALU.mult), reads=[bsc_, blr], writes=[bz_])
                        k.op("dve", lambda e: e.tensor_tensor(t1[:], sn[:], liw[:], ALU.mult), reads=[bsc_, bli], writes=[bt1])
                        k.op("dve", lambda e: e.tensor_tensor(zr[:], zr[:], t1[:], ALU.add), reads=[bz_, bt1], writes=[bz_])
                        k.op("dve", lambda e: e.tensor_tensor(zr[:], zr[:], den[:], ALU.mult), reads=[bz_, bden], writes=[bz_])
                        k.op("dve", lambda e: e.tensor_tensor(zi[:], sn[:], lrw[:], ALU.mult), reads=[bsc_, blr], writes=[bz_])
                        k.op("dve", lambda e: e.tensor_tensor(t1[:], cs[:], liw[:], ALU.mult), reads=[bsc_, bli], writes=[bt1])
                        k.op("dve", lambda e: e.tensor_tensor(zi[:], zi[:], t1[:], ALU.subtract), reads=[bz_, bt1], writes=[bz_])
                        k.op("dve", lambda e: e.tensor_tensor(zi[:], zi[:], den[:], ALU.mult), reads=[bz_, bden], writes=[bz_])
                        k.op("dve", lambda e: e.tensor_tensor(mag[:], zr[:], brw[:], ALU.mult), reads=[bz_, bbw], writes=[bm_])
                        k.op("dve", lambda e: e.tensor_tensor(t1[:], zi[:], biw[:], ALU.mult), reads=[bz_, bbw], writes=[bt1])
                        k.op("dve", lambda e: e.tensor_tensor(mag[:], mag[:], t1[:], ALU.subtract), reads=[bm_, bt1], writes=[bm_])
                        k.op("dve", lambda e: e.tensor_tensor(th[:], zr[:], biw[:], ALU.mult), reads=[bz_, bbw], writes=[bth])
                        k.op("dve", lambda e: e.tensor_tensor(t1[:], zi[:], brw[:], ALU.mult), reads=[bz_, bbw], writes=[bt1])
                        k.op("dve", lambda e: e.tensor_tensor(th[:], th[:], t1[:], ALU.add), reads=[bth, bt1], writes=[bth])
                        if debug_sub == "pre":
                            for i_, (t_, b_) in enumerate(((brw, bbw), (mag, bm_), (zr, bz_), (den, bden), (sn, bsc_), (lrw, blr), (liw, bli))):
                                k.dma("sp", outT[i_ * 128:(i_ + 1) * 128, 0:512], t_[:].rearrange("p c q -> p (c q)"), reads=[b_], writes=[Buf()])
                            k.dma("sp", outT[896:1024, 0:8], stw[:], reads=[bst], writes=[Buf()])
                            k.dma("sp", outT[896:1024, 8:16], sel[:], reads=[bsel], writes=[Buf()])
                            k.barrier()
                            return
                        for q in range(4):
                            for gl2 in range(2):
                                k.op("dve", lambda e, q=q, gl2=gl2: e.tensor_scalar(Wre[:, :, q, gl2 * 64:(gl2 + 1) * 64], mag[:], sel[:, q * 2 + gl2:q * 2 + gl2 + 1], None, ALU.mult),
                                     reads=[bm_, bsel], writes=[bW])
                                k.op("dve", lambda e, q=q, gl2=gl2: e.tensor_scalar(Wim[:, :, q, gl2 * 64:(gl2 + 1) * 64], th[:], sel[:, q * 2 + gl2:q * 2 + gl2 + 1], None, ALU.mult),
                                     reads=[bth, bsel], writes=[bW])
                    k.barrier()
                    if debug_sub == "pre2":
                        for c_ in range(8):
                            k.dma("sp", outT[c_ * 128:(c_ + 1) * 128, 0:512], Wre[:, c_, :, :].rearrange("p q m -> p (q m)"), reads=[bW], writes=[Buf()])
                        k.barrier()
                        return
                    WcR = sb("WcR", [128, 32, 128], BF16, st); WcI = sb("WcI", [128, 32, 128], BF16, st); bWc = Buf()
                    for b_ in range(32):
                        k.op("pool", lambda e, b_=b_: e.memset(WcR[:, b_, :], 0.0), writes=[bWc])
                        k.op("pool", lambda e, b_=b_: e.memset(WcI[:, b_, :], 0.0), writes=[bWc])
                    if debug_sub == "p_a":
                        for c_ in range(8):
                            k.dma("sp", outT[c_ * 128:(c_ + 1) * 128, 0:512], Wre[:, c_, :, :].rearrange("p q m -> p (q m)"), reads=[bW], writes=[Buf()])
                        k.barrier()
                        return
                    with ExitStack() as st2:
                        for (Wc_, src, sgn) in ((WcR, s5_cre[0, d], 1.0), (WcI, s5_cim[0, d], -1.0)):
                            t2 = sb("t2c", [128, 32, 16], F32, st2); bt2 = Buf()
                            v = src.rearrange("(b gl) j p -> gl p b j", gl=2)
                            for gl2 in range(2):
                                for hb in range(32):
                                    k.dma("sp" if hb % 2 else "pool", t2[gl2 * 64:(gl2 + 1) * 64, hb, :], v[gl2][:, hb, :], writes=[bt2],
                                          allow_slow_non_contiguous=True)
                            for gl2 in range(2):
                                for q in range(4):
                                    col = (2 * q + gl2) * 16
                                    k.op("dve", lambda e, gl2=gl2, q=q, col=col, Wc_=Wc_, t2=t2, sgn=sgn: e.tensor_scalar(
                                        Wc_[gl2 * 64:(gl2 + 1) * 64, q::4, col:col + 16], t2[gl2 * 64:(gl2 + 1) * 64, q::4, :], sgn, None, ALU.mult),
                                        reads=[bt2], writes=[bWc])
                    k.barrier()
                    if debug_sub == "pre3":
                        for c_ in range(8):
                            k.dma("sp", outT[c_ * 128:(c_ + 1) * 128, 0:512], Wre[:, c_, :, :].rearrange("p q m -> p (q m)"), reads=[bW], writes=[Buf()])
                        k.barrier()
                        return
                    if d == 1:
                        gw_, bgw_ = load_w(st, "s5glu", s5_glu_w[0], 8, 2 * D)
                        gbias, bgb_ = small("s5gb", [128, 16], s5_glu_b[0].rearrange("(m p) -> p m", p=128))
                        dskT, bdsk_ = small("s5d", [128, 8], s5_d[0].rearrange("(c p) -> p c", p=128))
                    nscs = [norm_scratch(st, T) for _ in range(2)]
                    xt = [sb("s5_x%d" % i, [128, 8, T], F32, st) for i in range(2)]; bx = [Buf(), Buf()]
                    hTs = [sb("s5_h%d" % i, [128, 8, T], BF16, st) for i in range(2)]; bhs = [Buf(), Buf()]
                    stR = sb("s5_stR", [128, 32], F32, st); stI = sb("s5_stI", [128, 32], F32, st); bstate = [Buf() for _ in range(32)]
                    k.op("pool", lambda e: e.memset(stR[:], 0.0), writes=bstate)
                    k.op("pool", lambda e: e.memset(stI[:], 0.0), writes=bstate)
                    NB = 6
                    def mk(name, dt=F32):
                        return [sb("%s%d" % (name, i), [128, T], dt, st) for i in range(NB)], [Buf() for _ in range(NB)]
                    ur, bur = mk("s5_ur"); ui, bui = mk("s5_ui")
                    ta, bta = mk("s5_ta"); tb_, btb = mk("s5_tb"); tc, btc = mk("s5_tc"); td, btd = mk("s5_td")
                    vr, bvr = mk("s5_vr"); vi, bvi = mk("s5_vi")
                    q1, bq1 = mk("s5_q1", BF16); q2, bq2 = mk("s5_q2", BF16); q3, bq3 = mk("s5_q3", BF16); q4, bq4 = mk("s5_q4", BF16)
                    stt_ = [sb("s5_stt%d" % i, [128, 2], F32, st) for i in range(NB)]; bstt = [Buf() for _ in range(NB)]
                    yo = [sb("s5_yo%d" % i, [128, T], F32, st) for i in range(2)]; byo = [Buf(), Buf()]
                    if d == 1:
                        yf = [sb("s5_yf%d" % i, [128, T], F32, st) for i in range(2)]; byf = [Buf(), Buf()]
                        gT = sb("s5_g", [128, 8, T], BF16, st); bg = Buf()
                        ga = [sb("s5_ga%d" % i, [128, T], F32, st) for i in range(2)]; bga = [Buf(), Buf()]
                    order = tiles if d == 0 else [tiles[1], tiles[0]] + tiles[:1:-1]
                    rv = (lambda ap: ap) if d == 0 else (lambda ap: ap[:, ::-1])
                    pspool[0] = [6, 7]
                    units = []

                    def make_unit(ti, t0, n, vsel, c, q, un):
                        x, b_x = xt[ti % 2], bx[ti % 2]
                        hT, bh = hTs[ti % 2], bhs[ti % 2]
                        b = c * 4 + q
                        i3 = un % NB
                        pu, pub = ps[un % 3], psb[un % 3]
                        py, pyb = ps[3 + (un // 4) % 3], psb[3 + (un // 4) % 3]
                        cs_ = rv(cosT[:, b, :]); sn_ = rv(sinT[:, b, :])
                        rb = rhoT[:, b, :]
                        last = (n - 1) if d == 0 else 0

                        def s0():
                            if c == 0 and q == 0:
                                k.dma("sp", x[:, :, :n], xres[:, t0:t0 + n].rearrange("(c p) t -> p c t", p=128), writes=[b_x])
                                norm_mod(nscs[ti % 2], x, b_x, n, vsel, A1, 0, hT, bh)
                            k.op("pe", lambda e: e.matmul(pu[:, 0:128], Wre[:, c, q, :], hT[:, c, :], start=True, stop=True), reads=[bW, bh], writes=[pub], inc=False)
                            k.op("pe", lambda e: e.matmul(pu[:, 128:256], Wim[:, c, q, :], hT[:, c, :], start=True, stop=True), reads=[bW, bh], writes=[pub])

                        def s1():
                            k.op("act", lambda e: e.copy(ur[i3][:], pu[:, 0:128]), reads=[pub], writes=[bur[i3]])
                            k.op("act", lambda e: e.copy(ui[i3][:], pu[:, 128:256]), reads=[pub], writes=[bui[i3]])

                        def s2():
                            k.op("dve", lambda e: e.tensor_tensor(ta[i3][:], ur[i3][:], cs_, ALU.mult), reads=[bur[i3], btab], writes=[bta[i3]])
                            k.op("pool", lambda e: e.tensor_tensor(tb_[i3][:], ui[i3][:], sn_, ALU.mult), reads=[bui[i3], btab], writes=[btb[i3]])
                            k.op("dve", lambda e: e.tensor_tensor(tc[i3][:], ui[i3][:], cs_, ALU.mult), reads=[bui[i3], btab], writes=[btc[i3]])
                            k.op("pool", lambda e: e.tensor_tensor(td[i3][:], ur[i3][:], sn_, ALU.mult), reads=[bur[i3], btab], writes=[btd[i3]])

                        def s3():
                            k.op("pool", lambda e: e.tensor_tensor(ta[i3][:], ta[i3][:], tb_[i3][:], ALU.add), reads=[bta[i3], btb[i3]], writes=[bta[i3]])
                            k.op("pool", lambda e: e.tensor_tensor(tc[i3][:], tc[i3][:], td[i3][:], ALU.subtract), reads=[btc[i3], btd[i3]], writes=[btc[i3]])

                        def s4():
                            k.op("dve", lambda e: e.tensor_tensor_scan(rv(vr[i3][:]), rb, rv(ta[i3][:]), stR[:, b:b + 1], ALU.mult, ALU.add),
                                 reads=[bta[i3], bpar, bstate[b]], writes=[bvr[i3]])
                            k.op("dve", lambda e: e.tensor_tensor_scan(rv(vi[i3][:]), rb, rv(tc[i3][:]), stI[:, b:b + 1], ALU.mult, ALU.add),
                                 reads=[btc[i3], bpar, bstate[b]], writes=[bvi[i3]])

                        def s5():
                            k.op("pool", lambda e: e.tensor_tensor(q1[i3][:], vr[i3][:], cs_, ALU.mult), reads=[bvr[i3], btab], writes=[bq1[i3]])
                            k.op("dve", lambda e: e.scalar_tensor_tensor(q2[i3][:], vi[i3][:], -1.0, sn_, ALU.mult, ALU.mult), reads=[bvi[i3], btab], writes=[bq2[i3]])
                            k.op("pool", lambda e: e.tensor_tensor(q3[i3][:], vi[i3][:], cs_, ALU.mult), reads=[bvi[i3], btab], writes=[bq3[i3]])
                            k.op("dve", lambda e: e.tensor_tensor(q4[i3][:], vr[i3][:], sn_, ALU.mult), reads=[bvr[i3], btab], writes=[bq4[i3]])
                            k.op("act", lambda e: e.activation(stt_[i3][:, 0:1], vi[i3][:, last:last + 1], AF.Identity, scale=sn_[:, last:last + 1]),
                                 reads=[bvi[i3], btab], writes=[bstt[i3]])
                            k.op("act", lambda e: e.activation(stt_[i3][:, 0:1], stt_[i3][:, 0:1], AF.Identity, scale=-1.0), reads=[bstt[i3]], writes=[bstt[i3]])
                            k.op("act", lambda e: e.activation(stt_[i3][:, 1:2], vr[i3][:, last:last + 1], AF.Identity, scale=sn_[:, last:last + 1]),
                                 reads=[bvr[i3], btab], writes=[bstt[i3]])

                        def s6():
                            k.op("act", lambda e: e.activation(stR[:, b:b + 1], vr[i3][:, last:last + 1], AF.Identity, bias=stt_[i3][:, 0:1], scale=cs_[:, last:last + 1]),
                                 reads=[bvr[i3], btab, bstt[i3]], writes=[bstate[b]])
                            k.op("act", lambda e: e.activation(stI[:, b:b + 1], vi[i3][:, last:last + 1], AF.Identity, bias=stt_[i3][:, 1:2], scale=cs_[:, last:last + 1]),
                                 reads=[bvi[i3], btab, bstt[i3]], writes=[bstate[b]])
                            k.op("pe", lambda e: e.matmul(py[:, 0:128], WcR[:, b, :], q1[i3][:], start=(q == 0), stop=False), reads=[bWc, bq1[i3]], writes=[pyb], inc=False)
                            k.op("pe", lambda e: e.matmul(py[:, 0:128], WcR[:, b, :], q2[i3][:], start=False, stop=False), reads=[bWc, bq2[i3]], writes=[pyb], inc=False)
                            k.op("pe", lambda e: e.matmul(py[:, 0:128], WcI[:, b, :], q3[i3][:], start=False, stop=False), reads=[bWc, bq3[i3]], writes=[pyb], inc=False)
                            k.op("pe", lambda e: e.matmul(py[:, 0:128], WcI[:, b, :], q4[i3][:], start=False, stop=(q == 3)), reads=[bWc, bq4[i3]], writes=[pyb], inc=(q == 3))
                            if q != 3:
                                return
                            if d == 0:
                                o, bo = yo[c % 2], byo[c % 2]
                                k.op("act", lambda e: e.copy(o[:], py[:, 0:128]), reads=[pyb], writes=[bo])
                                k.dma("pool", yfT[c * 128:(c + 1) * 128, t0:t0 + n], o[:], reads=[bo], writes=[Buf()])
                                return
                            if vsel == 1 and not ctx_out:
                                return
                            f_, bf_ = yf[c % 2], byf[c % 2]
                            k.dma("sp", f_[:], yfT[c * 128:(c + 1) * 128, t0:t0 + n], writes=[bf_])
                            k.op("dve", lambda e: e.tensor_tensor(f_[:], f_[:], py[:, 0:128], ALU.add), reads=[pyb, bf_], writes=[bf_])
                            k.op("dve", lambda e: e.scalar_tensor_tensor(f_[:], hT[:, c, :], dskT[:, c:c + 1], f_[:], ALU.mult, ALU.add), reads=[bh, bdsk_, bf_], writes=[bf_])
                            k.op("act", lambda e: e.activation(gT[:, c, :], f_[:], AF.Gelu_apprx_tanh), reads=[bf_], writes=[bg])
                            if c != 7:
                                return
                            for c2 in range(8):
                                pa_, pab_ = next_ps()
                                pg_, pgb_ = next_ps()
                                for kc in range(8):
                                    k.op("pe", lambda e, kc=kc, c2=c2, pa_=pa_: e.matmul(pa_[:, 0:128], gw_[:, kc, c2 * 128:(c2 + 1) * 128], gT[:, kc, :], start=(kc == 0), stop=(kc == 7)),
                                         reads=[bgw_, bg], writes=[pab_], inc=(kc == 7))
                                for kc in range(8):
                                    k.op("pe", lambda e, kc=kc, c2=c2, pg_=pg_: e.matmul(pg_[:, 0:128], gw_[:, kc, D + c2 * 128:D + (c2 + 1) * 128], gT[:, kc, :], start=(kc == 0), stop=(kc == 7)),
                                         reads=[bgw_, bg], writes=[pgb_], inc=(kc == 7))
                                a_, ba_ = ga[c2 % 2], bga[c2 % 2]
                                k.op("act", lambda e, pg_=pg_, a_=a_, c2=c2: e.activation(a_[:], pg_[:, 0:128], AF.Sigmoid, bias=gbias[:, 8 + c2:9 + c2], scale=1.0), reads=[pgb_, bgb_], writes=[ba_])
                                k.op("dve", lambda e, pa_=pa_, a_=a_, c2=c2: e.scalar_tensor_tensor(a_[:], pa_[:, 0:128], gbias[:, c2:c2 + 1], a_[:], ALU.add, ALU.mult), reads=[pab_, bgb_, ba_], writes=[ba_])
                                k.op("dve", lambda e, a_=a_, c2=c2: e.scalar_tensor_tensor(x[:, c2, :], a_[:], mod[:, 16 + c2, vsel:vsel + 1], x[:, c2, :], ALU.mult, ALU.add),
                                     reads=[ba_, b_mod, b_x], writes=[b_x])
                            k.dma("pool", xres[:, t0:t0 + n].rearrange("(c p) t -> p c t", p=128), x[:, :, :n], reads=[b_x], writes=[Buf()])

                        return [s0, s1, s2, s3, s4, s5, s6]

                    un = 0
                    for ti, (t0, n, vsel) in enumerate(order):
                        for c in range(8):
                            for q in range(4):
                                units.append(make_unit(ti, t0, n, vsel, c, q, un))
                                un += 1
                    NS = 7
                    for step in range(len(units) + NS - 1):
                        for j in range(NS - 1, -1, -1):
                            u = step - j
                            if 0 <= u < len(units):
                                units[u][j]()
                    pspool[0] = list(range(8))
                k.barrier()
                if debug == "s5f" and d == 0:
                    return

        def mla_layer(li, ctx_out):
            TT = 512
            SCALE = 96.0 ** -0.5
            with ExitStack() as L:
                qnT = sb("qnT", [128, 3, NLAT], BF16, L); bqn = Buf()
                kvnT = sb("kvnT", [128, 2, NTOK], BF16, L); bkvn = Buf()
                KT = sb("KT", [128, NTOK], BF16, L); bKr = Buf(); bKn = Buf()
                k.op("pool", lambda e: e.memset(KT[96:97, :], 1.0), writes=[bKr])
                with ExitStack() as st:
                    w, bw = load_w(st, "mlin", mla_in_w[0], 8, 672)
                    wkr = sb("wkr", [128, 8, 96], BF16, st); wkrs = sb("wkrs", [128, 8, 96], BF16, st); bwk = Buf()
                    k.op("pool", lambda e: e.memset(wkr[:], 0.0), writes=[bwk])
                    k.op("pool", lambda e: e.memset(wkrs[:], 0.0), writes=[bwk])
                    k.dma("pool", wkr[:, :, 64:96], mla_in_w[0][:, 640:672].rearrange("(c p) n -> p c n", p=128), writes=[bwk], allow_slow_non_contiguous=True)
                    k.dma("pool", wkrs[:, :, 64:96], mla_inw_sw.rearrange("(c p) n -> p c n", p=128), writes=[bwk], allow_slow_non_contiguous=True)
                    qnw = sb("qnw", [128, 3], F32, st); kvnw = sb("kvnw", [128, 2], F32, st); bnw_ = Buf()
                    k.dma("sp", qnw[:], mla_q_norm_w[0].rearrange("(c p) -> p c", p=128), writes=[bnw_], allow_slow_non_contiguous=True)
                    k.dma("sp", kvnw[:], mla_kv_norm_w[0].rearrange("(c p) -> p c", p=128), writes=[bnw_], allow_slow_non_contiguous=True)
                    nsc = norm_scratch(st, TT)
                    xt = [sb("ma_x%d" % i, [128, 8, TT], F32, st) for i in range(2)]; bx = [Buf(), Buf()]
                    hT = sb("ma_h", [128, 8, TT], BF16, st); bh = Buf()
                    ql = sb("ma_ql", [128, 3, TT], F32, st); bql = Buf()
                    sq = sb("ma_sq", [128, TT], F32, st); bsq = Buf()
                    rs = sb("ma_rs", [128, TT], F32, st); brs = Buf()
                    rc = [sb("ma_rc%d" % i, [128, TT], F32, st) for i in range(2)]; rsn = [sb("ma_rsn%d" % i, [128, TT], F32, st) for i in range(2)]; brc = [Buf(), Buf()]
                    t1 = sb("ma_t1", [128, TT], F32, st); t2 = sb("ma_t2", [128, TT], F32, st); bt1 = Buf(); bt2 = Buf()

                    def lat_norm(ncl, col0, nw_t, dim, out_t, out_b, tsl):
                        pss, pssb = next_ps()
                        for c in range(ncl):
                            p, pb = next_ps()
                            for kc in range(8):
                                k.op("pe", lambda e, p=p, kc=kc, c=c: e.matmul(p[:, :n], w[:, kc, col0 + c * 128:col0 + (c + 1) * 128], hT[:, kc, :n], start=(kc == 0), stop=(kc == 7)),
                                     reads=[bw, bh], writes=[pb], inc=(kc == 7))
                            k.op("act", lambda e, p=p, c=c: e.copy(ql[:, c, :n], p[:, :n]), reads=[pb], writes=[bql])
                            k.op("act", lambda e, p=p: e.activation(sq[:, :n], p[:, :n], AF.Square), reads=[pb], writes=[bsq])
                            k.op("pe", lambda e, pss=pss, c=c: e.matmul(pss[:, :n], ones_f[:], sq[:, :n], start=(c == 0), stop=(c == ncl - 1)), reads=[b_ones, bsq], writes=[pssb])
                        k.op("act", lambda e, pss=pss: e.activation(rs[:, :n], pss[:, :n], AF.Sqrt, bias=EPS, scale=1.0 / dim), reads=[pssb], writes=[brs])
                        k.op("dve", lambda e: e.reciprocal(rs[:, :n], rs[:, :n]), reads=[brs], writes=[brs])
                        for c in range(ncl):
                            k.op("dve", lambda e, c=c: e.scalar_tensor_tensor(out_t[:, c, tsl], ql[:, c, :n], nw_t[:, c:c + 1], rs[:, :n], ALU.mult, ALU.mult),
                                 reads=[bql, bnw_, brs], writes=[out_b])

                    for ti, (t0, n, vsel) in enumerate(seg_tiles(TT)):
                        x, b_x = xt[ti % 2], bx[ti % 2]
                        k.dma("sp", x[:, :, :n], xres[:, t0:t0 + n].rearrange("(c p) t -> p c t", p=128), writes=[b_x])
                        norm_mod(nsc, x, b_x, n, vsel, A1, 0, hT, bh)
                        if vsel == 0:
                            lat_norm(3, 0, qnw, 384.0, qnT, bqn, slice(t0 - NCTX, t0 - NCTX + n))
                        lat_norm(2, 384, kvnw, 256.0, kvnT, bkvn, slice(t0, t0 + n))
                        pA, pAb = next_ps()
                        for kc in range(8):
                            k.op("pe", lambda e, pA=pA, kc=kc: e.matmul(pA[0:96, :n], wkr[:, kc, :], hT[:, kc, :n], start=(kc == 0), stop=(kc == 7)), reads=[bwk, bh], writes=[pAb], inc=(kc == 7))
                        if vsel == 1:
                            k.op("act", lambda e, pA=pA: e.copy(KT[64:96, t0:t0 + n], pA[64:96, :n]), reads=[pAb], writes=[bKr])
                        else:
                            pB, pBb = next_ps()
                            for kc in range(8):
                                k.op("pe", lambda e, pB=pB, kc=kc: e.matmul(pB[0:96, :n], wkrs[:, kc, :], hT[:, kc, :n], start=(kc == 0), stop=(kc == 7)), reads=[bwk, bh], writes=[pBb], inc=(kc == 7))
                            i2 = ti % 2
                            k.dma("sp", rc[i2][64:96, :n], ropeC_in[:, t0 - NCTX:t0 - NCTX + n], writes=[brc[i2]])
                            k.dma("sp", rsn[i2][64:96, :n], ropeS_in[:, t0 - NCTX:t0 - NCTX + n], writes=[brc[i2]])
                            k.op("dve", lambda e, pA=pA, i2=i2: e.tensor_tensor(t1[64:96, :n], pA[64:96, :n], rc[i2][64:96, :n], ALU.mult), reads=[pAb, brc[i2]], writes=[bt1])
                            k.op("dve", lambda e, pB=pB, i2=i2: e.tensor_tensor(t2[64:96, :n], pB[64:96, :n], rsn[i2][64:96, :n], ALU.mult), reads=[pBb, brc[i2]], writes=[bt2])
                            k.op("pool", lambda e: e.tensor_tensor(KT[64:96, t0:t0 + n], t1[64:96, :n], t2[64:96, :n], ALU.add), reads=[bt1, bt2], writes=[bKr])
                k.barrier()
                with ExitStack() as st:
                    kvbw, bkvbw = load_w(st, "kvbw", mla_kvb_w[0], 2, 2048)
                    qbw, bqbw = load_w(st, "qbw", mla_qb_w[0], 3, 1536)
                    qbws, bqbws = load_w(st, "qbws", mla_qbw_sw, 3, 512)
                    e96 = sb("e96", [128, 128], BF16, st); be96 = Buf()
                    k.dma("pool", e96[:], e96_in, writes=[be96])
                    odm = sb("odm", [128, 2, 128], BF16, st); bodm = Buf()
                    k.dma("pool", odm[:], odm_in.rearrange("a p m -> p a m"), writes=[bodm])
                    wv = [sb("wv%d" % i, [128, 2, 128], BF16, st) for i in range(2)]; bwv = [Buf(), Buf()]
                    wqr = sb("wqr", [128, 3, 96], BF16, st); wqrs = sb("wqrs", [128, 3, 96], BF16, st); bwq = Buf()
                    for t_ in (wv[0], wv[1], wqr, wqrs):
                        k.op("pool", lambda e, t_=t_: e.memset(t_[:], 0.0), writes=[bwv[0], bwv[1], bwq])
                    QT = sb("QT", [128, NLAT], BF16, st); bQ = Buf()
                    Vt = sb("Vt", [128, NCH, 128], BF16, st); bV = Buf()
                    sqb = sb("mb_sq", [128, TT], BF16, st); bsqb = Buf()
                    kk = sb("mb_kk", [128, TT], F32, st); bkk = Buf()
                    kmx = sb("mb_kmx", [128, 2], F32, st); bkmx = Buf()
                    rc = [sb("mb_rc%d" % i, [128, TT], F32, st) for i in range(2)]; rsn = [sb("mb_rsn%d" % i, [128, TT], F32, st) for i in range(2)]; brc = [Buf(), Buf()]
                    t1 = sb("mb_t1", [128, TT], F32, st); t2 = sb("mb_t2", [128, TT], F32, st); bt1 = Buf(); bt2 = Buf()
                    Pt = [sb("mb_P%d" % i, [128, TT], BF16, st) for i in range(3)]; bP = [Buf() for _ in range(3)]
                    rd = sb("mb_rd", [128, TT], F32, st); brd = Buf()
                    xs_ = sb("mb_xs", [128, TT], F32, st); bxs_ = Buf()
                    swp = sb("mb_swp", [128, 128], F32, st); bswp = Buf()
                    k.dma("sp", swp[:], swp_in, writes=[bswp])
                    ot = [sb("mb_o%d" % i, [128, TT], BF16, st) for i in range(2)]; bot = [Buf(), Buf()]
                    pspool[0] = [4, 5, 6, 7]
                    ktiles = seg_tiles(TT)
                    pn = 0
                    for h in range(16):
                        par = h % 2
                        off = par * 64
                        k.op("pool", lambda e, h=h, par=par, off=off: e.tensor_copy(wv[par][:, :, off:off + 64], kvbw[:, :, h * 128 + 64:h * 128 + 128]), reads=[bkvbw], writes=[bwv[par]])
                        k.op("pool", lambda e, h=h: e.tensor_copy(wqr[:, :, 64:96], qbw[:, :, h * 96 + 64:h * 96 + 96]), reads=[bqbw], writes=[bwq])
                        k.op("pool", lambda e, h=h: e.tensor_copy(wqrs[:, :, 64:96], qbws[:, :, h * 32:h * 32 + 32]), reads=[bqbws], writes=[bwq])
                        k.op("pool", lambda e: e.memset(kmx[:], 0.0), writes=[bkmx])
                        for (t0, n, vsel) in ktiles:
                            p, pb = next_ps()
                            for c in range(2):
                                k.op("pe", lambda e, p=p, c=c, h=h: e.matmul(p[0:64, :n], kvbw[:, c, h * 128:h * 128 + 64], kvnT[:, c, t0:t0 + n], start=(c == 0), stop=(c == 1)),
                                     reads=[bkvbw, bkvn], writes=[pb], inc=(c == 1))
                            k.op("act", lambda e, p=p: e.copy(KT[0:64, t0:t0 + n], p[0:64, :n]), reads=[pb], writes=[bKn])
                            k.op("act", lambda e: e.activation(sqb[0:96, :n], KT[0:96, t0:t0 + n], AF.Square), reads=[bKn, bKr], writes=[bsqb])
                            p2, p2b = next_ps()
                            k.op("pe", lambda e, p2=p2: e.matmul(p2[0:97, :n], e96[0:96, 0:97], sqb[0:96, :n], start=True, stop=True), reads=[be96, bsqb], writes=[p2b])
                            k.op("dve", lambda e, p2=p2: e.reduce_max(kmx[96:97, 1:2], p2[96:97, :n], AX.X), reads=[p2b], writes=[bkmx])
                            k.op("dve", lambda e: e.tensor_tensor(kmx[96:97, 0:1], kmx[96:97, 0:1], kmx[96:97, 1:2], ALU.max), reads=[bkmx], writes=[bkmx])
                        k.op("act", lambda e: e.activation(kmx[96:97, 0:1], kmx[96:97, 0:1], AF.Sqrt), reads=[bkmx], writes=[bkmx])
                        k.op("dve", lambda e: e.tensor_scalar(kmx[96:97, 0:1], kmx[96:97, 0:1], -1.0, None, ALU.mult), reads=[bkmx], writes=[bkmx])
                        for b4 in range(0, NCH, 4):
                            p, pb = next_ps()
                            nb = min(4, NCH - b4)
                            for j in range(nb):
                                blk = b4 + j
                                for c in range(2):
                                    k.op("pe", lambda e, p=p, j=j, blk=blk, c=c, par=par: e.matmul(p[:, j * 128:(j + 1) * 128], kvnT[:, c, blk * 128:(blk + 1) * 128], wv[par][:, c, :],
                                                                                                start=(c == 0), stop=(c == 1)),
                                         reads=[bkvn, bwv[par]], writes=[pb], inc=(c == 1 and j == nb - 1))
                            k.op("dve", lambda e, p=p, b4=b4, nb=nb, off=off: e.tensor_copy(Vt[:, b4:b4 + nb, off:off + 64], p[:, 0:nb * 128].rearrange("p (j m) -> p j m", m=128)[:, :, off:off + 64]),
                                 reads=[pb], writes=[bV])
                        k.op("pool", lambda e, off=off: e.memset(Vt[:, :, 64 - off:128 - off], 1.0), writes=[bV])
                        for qi in range(NLAT // TT):
                            q0 = qi * TT
                            p, pb = next_ps()
                            for c in range(3):
                                k.op("pe", lambda e, p=p, c=c, h=h: e.matmul(p[0:64, :], qbw[:, c, h * 96:h * 96 + 64], qnT[:, c, q0:q0 + TT], start=(c == 0), stop=(c == 2)),
                                     reads=[bqbw, bqn], writes=[pb], inc=(c == 2))
                            k.op("act", lambda e, p=p: e.copy(QT[0:64, q0:q0 + TT], p[0:64, :]), reads=[pb], writes=[bQ])
                            pA, pAb = next_ps()
                            pB, pBb = next_ps()
                            for c in range(3):
                                k.op("pe", lambda e, pA=pA, c=c: e.matmul(pA[0:96, :], wqr[:, c, :], qnT[:, c, q0:q0 + TT], start=(c == 0), stop=(c == 2)), reads=[bwq, bqn], writes=[pAb], inc=(c == 2))
                            for c in range(3):
                                k.op("pe", lambda e, pB=pB, c=c: e.matmul(pB[0:96, :], wqrs[:, c, :], qnT[:, c, q0:q0 + TT], start=(c == 0), stop=(c == 2)), reads=[bwq, bqn], writes=[pBb], inc=(c == 2))
                            i2 = qi % 2
                            k.dma("sp", rc[i2][64:96, :], ropeC_in[:, q0:q0 + TT], writes=[brc[i2]])
                            k.dma("sp", rsn[i2][64:96, :], ropeS_in[:, q0:q0 + TT], writes=[brc[i2]])
                            k.op("dve", lambda e, pA=pA, i2=i2: e.tensor_tensor(t1[64:96, :], pA[64:96, :], rc[i2][64:96, :], ALU.mult), reads=[pAb, brc[i2]], writes=[bt1])
                            k.op("dve", lambda e, pB=pB, i2=i2: e.tensor_tensor(t2[64:96, :], pB[64:96, :], rsn[i2][64:96, :], ALU.mult), reads=[pBb, brc[i2]], writes=[bt2])
                            k.op("pool", lambda e: e.tensor_tensor(QT[64:96, q0:q0 + TT], t1[64:96, :], t2[64:96, :], ALU.add), reads=[bt1, bt2], writes=[bQ])
                            k.op("act", lambda e: e.activation(sqb[0:96, :], QT[0:96, q0:q0 + TT], AF.Square), reads=[bQ], writes=[bsqb])
                            p2, p2b = next_ps()
                            k.op("pe", lambda e, p2=p2: e.matmul(p2[0:97, :], e96[0:96, 0:97], sqb[0:96, :], start=True, stop=True), reads=[be96, bsqb], writes=[p2b])
                            k.op("act", lambda e, p2=p2: e.activation(kk[96:97, :], p2[96:97, :], AF.Sqrt), reads=[p2b], writes=[bkk])
                            k.op("dve", lambda e: e.tensor_scalar(QT[96:97, q0:q0 + TT], kk[96:97, :], kmx[96:97, 0:1], None, ALU.mult), reads=[bkk, bkmx], writes=[bQ])
                        for qi in range(NLAT // TT):
                            q0 = qi * TT
                            pnum, pnb = ps[qi % 2], psb[qi % 2]
                            sbank = {}

                            def emit_S(blk):
                                nonlocal pn
                                p, pb = next_ps()
                                k.op("pe", lambda e, p=p, blk=blk: e.matmul(p[:, :], KT[0:97, blk * 128:(blk + 1) * 128], QT[0:97, q0:q0 + TT], start=True, stop=True),
                                     reads=[bKn, bKr, bQ], writes=[pb])
                                i3 = pn % 3
                                pn += 1
                                k.op("act", lambda e, p=p, i3=i3: e.activation(Pt[i3][:], p[:, :], AF.Exp, scale=SCALE), reads=[pb], writes=[bP[i3]])
                                sbank[blk] = i3
                            emit_S(0)
                            emit_S(1)
                            for blk in range(NCH):
                                i3 = sbank.pop(blk)
                                k.op("pe", lambda e, blk=blk, i3=i3, pnum=pnum: e.matmul(pnum[:, :], Vt[:, blk, :], Pt[i3][:], start=(blk == 0), stop=(blk == NCH - 1)),
                                     reads=[bV, bP[i3]], writes=[pnb])
                                if blk + 2 < NCH:
                                    emit_S(blk + 2)
                            k.op("act", lambda e, pnum=pnum: e.copy(xs_[:], pnum[:, :]), reads=[pnb], writes=[bxs_])
                            psw, pswb = ps[2 + qi % 2], psb[2 + qi % 2]
                            k.op("pe", lambda e, psw=psw: e.matmul(psw[:, :], swp[:], xs_[:], start=True, stop=True), reads=[bswp, bxs_], writes=[pswb])
                            k.op("dve", lambda e, psw=psw, off=off: e.reciprocal(rd[off:off + 64, :], psw[off:off + 64, :]), reads=[pswb], writes=[brd])
                            o_, b_o = ot[qi % 2], bot[qi % 2]
                            k.op("dve", lambda e, off=off, o_=o_: e.tensor_tensor(o_[off:off + 64, :], xs_[off:off + 64, :], rd[off:off + 64, :], ALU.mult),
                                 reads=[bxs_, brd], writes=[b_o])
                            k.dma("sp", scrM[h * 64:(h + 1) * 64, NCTX + q0:NCTX + q0 + TT], o_[off:off + 64, :], reads=[b_o], writes=[Buf()])
                    pspool[0] = list(range(8))
                k.barrier()
            k.barrier()
            outproj_phase(mla_out_w[0], 8, scrM, ctx_out)

        with ExitStack() as st:
            cp = [sb("cp%d" % i, [128, 8, 512], F32, st) for i in range(2)]; bcp = [Buf(), Buf()]
            for ti, (t0, n, vsel) in enumerate(seg_tiles(512)):
                k.dma("sp", cp[ti % 2][:, :, :n], xT_in[:, t0:t0 + n].rearrange("(c p) t -> p c t", p=128), writes=[bcp[ti % 2]])
                k.dma("pool", xres[:, t0:t0 + n].rearrange("(c p) t -> p c t", p=128), cp[ti % 2][:, :, :n], reads=[bcp[ti % 2]], writes=[Buf()])
        k.barrier()
        for li in layers:
            ctx_out = li < 3
            compute_mod(li)
            if debug == "mod_only":
                continue
            if debug == "ffn_only":
                ffn_phase(li, ctx_out)
                continue
            if li == 0:
                ssd_layer(li, ctx_out)
                if debug in ("ssdA", "ssd1", "ssd2"):
                    continue
            if li == 1:
                lru_layer(li, ctx_out)
            if li == 3:
                mla_layer(li, ctx_out)
            if li == 2:
                s5_layer(li, ctx_out)
                if debug == "s5f" and debug_sub in ("pre", "pre2", "pre3", "p_a"):
                    k.barrier()
                    return nc, list(di.keys())
                if debug == "s5f":
                    with ExitStack() as st:
                        cp = [sb("dq%d" % i, [128, 8, 512], F32, st) for i in range(2)]; bcp = [Buf(), Buf()]
                        for ti, (t0, n, vsel) in enumerate(seg_tiles(512)):
                            if vsel == 1:
                                continue
                            k.dma("sp", cp[ti % 2][:, :, :n], scrA[0:D, t0:t0 + n].rearrange("(c p) t -> p c t", p=128), writes=[bcp[ti % 2]])
                            k.dma("pool", outT[:, t0 - NCTX:t0 - NCTX + n].rearrange("(c p) t -> p c t", p=128), cp[ti % 2][:, :, :n], reads=[bcp[ti % 2]], writes=[Buf()])
                    k.barrier()
                    print("ninst", k.ninst)
                    return nc, list(di.keys())
            ffn_phase(li, ctx_out)
        if do_final:
            final_phase()
        else:
            with ExitStack() as st:
                cp = [sb("dp%d" % i, [128, 8, 512], F32, st) for i in range(2)]; bcp = [Buf(), Buf()]
                for ti, (t0, n, vsel) in enumerate(seg_tiles(512)):
                    if vsel == 1:
                        continue
                    k.dma("sp", cp[ti % 2][:, :, :n], xres[:, t0:t0 + n].rearrange("(c p) t -> p c t", p=128), writes=[bcp[ti % 2]])
                    k.dma("pool", outT[:, t0 - NCTX:t0 - NCTX + n].rearrange("(c p) t -> p c t", p=128), cp[ti % 2][:, :, :n], reads=[bcp[ti % 2]], writes=[Buf()])
            k.barrier()
        k.barrier()
        print("ninst", k.ninst)
    return nc, list(di.keys())


def make_in_map(inputs, b, names, x_override=None, ctx_override=None):
    xb = inputs["x"][b] if x_override is None else x_override
    cb = inputs["ctx"][b] if ctx_override is None else ctx_override
    xT = np.ascontiguousarray(np.concatenate([cb, xb], axis=0).T.astype(np.float32))
    cc = np.ascontiguousarray(np.stack([inputs["c"][b], inputs["c_ctx"]], axis=1).astype(np.float32))
    ii = np.arange(128)
    U = (ii[:, None] <= ii[None, :]).astype(np.float32)
    masks = np.stack([U, U.T.copy(), (ii[:, None] > ii[None, :]).astype(np.float32), (ii[:, None] < ii[None, :]).astype(np.float32)], axis=0)
    m = {"xT": xT, "cc": cc, "ident": np.eye(128, dtype=np.float32), "masks": np.ascontiguousarray(masks)}
    m["ramp"] = np.ascontiguousarray(np.tile(np.arange(1, 129, dtype=np.float32)[None, :], (128, 1)))
    sel = np.zeros((128, 8), np.float32)
    for gl8 in range(8):
        sel[gl8 * 16:(gl8 + 1) * 16, gl8] = 1.0
    m["sel"] = sel
    perm = np.concatenate([np.arange(8, 16), np.arange(0, 8), np.arange(24, 32), np.arange(16, 24)])
    m["mla_inw_sw"] = np.ascontiguousarray(np.asarray(inputs["mla_in_w"], np.float32)[0][:, 640:672][:, perm])
    qb = np.asarray(inputs["mla_qb_w"], np.float32)[0].reshape(384, 16, 96)
    m["mla_qbw_sw"] = np.ascontiguousarray(qb[:, :, 64:96][:, :, perm].reshape(384, 512))
    rows = NLAT // 64
    row = np.repeat(np.arange(rows, dtype=np.float32), 64)
    col = np.tile(np.arange(64, dtype=np.float32), rows)
    inv_freq = (np.float32(10000.0) ** (-np.arange(8, dtype=np.float32) / np.float32(8))).astype(np.float32)
    ang = np.stack([row[:, None] * inv_freq, col[:, None] * inv_freq], axis=1).astype(np.float32)
    cs, sn = np.cos(ang).astype(np.float32), np.sin(ang).astype(np.float32)
    C = np.zeros((32, NLAT), np.float32); S = np.zeros((32, NLAT), np.float32)
    for half in range(2):
        for part in range(2):
            r0 = half * 16 + part * 8
            C[r0:r0 + 8] = cs[:, half, :].T
            S[r0:r0 + 8] = (-sn[:, half, :].T) if part == 0 else sn[:, half, :].T
    m["ropeC"] = C; m["ropeS"] = S
    e96 = np.zeros((128, 128), np.float32); e96[0:96, 96] = 1.0
    m["e96"] = e96
    odm = np.zeros((2, 128, 128), np.float32); odm[0, :, 0:64] = 1.0; odm[1, :, 64:128] = 1.0
    m["odm"] = odm
    swp = np.zeros((128, 128), np.float32)
    swp[(np.arange(128) + 64) % 128, np.arange(128)] = 1.0
    m["swp"] = swp
    for n in names:
        if n not in m:
            m[n] = np.ascontiguousarray(np.asarray(inputs[n], dtype=np.float32))
    return m


def kernel(**inputs):
    inputs = {k_: np.asarray(v) for k_, v in inputs.items()}
    nc, names = build()
    nb = inputs["x"].shape[0]
    in_maps = [make_in_map(inputs, b, names) for b in range(nb)]
    res = run_bass_kernel_spmd(nc, in_maps, core_ids=list(range(nb)))
    out = np.stack([np.ascontiguousarray(r["outT"].T) for r in res.results], axis=0)
    return out.astype(np.float32)
```

```python
import numpy as np
from contextlib import ExitStack
import concourse.bass as bass
import concourse.mybir as mybir
from concourse.bass_utils import run_bass_kernel_spmd

F32 = mybir.dt.float32
BF16 = mybir.dt.bfloat16
I32 = mybir.dt.int32
AF = mybir.ActivationFunctionType
ALU = mybir.AluOpType
AX = mybir.AxisListType

D = 1024
NCTX = 256
NLAT = 8192
NTOK = NCTX + NLAT
FH = 2816
EPS = 1e-6


class Buf:
    __slots__ = ("name", "lw", "rd")

    def __init__(self, name=""):
        self.name = name
        self.lw = None
        self.rd = []


ATTACH_WAITS = True


class K:
    NDMA = 8

    def __init__(self, nc, es):
        self.nc = nc
        self.es = es
        self.eng = {"pe": nc.tensor, "dve": nc.vector, "act": nc.scalar, "pool": nc.gpsimd, "sp": nc.sync}
        self.sem = {}
        self.cnt = {}
        for e in self.eng:
            self.sem[e] = es.enter_context(nc.semaphore("s_" + e))
            self.cnt[e] = 0
        self.dslots = {}
        for q in ("sp", "pool", "act"):
            self.dslots[q] = []
            for j in range(self.NDMA):
                key = "d_%s_%d" % (q, j)
                self.sem[key] = es.enter_context(nc.semaphore(key))
                self.cnt[key] = 0
                self.dslots[q].append(key)
        self.dnext = {q: 0 for q in self.dslots}
        self.seen = {e: {} for e in self.eng}
        self.ninst = 0

    def _need(self, e, deps, attach=False):
        best = {}
        for (k, v) in deps:
            if k == e and e == "pe":
                continue
            if best.get(k, 0) < v:
                best[k] = v
        todo = [(k, v) for k, v in best.items() if self.seen[e].get(k, 0) < v]
        held = None
        if attach and ATTACH_WAITS and todo:
            held = todo.pop()
        for k, v in todo:
            self.eng[e].wait_ge(self.sem[k], v)
            self.seen[e][k] = v
            self.ninst += 1
        if held is not None:
            self.seen[e][held[0]] = held[1]
        return held

    def _deps(self, reads, writes):
        deps = []
        for b in reads:
            if b.lw is not None:
                deps.append(b.lw)
        for b in writes:
            if b.lw is not None:
                deps.append(b.lw)
            deps.extend(b.rd)
        return deps

    def op(self, e, fn, reads=(), writes=(), inc=True):
        held = self._need(e, self._deps(reads, writes), attach=True)
        ins = fn(self.eng[e])
        if held is not None:
            ins._wait_ge(self.sem[held[0]], held[1])
        self.ninst += 1
        if inc:
            self.cnt[e] += 1
            ins.then_inc(self.sem[e], 1)
            v = self.cnt[e]
        else:
            v = self.cnt[e] + 1
        for b in reads:
            b.rd.append((e, v))
            if len(b.rd) > 24:
                b.rd = self._compact(b.rd)
        for b in writes:
            b.lw = (e, v)
            b.rd = []
        return ins

    @staticmethod
    def _compact(rd):
        best = {}
        for (k, v) in rd:
            if best.get(k, 0) < v:
                best[k] = v
        return list(best.items())

    def dma(self, q, out, in_, reads=(), writes=(), **kw):
        if out.dtype != in_.dtype:
            q = "pool"
        slot = self.dslots[q][self.dnext[q] % self.NDMA]
        self.dnext[q] += 1
        deps = self._deps(reads, writes)
        if self.cnt[slot] > 0:
            deps.append((slot, self.cnt[slot]))
        self._need(q, deps)
        ins = self.eng[q].dma_start(out=out, in_=in_, **kw)
        self.ninst += 1
        self.cnt[slot] += 16
        ins.then_inc(self.sem[slot], 16)
        v = self.cnt[slot]
        for b in reads:
            b.rd.append((slot, v))
            if len(b.rd) > 24:
                b.rd = self._compact(b.rd)
        for b in writes:
            b.lw = (slot, v)
            b.rd = []
        return ins

    def barrier(self):
        deps = [(k, v) for k, v in self.cnt.items() if v > 0]
        for e in self.eng:
            self._need(e, deps)


def build(layers=(0, 1, 2, 3), do_final=True, debug=None, debug_sub=None):
    nc = bass.Bass("TRN2", target_bir_lowering=False)
    di = {}

    def inp(name, shape):
        di[name] = nc.dram_tensor(name, list(shape), F32, kind="ExternalInput").ap()
        return di[name]

    xT_in = inp("xT", [D, NTOK])
    cc_in = inp("cc", [D, 2])
    ident_in = inp("ident", [128, 128])
    ada_w = inp("ada_w", [4, D, 6 * D]); ada_b = inp("ada_b", [4, 6 * D])
    norm1_w = inp("norm1_w", [4, D]); norm2_w = inp("norm2_w", [4, D])
    ffn_w13 = inp("ffn_w13", [4, D, 2 * FH]); ffn_w2 = inp("ffn_w2", [4, FH, D])
    lru_in_w = inp("lru_in_w", [1, D, 2560]); lru_conv_w = inp("lru_conv_w", [1, 4, 1280]); lru_conv_b = inp("lru_conv_b", [1, 1280])
    lru_gate_w = inp("lru_gate_w", [1, 2, 10, 128, 256]); lru_gate_b = inp("lru_gate_b", [1, 2, 10, 256])
    lru_a_param = inp("lru_a_param", [1, 2, 1280]); lru_out_w = inp("lru_out_w", [1, 1280, D])
    masks_in = inp("masks", [4, 128, 128])
    ramp_in = inp("ramp", [128, 128]); sel_in = inp("sel", [128, 8])
    s5_lre = inp("s5_lambda_re", [1, 2, 64, 64]); s5_lim = inp("s5_lambda_im", [1, 2, 64, 64]); s5_ls = inp("s5_log_step", [1, 2, 64])
    s5_bre = inp("s5_b_re", [1, 64, 64, 16]); s5_bim = inp("s5_b_im", [1, 64, 64, 16])
    s5_cre = inp("s5_c_re", [1, 2, 64, 16, 64]); s5_cim = inp("s5_c_im", [1, 2, 64, 16, 64])
    s5_d = inp("s5_d", [1, D]); s5_glu_w = inp("s5_glu_w", [1, D, 2 * D]); s5_glu_b = inp("s5_glu_b", [1, 2 * D])
    m2_in_w = inp("m2_in_w", [1, D, 5184]); m2_conv_w = inp("m2_conv_w", [1, 4, 3072]); m2_conv_b = inp("m2_conv_b", [1, 3072])
    m2_dt_bias = inp("m2_dt_bias", [1, 2, 32]); m2_a_log = inp("m2_a_log", [1, 2, 32]); m2_d = inp("m2_d", [1, 32])
    m2_norm_w = inp("m2_norm_w", [1, 2048]); m2_out_w = inp("m2_out_w", [1, 2048, D])
    mla_in_w = inp("mla_in_w", [1, D, 672]); mla_q_norm_w = inp("mla_q_norm_w", [1, 384]); mla_kv_norm_w = inp("mla_kv_norm_w", [1, 256])
    mla_qb_w = inp("mla_qb_w", [1, 384, 1536]); mla_kvb_w = inp("mla_kvb_w", [1, 256, 2048]); mla_out_w = inp("mla_out_w", [1, D, D])
    mla_inw_sw = inp("mla_inw_sw", [D, 32]); mla_qbw_sw = inp("mla_qbw_sw", [384, 512])
    ropeC_in = inp("ropeC", [32, NLAT]); ropeS_in = inp("ropeS", [32, NLAT])
    e96_in = inp("e96", [128, 128]); odm_in = inp("odm", [2, 128, 128]); swp_in = inp("swp", [128, 128])
    final_norm_w = inp("final_norm_w", [D])
    outT = nc.dram_tensor("outT", [D, NLAT], F32, kind="ExternalOutput").ap()
    xres = nc.dram_tensor("xres", [D, NTOK], F32).ap()
    scrA = nc.dram_tensor("scrA", [2048, NTOK], F32).ap()
    scrB = nc.dram_tensor("scrB", [3072, NTOK], F32).ap()
    scrM = nc.dram_tensor("scrM", [2048, NTOK], BF16).ap()
    NCH = NTOK // 128
    scrZ = nc.dram_tensor("scrZ", [NTOK, 2048], F32).ap()
    scrDT = nc.dram_tensor("scrDT", [NTOK, 64], F32).ap()
    scrS = nc.dram_tensor("scrS", [2, NCH, 128, 2048], BF16).ap()
    scrXB = nc.dram_tensor("scrXB", [3072, NTOK], BF16).ap()
    scrH = nc.dram_tensor("scrH", [2, NCH, 128, 2048], BF16).ap()
    scrDec = nc.dram_tensor("scrDec", [NCH, 128, 64], F32).ap()
    scrY = nc.dram_tensor("scrY", [NCH, 128, 2048], F32).ap()
    scrC = nc.dram_tensor("scrC", [NCH, 128, 512], BF16).ap()
    scrE = nc.dram_tensor("scrE", [NCH, 128, 64], F32).ap()

    es = ExitStack()
    with es:
        k = K(nc, es)
        uniq = [0]

        def sb(name, shape, dt, st=es):
            uniq[0] += 1
            return st.enter_context(nc.sbuf_tensor("%s_%d" % (name, uniq[0]), shape, dt))

        ps = [es.enter_context(nc.psum_tensor("ps%d" % i, [128, 512], F32)) for i in range(8)]
        psb = [Buf("ps%d" % i) for i in range(8)]
        psn = [0]

        pspool = [list(range(8))]

        def next_ps():
            pl = pspool[0]
            i = pl[psn[0] % len(pl)]
            psn[0] += 1
            return ps[i], psb[i]

        ones_f = sb("ones_f", [128, 128], F32); b_ones = Buf()
        k.op("pool", lambda e: e.memset(ones_f[:], 1.0), writes=[b_ones])
        ident_f = sb("ident_f", [128, 128], F32); b_ident = Buf()
        k.dma("sp", ident_f[:], ident_in, writes=[b_ident])
        ident_b = sb("ident_b", [128, 128], BF16); b_identb = Buf()
        k.dma("sp", ident_b[:], ident_in, writes=[b_identb])
        ccT = sb("ccT", [128, 8, 2], F32); b_cc = Buf()
        k.dma("sp", ccT[:], cc_in.rearrange("(c p) v -> p c v", p=128), writes=[b_cc])
        scT = sb("scT", [128, 8, 2], BF16); b_sc = Buf()
        k.op("act", lambda e: e.activation(scT[:], ccT[:], AF.Silu), reads=[b_cc], writes=[b_sc])
        mod = sb("mod", [128, 48, 2], F32); b_mod = Buf()
        adab = sb("adab", [128, 48], F32); b_adab = Buf()
        nw1 = sb("nw1", [128, 8], F32); nw2 = sb("nw2", [128, 8], F32); b_nw = Buf()
        A1 = sb("A1", [128, 8, 2], F32); A2 = sb("A2", [128, 8, 2], F32); b_A = Buf()

        def seg_tiles(tt):
            res = []
            t = 0
            while t < NCTX:
                n = min(tt, NCTX - t)
                res.append((t, n, 1))
                t += n
            while t < NTOK:
                n = min(tt, NTOK - t)
                res.append((t, n, 0))
                t += n
            return res

        def load_w(st, name, w_ap, kc, ncols, q="sp"):
            t = sb(name, [128, kc, ncols], BF16, st)
            b = Buf(name)
            src = w_ap.rearrange("(c p) n -> p c n", p=128)
            for c in range(kc):
                k.dma(q if c % 2 == 0 else "pool", t[:, c, :], src[:, c, :], writes=[b])
            return t, b

        def compute_mod(li):
            with ExitStack() as st:
                k.dma("sp", adab[:], ada_b[li].rearrange("(j p) -> p j", p=128), writes=[b_adab], allow_slow_non_contiguous=True)
                k.dma("sp", nw1[:], norm1_w[li].rearrange("(c p) -> p c", p=128), writes=[b_nw], allow_slow_non_contiguous=True)
                k.dma("sp", nw2[:], norm2_w[li].rearrange("(c p) -> p c", p=128), writes=[b_nw], allow_slow_non_contiguous=True)
                wt = [sb("adaw%d" % i, [128, 8, 1024], BF16, st) for i in range(2)]
                wb = [Buf(), Buf()]
                for j in range(6):
                    w, b = wt[j % 2], wb[j % 2]
                    src = ada_w[li][:, j * 1024:(j + 1) * 1024].rearrange("(c p) n -> p c n", p=128)
                    for c in range(8):
                        k.dma("sp" if c % 2 == 0 else "pool", w[:, c, :], src[:, c, :], writes=[b])
                    for m in range(8):
                        p, pb = next_ps()
                        for c in range(8):
                            k.op("pe", lambda e, p=p, w=w, c=c, m=m: e.matmul(p[:, 0:2], w[:, c, m * 128:(m + 1) * 128], scT[:, c, :],
                                                                          start=(c == 0), stop=(c == 7)),
                                 reads=[b, b_sc], writes=[pb], inc=(c == 7))
                        jj = j * 8 + m
                        k.op("act", lambda e, p=p, jj=jj: e.activation(mod[:, jj, :], p[:, 0:2], AF.Identity, bias=adab[:, jj:jj + 1], scale=1.0),
                             reads=[pb, b_adab], writes=[b_mod])
                for v in range(2):
                    k.op("dve", lambda e, v=v: e.scalar_tensor_tensor(A1[:, :, v], mod[:, 8:16, v], 1.0, nw1[:], ALU.add, ALU.mult),
                         reads=[b_mod, b_nw], writes=[b_A])
                    k.op("dve", lambda e, v=v: e.scalar_tensor_tensor(A2[:, :, v], mod[:, 32:40, v], 1.0, nw2[:], ALU.add, ALU.mult),
                         reads=[b_mod, b_nw], writes=[b_A])
            k.barrier()

        def norm_mod(st_tiles, xt, bx, n, vsel, A, shoff, hT, bh):
            sq, bsq, rstd, brs = st_tiles["sq"], st_tiles["bsq"], st_tiles["rstd"], st_tiles["brs"]
            p, pb = next_ps()
            for c in range(8):
                k.op("act", lambda e, c=c: e.activation(sq[:, :n], xt[:, c, :n], AF.Square), reads=[bx], writes=[bsq])
                k.op("pe", lambda e, c=c: e.matmul(p[:, :n], ones_f[:], sq[:, :n], start=(c == 0), stop=(c == 7)),
                     reads=[b_ones, bsq], writes=[pb])
            k.op("act", lambda e: e.activation(rstd[:, :n], p[:, :n], AF.Sqrt, bias=EPS, scale=1.0 / D), reads=[pb], writes=[brs])
            k.op("dve", lambda e: e.reciprocal(rstd[:, :n], rstd[:, :n]), reads=[brs], writes=[brs])
            for c in range(8):
                k.op("dve", lambda e, c=c: e.tensor_tensor(sq[:, :n], xt[:, c, :n], rstd[:, :n], ALU.mult), reads=[bx, brs], writes=[bsq])
                k.op("act", lambda e, c=c: e.activation(hT[:, c, :n], sq[:, :n], AF.Identity, bias=mod[:, shoff + c, vsel:vsel + 1],
                                                        scale=A[:, c, vsel:vsel + 1]),
                     reads=[bsq, b_mod, b_A], writes=[bh])

        def norm_scratch(st, tt):
            return {"sq": sb("n_sq", [128, tt], F32, st), "bsq": Buf(), "rstd": sb("n_rstd", [128, tt], F32, st), "brs": Buf()}

        def outproj_phase(w_ap, kc, src_scr, ctx_out):
            TT = 512
            with ExitStack() as st:
                w, bw = load_w(st, "opw", w_ap, kc, D)
                mt = [sb("op_m%d" % i, [128, kc, TT], BF16, st) for i in range(2)]; bm = [Buf(), Buf()]
                xt = [sb("op_x%d" % i, [128, 8, TT], F32, st) for i in range(2)]; bx = [Buf(), Buf()]
                for ti, (t0, n, vsel) in enumerate(seg_tiles(TT)):
                    if vsel == 1 and not ctx_out:
                        continue
                    m, b_m, x, b_x = mt[ti % 2], bm[ti % 2], xt[ti % 2], bx[ti % 2]
                    k.dma("sp", m[:, :, :n], src_scr[0:kc * 128, t0:t0 + n].rearrange("(c p) t -> p c t", p=128), writes=[b_m])
                    k.dma("pool", x[:, :, :n], xres[:, t0:t0 + n].rearrange("(c p) t -> p c t", p=128), writes=[b_x])
                    for c in range(8):
                        p, pb = next_ps()
                        for j in range(kc):
                            k.op("pe", lambda e, p=p, j=j, c=c, m=m: e.matmul(p[:, :n], w[:, j, c * 128:(c + 1) * 128], m[:, j, :n],
                                                                          start=(j == 0), stop=(j == kc - 1)),
                                 reads=[bw, b_m], writes=[pb], inc=(j == kc - 1))
                        k.op("dve", lambda e, p=p, c=c, x=x: e.scalar_tensor_tensor(x[:, c, :n], p[:, :n], mod[:, 16 + c, vsel:vsel + 1], x[:, c, :n],
                                                                                 ALU.mult, ALU.add),
                             reads=[pb, b_mod, b_x], writes=[b_x])
                    k.dma("pool", xres[:, t0:t0 + n].rearrange("(c p) t -> p c t", p=128), x[:, :, :n], reads=[b_x], writes=[Buf()])
            k.barrier()

        def resid_phase(src_scr, ctx_out):
            TT = 512
            with ExitStack() as st:
                yt = [sb("rp_y%d" % i, [128, 8, TT], F32, st) for i in range(2)]; by = [Buf(), Buf()]
                xt = [sb("rp_x%d" % i, [128, 8, TT], F32, st) for i in range(2)]; bx = [Buf(), Buf()]
                for ti, (t0, n, vsel) in enumerate(seg_tiles(TT)):
                    if vsel == 1 and not ctx_out:
                        continue
                    y, b_y, x, b_x = yt[ti % 2], by[ti % 2], xt[ti % 2], bx[ti % 2]
                    k.dma("sp", y[:, :, :n], src_scr[0:D, t0:t0 + n].rearrange("(c p) t -> p c t", p=128), writes=[b_y])
                    k.dma("pool", x[:, :, :n], xres[:, t0:t0 + n].rearrange("(c p) t -> p c t", p=128), writes=[b_x])
                    for c in range(8):
                        k.op("dve", lambda e, c=c, x=x, y=y: e.scalar_tensor_tensor(x[:, c, :n], y[:, c, :n], mod[:, 16 + c, vsel:vsel + 1], x[:, c, :n],
                                                                                 ALU.mult, ALU.add),
                             reads=[b_y, b_mod, b_x], writes=[b_x])
                    k.dma("pool", xres[:, t0:t0 + n].rearrange("(c p) t -> p c t", p=128), x[:, :, :n], reads=[b_x], writes=[Buf()])
            k.barrier()

        def ffn_phase(li, ctx_out):
            TT = 256
            NJ = FH // 128
            with ExitStack() as st:
                w13, b13 = load_w(st, "w13", ffn_w13[li], 8, 2 * FH)
                w2, b2 = load_w(st, "w2", ffn_w2[li], NJ, D)
                nscs = [norm_scratch(st, TT) for _ in range(2)]
                xt = [sb("f_x%d" % i, [128, 8, TT], F32, st) for i in range(2)]; bx = [Buf(), Buf()]
                hTs = [sb("f_h%d" % i, [128, 8, TT], BF16, st) for i in range(2)]; bhs = [Buf(), Buf()]
                sT = sb("f_s", [128, NJ, TT], BF16, st); bs = Buf()
                sa = [sb("f_sa%d" % i, [128, TT], F32, st) for i in range(2)]; bsa = [Buf(), Buf()]
                tl = [t for t in seg_tiles(TT) if not (t[2] == 1 and not ctx_out)]

                def prep(i):
                    t0, n, vsel = tl[i]
                    k.dma("sp", xt[i % 2][:, :, :n], xres[:, t0:t0 + n].rearrange("(c p) t -> p c t", p=128), writes=[bx[i % 2]])
                    norm_mod(nscs[i % 2], xt[i % 2], bx[i % 2], n, vsel, A2, 24, hTs[i % 2], bhs[i % 2])

                prep(0)
                for i, (t0, n, vsel) in enumerate(tl):
                    x, b_x = xt[i % 2], bx[i % 2]
                    hT, bh = hTs[i % 2], bhs[i % 2]
                    for j in range(NJ):
                        pa, pab = next_ps()
                        pg, pgb = next_ps()
                        for c in range(8):
                            k.op("pe", lambda e, pa=pa, c=c, j=j: e.matmul(pa[:, :n], w13[:, c, j * 128:(j + 1) * 128], hT[:, c, :n],
                                                                       start=(c == 0), stop=(c == 7)),
                                 reads=[b13, bh], writes=[pab], inc=(c == 7))
                        for c in range(8):
                            k.op("pe", lambda e, pg=pg, c=c, j=j: e.matmul(pg[:, :n], w13[:, c, FH + j * 128:FH + (j + 1) * 128], hT[:, c, :n],
                                                                       start=(c == 0), stop=(c == 7)),
                                 reads=[b13, bh], writes=[pgb], inc=(c == 7))
                        s_a, b_sa = sa[j % 2], bsa[j % 2]
                        k.op("act", lambda e, pa=pa, s_a=s_a: e.activation(s_a[:, :n], pa[:, :n], AF.Silu), reads=[pab], writes=[b_sa])
                        k.op("dve", lambda e, pg=pg, s_a=s_a, j=j: e.tensor_tensor(sT[:, j, :n], s_a[:, :n], pg[:, :n], ALU.mult),
                             reads=[b_sa, pgb], writes=[bs])
                    if i + 1 < len(tl):
                        prep(i + 1)
                    for c in range(8):
                        p, pb = next_ps()
                        for j in range(NJ):
                            k.op("pe", lambda e, p=p, c=c, j=j: e.matmul(p[:, :n], w2[:, j, c * 128:(c + 1) * 128], sT[:, j, :n],
                                                                     start=(j == 0), stop=(j == NJ - 1)),
                                 reads=[b2, bs], writes=[pb], inc=(j == NJ - 1))
                        k.op("dve", lambda e, p=p, c=c, x=x: e.scalar_tensor_tensor(x[:, c, :n], p[:, :n], mod[:, 40 + c, vsel:vsel + 1], x[:, c, :n],
                                                                                 ALU.mult, ALU.add),
                             reads=[pb, b_mod, b_x], writes=[b_x])
                    k.dma("sp", xres[:, t0:t0 + n].rearrange("(c p) t -> p c t", p=128), x[:, :, :n], reads=[b_x], writes=[Buf()])
            k.barrier()

        def final_phase():
            TT = 512
            with ExitStack() as st:
                fw = sb("fnw", [128, 8], F32, st); bfw = Buf()
                k.dma("sp", fw[:], final_norm_w.rearrange("(c p) -> p c", p=128), writes=[bfw], allow_slow_non_contiguous=True)
                nsc = norm_scratch(st, TT)
                xt = [sb("fn_x%d" % i, [128, 8, TT], F32, st) for i in range(2)]; bx = [Buf(), Buf()]
                for ti, (t0, n, vsel) in enumerate(seg_tiles(TT)):
                    if vsel == 1:
                        continue
                    x, b_x = xt[ti % 2], bx[ti % 2]
                    k.dma("sp", x[:, :, :n], xres[:, t0:t0 + n].rearrange("(c p) t -> p c t", p=128), writes=[b_x])
                    sq, bsq, rstd, brs = nsc["sq"], nsc["bsq"], nsc["rstd"], nsc["brs"]
                    p, pb = next_ps()
                    for c in range(8):
                        k.op("act", lambda e, c=c, x=x: e.activation(sq[:, :n], x[:, c, :n], AF.Square), reads=[b_x], writes=[bsq])
                        k.op("pe", lambda e, c=c, p=p: e.matmul(p[:, :n], ones_f[:], sq[:, :n], start=(c == 0), stop=(c == 7)),
                             reads=[b_ones, bsq], writes=[pb])
                    k.op("act", lambda e, p=p: e.activation(rstd[:, :n], p[:, :n], AF.Sqrt, bias=EPS, scale=1.0 / D), reads=[pb], writes=[brs])
                    k.op("dve", lambda e: e.reciprocal(rstd[:, :n], rstd[:, :n]), reads=[brs], writes=[brs])
                    for c in range(8):
                        k.op("dve", lambda e, c=c, x=x: e.scalar_tensor_tensor(x[:, c, :n], x[:, c, :n], fw[:, c:c + 1], rstd[:, :n], ALU.mult, ALU.mult),
                             reads=[b_x, brs, bfw], writes=[b_x])
                    k.dma("pool", outT[:, t0 - NCTX:t0 - NCTX + n].rearrange("(c p) t -> p c t", p=128), x[:, :, :n], reads=[b_x], writes=[Buf()])
            k.barrier()

        def lru_layer(li, ctx_out):
            TT = 512
            gyT = scrA
            xrT = scrB
            hfT = scrB[1280:2560]
            with ExitStack() as st:
                w, bw = load_w(st, "lin", lru_in_w[0], 8, 2560)
                nsc = norm_scratch(st, TT)
                xt = [sb("la_x%d" % i, [128, 8, TT], F32, st) for i in range(2)]; bx = [Buf(), Buf()]
                hT = sb("la_h", [128, 8, TT], BF16, st); bh = Buf()
                og = [sb("la_o%d" % i, [128, TT], F32, st) for i in range(4)]; bog = [Buf() for _ in range(4)]
                on = 0
                for ti, (t0, n, vsel) in enumerate(seg_tiles(TT)):
                    x, b_x = xt[ti % 2], bx[ti % 2]
                    k.dma("sp", x[:, :, :n], xres[:, t0:t0 + n].rearrange("(c p) t -> p c t", p=128), writes=[b_x])
                    norm_mod(nsc, x, b_x, n, vsel, A1, 0, hT, bh)
                    for m in range(20):
                        p, pb = next_ps()
                        for c in range(8):
                            k.op("pe", lambda e, p=p, c=c, m=m: e.matmul(p[:, :n], w[:, c, m * 128:(m + 1) * 128], hT[:, c, :n],
                                                                     start=(c == 0), stop=(c == 7)),
                                 reads=[bw, bh], writes=[pb], inc=(c == 7))
                        o, bo = og[on % 4], bog[on % 4]
                        on += 1
                        if m < 10:
                            k.op("act", lambda e, p=p, o=o: e.activation(o[:, :n], p[:, :n], AF.Gelu_apprx_tanh), reads=[pb], writes=[bo])
                            k.dma("sp", gyT[m * 128:(m + 1) * 128, t0:t0 + n], o[:, :n], reads=[bo], writes=[Buf()])
                        else:
                            k.op("dve", lambda e, p=p, o=o: e.tensor_copy(o[:, :n], p[:, :n]), reads=[pb], writes=[bo])
                            k.dma("pool", xrT[(m - 10) * 128:(m - 9) * 128, t0:t0 + n], o[:, :n], reads=[bo], writes=[Buf()])
            k.barrier()
            with ExitStack() as st:
                gw = sb("lgw", [128, 2, 10, 256], BF16, st); bgw = Buf()
                for d in range(2):
                    for blk in range(10):
                        k.dma("sp" if blk % 2 else "pool", gw[:, d, blk, :], lru_gate_w[0, d, blk], writes=[bgw])
                gb = sb("lgb", [128, 2, 10, 2], F32, st); bgb = Buf()
                k.dma("sp", gb[:], lru_gate_b[0].rearrange("d b (h p) -> p d b h", p=128), writes=[bgb], allow_slow_non_contiguous=True)
                cw = sb("lcw", [128, 4, 10], F32, st); cb = sb("lcb", [128, 10], F32, st); bcw = Buf()
                k.dma("sp", cw[:], lru_conv_w[0].rearrange("k (b p) -> p k b", p=128), writes=[bcw], allow_slow_non_contiguous=True)
                k.dma("sp", cb[:], lru_conv_b[0].rearrange("(b p) -> p b", p=128), writes=[bcw], allow_slow_non_contiguous=True)
                lb = sb("llb", [128, 2, 10], F32, st); lb2 = sb("llb2", [128, 2, 10], F32, st); blb = Buf()
                k.dma("sp", lb[:], lru_a_param[0].rearrange("d (b p) -> p d b", p=128), writes=[blb], allow_slow_non_contiguous=True)
                k.op("act", lambda e: e.activation(lb[:], lb[:], AF.Exp, scale=-1.0), reads=[blb], writes=[blb])
                k.op("act", lambda e: e.activation(lb[:], lb[:], AF.Ln, bias=1.0, scale=1.0), reads=[blb], writes=[blb])
                k.op("dve", lambda e: e.tensor_scalar(lb[:], lb[:], -8.0, None, ALU.mult), reads=[blb], writes=[blb])
                k.op("dve", lambda e: e.tensor_scalar(lb2[:], lb[:], 2.0, None, ALU.mult), reads=[blb], writes=[blb])
                hgb = sb("lhgb", [128, 2, 10, 2], F32, st); hlb = sb("lhlb", [128, 2, 10], F32, st); hlb2 = sb("lhlb2", [128, 2, 10], F32, st); bhl = Buf()
                k.op("dve", lambda e: e.tensor_scalar(hgb[:], gb[:], 0.5, None, ALU.mult), reads=[bgb], writes=[bhl])
                k.op("dve", lambda e: e.tensor_scalar(hlb[:], lb[:], 0.5, None, ALU.mult), reads=[blb], writes=[bhl])
                k.op("dve", lambda e: e.tensor_scalar(hlb2[:], lb2[:], 0.5, None, ALU.mult), reads=[blb], writes=[bhl])
                HB = TT + 3
                xr = [sb("lb_xr%d" % i, [128, HB], F32, st) for i in range(8)]; bxr = [Buf() for _ in range(8)]
                xc = [sb("lb_xc%d" % i, [128, TT], F32, st) for i in range(8)]; bxc = [Buf() for _ in range(8)]
                xcb = [sb("lb_xcb%d" % i, [128, TT], BF16, st) for i in range(8)]; bxcb = [Buf() for _ in range(8)]
                ta = [sb("lb_a%d" % i, [128, TT], F32, st) for i in range(8)]; bta = [Buf() for _ in range(8)]
                tm = [sb("lb_m%d" % i, [128, TT], F32, st) for i in range(8)]; btm = [Buf() for _ in range(8)]
                tu = [sb("lb_u%d" % i, [128, TT], F32, st) for i in range(8)]; btu = [Buf() for _ in range(8)]
                th = [sb("lb_h%d" % i, [128, TT], F32, st) for i in range(8)]; bth = [Buf() for _ in range(8)]
                stc = sb("lb_stc", [128, 2, 10], F32, st); bstc = [[Buf() for _ in range(10)] for _ in range(2)]
                k.op("pool", lambda e: e.memset(stc[:], 0.0), writes=[b for r in bstc for b in r])
                tg = [sb("lb_g%d" % i, [128, TT], F32, st) for i in range(8)]; btg = [Buf() for _ in range(8)]
                to = [sb("lb_o%d" % i, [128, TT], BF16, st) for i in range(8)]; bto = [Buf() for _ in range(8)]
                zero = sb("lb_z", [128, 1], F32, st); bz = Buf()
                k.op("pool", lambda e: e.memset(zero[:], 0.0), writes=[bz])
                tiles = seg_tiles(TT)
                segs = {1: (0, NCTX), 0: (NCTX, NTOK)}
                hfbufs = {}
                NBL = 8
                lunits = []

                def make_lunit(d, t0, n, vsel, blk, un):
                    s0, s1 = segs[vsel]
                    i2 = un % NBL
                    x_r, b_xr = xr[i2], bxr[i2]
                    x_c, b_xc = xc[i2], bxc[i2]
                    x_cb, b_xcb = xcb[i2], bxcb[i2]
                    a_, b_a = ta[i2], bta[i2]
                    m_, b_m = tm[i2], btm[i2]
                    u_, b_u = tu[i2], btu[i2]
                    h_, b_h = th[i2], bth[i2]
                    g_, b_g = tg[i2], btg[i2]
                    o_, b_o = to[i2], bto[i2]
                    pr, prb = ps[(un % 4) * 2], psb[(un % 4) * 2]
                    pi, pib = ps[(un % 4) * 2 + 1], psb[(un % 4) * 2 + 1]
                    lo = max(t0 - 2, s0); hi = min(t0 + n + 1, s1)
                    skip_out = (d == 1 and vsel == 1 and not ctx_out)

                    def l0():
                        if lo > t0 - 2 or hi < t0 + n + 1:
                            k.op("pool", lambda e: e.memset(x_r[:], 0.0), writes=[b_xr])
                        k.dma("sp", x_r[:, lo - (t0 - 2):hi - (t0 - 2)], xrT[blk * 128:(blk + 1) * 128, lo:hi], writes=[b_xr])
                        if d == 1 and not skip_out:
                            k.dma("sp", g_[:, :n], gyT[blk * 128:(blk + 1) * 128, t0:t0 + n], writes=[b_g])

                    def l1():
                        k.op("act", lambda e: e.activation(x_c[:, :n], x_r[:, 0:n], AF.Identity, bias=cb[:, blk:blk + 1], scale=cw[:, 0, blk:blk + 1]),
                             reads=[b_xr, bcw], writes=[b_xc])

                    def l2():
                        for kk in range(1, 4):
                            k.op("dve", lambda e, kk=kk: e.scalar_tensor_tensor(x_c[:, :n], x_r[:, kk:kk + n], cw[:, kk, blk:blk + 1], x_c[:, :n], ALU.mult, ALU.add),
                                 reads=[b_xr, bcw, b_xc], writes=[b_xc])
                        k.op("pool", lambda e: e.tensor_copy(x_cb[:, :n], x_c[:, :n]), reads=[b_xc], writes=[b_xcb])

                    def l3():
                        k.op("pe", lambda e: e.matmul(pr[:, :n], gw[:, d, blk, 0:128], x_cb[:, :n], start=True, stop=True), reads=[bgw, b_xcb], writes=[prb])
                        k.op("pe", lambda e: e.matmul(pi[:, :n], gw[:, d, blk, 128:256], x_cb[:, :n], start=True, stop=True), reads=[bgw, b_xcb], writes=[pib])

                    def l4():
                        k.op("act", lambda e: e.activation(a_[:, :n], pr[:, :n], AF.Tanh, bias=hgb[:, d, blk, 0:1], scale=0.5), reads=[prb, bhl], writes=[b_a])
                        k.op("act", lambda e: e.activation(u_[:, :n], pi[:, :n], AF.Tanh, bias=hgb[:, d, blk, 1:2], scale=0.5), reads=[pib, bhl], writes=[b_u])
                        k.op("act", lambda e: e.activation(m_[:, :n], a_[:, :n], AF.Exp, bias=hlb2[:, d, blk:blk + 1], scale=hlb2[:, d, blk:blk + 1]), reads=[b_a, bhl], writes=[b_m])
                        k.op("act", lambda e: e.activation(a_[:, :n], a_[:, :n], AF.Exp, bias=hlb[:, d, blk:blk + 1], scale=hlb[:, d, blk:blk + 1]), reads=[b_a, bhl], writes=[b_a])
                        k.op("act", lambda e: e.activation(m_[:, :n], m_[:, :n], AF.Sqrt, bias=0.25, scale=-0.25), reads=[b_m], writes=[b_m])

                    def l5():
                        k.op("dve", lambda e: e.scalar_tensor_tensor(u_[:, :n], u_[:, :n], 1.0, x_c[:, :n], ALU.add, ALU.mult), reads=[b_u, b_xc], writes=[b_u])
                        k.op("pool", lambda e: e.tensor_tensor(u_[:, :n], u_[:, :n], m_[:, :n], ALU.mult), reads=[b_u, b_m], writes=[b_u])

                    def l6():
                        init = stc[:, d, blk:blk + 1]
                        if d == 0:
                            k.op("dve", lambda e: e.tensor_tensor_scan(h_[:, :n], a_[:, :n], u_[:, :n], init, ALU.mult, ALU.add),
                                 reads=[b_a, b_u, bstc[d][blk]], writes=[b_h])
                            k.op("act", lambda e: e.copy(stc[:, d, blk:blk + 1], h_[:, n - 1:n]), reads=[b_h], writes=[bstc[d][blk]])
                            hb = Buf()
                            hfbufs[(blk, t0)] = hb
                            k.dma("sp", hfT[blk * 128:(blk + 1) * 128, t0:t0 + n], h_[:, :n], reads=[b_h], writes=[hb])
                        else:
                            k.op("dve", lambda e: e.tensor_tensor_scan(h_[:, n - 1::-1], a_[:, n - 1::-1], u_[:, n - 1::-1], init, ALU.mult, ALU.add),
                                 reads=[b_a, b_u, bstc[d][blk]], writes=[b_h])
                            k.op("act", lambda e: e.copy(stc[:, d, blk:blk + 1], h_[:, 0:1]), reads=[b_h], writes=[bstc[d][blk]])
                            if not skip_out:
                                k.dma("sp", x_c[:, :n], hfT[blk * 128:(blk + 1) * 128, t0:t0 + n], reads=[hfbufs[(blk, t0)]], writes=[b_xc])

                    def l7():
                        if d == 0 or skip_out:
                            return
                        k.op("pool", lambda e: e.tensor_tensor(x_c[:, :n], x_c[:, :n], h_[:, :n], ALU.add), reads=[b_xc, b_h], writes=[b_xc])
                        k.op("dve", lambda e: e.tensor_tensor(o_[:, :n], x_c[:, :n], g_[:, :n], ALU.mult), reads=[b_xc, b_g], writes=[b_o])
                        k.dma("sp", scrM[blk * 128:(blk + 1) * 128, t0:t0 + n], o_[:, :n], reads=[b_o], writes=[Buf()])

                    return [l0, l1, l2, l3, l4, l5, l6, l7]

                for d in range(2):
                    order = tiles if d == 0 else ([tiles[0]] + tiles[:0:-1])
                    for (t0, n, vsel) in order:
                        for blk in range(10):
                            lunits.append(make_lunit(d, t0, n, vsel, blk, len(lunits)))
                NSL = 8
                for step in range(len(lunits) + NSL - 1):
                    for j in range(NSL - 1, -1, -1):
                        u = step - j
                        if 0 <= u < len(lunits):
                            lunits[u][j]()
            k.barrier()
            outproj_phase(lru_out_w[0], 10, scrM, ctx_out)


        def ssd_layer(li, ctx_out):
            TT = 512
            xbcT = scrXB
            with ExitStack() as st:
                w, bw = load_w(st, "m2in", m2_in_w[0], 8, 5184)
                nsc = norm_scratch(st, TT)
                xt = [sb("sa_x%d" % i, [128, 8, TT], F32, st) for i in range(2)]; bx = [Buf(), Buf()]
                hT = sb("sa_h", [128, 8, TT], BF16, st); bh = Buf()
                og = [sb("sa_o%d" % i, [128, TT], F32, st) for i in range(4)]; bog = [Buf() for _ in range(4)]
                ogb = [sb("sa_ob%d" % i, [128, TT], BF16, st) for i in range(4)]; bogb = [Buf() for _ in range(4)]
                on = 0
                for ti, (t0, n, vsel) in enumerate(seg_tiles(TT)):
                    x, b_x = xt[ti % 2], bx[ti % 2]
                    k.dma("sp", x[:, :, :n], xres[:, t0:t0 + n].rearrange("(c p) t -> p c t", p=128), writes=[b_x])
                    norm_mod(nsc, x, b_x, n, vsel, A1, 0, hT, bh)
                    for m in range(24):
                        p, pb = next_ps()
                        for c in range(8):
                            k.op("pe", lambda e, p=p, c=c, m=m: e.matmul(p[:, :n], w[:, c, 2048 + m * 128:2048 + (m + 1) * 128], hT[:, c, :n],
                                                                     start=(c == 0), stop=(c == 7)),
                                 reads=[bw, bh], writes=[pb], inc=(c == 7))
                        o, bo = ogb[on % 4], bogb[on % 4]
                        on += 1
                        if m % 2 == 0:
                            k.op("act", lambda e, p=p, o=o: e.copy(o[:, :n], p[:, :n]), reads=[pb], writes=[bo])
                        else:
                            k.op("dve", lambda e, p=p, o=o: e.tensor_copy(o[:, :n], p[:, :n]), reads=[pb], writes=[bo])
                        k.dma("sp", xbcT[m * 128:(m + 1) * 128, t0:t0 + n], o[:, :n], reads=[bo], writes=[Buf()])
                    for tb in range(n // 128):
                        for cb_ in range(4):
                            p, pb = next_ps()
                            for c in range(8):
                                k.op("pe", lambda e, p=p, c=c, tb=tb, cb_=cb_: e.matmul(p[:, :], hT[:, c, tb * 128:(tb + 1) * 128], w[:, c, cb_ * 512:(cb_ + 1) * 512],
                                                                                    start=(c == 0), stop=(c == 7)),
                                     reads=[bw, bh], writes=[pb], inc=(c == 7))
                            o, bo = og[on % 4], bog[on % 4]
                            on += 1
                            if cb_ % 2 == 0:
                                k.op("act", lambda e, p=p, o=o: e.copy(o[:, :], p[:, :]), reads=[pb], writes=[bo])
                            else:
                                k.op("dve", lambda e, p=p, o=o: e.tensor_copy(o[:, :], p[:, :]), reads=[pb], writes=[bo])
                            k.dma("sp", scrZ[t0 + tb * 128:t0 + (tb + 1) * 128, cb_ * 512:(cb_ + 1) * 512], o[:, :], reads=[bo], writes=[Buf()])
                        p, pb = next_ps()
                        for c in range(8):
                            k.op("pe", lambda e, p=p, c=c, tb=tb: e.matmul(p[:, 0:64], hT[:, c, tb * 128:(tb + 1) * 128], w[:, c, 5120:5184],
                                                                      start=(c == 0), stop=(c == 7)),
                                 reads=[bw, bh], writes=[pb], inc=(c == 7))
                        o, bo = og[on % 4], bog[on % 4]
                        on += 1
                        k.op("dve", lambda e, p=p, o=o: e.tensor_copy(o[:, 0:64], p[:, 0:64]), reads=[pb], writes=[bo])
                        k.dma("sp", scrDT[t0 + tb * 128:t0 + (tb + 1) * 128, :], o[:, 0:64], reads=[bo], writes=[Buf()])
            k.barrier()
            if debug == "ssdA":
                return
            with ExitStack() as st:
                def small(name, shape, src, **kw):
                    t = sb(name, shape, F32, st); b = Buf()
                    k.dma("sp", t[:], src, writes=[b], allow_slow_non_contiguous=True)
                    return t, b
                cw, bcw = small("scw", [128, 4, 24], m2_conv_w[0].rearrange("k (m p) -> p k m", p=128))
                cb, bcb = small("scb", [128, 24], m2_conv_b[0].rearrange("(m p) -> p m", p=128))
                dtb, bdtb = small("sdtb", [128, 64], m2_dt_bias[0].rearrange("d h -> (d h)").partition_broadcast(128))
                abc, babc = small("sabc", [128, 64], m2_a_log[0].rearrange("d h -> (d h)").partition_broadcast(128))
                dsk, bdsk = small("sdsk", [128, 32], m2_d[0].partition_broadcast(128))
                k.op("dve", lambda e: e.tensor_scalar(cw[:], cw[:], 0.5, None, ALU.mult), reads=[bcw], writes=[bcw])
                k.op("dve", lambda e: e.tensor_scalar(cb[:], cb[:], 0.5, None, ALU.mult), reads=[bcb], writes=[bcb])
                k.op("act", lambda e: e.activation(abc[:], abc[:], AF.Exp), reads=[babc], writes=[babc])
                k.op("dve", lambda e: e.tensor_scalar(abc[:], abc[:], -1.0, None, ALU.mult), reads=[babc], writes=[babc])
                msk = sb("smask", [128, 4, 128], F32, st); bmsk = Buf()
                k.dma("sp", msk[:], masks_in.rearrange("m p l -> p m l"), writes=[bmsk])
                MU, ML, MSL, MSU = (msk[:, i, :] for i in range(4))
                xr = [sb("s1_xr%d" % i, [128, 24, 516], BF16, st) for i in range(2)]; bxr = [Buf(), Buf()]
                xc = sb("s1_xc", [128, 128], F32, st); bxc = Buf()
                xsT = sb("s1_xsT", [128, 24, 128], BF16, st); bxsT = Buf()
                xs_tok = [sb("s1_xst%d" % i, [128, 2048], BF16, st) for i in range(2)]; bxst = [Buf(), Buf()]
                b_tok = sb("s1_btok", [128, 512], BF16, st); bbtok = Buf()
                dtr = sb("s1_dtr", [128, 64], F32, st); bdtr = Buf()
                dtt = sb("s1_dt", [128, 64], F32, st); bdtt = Buf()
                da = sb("s1_da", [128, 64], F32, st); bda = Buf()
                tmp64 = sb("s1_t64", [128, 64], F32, st); bt64 = Buf()
                wgt = sb("s1_wgt", [128, 64], F32, st); bwgt = Buf()
                Et = [sb("s1_E%d" % i, [128, 64], F32, st) for i in range(2)]; bE = [Buf(), Buf()]
                dec = [sb("s1_dec%d" % i, [128, 64], F32, st) for i in range(2)]; bdec = [Buf(), Buf()]
                Gf = sb("s1_Gf", [128, 4, 128], F32, st); Gb = sb("s1_Gb", [128, 4, 128], F32, st); bG = Buf()
                NBH = 4
                Lf = [sb("s1_Lf%d" % i, [128, 128], F32, st) for i in range(NBH)]; bLf = [Buf() for _ in range(NBH)]
                Lb = [sb("s1_Lb%d" % i, [128, 128], F32, st) for i in range(NBH)]; bLb = [Buf() for _ in range(NBH)]
                Df = [sb("s1_Df%d" % i, [128, 128], F32, st) for i in range(NBH)]; bDf = [Buf() for _ in range(NBH)]
                Db = [sb("s1_Db%d" % i, [128, 128], F32, st) for i in range(NBH)]; bDb = [Buf() for _ in range(NBH)]
                Mt = [sb("s1_M%d" % i, [128, 128], BF16, st) for i in range(NBH)]; bMt = [Buf() for _ in range(NBH)]
                Mb = [sb("s1_Mb%d" % i, [128, 128], BF16, st) for i in range(NBH)]; bMb = [Buf() for _ in range(NBH)]
                dskI = sb("s1_dskI", [128, 32, 128], BF16, st); bdskI = Buf()
                for h_ in range(32):
                    k.op("dve", lambda e, h_=h_: e.tensor_scalar(dskI[:, h_, :], ident_f[:], dsk[:, h_:h_ + 1], None, ALU.mult), reads=[b_ident, bdsk], writes=[bdskI])
                xw = sb("s1_xw", [128, 2048], BF16, st); bxw = Buf()
                yo = [sb("s1_yo%d" % i, [128, 2048], F32, st) for i in range(2)]; byo = [Buf(), Buf()]
                so = [sb("s1_so%d" % i, [128, 512], BF16, st) for i in range(4)]; bso = [Buf() for _ in range(4)]
                dtt2 = [dtt, sb("s1_dt2", [128, 64], F32, st)]; bdtt2 = [bdtt, Buf()]
                da2 = [da, sb("s1_da2", [128, 64], F32, st)]; bda2 = [bda, Buf()]
                Gf2 = [Gf, sb("s1_Gf2", [128, 4, 128], F32, st)]; Gb2 = [Gb, sb("s1_Gb2", [128, 4, 128], F32, st)]; bG2 = [bG, Buf()]
                pspool[0] = [4]

                xc4 = [xc] + [sb("s1_xc%d" % i, [128, 128], F32, st) for i in range(3)]; bxc4 = [bxc, Buf(), Buf(), Buf()]
                th4 = [sb("s1_th%d" % i, [128, 128], F32, st) for i in range(4)]; bth4 = [Buf() for _ in range(4)]

                def pieces(ch):
                    dtt_, bdtt_ = dtt2[ch % 2], bdtt2[ch % 2]
                    da_, bda_ = da2[ch % 2], bda2[ch % 2]
                    Gf_, Gb_, bG_ = Gf2[ch % 2], Gb2[ch % 2], bG2[ch % 2]
                    t0 = ch * 128
                    s0, s1 = (0, NCTX) if t0 < NCTX else (NCTX, NTOK)
                    if t0 < NCTX:
                        big_i, big_t0, big_n = 0, 0, NCTX
                    else:
                        big_i = 1 + (t0 - NCTX) // 512
                        big_t0 = NCTX + (big_i - 1) * 512
                        big_n = 512
                    xbig, b_xr = xr[big_i % 2], bxr[big_i % 2]
                    boff = t0 - big_t0
                    x_r = xbig[:, :, boff:boff + 131]
                    blo = max(big_t0 - 2, s0); bhi = min(big_t0 + big_n + 1, s1)
                    xst, b_xst = xs_tok[ch % 2], bxst[ch % 2]
                    E_, b_E = Et[ch % 2], bE[ch % 2]
                    d_, b_d = dec[ch % 2], bdec[ch % 2]
                    lo = max(t0 - 2, s0); hi = min(t0 + 129, s1)
                    pa, pab = ps[5], psb[5]
                    P = []

                    def convA(m):
                        if m == 0 and boff == 0:
                            if blo > big_t0 - 2 or bhi < big_t0 + big_n + 1:
                                k.op("pool", lambda e: e.memset(xbig[:], 0.0), writes=[b_xr])
                            k.dma("sp", xbig[:, :, blo - (big_t0 - 2):bhi - (big_t0 - 2)], xbcT[:, blo:bhi].rearrange("(m p) t -> p m t", p=128), writes=[b_xr])
                        xc_, bxc_ = xc4[m % 4], bxc4[m % 4]
                        k.op("act", lambda e: e.activation(xc_[:], x_r[:, m, 0:128], AF.Identity, bias=cb[:, m:m + 1], scale=cw[:, 0, m:m + 1]),
                             reads=[b_xr, bcw, bcb], writes=[bxc_])

                    def convT(m):
                        xc_, bxc_ = xc4[m % 4], bxc4[m % 4]
                        for kk in range(1, 4):
                            k.op("dve", lambda e, kk=kk: e.scalar_tensor_tensor(xc_[:], x_r[:, m, kk:kk + 128], cw[:, kk, m:m + 1], xc_[:], ALU.mult, ALU.add),
                                 reads=[b_xr, bcw, bxc_], writes=[bxc_])

                    def convS(m):
                        xc_, bxc_ = xc4[m % 4], bxc4[m % 4]
                        th_, bth_ = th4[m % 4], bth4[m % 4]
                        k.op("act", lambda e: e.activation(th_[:], xc_[:], AF.Tanh), reads=[bxc_], writes=[bth_])
                        k.op("dve", lambda e: e.scalar_tensor_tensor(xsT[:, m, :], th_[:], 1.0, xc_[:], ALU.add, ALU.mult), reads=[bth_, bxc_], writes=[bxsT])

                    def dtp():
                        k.dma("sp", dtr[:], scrDT[t0:t0 + 128, :], writes=[bdtr])
                        k.op("dve", lambda e: e.tensor_tensor(dtr[:], dtr[:], dtb[:], ALU.add), reads=[bdtr, bdtb], writes=[bdtr])
                        k.op("act", lambda e: e.activation(tmp64[:], dtr[:], AF.Abs), reads=[bdtr], writes=[bt64])
                        k.op("act", lambda e: e.activation(tmp64[:], tmp64[:], AF.Exp, scale=-1.0), reads=[bt64], writes=[bt64])
                        k.op("act", lambda e: e.activation(tmp64[:], tmp64[:], AF.Ln, bias=1.0, scale=1.0), reads=[bt64], writes=[bt64])
                        k.op("dve", lambda e: e.scalar_tensor_tensor(dtt_[:], dtr[:], 0.0, tmp64[:], ALU.max, ALU.add), reads=[bdtr, bt64], writes=[bdtt_])
                        k.op("dve", lambda e: e.tensor_tensor(da_[:], dtt_[:], abc[:], ALU.mult), reads=[bdtt_, babc], writes=[bda_])

                    def convpiece(i):
                        if i - 2 >= 0 and i - 2 < 24:
                            convS(i - 2)
                        if i - 1 >= 0 and i - 1 < 24:
                            convT(i - 1)
                        if i < 24:
                            convA(i)
                        if i == 8:
                            dtp()
                        if i == 12:
                            cums()
                    for i in range(26):
                        P.append(lambda i=i: convpiece(i))

                    def tr(qs, last):
                        for q in qs:
                            p, pb = next_ps()
                            pbf = p[:].bitcast(BF16)
                            for j in range(4):
                                m = q * 4 + j
                                k.op("pe", lambda e, pbf=pbf, j=j, m=m: e.transpose(pbf[:, j * 128:(j + 1) * 128], xsT[:, m, :], ident_b[:]),
                                     reads=[bxsT, b_identb], writes=[pb], inc=(j == 3))
                            if q < 4:
                                k.op("dve", lambda e, pbf=pbf, q=q: e.tensor_copy(xst[:, q * 512:(q + 1) * 512], pbf[:, 0:512]), reads=[pb], writes=[b_xst])
                            else:
                                k.op("dve", lambda e, pbf=pbf: e.tensor_copy(b_tok[:], pbf[:, 0:512]), reads=[pb], writes=[bbtok])
                        if last:
                            k.dma("sp", scrC[ch].rearrange("p (g t) -> p g t", t=128), xsT[:, 20:24, :], reads=[bxsT], writes=[Buf()])
                    P.append(lambda: tr((0, 1), False))
                    P.append(lambda: tr((2, 3), False))
                    P.append(lambda: tr((4,), True))

                    def cums():
                        k.op("pe", lambda e: e.matmul(pa[:, 0:32], MU, da_[:, 0:32], start=True, stop=True), reads=[bmsk, bda_], writes=[pab], inc=False)
                        k.op("pe", lambda e: e.matmul(pa[:, 32:64], ML, da_[:, 32:64], start=True, stop=True), reads=[bmsk, bda_], writes=[pab], inc=False)
                        k.op("pe", lambda e: e.matmul(pa[:, 64:128], ones_f[:], da_[:, 0:64], start=True, stop=True), reads=[b_ones, bda_], writes=[pab])
                        k.op("act", lambda e: e.activation(E_[:], pa[:, 0:64], AF.Exp), reads=[pab], writes=[b_E])
                        k.op("act", lambda e: e.activation(d_[:], pa[:, 64:128], AF.Exp), reads=[pab], writes=[b_d])
                        k.dma("sp", scrE[ch], E_[:], reads=[b_E], writes=[Buf()])
                        k.dma("sp", scrDec[ch], d_[:], reads=[b_d], writes=[Buf()])
                        k.op("act", lambda e: e.copy(tmp64[:], pa[:, 0:64]), reads=[pab], writes=[bt64])
                        k.op("dve", lambda e: e.tensor_tensor(wgt[:], pa[:, 64:128], tmp64[:], ALU.subtract), reads=[pab, bt64], writes=[bwgt])
                        k.op("act", lambda e: e.activation(wgt[:], wgt[:], AF.Exp), reads=[bwgt], writes=[bwgt])
                        k.op("dve", lambda e: e.tensor_tensor(wgt[:], wgt[:], dtt_[:], ALU.mult), reads=[bwgt, bdtt_], writes=[bwgt])

                    def states(d):
                        k.op("dve", lambda e: e.tensor_tensor(xw[:].rearrange("p (h q) -> p h q", q=64), xst[:].rearrange("p (h q) -> p h q", q=64),
                                                              wgt[:, d * 32:(d + 1) * 32].unsqueeze(2).to_broadcast([128, 32, 64]), ALU.mult),
                             reads=[b_xst, bwgt], writes=[bxw])
                        for g in range(4):
                            p, pb = next_ps()
                            k.op("pe", lambda e, p=p, g=g: e.matmul(p[:, :], b_tok[:, g * 128:(g + 1) * 128], xw[:, g * 512:(g + 1) * 512], start=True, stop=True),
                                 reads=[bbtok, bxw], writes=[pb])
                            s_, b_s = so[g], bso[g]
                            k.op("act", lambda e, p=p, s_=s_: e.copy(s_[:], p[:]), reads=[pb], writes=[b_s])
                            k.dma("sp", scrS[d, ch, :, g * 512:(g + 1) * 512], s_[:], reads=[b_s], writes=[Buf()])
                    P.append(lambda: states(0))
                    P.append(lambda: states(1))

                    def gmat():
                        for g in range(4):
                            p, pb = next_ps()
                            k.op("pe", lambda e, p=p, g=g: e.matmul(p[:, 0:128], xsT[:, 16 + g, :], xsT[:, 20 + g, :], start=True, stop=True),
                                 reads=[bxsT], writes=[pb])
                            k.op("dve", lambda e, p=p, g=g: e.tensor_tensor(Gf_[:, g, :], p[:, 0:128], MU, ALU.mult), reads=[pb, bmsk], writes=[bG_])
                            k.op("dve", lambda e, p=p, g=g: e.tensor_tensor(Gb_[:, g, :], p[:, 0:128], ML, ALU.mult), reads=[pb, bmsk], writes=[bG_])
                    P.append(gmat)
                    assert len(P) == 32
                    return P

                pieces_cache = {}

                def piece(ch, i):
                    if ch >= NCH:
                        return
                    if ch not in pieces_cache:
                        pieces_cache.clear()
                        pieces_cache[ch] = pieces(ch)
                    pieces_cache[ch][i]()

                def prologue(ch):
                    if ch == 0:
                        for i in range(32):
                            piece(0, i)

                def make_head(ch, h, hc):
                    dtt_, bdtt_ = dtt2[ch % 2], bdtt2[ch % 2]
                    da_, bda_ = da2[ch % 2], bda2[ch % 2]
                    Gf_, Gb_, bG_ = Gf2[ch % 2], Gb2[ch % 2], bG2[ch % 2]
                    xst, b_xst = xs_tok[ch % 2], bxst[ch % 2]
                    g = h // 8
                    i2 = hc % NBH
                    p, pb = ps[6 + hc % 2], psb[6 + hc % 2]
                    yb = h // 8

                    def h0():
                        if h == 0:
                            prologue(ch)
                        piece(ch + 1, h)
                        k.op("act", lambda e: e.activation(Lf[i2][:], MSL, AF.Identity, scale=da_[:, h:h + 1]), reads=[bmsk, bda_], writes=[bLf[i2]])
                        k.op("act", lambda e: e.activation(Lb[i2][:], MSU, AF.Identity, scale=da_[:, 32 + h:33 + h]), reads=[bmsk, bda_], writes=[bLb[i2]])

                    def h1():
                        k.op("pe", lambda e: e.matmul(p[:, 0:128], Lf[i2][:], MU, start=True, stop=True), reads=[bLf[i2], bmsk], writes=[pb], inc=False)
                        k.op("pe", lambda e: e.matmul(p[:, 128:256], Lb[i2][:], ML, start=True, stop=True), reads=[bLb[i2], bmsk], writes=[pb])

                    def h2():
                        k.op("act", lambda e: e.activation(Df[i2][:], p[:, 0:128], AF.Exp), reads=[pb], writes=[bDf[i2]])
                        k.op("act", lambda e: e.activation(Db[i2][:], p[:, 128:256], AF.Exp), reads=[pb], writes=[bDb[i2]])

                    def h3():
                        k.op("dve", lambda e: e.scalar_tensor_tensor(Mt[i2][:], Df[i2][:], dtt_[:, h:h + 1], Gf_[:, g, :], ALU.mult, ALU.mult),
                             reads=[bDf[i2], bdtt_, bG_], writes=[bMt[i2]])
                        k.op("dve", lambda e: e.scalar_tensor_tensor(Mb[i2][:], Db[i2][:], dtt_[:, 32 + h:33 + h], Gb_[:, g, :], ALU.mult, ALU.mult),
                             reads=[bDb[i2], bdtt_, bG_], writes=[bMb[i2]])

                    def h4():
                        yout = ps[yb][:, (h % 8) * 64:(h % 8 + 1) * 64]
                        k.op("pe", lambda e: e.matmul(yout, Mt[i2][:], xst[:, h * 64:(h + 1) * 64], start=True, stop=False),
                             reads=[bMt[i2], b_xst], writes=[psb[yb]], inc=False)
                        k.op("pe", lambda e: e.matmul(yout, Mb[i2][:], xst[:, h * 64:(h + 1) * 64], start=False, stop=False),
                             reads=[bMb[i2], b_xst], writes=[psb[yb]], inc=False)
                        k.op("pe", lambda e: e.matmul(yout, dskI[:, h, :], xst[:, h * 64:(h + 1) * 64], start=False, stop=True),
                             reads=[bdskI, b_xst], writes=[psb[yb]])
                        if h != 31:
                            return
                        y_, b_y = yo[ch % 2], byo[ch % 2]
                        for yb2 in range(4):
                            if yb2 % 2 == 0:
                                k.op("act", lambda e, yb2=yb2: e.copy(y_[:, yb2 * 512:(yb2 + 1) * 512], ps[yb2][:]), reads=[psb[yb2]], writes=[b_y])
                            else:
                                k.op("dve", lambda e, yb2=yb2: e.tensor_copy(y_[:, yb2 * 512:(yb2 + 1) * 512], ps[yb2][:]), reads=[psb[yb2]], writes=[b_y])
                        k.dma("sp", scrY[ch], y_[:], reads=[b_y], writes=[Buf()])

                    return [h0, h1, h2, h3, h4]

                hunits = []
                for ch in range(NCH):
                    for h in range(32):
                        hunits.append(make_head(ch, h, len(hunits)))
                NSH = 5
                for step in range(len(hunits) + NSH - 1):
                    for j in range(NSH - 1, -1, -1):
                        u = step - j
                        if 0 <= u < len(hunits):
                            hunits[u][j]()
                pspool[0] = list(range(8))
            k.barrier()
            if debug == "ssd1":
                return
            with ExitStack() as st:
                Hs = [sb("s2_H%d" % d, [128, 2048], F32, st) for d in range(2)]; bH = [Buf(), Buf()]
                Hb = [sb("s2_Hb%d" % i, [128, 2048], BF16, st) for i in range(2)]; bHb = [Buf(), Buf()]
                St = [sb("s2_S%d" % i, [128, 2048], BF16, st) for i in range(2)]; bSt = [Buf(), Buf()]
                dc = [sb("s2_d%d" % i, [128, 64], F32, st) for i in range(2)]; bdc = [Buf(), Buf()]
                it = 0
                for d in range(2):
                    order = list(range(NCH)) if d == 0 else [1, 0] + list(range(NCH - 1, 1, -1))
                    k.op("pool", lambda e, d=d: e.memset(Hs[d][:], 0.0), writes=[bH[d]])
                    for ch in order:
                        i2 = it % 2
                        it += 1
                        k.op("act", lambda e, d=d, i2=i2: e.copy(Hb[i2][:], Hs[d][:]), reads=[bH[d]], writes=[bHb[i2]])
                        k.dma("pool", scrH[d, ch], Hb[i2][:], reads=[bHb[i2]], writes=[Buf()])
                        k.dma("sp", St[i2][:], scrS[d, ch], writes=[bSt[i2]])
                        k.dma("sp", dc[i2][:], scrDec[ch], writes=[bdc[i2]])
                        k.op("dve", lambda e, d=d, i2=i2: e.tensor_tensor(Hs[d][:].rearrange("p (h q) -> p h q", q=64), Hs[d][:].rearrange("p (h q) -> p h q", q=64),
                                                                      dc[i2][:, d * 32:(d + 1) * 32].unsqueeze(2).to_broadcast([128, 32, 64]), ALU.mult),
                             reads=[bH[d], bdc[i2]], writes=[bH[d]])
                        k.op("dve", lambda e, d=d, i2=i2: e.tensor_tensor(Hs[d][:], Hs[d][:], St[i2][:], ALU.add), reads=[bH[d], bSt[i2]], writes=[bH[d]])
            k.barrier()
            if debug == "ssd2":
                return
            with ExitStack() as st:
                ow, bow = load_w(st, "m2ow", m2_out_w[0], 16, D)
                nwb = sb("s3_nw", [128, 2048], F32, st); bnwb = Buf()
                k.dma("sp", nwb[:], m2_norm_w[0].partition_broadcast(128), writes=[bnwb])
                yt = [sb("s3_y%d" % i, [128, 2048], F32, st) for i in range(2)]; byt = [Buf(), Buf()]
                zt = [sb("s3_z%d" % i, [128, 2048], F32, st) for i in range(2)]; bzt = [Buf(), Buf()]
                Hf = [sb("s3_Hf%d" % i, [128, 2048], BF16, st) for i in range(2)]; bHf = [Buf(), Buf()]
                Hbk = [sb("s3_Hb%d" % i, [128, 2048], BF16, st) for i in range(2)]; bHbk = [Buf(), Buf()]
                Ct = [sb("s3_C%d" % i, [128, 512], BF16, st) for i in range(2)]; bCt = [Buf(), Buf()]
                Ee = [sb("s3_E%d" % i, [128, 64], F32, st) for i in range(2)]; bEe = [Buf(), Buf()]
                tmp = sb("s3_tmp", [128, 512], F32, st); btmp = Buf()
                ss = sb("s3_ss", [128, 2], F32, st); bss = Buf()
                sqj = sb("s3_sq", [128, 2048], F32, st); bsqj = Buf()
                yn = sb("s3_yn", [128, 2048], BF16, st); byn = Buf()
                ynT = sb("s3_ynT", [128, 16, 128], BF16, st); bynT = Buf()
                xt = [sb("s3_x%d" % i, [128, 8, 128], F32, st) for i in range(2)]; bx = [Buf(), Buf()]
                for ch in range(NCH):
                    t0 = ch * 128
                    vsel = 1 if t0 < NCTX else 0
                    if vsel == 1 and not ctx_out:
                        continue
                    i2 = ch % 2
                    y_, z_, b_y, b_z = yt[i2], zt[i2], byt[i2], bzt[i2]
                    k.dma("sp", y_[:], scrY[ch], writes=[b_y])
                    k.dma("sp", z_[:], scrZ[t0:t0 + 128, :], writes=[b_z])
                    k.dma("pool", Hf[i2][:], scrH[0, ch], writes=[bHf[i2]])
                    k.dma("pool", Hbk[i2][:], scrH[1, ch], writes=[bHbk[i2]])
                    k.dma("sp", Ct[i2][:], scrC[ch], writes=[bCt[i2]])
                    k.dma("sp", Ee[i2][:], scrE[ch], writes=[bEe[i2]])
                    k.dma("pool", xt[i2][:], xres[:, t0:t0 + 128].rearrange("(c p) t -> p c t", p=128), writes=[bx[i2]])
                    for d in range(2):
                        Hd, bHd = (Hf[i2], bHf[i2]) if d == 0 else (Hbk[i2], bHbk[i2])
                        for g in range(4):
                            p, pb = next_ps()
                            k.op("pe", lambda e, p=p, g=g, Hd=Hd, i2=i2: e.matmul(p[:], Ct[i2][:, g * 128:(g + 1) * 128], Hd[:, g * 512:(g + 1) * 512], start=True, stop=True),
                                 reads=[bCt[i2], bHd], writes=[pb])
                            k.op("dve", lambda e, p=p, g=g, d=d, i2=i2: e.tensor_tensor(tmp[:].rearrange("p (h q) -> p h q", q=64), p[:].rearrange("p (h q) -> p h q", q=64),
                                                                                   Ee[i2][:, d * 32 + g * 8:d * 32 + (g + 1) * 8].unsqueeze(2).to_broadcast([128, 8, 64]), ALU.mult),
                                 reads=[pb, bEe[i2]], writes=[btmp])
                            k.op("pool", lambda e, g=g, y_=y_: e.tensor_tensor(y_[:, g * 512:(g + 1) * 512], y_[:, g * 512:(g + 1) * 512], tmp[:], ALU.add),
                                 reads=[btmp, b_y], writes=[b_y])
                    k.op("act", lambda e, z_=z_: e.activation(z_[:], z_[:], AF.Silu), reads=[b_z], writes=[b_z])
                    k.op("dve", lambda e, y_=y_, z_=z_: e.tensor_tensor(y_[:], y_[:], z_[:], ALU.mult), reads=[b_y, b_z], writes=[b_y])
                    k.op("pool", lambda e: e.memset(ss[:], 0.0), writes=[bss])
                    k.op("act", lambda e, y_=y_: e.activation(sqj[:], y_[:], AF.Square, accum_out=ss[:, 0:1]), reads=[b_y], writes=[bsqj, bss])
                    k.op("act", lambda e: e.activation(ss[:, 1:2], ss[:, 0:1], AF.Sqrt, bias=EPS, scale=1.0 / 2048), reads=[bss], writes=[bss])
                    k.op("dve", lambda e: e.reciprocal(ss[:, 1:2], ss[:, 1:2]), reads=[bss], writes=[bss])
                    k.op("dve", lambda e, y_=y_: e.scalar_tensor_tensor(yn[:], y_[:], ss[:, 1:2], nwb[:], ALU.mult, ALU.mult), reads=[b_y, bss, bnwb], writes=[byn])
                    for q in range(4):
                        p, pb = next_ps()
                        pbf = p[:].bitcast(BF16)
                        for j in range(4):
                            m = q * 4 + j
                            k.op("pe", lambda e, pbf=pbf, j=j, m=m: e.transpose(pbf[:, j * 128:(j + 1) * 128], yn[:, m * 128:(m + 1) * 128], ident_b[:]),
                                 reads=[byn, b_identb], writes=[pb], inc=(j == 3))
                        k.op("act" if q % 2 else "dve", (lambda e, pbf=pbf, q=q: e.copy(ynT[:, q * 4:(q + 1) * 4, :], pbf[:, 0:512].rearrange("p (j t) -> p j t", t=128))) if q % 2 else
                             (lambda e, pbf=pbf, q=q: e.tensor_copy(ynT[:, q * 4:(q + 1) * 4, :], pbf[:, 0:512].rearrange("p (j t) -> p j t", t=128))),
                             reads=[pb], writes=[bynT])
                    for c in range(8):
                        p, pb = next_ps()
                        for j in range(16):
                            k.op("pe", lambda e, p=p, j=j, c=c: e.matmul(p[:, 0:128], ow[:, j, c * 128:(c + 1) * 128], ynT[:, j, :], start=(j == 0), stop=(j == 15)),
                                 reads=[bow, bynT], writes=[pb], inc=(j == 15))
                        k.op("dve", lambda e, p=p, c=c, i2=i2: e.scalar_tensor_tensor(xt[i2][:, c, :], p[:, 0:128], mod[:, 16 + c, vsel:vsel + 1], xt[i2][:, c, :], ALU.mult, ALU.add),
                             reads=[pb, b_mod, bx[i2]], writes=[bx[i2]])
                    k.dma("pool", xres[:, t0:t0 + 128].rearrange("(c p) t -> p c t", p=128), xt[i2][:], reads=[bx[i2]], writes=[Buf()])
            k.barrier()

        TWO_PI = 6.283185307179586

        def sincos(st, tag, phi, bphi, shape, out_sin, out_cos, bout):
            t = sb("sc_t" + tag, shape, F32, st); ki = sb("sc_k" + tag, shape, I32, st); bt = Buf()
            for (o, off) in ((out_sin, 0.0), (out_cos, 0.5 * np.pi)):
                k.op("dve", lambda e, off=off: e.tensor_scalar(t[:], phi, 1.0 / TWO_PI, off / TWO_PI, ALU.mult, ALU.add), reads=[bphi], writes=[bt])
                k.op("dve", lambda e: e.tensor_copy(ki[:], t[:]), reads=[bt], writes=[bt])
                k.op("dve", lambda e: e.tensor_copy(t[:], ki[:]), reads=[bt], writes=[bt])
                k.op("dve", lambda e: e.scalar_tensor_tensor(t[:], t[:], -TWO_PI, phi, ALU.mult, ALU.add), reads=[bt, bphi], writes=[bt])
                k.op("dve", lambda e, off=off: e.tensor_scalar(t[:], t[:], off, 3.1415925, ALU.add, ALU.min), reads=[bt], writes=[bt])
                k.op("dve", lambda e: e.tensor_scalar(t[:], t[:], -3.1415925, None, ALU.max), reads=[bt], writes=[bt])
                k.op("act", lambda e, o=o: e.activation(o, t[:], AF.Sin), reads=[bt], writes=[bout])

        def s5_layer(li, ctx_out):
            T = 128
            yfT = scrA
            tiles = seg_tiles(T)
            for d in range(2):
                with ExitStack() as st:
                    def small(name, shape, src, q="sp"):
                        t = sb(name, shape, F32, st); b = Buf()
                        k.dma(q, t[:], src, writes=[b], allow_slow_non_contiguous=True)
                        return t, b
                    ramp, bramp = small("ramp", [128, 128], ramp_in)
                    sel, bsel = small("sel", [128, 8], sel_in)
                    lrs, b1 = small("lrs", [128, 32], s5_lre[0, d].rearrange("g p -> (g p)").rearrange("(b q) -> q b", q=128))
                    lis, b2 = small("lis", [128, 32], s5_lim[0, d].rearrange("g p -> (g p)").rearrange("(b q) -> q b", q=128))
                    sts = sb("sts", [128, 32], F32, st); b3 = Buf()
                    for gl in range(2):
                        k.dma("sp", sts[gl * 64:(gl + 1) * 64, :], s5_ls[0, d].rearrange("(b gl) -> gl b", gl=2)[gl].partition_broadcast(64), writes=[b3],
                              allow_slow_non_contiguous=True)
                    rho = sb("rho", [128, 32], F32, st); theta = sb("theta", [128, 32], F32, st); bpar = Buf()
                    k.op("dve", lambda e: e.tensor_scalar(lrs[:], lrs[:], -1e-4, None, ALU.min), reads=[b1], writes=[b1])
                    k.op("act", lambda e: e.activation(sts[:], sts[:], AF.Exp), reads=[b3], writes=[b3])
                    k.op("dve", lambda e: e.tensor_tensor(rho[:], lrs[:], sts[:], ALU.mult), reads=[b1, b3], writes=[bpar])
                    k.op("act", lambda e: e.activation(rho[:], rho[:], AF.Exp), reads=[bpar], writes=[bpar])
                    k.op("dve", lambda e: e.tensor_tensor(theta[:], lis[:], sts[:], ALU.mult), reads=[b2, b3], writes=[bpar])
                    rhoT = sb("rhoT", [128, 32, 128], F32, st)
                    for b in range(32):
                        k.op("pool", lambda e, b=b: e.tensor_scalar(rhoT[:, b, :], ramp[:], 0.0, rho[:, b:b + 1], ALU.mult, ALU.add), reads=[bramp, bpar], writes=[bpar])
                    cosT = sb("cosT", [128, 32, 128], F32, st); sinT = sb("sinT", [128, 32, 128], F32, st); btab = Buf()
                    phi = sb("phi", [128, 128], F32, st); bphi = Buf()
                    with ExitStack() as st2:
                        for b in range(32):
                            k.op("dve", lambda e, b=b: e.tensor_scalar(phi[:], ramp[:], theta[:, b:b + 1], None, ALU.mult), reads=[bramp, bpar], writes=[bphi])
                            if b == 0:
                                tt_ = sb("sc_t", [128, 128], F32, st2); ki_ = sb("sc_k", [128, 128], I32, st2); bt_ = Buf()
                            for (o, off) in ((sinT[:, b, :], 0.0), (cosT[:, b, :], 0.5 * np.pi)):
                                k.op("dve", lambda e, off=off: e.tensor_scalar(tt_[:], phi[:], 1.0 / TWO_PI, off / TWO_PI, ALU.mult, ALU.add), reads=[bphi], writes=[bt_])
                                k.op("dve", lambda e: e.tensor_copy(ki_[:], tt_[:]), reads=[bt_], writes=[bt_])
                                k.op("dve", lambda e: e.tensor_copy(tt_[:], ki_[:]), reads=[bt_], writes=[bt_])
                                k.op("dve", lambda e: e.scalar_tensor_tensor(tt_[:], tt_[:], -TWO_PI, phi[:], ALU.mult, ALU.add), reads=[bt_, bphi], writes=[bt_])
                                k.op("dve", lambda e, off=off: e.tensor_scalar(tt_[:], tt_[:], off, 3.1415925, ALU.add, ALU.min), reads=[bt_], writes=[bt_])
                                k.op("dve", lambda e: e.tensor_scalar(tt_[:], tt_[:], -3.1415925, None, ALU.max), reads=[bt_], writes=[bt_])
                                k.op("act", lambda e, o=o: e.activation(o, tt_[:], AF.Sin), reads=[bt_], writes=[btab])
                    k.barrier()
                    Wre = sb("Wre", [128, 8, 4, 128], BF16, st); Wim = sb("Wim", [128, 8, 4, 128], BF16, st); bW = Buf()
                    with ExitStack() as st2:
                        def wl(name, src3):
                            t = sb(name, [128, 8, 64], F32, st2); b = Buf()
                            v = src3.rearrange("(c gl) p -> gl c p", gl=8)
                            for gl in range(8):
                                k.dma("sp" if gl % 2 else "pool", t[gl * 16:(gl + 1) * 16, :, :], v[gl].partition_broadcast(16), writes=[b], allow_slow_non_contiguous=True)
                            return t, b
                        lrw, blr = wl("lrw", s5_lre[0, d]); liw, bli = wl("liw", s5_lim[0, d])
                        stw = sb("stw", [128, 8], F32, st2); bst = Buf()
                        v = s5_ls[0, d].rearrange("(c gl) -> gl c", gl=8)
                        for gl in range(8):
                            k.dma("sp", stw[gl * 16:(gl + 1) * 16, :], v[gl].partition_broadcast(16), writes=[bst], allow_slow_non_contiguous=True)
                        brw = sb("brw", [128, 8, 64], F32, st2); biw = sb("biw", [128, 8, 64], F32, st2); bbw = Buf()
                        for (t_, src) in ((brw, s5_bre[0]), (biw, s5_bim[0])):
                            v = src.rearrange("(c gl) p j -> gl j c p", gl=8)
                            for gl in range(8):
                                for c_ in range(8):
                                    k.dma("sp" if c_ % 2 else "pool", t_[gl * 16:(gl + 1) * 16, c_, :], v[gl][:, c_, :], writes=[bbw], allow_slow_non_contiguous=True)
                        k.op("dve", lambda e: e.tensor_scalar(lrw[:], lrw[:], -1e-4, None, ALU.min), reads=[blr], writes=[blr])
                        k.op("act", lambda e: e.activation(stw[:], stw[:], AF.Exp), reads=[bst], writes=[bst])
                        stb = stw[:].unsqueeze(2).to_broadcast([128, 8, 64])
                        mag = sb("mag", [128, 8, 64], F32, st2); th = sb("thw", [128, 8, 64], F32, st2); bm_ = Buf(); bth = Buf()
                        k.op("dve", lambda e: e.tensor_tensor(mag[:], lrw[:], stb, ALU.mult), reads=[blr, bst], writes=[bm_])
                        k.op("act", lambda e: e.activation(mag[:], mag[:], AF.Exp), reads=[bm_], writes=[bm_])
                        k.op("dve", lambda e: e.tensor_tensor(th[:], liw[:], stb, ALU.mult), reads=[bli, bst], writes=[bth])
                        sn = sb("snw", [128, 8, 64], F32, st2); cs = sb("csw", [128, 8, 64], F32, st2); bsc_ = Buf()
                        sincos(st2, "w", th[:], bth, [128, 8, 64], sn[:], cs[:], bsc_)
                        k.op("dve", lambda e: e.tensor_tensor(cs[:], cs[:], mag[:], ALU.mult), reads=[bsc_, bm_], writes=[bsc_])
                        k.op("dve", lambda e: e.tensor_scalar(cs[:], cs[:], -1.0, None, ALU.add), reads=[bsc_], writes=[bsc_])
                        k.op("dve", lambda e: e.tensor_tensor(sn[:], sn[:], mag[:], ALU.mult), reads=[bsc_, bm_], writes=[bsc_])
                        den = sb("den", [128, 8, 64], F32, st2); t1 = sb("t1w", [128, 8, 64], F32, st2); bden = Buf(); bt1 = Buf()
                        zr = sb("zr", [128, 8, 64], F32, st2); zi = sb("zi", [128, 8, 64], F32, st2); bz_ = Buf()
                        k.op("dve", lambda e: e.tensor_tensor(den[:], lrw[:], lrw[:], ALU.mult), reads=[blr], writes=[bden])
                        k.op("dve", lambda e: e.tensor_tensor(t1[:], liw[:], liw[:], ALU.mult), reads=[bli], writes=[bt1])
                        k.op("dve", lambda e: e.tensor_tensor(den[:], den[:], t1[:], ALU.add), reads=[bden, bt1], writes=[bden])
                        k.op("dve", lambda e: e.reciprocal(den[:], den[:]), reads=[bden], writes=[bden])
                        k.op("dve", lambda e: e.tensor_tensor(zr[:], cs[:], lrw[:], ALU.mult), reads=[bsc_, blr], writes=[bz_])
                        k.op("dve", lambda e: e.tensor_tensor(t1[:], sn[:], liw[:], ALU.mult), reads=[bsc_, bli], writes=[bt1])
                        k.op("dve", lambda e: e.tensor_tensor(zr[:], zr[:], t1[:], ALU.add), reads=[bz_, bt1], writes=[bz_])
                        k.op("dve", lambda e: e.tensor_tensor(zr[:], zr[:], den[:], ALU.mult), reads=[bz_, bden], writes=[bz_])
                        k.op("dve", lambda e: e.tensor_tensor(zi[:], sn[:], lrw[:], ALU.mult), reads=[bsc_, blr], writes=[bz_])
                        k.op("dve", lambda e: e.tensor_tensor(t1[:], cs[:], liw[:], ALU.mult), reads=[bsc_, bli], writes=[bt1])
                        k.op("dve", lambda e: e.tensor_tensor(zi[:], zi[:], t1[:], ALU.subtract), reads=[bz_, bt1], writes=[bz_])
                        k.op("dve", lambda e: e.tensor_tensor(zi[:], zi[:], den[:], ALU.mult), reads=[bz_, bden], writes=[bz_])
                        k.op("dve", lambda e: e.tensor_tensor(mag[:], zr[:], brw[:], ALU.mult), reads=[bz_, bbw], writes=[bm_])
                        k.op("dve", lambda e: e.tensor_tensor(t1[:], zi[:], biw[:], ALU.mult), reads=[bz_, bbw], writes=[bt1])
                        k.op("dve", lambda e: e.tensor_tensor(mag[:], mag[:], t1[:], ALU.subtract), reads=[bm_, bt1], writes=[bm_])
                        k.op("dve", lambda e: e.tensor_tensor(th[:], zr[:], biw[:], ALU.mult), reads=[bz_, bbw], writes=[bth])
                        k.op("dve", lambda e: e.tensor_tensor(t1[:], zi[:], brw[:], ALU.mult), reads=[bz_, bbw], writes=[bt1])
                        k.op("dve", lambda e: e.tensor_tensor(th[:], th[:], t1[:], ALU.add), reads=[bth, bt1], writes=[bth])
                        if debug_sub == "pre":
                            for i_, (t_, b_) in enumerate(((brw, bbw), (mag, bm_), (zr, bz_), (den, bden), (sn, bsc_), (lrw, blr), (liw, bli))):
                                k.dma("sp", outT[i_ * 128:(i_ + 1) * 128, 0:512], t_[:].rearrange("p c q -> p (c q)"), reads=[b_], writes=[Buf()])
                            k.dma("sp", outT[896:1024, 0:8], stw[:], reads=[bst], writes=[Buf()])
                            k.dma("sp", outT[896:1024, 8:16], sel[:], reads=[bsel], writes=[Buf()])
                            k.barrier()
                            return
                        for q in range(4):
                            for gl2 in range(2):
                                k.op("dve", lambda e, q=q, gl2=gl2: e.tensor_scalar(Wre[:, :, q, gl2 * 64:(gl2 + 1) * 64], mag[:], sel[:, q * 2 + gl2:q * 2 + gl2 + 1], None, ALU.mult),
                                     reads=[bm_, bsel], writes=[bW])
                                k.op("dve", lambda e, q=q, gl2=gl2: e.tensor_scalar(Wim[:, :, q, gl2 * 64:(gl2 + 1) * 64], th[:], sel[:, q * 2 + gl2:q * 2 + gl2 + 1], None, ALU.mult),
                                     reads=[bth, bsel], writes=[bW])
                    k.barrier()
                    if debug_sub == "pre2":
                        for c_ in range(8):
                            k.dma("sp", outT[c_ * 128:(c_ + 1) * 128, 0:512], Wre[:, c_, :, :].rearrange("p q m -> p (q m)"), reads=[bW], writes=[Buf()])
                        k.barrier()
                        return
                    WcR = sb("WcR", [128, 32, 128], BF16, st); WcI = sb("WcI", [128, 32, 128], BF16, st); bWc = Buf()
                    for b_ in range(32):
                        k.op("pool", lambda e, b_=b_: e.memset(WcR[:, b_, :], 0.0), writes=[bWc])
                        k.op("pool", lambda e, b_=b_: e.memset(WcI[:, b_, :], 0.0), writes=[bWc])
                    if debug_sub == "p_a":
                        for c_ in range(8):
                            k.dma("sp", outT[c_ * 128:(c_ + 1) * 128, 0:512], Wre[:, c_, :, :].rearrange("p q m -> p (q m)"), reads=[bW], writes=[Buf()])
                        k.barrier()
                        return
                    with ExitStack() as st2:
                        for (Wc_, src, sgn) in ((WcR, s5_cre[0, d], 1.0), (WcI, s5_cim[0, d], -1.0)):
                            t2 = sb("t2c", [128, 32, 16], F32, st2); bt2 = Buf()
                            v = src.rearrange("(b gl) j p -> gl p b j", gl=2)
                            for gl2 in range(2):
                                for hb in range(32):
                                    k.dma("sp" if hb % 2 else "pool", t2[gl2 * 64:(gl2 + 1) * 64, hb, :], v[gl2][:, hb, :], writes=[bt2],
                                          allow_slow_non_contiguous=True)
                            for gl2 in range(2):
                                for q in range(4):
                                    col = (2 * q + gl2) * 16
                                    k.op("dve", lambda e, gl2=gl2, q=q, col=col, Wc_=Wc_, t2=t2, sgn=sgn: e.tensor_scalar(
                                        Wc_[gl2 * 64:(gl2 + 1) * 64, q::4, col:col + 16], t2[gl2 * 64:(gl2 + 1) * 64, q::4, :], sgn, None, ALU.mult),
                                        reads=[bt2], writes=[bWc])
                    k.barrier()
                    if debug_sub == "pre3":
                        for c_ in range(8):
                            k.dma("sp", outT[c_ * 128:(c_ + 1) * 128, 0:512], Wre[:, c_, :, :].rearrange("p q m -> p (q m)"), reads=[bW], writes=[Buf()])
                        k.barrier()
                        return
                    if d == 1:
                        gw_, bgw_ = load_w(st, "s5glu", s5_glu_w[0], 8, 2 * D)
                        gbias, bgb_ = small("s5gb", [128, 16], s5_glu_b[0].rearrange("(m p) -> p m", p=128))
                        dskT, bdsk_ = small("s5d", [128, 8], s5_d[0].rearrange("(c p) -> p c", p=128))
                    nscs = [norm_scratch(st, T) for _ in range(2)]
                    xt = [sb("s5_x%d" % i, [128, 8, T], F32, st) for i in range(2)]; bx = [Buf(), Buf()]
                    hTs = [sb("s5_h%d" % i, [128, 8, T], BF16, st) for i in range(2)]; bhs = [Buf(), Buf()]
                    stR = sb("s5_stR", [128, 32], F32, st); stI = sb("s5_stI", [128, 32], F32, st); bstate = [Buf() for _ in range(32)]
                    k.op("pool", lambda e: e.memset(stR[:], 0.0), writes=bstate)
                    k.op("pool", lambda e: e.memset(stI[:], 0.0), writes=bstate)
                    NB = 6
                    def mk(name, dt=F32):
                        return [sb("%s%d" % (name, i), [128, T], dt, st) for i in range(NB)], [Buf() for _ in range(NB)]
                    ur, bur = mk("s5_ur"); ui, bui = mk("s5_ui")
                    ta, bta = mk("s5_ta"); tb_, btb = mk("s5_tb"); tc, btc = mk("s5_tc"); td, btd = mk("s5_td")
                    vr, bvr = mk("s5_vr"); vi, bvi = mk("s5_vi")
                    q1, bq1 = mk("s5_q1", BF16); q2, bq2 = mk("s5_q2", BF16); q3, bq3 = mk("s5_q3", BF16); q4, bq4 = mk("s5_q4", BF16)
                    stt_ = [sb("s5_stt%d" % i, [128, 2], F32, st) for i in range(NB)]; bstt = [Buf() for _ in range(NB)]
                    yo = [sb("s5_yo%d" % i, [128, T], F32, st) for i in range(2)]; byo = [Buf(), Buf()]
                    if d == 1:
                        yf = [sb("s5_yf%d" % i, [128, T], F32, st) for i in range(2)]; byf = [Buf(), Buf()]
                        gT = sb("s5_g", [128, 8, T], BF16, st); bg = Buf()
                        ga = [sb("s5_ga%d" % i, [128, T], F32, st) for i in range(2)]; bga = [Buf(), Buf()]
                    order = tiles if d == 0 else [tiles[1], tiles[0]] + tiles[:1:-1]
                    rv = (lambda ap: ap) if d == 0 else (lambda ap: ap[:, ::-1])
                    pspool[0] = [6, 7]
                    units = []

                    def make_unit(ti, t0, n, vsel, c, q, un):
                        x, b_x = xt[ti % 2], bx[ti % 2]
                        hT, bh = hTs[ti % 2], bhs[ti % 2]
                        b = c * 4 + q
                        i3 = un % NB
                        pu, pub = ps[un % 3], psb[un % 3]
                        py, pyb = ps[3 + (un // 4) % 3], psb[3 + (un // 4) % 3]
                        cs_ = rv(cosT[:, b, :]); sn_ = rv(sinT[:, b, :])
                        rb = rhoT[:, b, :]
                        last = (n - 1) if d == 0 else 0

                        def s0():
                            if c == 0 and q == 0:
                                k.dma("sp", x[:, :, :n], xres[:, t0:t0 + n].rearrange("(c p) t -> p c t", p=128), writes=[b_x])
                                norm_mod(nscs[ti % 2], x, b_x, n, vsel, A1, 0, hT, bh)
                            k.op("pe", lambda e: e.matmul(pu[:, 0:128], Wre[:, c, q, :], hT[:, c, :], start=True, stop=True), reads=[bW, bh], writes=[pub], inc=False)
                            k.op("pe", lambda e: e.matmul(pu[:, 128:256], Wim[:, c, q, :], hT[:, c, :], start=True, stop=True), reads=[bW, bh], writes=[pub])

                        def s1():
                            k.op("act", lambda e: e.copy(ur[i3][:], pu[:, 0:128]), reads=[pub], writes=[bur[i3]])
                            k.op("act", lambda e: e.copy(ui[i3][:], pu[:, 128:256]), reads=[pub], writes=[bui[i3]])

                        def s2():
                            k.op("dve", lambda e: e.tensor_tensor(ta[i3][:], ur[i3][:], cs_, ALU.mult), reads=[bur[i3], btab], writes=[bta[i3]])
                            k.op("pool", lambda e: e.tensor_tensor(tb_[i3][:], ui[i3][:], sn_, ALU.mult), reads=[bui[i3], btab], writes=[btb[i3]])
                            k.op("dve", lambda e: e.tensor_tensor(tc[i3][:], ui[i3][:], cs_, ALU.mult), reads=[bui[i3], btab], writes=[btc[i3]])
                            k.op("pool", lambda e: e.tensor_tensor(td[i3][:], ur[i3][:], sn_, ALU.mult), reads=[bur[i3], btab], writes=[btd[i3]])

                        def s3():
                            k.op("pool", lambda e: e.tensor_tensor(ta[i3][:], ta[i3][:], tb_[i3][:], ALU.add), reads=[bta[i3], btb[i3]], writes=[bta[i3]])
                            k.op("pool", lambda e: e.tensor_tensor(tc[i3][:], tc[i3][:], td[i3][:], ALU.subtract), reads=[btc[i3], btd[i3]], writes=[btc[i3]])

                        def s4():
                            k.op("dve", lambda e: e.tensor_tensor_scan(rv(vr[i3][:]), rb, rv(ta[i3][:]), stR[:, b:b + 1], ALU.mult, ALU.add),
                                 reads=[bta[i3], bpar, bstate[b]], writes=[bvr[i3]])
                            k.op("dve", lambda e: e.tensor_tensor_scan(rv(vi[i3][:]), rb, rv(tc[i3][:]), stI[:, b:b + 1], ALU.mult, ALU.add),
                                 reads=[btc[i3], bpar, bstate[b]], writes=[bvi[i3]])

                        def s5():
                            k.op("pool", lambda e: e.tensor_tensor(q1[i3][:], vr[i3][:], cs_, ALU.mult), reads=[bvr[i3], btab], writes=[bq1[i3]])
                            k.op("dve", lambda e: e.scalar_tensor_tensor(q2[i3][:], vi[i3][:], -1.0, sn_, ALU.mult, ALU.mult), reads=[bvi[i3], btab], writes=[bq2[i3]])
                            k.op("pool", lambda e: e.tensor_tensor(q3[i3][:], vi[i3][:], cs_, ALU.mult), reads=[bvi[i3], btab], writes=[bq3[i3]])
                            k.op("dve", lambda e: e.tensor_tensor(q4[i3][:], vr[i3][:], sn_, ALU.mult), reads=[bvr[i3], btab], writes=[bq4[i3]])
                            k.op("act", lambda e: e.activation(stt_[i3][:, 0:1], vi[i3][:, last:last + 1], AF.Identity, scale=sn_[:, last:last + 1]),
                                 reads=[bvi[i3], btab], writes=[bstt[i3]])
                            k.op("act", lambda e: e.activation(stt_[i3][:, 0:1], stt_[i3][:, 0:1], AF.Identity, scale=-1.0), reads=[bstt[i3]], writes=[bstt[i3]])
                            k.op("act", lambda e: e.activation(stt_[i3][:, 1:2], vr[i3][:, last:last + 1], AF.Identity, scale=sn_[:, last:last + 1]),
                                 reads=[bvr[i3], btab], writes=[bstt[i3]])

                        def s6():
                            k.op("act", lambda e: e.activation(stR[:, b:b + 1], vr[i3][:, last:last + 1], AF.Identity, bias=stt_[i3][:, 0:1], scale=cs_[:, last:last + 1]),
                                 reads=[bvr[i3], btab, bstt[i3]], writes=[bstate[b]])
                            k.op("act", lambda e: e.activation(stI[:, b:b + 1], vi[i3][:, last:last + 1], AF.Identity, bias=stt_[i3][:, 1:2], scale=cs_[:, last:last + 1]),
                                 reads=[bvi[i3], btab, bstt[i3]], writes=[bstate[b]])
                            k.op("pe", lambda e: e.matmul(py[:, 0:128], WcR[:, b, :], q1[i3][:], start=(q == 0), stop=False), reads=[bWc, bq1[i3]], writes=[pyb], inc=False)
                            k.op("pe", lambda e: e.matmul(py[:, 0:128], WcR[:, b, :], q2[i3][:], start=False, stop=False), reads=[bWc, bq2[i3]], writes=[pyb], inc=False)
                            k.op("pe", lambda e: e.matmul(py[:, 0:128], WcI[:, b, :], q3[i3][:], start=False, stop=False), reads=[bWc, bq3[i3]], writes=[pyb], inc=False)
                            k.op("pe", lambda e: e.matmul(py[:, 0:128], WcI[:, b, :], q4[i3][:], start=False, stop=(q == 3)), reads=[bWc, bq4[i3]], writes=[pyb], inc=(q == 3))
                            if q != 3:
                                return
                            if d == 0:
                                o, bo = yo[c % 2], byo[c % 2]
                                k.op("act", lambda e: e.copy(o[:], py[:, 0:128]), reads=[pyb], writes=[bo])
                                k.dma("pool", yfT[c * 128:(c + 1) * 128, t0:t0 + n], o[:], reads=[bo], writes=[Buf()])
                                return
                            if vsel == 1 and not ctx_out:
                                return
                            f_, bf_ = yf[c % 2], byf[c % 2]
                            k.dma("sp", f_[:], yfT[c * 128:(c + 1) * 128, t0:t0 + n], writes=[bf_])
                            k.op("dve", lambda e: e.tensor_tensor(f_[:], f_[:], py[:, 0:128], ALU.add), reads=[pyb, bf_], writes=[bf_])
                            k.op("dve", lambda e: e.scalar_tensor_tensor(f_[:], hT[:, c, :], dskT[:, c:c + 1], f_[:], ALU.mult, ALU.add), reads=[bh, bdsk_, bf_], writes=[bf_])
                            k.op("act", lambda e: e.activation(gT[:, c, :], f_[:], AF.Gelu_apprx_tanh), reads=[bf_], writes=[bg])
                            if c != 7:
                                return
                            for c2 in range(8):
                                pa_, pab_ = next_ps()
                                pg_, pgb_ = next_ps()
                                for kc in range(8):
                                    k.op("pe", lambda e, kc=kc, c2=c2, pa_=pa_: e.matmul(pa_[:, 0:128], gw_[:, kc, c2 * 128:(c2 + 1) * 128], gT[:, kc, :], start=(kc == 0), stop=(kc == 7)),
                                         reads=[bgw_, bg], writes=[pab_], inc=(kc == 7))
                                for kc in range(8):
                                    k.op("pe", lambda e, kc=kc, c2=c2, pg_=pg_: e.matmul(pg_[:, 0:128], gw_[:, kc, D + c2 * 128:D + (c2 + 1) * 128], gT[:, kc, :], start=(kc == 0), stop=(kc == 7)),
                                         reads=[bgw_, bg], writes=[pgb_], inc=(kc == 7))
                                a_, ba_ = ga[c2 % 2], bga[c2 % 2]
                                k.op("act", lambda e, pg_=pg_, a_=a_, c2=c2: e.activation(a_[:], pg_[:, 0:128], AF.Sigmoid, bias=gbias[:, 8 + c2:9 + c2], scale=1.0), reads=[pgb_, bgb_], writes=[ba_])
                                k.op("dve", lambda e, pa_=pa_, a_=a_, c2=c2: e.scalar_tensor_tensor(a_[:], pa_[:, 0:128], gbias[:, c2:c2 + 1], a_[:], ALU.add, ALU.mult), reads=[pab_, bgb_, ba_], writes=[ba_])
                                k.op("dve", lambda e, a_=a_, c2=c2: e.scalar_tensor_tensor(x[:, c2, :], a_[:], mod[:, 16 + c2, vsel:vsel + 1], x[:, c2, :], ALU.mult, ALU.add),
                                     reads=[ba_, b_mod, b_x], writes=[b_x])
                            k.dma("pool", xres[:, t0:t0 + n].rearrange("(c p) t -> p c t", p=128), x[:, :, :n], reads=[b_x], writes=[Buf()])

                        return [s0, s1, s2, s3, s4, s5, s6]

                    un = 0
                    for ti, (t0, n, vsel) in enumerate(order):
                        for c in range(8):
                            for q in range(4):
                                units.append(make_unit(ti, t0, n, vsel, c, q, un))
                                un += 1
                    NS = 7
                    for step in range(len(units) + NS - 1):
                        for j in range(NS - 1, -1, -1):
                            u = step - j
                            if 0 <= u < len(units):
                                units[u][j]()
                    pspool[0] = list(range(8))
                k.barrier()
                if debug == "s5f" and d == 0:
                    return

        def mla_layer(li, ctx_out):
            TT = 512
            SCALE = 96.0 ** -0.5
            with ExitStack() as L:
                qnT = sb("qnT", [128, 3, NLAT], BF16, L); bqn = Buf()
                kvnT = sb("kvnT", [128, 2, NTOK], BF16, L); bkvn = Buf()
                KT = sb("KT", [128, NTOK], BF16, L); bKr = Buf(); bKn = Buf()
                k.op("pool", lambda e: e.memset(KT[96:97, :], 1.0), writes=[bKr])
                with ExitStack() as st:
                    w, bw = load_w(st, "mlin", mla_in_w[0], 8, 672)
                    wkr = sb("wkr", [128, 8, 96], BF16, st); wkrs = sb("wkrs", [128, 8, 96], BF16, st); bwk = Buf()
                    k.op("pool", lambda e: e.memset(wkr[:], 0.0), writes=[bwk])
                    k.op("pool", lambda e: e.memset(wkrs[:], 0.0), writes=[bwk])
                    k.dma("pool", wkr[:, :, 64:96], mla_in_w[0][:, 640:672].rearrange("(c p) n -> p c n", p=128), writes=[bwk], allow_slow_non_contiguous=True)
                    k.dma("pool", wkrs[:, :, 64:96], mla_inw_sw.rearrange("(c p) n -> p c n", p=128), writes=[bwk], allow_slow_non_contiguous=True)
                    qnw = sb("qnw", [128, 3], F32, st); kvnw = sb("kvnw", [128, 2], F32, st); bnw_ = Buf()
                    k.dma("sp", qnw[:], mla_q_norm_w[0].rearrange("(c p) -> p c", p=128), writes=[bnw_], allow_slow_non_contiguous=True)
                    k.dma("sp", kvnw[:], mla_kv_norm_w[0].rearrange("(c p) -> p c", p=128), writes=[bnw_], allow_slow_non_contiguous=True)
                    nsc = norm_scratch(st, TT)
                    xt = [sb("ma_x%d" % i, [128, 8, TT], F32, st) for i in range(2)]; bx = [Buf(), Buf()]
                    hT = sb("ma_h", [128, 8, TT], BF16, st); bh = Buf()
                    ql = sb("ma_ql", [128, 3, TT], F32, st); bql = Buf()
                    sq = sb("ma_sq", [128, TT], F32, st); bsq = Buf()
                    rs = sb("ma_rs", [128, TT], F32, st); brs = Buf()
                    rc = [sb("ma_rc%d" % i, [128, TT], F32, st) for i in range(2)]; rsn = [sb("ma_rsn%d" % i, [128, TT], F32, st) for i in range(2)]; brc = [Buf(), Buf()]
                    t1 = sb("ma_t1", [128, TT], F32, st); t2 = sb("ma_t2", [128, TT], F32, st); bt1 = Buf(); bt2 = Buf()

                    def lat_norm(ncl, col0, nw_t, dim, out_t, out_b, tsl):
                        pss, pssb = next_ps()
                        for c in range(ncl):
                            p, pb = next_ps()
                            for kc in range(8):
                                k.op("pe", lambda e, p=p, kc=kc, c=c: e.matmul(p[:, :n], w[:, kc, col0 + c * 128:col0 + (c + 1) * 128], hT[:, kc, :n], start=(kc == 0), stop=(kc == 7)),
                                     reads=[bw, bh], writes=[pb], inc=(kc == 7))
                            k.op("act", lambda e, p=p, c=c: e.copy(ql[:, c, :n], p[:, :n]), reads=[pb], writes=[bql])
                            k.op("act", lambda e, p=p: e.activation(sq[:, :n], p[:, :n], AF.Square), reads=[pb], writes=[bsq])
                            k.op("pe", lambda e, pss=pss, c=c: e.matmul(pss[:, :n], ones_f[:], sq[:, :n], start=(c == 0), stop=(c == ncl - 1)), reads=[b_ones, bsq], writes=[pssb])
                        k.op("act", lambda e, pss=pss: e.activation(rs[:, :n], pss[:, :n], AF.Sqrt, bias=EPS, scale=1.0 / dim), reads=[pssb], writes=[brs])
                        k.op("dve", lambda e: e.reciprocal(rs[:, :n], rs[:, :n]), reads=[brs], writes=[brs])
                        for c in range(ncl):
                            k.op("dve", lambda e, c=c: e.scalar_tensor_tensor(out_t[:, c, tsl], ql[:, c, :n], nw_t[:, c:c + 1], rs[:, :n], ALU.mult, ALU.mult),
                                 reads=[bql, bnw_, brs], writes=[out_b])

                    for ti, (t0, n, vsel) in enumerate(seg_tiles(TT)):
                        x, b_x = xt[ti % 2], bx[ti % 2]
                        k.dma("sp", x[:, :, :n], xres[:, t0:t0 + n].rearrange("(c p) t -> p c t", p=128), writes=[b_x])
                        norm_mod(nsc, x, b_x, n, vsel, A1, 0, hT, bh)
                        if vsel == 0:
                            lat_norm(3, 0, qnw, 384.0, qnT, bqn, slice(t0 - NCTX, t0 - NCTX + n))
                        lat_norm(2, 384, kvnw, 256.0, kvnT, bkvn, slice(t0, t0 + n))
                        pA, pAb = next_ps()
                        for kc in range(8):
                            k.op("pe", lambda e, pA=pA, kc=kc: e.matmul(pA[0:96, :n], wkr[:, kc, :], hT[:, kc, :n], start=(kc == 0), stop=(kc == 7)), reads=[bwk, bh], writes=[pAb], inc=(kc == 7))
                        if vsel == 1:
                            k.op("act", lambda e, pA=pA: e.copy(KT[64:96, t0:t0 + n], pA[64:96, :n]), reads=[pAb], writes=[bKr])
                        else:
                            pB, pBb = next_ps()
                            for kc in range(8):
                                k.op("pe", lambda e, pB=pB, kc=kc: e.matmul(pB[0:96, :n], wkrs[:, kc, :], hT[:, kc, :n], start=(kc == 0), stop=(kc == 7)), reads=[bwk, bh], writes=[pBb], inc=(kc == 7))
                            i2 = ti % 2
                            k.dma("sp", rc[i2][64:96, :n], ropeC_in[:, t0 - NCTX:t0 - NCTX + n], writes=[brc[i2]])
                            k.dma("sp", rsn[i2][64:96, :n], ropeS_in[:, t0 - NCTX:t0 - NCTX + n], writes=[brc[i2]])
                            k.op("dve", lambda e, pA=pA, i2=i2: e.tensor_tensor(t1[64:96, :n], pA[64:96, :n], rc[i2][64:96, :n], ALU.mult), reads=[pAb, brc[i2]], writes=[bt1])
                            k.op("dve", lambda e, pB=pB, i2=i2: e.tensor_tensor(t2[64:96, :n], pB[64:96, :n], rsn[i2][64:96, :n], ALU.mult), reads=[pBb, brc[i2]], writes=[bt2])
                            k.op("pool", lambda e: e.tensor_tensor(KT[64:96, t0:t0 + n], t1[64:96, :n], t2[64:96, :n], ALU.add), reads=[bt1, bt2], writes=[bKr])
                k.barrier()
                with ExitStack() as st:
                    kvbw, bkvbw = load_w(st, "kvbw", mla_kvb_w[0], 2, 2048)
                    qbw, bqbw = load_w(st, "qbw", mla_qb_w[0], 3, 1536)
                    qbws, bqbws = load_w(st, "qbws", mla_qbw_sw, 3, 512)
                    e96 = sb("e96", [128, 128], BF16, st); be96 = Buf()
                    k.dma("pool", e96[:], e96_in, writes=[be96])
                    odm = sb("odm", [128, 2, 128], BF16, st); bodm = Buf()
                    k.dma("pool", odm[:], odm_in.rearrange("a p m -> p a m"), writes=[bodm])
                    wv = [sb("wv%d" % i, [128, 2, 128], BF16, st) for i in range(2)]; bwv = [Buf(), Buf()]
                    wqr = sb("wqr", [128, 3, 96], BF16, st); wqrs = sb("wqrs", [128, 3, 96], BF16, st); bwq = Buf()
                    for t_ in (wv[0], wv[1], wqr, wqrs):
                        k.op("pool", lambda e, t_=t_: e.memset(t_[:], 0.0), writes=[bwv[0], bwv[1], bwq])
                    QT = sb("QT", [128, NLAT], BF16, st); bQ = Buf()
                    Vt = sb("Vt", [128, NCH, 128], BF16, st); bV = Buf()
                    sqb = sb("mb_sq", [128, TT], BF16, st); bsqb = Buf()
                    kk = sb("mb_kk", [128, TT], F32, st); bkk = Buf()
                    kmx = sb("mb_kmx", [128, 2], F32, st); bkmx = Buf()
                    rc = [sb("mb_rc%d" % i, [128, TT], F32, st) for i in range(2)]; rsn = [sb("mb_rsn%d" % i, [128, TT], F32, st) for i in range(2)]; brc = [Buf(), Buf()]
                    t1 = sb("mb_t1", [128, TT], F32, st); t2 = sb("mb_t2", [128, TT], F32, st); bt1 = Buf(); bt2 = Buf()
                    Pt = [sb("mb_P%d" % i, [128, TT], BF16, st) for i in range(3)]; bP = [Buf() for _ in range(3)]
                    rd = sb("mb_rd", [128, TT], F32, st); brd = Buf()
                    xs_ = sb("mb_xs", [128, TT], F32, st); bxs_ = Buf()
                    swp = sb("mb_swp", [128, 128], F32, st); bswp = Buf()
                    k.dma("sp", swp[:], swp_in, writes=[bswp])
                    ot = [sb("mb_o%d" % i, [128, TT], BF16, st) for i in range(2)]; bot = [Buf(), Buf()]
                    pspool[0] = [4, 5, 6, 7]
                    ktiles = seg_tiles(TT)
                    pn = 0
                    for h in range(16):
                        par = h % 2
                        off = par * 64
                        k.op("pool", lambda e, h=h, par=par, off=off: e.tensor_copy(wv[par][:, :, off:off + 64], kvbw[:, :, h * 128 + 64:h * 128 + 128]), reads=[bkvbw], writes=[bwv[par]])
                        k.op("pool", lambda e, h=h: e.tensor_copy(wqr[:, :, 64:96], qbw[:, :, h * 96 + 64:h * 96 + 96]), reads=[bqbw], writes=[bwq])
                        k.op("pool", lambda e, h=h: e.tensor_copy(wqrs[:, :, 64:96], qbws[:, :, h * 32:h * 32 + 32]), reads=[bqbws], writes=[bwq])
                        k.op("pool", lambda e: e.memset(kmx[:], 0.0), writes=[bkmx])
                        for (t0, n, vsel) in ktiles:
                            p, pb = next_ps()
                            for c in range(2):
                                k.op("pe", lambda e, p=p, c=c, h=h: e.matmul(p[0:64, :n], kvbw[:, c, h * 128:h * 128 + 64], kvnT[:, c, t0:t0 + n], start=(c == 0), stop=(c == 1)),
                                     reads=[bkvbw, bkvn], writes=[pb], inc=(c == 1))
                            k.op("act", lambda e, p=p: e.copy(KT[0:64, t0:t0 + n], p[0:64, :n]), reads=[pb], writes=[bKn])
                            k.op("act", lambda e: e.activation(sqb[0:96, :n], KT[0:96, t0:t0 + n], AF.Square), reads=[bKn, bKr], writes=[bsqb])
                            p2, p2b = next_ps()
                            k.op("pe", lambda e, p2=p2: e.matmul(p2[0:97, :n], e96[0:96, 0:97], sqb[0:96, :n], start=True, stop=True), reads=[be96, bsqb], writes=[p2b])
                            k.op("dve", lambda e, p2=p2: e.reduce_max(kmx[96:97, 1:2], p2[96:97, :n], AX.X), reads=[p2b], writes=[bkmx])
                            k.op("dve", lambda e: e.tensor_tensor(kmx[96:97, 0:1], kmx[96:97, 0:1], kmx[96:97, 1:2], ALU.max), reads=[bkmx], writes=[bkmx])
                        k.op("act", lambda e: e.activation(kmx[96:97, 0:1], kmx[96:97, 0:1], AF.Sqrt), reads=[bkmx], writes=[bkmx])
                        k.op("dve", lambda e: e.tensor_scalar(kmx[96:97, 0:1], kmx[96:97, 0:1], -1.0, None, ALU.mult), reads=[bkmx], writes=[bkmx])
                        for b4 in range(0, NCH, 4):
                            p, pb = next_ps()
                            nb = min(4, NCH - b4)
                            for j in range(nb):
                                blk = b4 + j
                                for c in range(2):
                                    k.op("pe", lambda e, p=p, j=j, blk=blk, c=c, par=par: e.matmul(p[:, j * 128:(j + 1) * 128], kvnT[:, c, blk * 128:(blk + 1) * 128], wv[par][:, c, :],
                                                                                                start=(c == 0), stop=(c == 1)),
                                         reads=[bkvn, bwv[par]], writes=[pb], inc=(c == 1 and j == nb - 1))
                            k.op("dve", lambda e, p=p, b4=b4, nb=nb, off=off: e.tensor_copy(Vt[:, b4:b4 + nb, off:off + 64], p[:, 0:nb * 128].rearrange("p (j m) -> p j m", m=128)[:, :, off:off + 64]),
                                 reads=[pb], writes=[bV])
                        k.op("pool", lambda e, off=off: e.memset(Vt[:, :, 64 - off:128 - off], 1.0), writes=[bV])
                        for qi in range(NLAT // TT):
                            q0 = qi * TT
                            p, pb = next_ps()
                            for c in range(3):
                                k.op("pe", lambda e, p=p, c=c, h=h: e.matmul(p[0:64, :], qbw[:, c, h * 96:h * 96 + 64], qnT[:, c, q0:q0 + TT], start=(c == 0), stop=(c == 2)),
                                     reads=[bqbw, bqn], writes=[pb], inc=(c == 2))
                            k.op("act", lambda e, p=p: e.copy(QT[0:64, q0:q0 + TT], p[0:64, :]), reads=[pb], writes=[bQ])
                            pA, pAb = next_ps()
                            pB, pBb = next_ps()
                            for c in range(3):
                                k.op("pe", lambda e, pA=pA, c=c: e.matmul(pA[0:96, :], wqr[:, c, :], qnT[:, c, q0:q0 + TT], start=(c == 0), stop=(c == 2)), reads=[bwq, bqn], writes=[pAb], inc=(c == 2))
                            for c in range(3):
                                k.op("pe", lambda e, pB=pB, c=c: e.matmul(pB[0:96, :], wqrs[:, c, :], qnT[:, c, q0:q0 + TT], start=(c == 0), stop=(c == 2)), reads=[bwq, bqn], writes=[pBb], inc=(c == 2))
                            i2 = qi % 2
                            k.dma("sp", rc[i2][64:96, :], ropeC_in[:, q0:q0 + TT], writes=[brc[i2]])
                            k.dma("sp", rsn[i2][64:96, :], ropeS_in[:, q0:q0 + TT], writes=[brc[i2]])
                            k.op("dve", lambda e, pA=pA, i2=i2: e.tensor_tensor(t1[64:96, :], pA[64:96, :], rc[i2][64:96, :], ALU.mult), reads=[pAb, brc[i2]], writes=[bt1])
                            k.op("dve", lambda e, pB=pB, i2=i2: e.tensor_tensor(t2[64:96, :], pB[64:96, :], rsn[i2][64:96, :], ALU.mult), reads=[pBb, brc[i2]], writes=[bt2])
                            k.op("pool", lambda e: e.tensor_tensor(QT[64:96, q0:q0 + TT], t1[64:96, :], t2[64:96, :], ALU.add), reads=[bt1, bt2], writes=[bQ])
                            k.op("act", lambda e: e.activation(sqb[0:96, :], QT[0:96, q0:q0 + TT], AF.Square), reads=[bQ], writes=[bsqb])
                            p2, p2b = next_ps()
                            k.op("pe", lambda e, p2=p2: e.matmul(p2[0:97, :], e96[0:96, 0:97], sqb[0:96, :], start=True, stop=True), reads=[be96, bsqb], writes=[p2b])
                            k.op("act", lambda e, p2=p2: e.activation(kk[96:97, :], p2[96:97, :], AF.Sqrt), reads=[p2b], writes=[bkk])
                            k.op("dve", lambda e: e.tensor_scalar(QT[96:97, q0:q0 + TT], kk[96:97, :], kmx[96:97, 0:1], None, ALU.mult), reads=[bkk, bkmx], writes=[bQ])
                        for qi in range(NLAT // TT):
                            q0 = qi * TT
                            pnum, pnb = ps[qi % 2], psb[qi % 2]
                            sbank = {}

                            def emit_S(blk):
                                nonlocal pn
                                p, pb = next_ps()
                                k.op("pe", lambda e, p=p, blk=blk: e.matmul(p[:, :], KT[0:97, blk * 128:(blk + 1) * 128], QT[0:97, q0:q0 + TT], start=True, stop=True),
                                     reads=[bKn, bKr, bQ], writes=[pb])
                                i3 = pn % 3
                                pn += 1
                                k.op("act", lambda e, p=p, i3=i3: e.activation(Pt[i3][:], p[:, :], AF.Exp, scale=SCALE), reads=[pb], writes=[bP[i3]])
                                sbank[blk] = i3
                            emit_S(0)
                            emit_S(1)
                            for blk in range(NCH):
                                i3 = sbank.pop(blk)
                                k.op("pe", lambda e, blk=blk, i3=i3, pnum=pnum: e.matmul(pnum[:, :], Vt[:, blk, :], Pt[i3][:], start=(blk == 0), stop=(blk == NCH - 1)),
                                     reads=[bV, bP[i3]], writes=[pnb])
                                if blk + 2 < NCH:
                                    emit_S(blk + 2)
                            k.op("act", lambda e, pnum=pnum: e.copy(xs_[:], pnum[:, :]), reads=[pnb], writes=[bxs_])
                            psw, pswb = ps[2 + qi % 2], psb[2 + qi % 2]
                            k.op("pe", lambda e, psw=psw: e.matmul(psw[:, :], swp[:], xs_[:], start=True, stop=True), reads=[bswp, bxs_], writes=[pswb])
                            k.op("dve", lambda e, psw=psw, off=off: e.reciprocal(rd[off:off + 64, :], psw[off:off + 64, :]), reads=[pswb], writes=[brd])
                            o_, b_o = ot[qi % 2], bot[qi % 2]
                            k.op("dve", lambda e, off=off, o_=o_: e.tensor_tensor(o_[off:off + 64, :], xs_[off:off + 64, :], rd[off:off + 64, :], ALU.mult),
                                 reads=[bxs_, brd], writes=[b_o])
                            k.dma("sp", scrM[h * 64:(h + 1) * 64, NCTX + q0:NCTX + q0 + TT], o_[off:off + 64, :], reads=[b_o], writes=[Buf()])
                    pspool[0] = list(range(8))
                k.barrier()
            k.barrier()
            outproj_phase(mla_out_w[0], 8, scrM, ctx_out)

        with ExitStack() as st:
            cp = [sb("cp%d" % i, [128, 8, 512], F32, st) for i in range(2)]; bcp = [Buf(), Buf()]
            for ti, (t0, n, vsel) in enumerate(seg_tiles(512)):
                k.dma("sp", cp[ti % 2][:, :, :n], xT_in[:, t0:t0 + n].rearrange("(c p) t -> p c t", p=128), writes=[bcp[ti % 2]])
                k.dma("pool", xres[:, t0:t0 + n].rearrange("(c p) t -> p c t", p=128), cp[ti % 2][:, :, :n], reads=[bcp[ti % 2]], writes=[Buf()])
        k.barrier()
        for li in layers:
            ctx_out = li < 3
            compute_mod(li)
            if debug == "mod_only":
                continue
            if debug == "ffn_only":
                ffn_phase(li, ctx_out)
                continue
            if li == 0:
                ssd_layer(li, ctx_out)
                if debug in ("ssdA", "ssd1", "ssd2"):
                    continue
            if li == 1:
                lru_layer(li, ctx_out)
            if li == 3:
                mla_layer(li, ctx_out)
            if li == 2:
                s5_layer(li, ctx_out)
                if debug == "s5f" and debug_sub in ("pre", "pre2", "pre3", "p_a"):
                    k.barrier()
                    return nc, list(di.keys())
                if debug == "s5f":
                    with ExitStack() as st:
                        cp = [sb("dq%d" % i, [128, 8, 512], F32, st) for i in range(2)]; bcp = [Buf(), Buf()]
                        for ti, (t0, n, vsel) in enumerate(seg_tiles(512)):
                            if vsel == 1:
                                continue
                            k.dma("sp", cp[ti % 2][:, :, :n], scrA[0:D, t0:t0 + n].rearrange("(c p) t -> p c t", p=128), writes=[bcp[ti % 2]])
                            k.dma("pool", outT[:, t0 - NCTX:t0 - NCTX + n].rearrange("(c p) t -> p c t", p=128), cp[ti % 2][:, :, :n], reads=[bcp[ti % 2]], writes=[Buf()])
                    k.barrier()
                    print("ninst", k.ninst)
                    return nc, list(di.keys())
            ffn_phase(li, ctx_out)
        if do_final:
            final_phase()
        else:
            with ExitStack() as st:
                cp = [sb("dp%d" % i, [128, 8, 512], F32, st) for i in range(2)]; bcp = [Buf(), Buf()]
                for ti, (t0, n, vsel) in enumerate(seg_tiles(512)):
                    if vsel == 1:
                        continue
                    k.dma("sp", cp[ti % 2][:, :, :n], xres[:, t0:t0 + n].rearrange("(c p) t -> p c t", p=128), writes=[bcp[ti % 2]])
                    k.dma("pool", outT[:, t0 - NCTX:t0 - NCTX + n].rearrange("(c p) t -> p c t", p=128), cp[ti % 2][:, :, :n], reads=[bcp[ti % 2]], writes=[Buf()])
            k.barrier()
        k.barrier()
        print("ninst", k.ninst)
    return nc, list(di.keys())


def make_in_map(inputs, b, names, x_override=None, ctx_override=None):
    xb = inputs["x"][b] if x_override is None else x_override
    cb = inputs["ctx"][b] if ctx_override is None else ctx_override
    xT = np.ascontiguousarray(np.concatenate([cb, xb], axis=0).T.astype(np.float32))
    cc = np.ascontiguousarray(np.stack([inputs["c"][b], inputs["c_ctx"]], axis=1).astype(np.float32))
    ii = np.arange(128)
    U = (ii[:, None] <= ii[None, :]).astype(np.float32)
    masks = np.stack([U, U.T.copy(), (ii[:, None] > ii[None, :]).astype(np.float32), (ii[:, None] < ii[None, :]).astype(np.float32)], axis=0)
    m = {"xT": xT, "cc": cc, "ident": np.eye(128, dtype=np.float32), "masks": np.ascontiguousarray(masks)}
    m["ramp"] = np.ascontiguousarray(np.tile(np.arange(1, 129, dtype=np.float32)[None, :], (128, 1)))
    sel = np.zeros((128, 8), np.float32)
    for gl8 in range(8):
        sel[gl8 * 16:(gl8 + 1) * 16, gl8] = 1.0
    m["sel"] = sel
    perm = np.concatenate([np.arange(8, 16), np.arange(0, 8), np.arange(24, 32), np.arange(16, 24)])
    m["mla_inw_sw"] = np.ascontiguousarray(np.asarray(inputs["mla_in_w"], np.float32)[0][:, 640:672][:, perm])
    qb = np.asarray(inputs["mla_qb_w"], np.float32)[0].reshape(384, 16, 96)
    m["mla_qbw_sw"] = np.ascontiguousarray(qb[:, :, 64:96][:, :, perm].reshape(384, 512))
    rows = NLAT // 64
    row = np.repeat(np.arange(rows, dtype=np.float32), 64)
    col = np.tile(np.arange(64, dtype=np.float32), rows)
    inv_freq = (np.float32(10000.0) ** (-np.arange(8, dtype=np.float32) / np.float32(8))).astype(np.float32)
    ang = np.stack([row[:, None] * inv_freq, col[:, None] * inv_freq], axis=1).astype(np.float32)
    cs, sn = np.cos(ang).astype(np.float32), np.sin(ang).astype(np.float32)
    C = np.zeros((32, NLAT), np.float32); S = np.zeros((32, NLAT), np.float32)
    for half in range(2):
        for part in range(2):
            r0 = half * 16 + part * 8
            C[r0:r0 + 8] = cs[:, half, :].T
            S[r0:r0 + 8] = (-sn[:, half, :].T) if part == 0 else sn[:, half, :].T
    m["ropeC"] = C; m["ropeS"] = S
    e96 = np.zeros((128, 128), np.float32); e96[0:96, 96] = 1.0
    m["e96"] = e96
    odm = np.zeros((2, 128, 128), np.float32); odm[0, :, 0:64] = 1.0; odm[1, :, 64:128] = 1.0
    m["odm"] = odm
    swp = np.zeros((128, 128), np.float32)
    swp[(np.arange(128) + 64) % 128, np.arange(128)] = 1.0
    m["swp"] = swp
    for n in names:
        if n not in m:
            m[n] = np.ascontiguousarray(np.asarray(inputs[n], dtype=np.float32))
    return m


def kernel(**inputs):
    inputs = {k_: np.asarray(v) for k_, v in inputs.items()}
    nc, names = build()
    nb = inputs["x"].shape[0]
    in_maps = [make_in_map(inputs, b, names) for b in range(nb)]
    res = run_bass_kernel_spmd(nc, in_maps, core_ids=list(range(nb)))
    out = np.stack([np.ascontiguousarray(r["outT"].T) for r in res.results], axis=0)
    return out.astype(np.float32)
```

```python
import numpy as np
from contextlib import ExitStack
import concourse.bass as bass
import concourse.mybir as mybir
from concourse.bass_utils import run_bass_kernel_spmd

F32 = mybir.dt.float32
BF16 = mybir.dt.bfloat16
I32 = mybir.dt.int32
AF = mybir.ActivationFunctionType
ALU = mybir.AluOpType
AX = mybir.AxisListType

D = 1024
NCTX = 256
NLAT = 8192
NTOK = NCTX + NLAT
FH = 2816
EPS = 1e-6


class Buf:
    __slots__ = ("name", "lw", "rd")

    def __init__(self, name=""):
        self.name = name
        self.lw = None
        self.rd = []


ATTACH_WAITS = True


class K:
    NDMA = 8

    def __init__(self, nc, es):
        self.nc = nc
        self.es = es
        self.eng = {"pe": nc.tensor, "dve": nc.vector, "act": nc.scalar, "pool": nc.gpsimd, "sp": nc.sync}
        self.sem = {}
        self.cnt = {}
        for e in self.eng:
            self.sem[e] = es.enter_context(nc.semaphore("s_" + e))
            self.cnt[e] = 0
        self.dslots = {}
        for q in ("sp", "pool", "act"):
            self.dslots[q] = []
            for j in range(self.NDMA):
                key = "d_%s_%d" % (q, j)
                self.sem[key] = es.enter_context(nc.semaphore(key))
                self.cnt[key] = 0
                self.dslots[q].append(key)
        self.dnext = {q: 0 for q in self.dslots}
        self.seen = {e: {} for e in self.eng}
        self.ninst = 0

    def _need(self, e, deps, attach=False):
        best = {}
        for (k, v) in deps:
            if k == e and e == "pe":
                continue
            if best.get(k, 0) < v:
                best[k] = v
        todo = [(k, v) for k, v in best.items() if self.seen[e].get(k, 0) < v]
        held = None
        if attach and ATTACH_WAITS and todo:
            held = todo.pop()
        for k, v in todo:
            self.eng[e].wait_ge(self.sem[k], v)
            self.seen[e][k] = v
            self.ninst += 1
        if held is not None:
            self.seen[e][held[0]] = held[1]
        return held

    def _deps(self, reads, writes):
        deps = []
        for b in reads:
            if b.lw is not None:
                deps.append(b.lw)
        for b in writes:
            if b.lw is not None:
                deps.append(b.lw)
            deps.extend(b.rd)
        return deps

    def op(self, e, fn, reads=(), writes=(), inc=True):
        held = self._need(e, self._deps(reads, writes), attach=True)
        ins = fn(self.eng[e])
        if held is not None:
            ins._wait_ge(self.sem[held[0]], held[1])
        self.ninst += 1
        if inc:
            self.cnt[e] += 1
            ins.then_inc(self.sem[e], 1)
            v = self.cnt[e]
        else:
            v = self.cnt[e] + 1
        for b in reads:
            b.rd.append((e, v))
            if len(b.rd) > 24:
                b.rd = self._compact(b.rd)
        for b in writes:
            b.lw = (e, v)
            b.rd = []
        return ins

    @staticmethod
    def _compact(rd):
        best = {}
        for (k, v) in rd:
            if best.get(k, 0) < v:
                best[k] = v
        return list(best.items())

    def dma(self, q, out, in_, reads=(), writes=(), **kw):
        if out.dtype != in_.dtype:
            q = "pool"
        slot = self.dslots[q][self.dnext[q] % self.NDMA]
        self.dnext[q] += 1
        deps = self._deps(reads, writes)
        if self.cnt[slot] > 0:
            deps.append((slot, self.cnt[slot]))
        self._need(q, deps)
        ins = self.eng[q].dma_start(out=out, in_=in_, **kw)
        self.ninst += 1
        self.cnt[slot] += 16
        ins.then_inc(self.sem[slot], 16)
        v = self.cnt[slot]
        for b in reads:
            b.rd.append((slot, v))
            if len(b.rd) > 24:
                b.rd = self._compact(b.rd)
        for b in writes:
            b.lw = (slot, v)
            b.rd = []
        return ins

    def barrier(self):
        deps = [(k, v) for k, v in self.cnt.items() if v > 0]
        for e in self.eng:
            self._need(e, deps)


def build(layers=(0, 1, 2, 3), do_final=True, debug=None, debug_sub=None):
    nc = bass.Bass("TRN2", target_bir_lowering=False)
    di = {}

    def inp(name, shape):
        di[name] = nc.dram_tensor(name, list(shape), F32, kind="ExternalInput").ap()
        return di[name]

    xT_in = inp("xT", [D, NTOK])
    cc_in = inp("cc", [D, 2])
    ident_in = inp("ident", [128, 128])
    ada_w = inp("ada_w", [4, D, 6 * D]); ada_b = inp("ada_b", [4, 6 * D])
    norm1_w = inp("norm1_w", [4, D]); norm2_w = inp("norm2_w", [4, D])
    ffn_w13 = inp("ffn_w13", [4, D, 2 * FH]); ffn_w2 = inp("ffn_w2", [4, FH, D])
    lru_in_w = inp("lru_in_w", [1, D, 2560]); lru_conv_w = inp("lru_conv_w", [1, 4, 1280]); lru_conv_b = inp("lru_conv_b", [1, 1280])
    lru_gate_w = inp("lru_gate_w", [1, 2, 10, 128, 256]); lru_gate_b = inp("lru_gate_b", [1, 2, 10, 256])
    lru_a_param = inp("lru_a_param", [1, 2, 1280]); lru_out_w = inp("lru_out_w", [1, 1280, D])
    masks_in = inp("masks", [4, 128, 128])
    ramp_in = inp("ramp", [128, 128]); sel_in = inp("sel", [128, 8])
    s5_lre = inp("s5_lambda_re", [1, 2, 64, 64]); s5_lim = inp("s5_lambda_im", [1, 2, 64, 64]); s5_ls = inp("s5_log_step", [1, 2, 64])
    s5_bre = inp("s5_b_re", [1, 64, 64, 16]); s5_bim = inp("s5_b_im", [1, 64, 64, 16])
    s5_cre = inp("s5_c_re", [1, 2, 64, 16, 64]); s5_cim = inp("s5_c_im", [1, 2, 64, 16, 64])
    s5_d = inp("s5_d", [1, D]); s5_glu_w = inp("s5_glu_w", [1, D, 2 * D]); s5_glu_b = inp("s5_glu_b", [1, 2 * D])
    m2_in_w = inp("m2_in_w", [1, D, 5184]); m2_conv_w = inp("m2_conv_w", [1, 4, 3072]); m2_conv_b = inp("m2_conv_b", [1, 3072])
    m2_dt_bias = inp("m2_dt_bias", [1, 2, 32]); m2_a_log = inp("m2_a_log", [1, 2, 32]); m2_d = inp("m2_d", [1, 32])
    m2_norm_w = inp("m2_norm_w", [1, 2048]); m2_out_w = inp("m2_out_w", [1, 2048, D])
    mla_in_w = inp("mla_in_w", [1, D, 672]); mla_q_norm_w = inp("mla_q_norm_w", [1, 384]); mla_kv_norm_w = inp("mla_kv_norm_w", [1, 256])
    mla_qb_w = inp("mla_qb_w", [1, 384, 1536]); mla_kvb_w = inp("mla_kvb_w", [1, 256, 2048]); mla_out_w = inp("mla_out_w", [1, D, D])
    mla_inw_sw = inp("mla_inw_sw", [D, 32]); mla_qbw_sw = inp("mla_qbw_sw", [384, 512])
    ropeC_in = inp("ropeC", [32, NLAT]); ropeS_in = inp("ropeS", [32, NLAT])
    e96_in = inp("e96", [128, 128]); odm_in = inp("odm", [2, 128, 128]); swp_in = inp("swp", [128, 128])
    final_norm_w = inp("final_norm_w", [D])
    outT = nc.dram_tensor("outT", [D, NLAT], F32, kind="ExternalOutput").ap()
    xres = nc.dram_tensor("xres", [D, NTOK], F32).ap()
    scrA = nc.dram_tensor("scrA", [2048, NTOK], F32).ap()
    scrB = nc.dram_tensor("scrB", [3072, NTOK], F32).ap()
    scrM = nc.dram_tensor("scrM", [2048, NTOK], BF16).ap()
    NCH = NTOK // 128
    scrZ = nc.dram_tensor("scrZ", [NTOK, 2048], F32).ap()
    scrDT = nc.dram_tensor("scrDT", [NTOK, 64], F32).ap()
    scrS = nc.dram_tensor("scrS", [2, NCH, 128, 2048], BF16).ap()
    scrXB = nc.dram_tensor("scrXB", [3072, NTOK], BF16).ap()
    scrH = nc.dram_tensor("scrH", [2, NCH, 128, 2048], BF16).ap()
    scrDec = nc.dram_tensor("scrDec", [NCH, 128, 64], F32).ap()
    scrY = nc.dram_tensor("scrY", [NCH, 128, 2048], F32).ap()
    scrC = nc.dram_tensor("scrC", [NCH, 128, 512], BF16).ap()
    scrE = nc.dram_tensor("scrE", [NCH, 128, 64], F32).ap()

    es = ExitStack()
    with es:
        k = K(nc, es)
        uniq = [0]

        def sb(name, shape, dt, st=es):
            uniq[0] += 1
            return st.enter_context(nc.sbuf_tensor("%s_%d" % (name, uniq[0]), shape, dt))

        ps = [es.enter_context(nc.psum_tensor("ps%d" % i, [128, 512], F32)) for i in range(8)]
        psb = [Buf("ps%d" % i) for i in range(8)]
        psn = [0]

        pspool = [list(range(8))]

        def next_ps():
            pl = pspool[0]
            i = pl[psn[0] % len(pl)]
            psn[0] += 1
            return ps[i], psb[i]

        ones_f = sb("ones_f", [128, 128], F32); b_ones = Buf()
        k.op("pool", lambda e: e.memset(ones_f[:], 1.0), writes=[b_ones])
        ident_f = sb("ident_f", [128, 128], F32); b_ident = Buf()
        k.dma("sp", ident_f[:], ident_in, writes=[b_ident])
        ident_b = sb("ident_b", [128, 128], BF16); b_identb = Buf()
        k.dma("sp", ident_b[:], ident_in, writes=[b_identb])
        ccT = sb("ccT", [128, 8, 2], F32); b_cc = Buf()
        k.dma("sp", ccT[:], cc_in.rearrange("(c p) v -> p c v", p=128), writes=[b_cc])
        scT = sb("scT", [128, 8, 2], BF16); b_sc = Buf()
        k.op("act", lambda e: e.activation(scT[:], ccT[:], AF.Silu), reads=[b_cc], writes=[b_sc])
        mod = sb("mod", [128, 48, 2], F32); b_mod = Buf()
        adab = sb("adab", [128, 48], F32); b_adab = Buf()
        nw1 = sb("nw1", [128, 8], F32); nw2 = sb("nw2", [128, 8], F32); b_nw = Buf()
        A1 = sb("A1", [128, 8, 2], F32); A2 = sb("A2", [128, 8, 2], F32); b_A = Buf()

        def seg_tiles(tt):
            res = []
            t = 0
            while t < NCTX:
                n = min(tt, NCTX - t)
                res.append((t, n, 1))
                t += n
            while t < NTOK:
                n = min(tt, NTOK - t)
                res.append((t, n, 0))
                t += n
            return res

        def load_w(st, name, w_ap, kc, ncols, q="sp"):
            t = sb(name, [128, kc, ncols], BF16, st)
            b = Buf(name)
            src = w_ap.rearrange("(c p) n -> p c n", p=128)
            for c in range(kc):
                k.dma(q if c % 2 == 0 else "pool", t[:, c, :], src[:, c, :], writes=[b])
            return t, b

        def compute_mod(li):
            with ExitStack() as st:
                k.dma("sp", adab[:], ada_b[li].rearrange("(j p) -> p j", p=128), writes=[b_adab], allow_slow_non_contiguous=True)
                k.dma("sp", nw1[:], norm1_w[li].rearrange("(c p) -> p c", p=128), writes=[b_nw], allow_slow_non_contiguous=True)
                k.dma("sp", nw2[:], norm2_w[li].rearrange("(c p) -> p c", p=128), writes=[b_nw], allow_slow_non_contiguous=True)
                wt = [sb("adaw%d" % i, [128, 8, 1024], BF16, st) for i in range(2)]
                wb = [Buf(), Buf()]
                for j in range(6):
                    w, b = wt[j % 2], wb[j % 2]
                    src = ada_w[li][:, j * 1024:(j + 1) * 1024].rearrange("(c p) n -> p c n", p=128)
                    for c in range(8):
                        k.dma("sp" if c % 2 == 0 else "pool", w[:, c, :], src[:, c, :], writes=[b])
                    for m in range(8):
                        p, pb = next_ps()
                        for c in range(8):
                            k.op("pe", lambda e, p=p, w=w, c=c, m=m: e.matmul(p[:, 0:2], w[:, c, m * 128:(m + 1) * 128], scT[:, c, :],
                                                                          start=(c == 0), stop=(c == 7)),
                                 reads=[b, b_sc], writes=[pb], inc=(c == 7))
                        jj = j * 8 + m
                        k.op("act", lambda e, p=p, jj=jj: e.activation(mod[:, jj, :], p[:, 0:2], AF.Identity, bias=adab[:, jj:jj + 1], scale=1.0),
                             reads=[pb, b_adab], writes=[b_mod])
                for v in range(2):
                    k.op("dve", lambda e, v=v: e.scalar_tensor_tensor(A1[:, :, v], mod[:, 8:16, v], 1.0, nw1[:], ALU.add, ALU.mult),
                         reads=[b_mod, b_nw], writes=[b_A])
                    k.op("dve", lambda e, v=v: e.scalar_tensor_tensor(A2[:, :, v], mod[:, 32:40, v], 1.0, nw2[:], ALU.add, ALU.mult),
                         reads=[b_mod, b_nw], writes=[b_A])
            k.barrier()

        def norm_mod(st_tiles, xt, bx, n, vsel, A, shoff, hT, bh):
            sq, bsq, rstd, brs = st_tiles["sq"], st_tiles["bsq"], st_tiles["rstd"], st_tiles["brs"]
            p, pb = next_ps()
            for c in range(8):
                k.op("act", lambda e, c=c: e.activation(sq[:, :n], xt[:, c, :n], AF.Square), reads=[bx], writes=[bsq])
                k.op("pe", lambda e, c=c: e.matmul(p[:, :n], ones_f[:], sq[:, :n], start=(c == 0), stop=(c == 7)),
                     reads=[b_ones, bsq], writes=[pb])
            k.op("act", lambda e: e.activation(rstd[:, :n], p[:, :n], AF.Sqrt, bias=EPS, scale=1.0 / D), reads=[pb], writes=[brs])
            k.op("dve", lambda e: e.reciprocal(rstd[:, :n], rstd[:, :n]), reads=[brs], writes=[brs])
            for c in range(8):
                k.op("dve", lambda e, c=c: e.tensor_tensor(sq[:, :n], xt[:, c, :n], rstd[:, :n], ALU.mult), reads=[bx, brs], writes=[bsq])
                k.op("act", lambda e, c=c: e.activation(hT[:, c, :n], sq[:, :n], AF.Identity, bias=mod[:, shoff + c, vsel:vsel + 1],
                                                        scale=A[:, c, vsel:vsel + 1]),
                     reads=[bsq, b_mod, b_A], writes=[bh])

        def norm_scratch(st, tt):
            return {"sq": sb("n_sq", [128, tt], F32, st), "bsq": Buf(), "rstd": sb("n_rstd", [128, tt], F32, st), "brs": Buf()}

        def outproj_phase(w_ap, kc, src_scr, ctx_out):
            TT = 512
            with ExitStack() as st:
                w, bw = load_w(st, "opw", w_ap, kc, D)
                mt = [sb("op_m%d" % i, [128, kc, TT], BF16, st) for i in range(2)]; bm = [Buf(), Buf()]
                xt = [sb("op_x%d" % i, [128, 8, TT], F32, st) for i in range(2)]; bx = [Buf(), Buf()]
                for ti, (t0, n, vsel) in enumerate(seg_tiles(TT)):
                    if vsel == 1 and not ctx_out:
                        continue
                    m, b_m, x, b_x = mt[ti % 2], bm[ti % 2], xt[ti % 2], bx[ti % 2]
                    k.dma("sp", m[:, :, :n], src_scr[0:kc * 128, t0:t0 + n].rearrange("(c p) t -> p c t", p=128), writes=[b_m])
                    k.dma("pool", x[:, :, :n], xres[:, t0:t0 + n].rearrange("(c p) t -> p c t", p=128), writes=[b_x])
                    for c in range(8):
                        p, pb = next_ps()
                        for j in range(kc):
                            k.op("pe", lambda e, p=p, j=j, c=c, m=m: e.matmul(p[:, :n], w[:, j, c * 128:(c + 1) * 128], m[:, j, :n],
                                                                          start=(j == 0), stop=(j == kc - 1)),
                                 reads=[bw, b_m], writes=[pb], inc=(j == kc - 1))
                        k.op("dve", lambda e, p=p, c=c, x=x: e.scalar_tensor_tensor(x[:, c, :n], p[:, :n], mod[:, 16 + c, vsel:vsel + 1], x[:, c, :n],
                                                                                 ALU.mult, ALU.add),
                             reads=[pb, b_mod, b_x], writes=[b_x])
                    k.dma("pool", xres[:, t0:t0 + n].rearrange("(c p) t -> p c t", p=128), x[:, :, :n], reads=[b_x], writes=[Buf()])
            k.barrier()

        def resid_phase(src_scr, ctx_out):
            TT = 512
            with ExitStack() as st:
                yt = [sb("rp_y%d" % i, [128, 8, TT], F32, st) for i in range(2)]; by = [Buf(), Buf()]
                xt = [sb("rp_x%d" % i, [128, 8, TT], F32, st) for i in range(2)]; bx = [Buf(), Buf()]
                for ti, (t0, n, vsel) in enumerate(seg_tiles(TT)):
                    if vsel == 1 and not ctx_out:
                        continue
                    y, b_y, x, b_x = yt[ti % 2], by[ti % 2], xt[ti % 2], bx[ti % 2]
                    k.dma("sp", y[:, :, :n], src_scr[0:D, t0:t0 + n].rearrange("(c p) t -> p c t", p=128), writes=[b_y])
                    k.dma("pool", x[:, :, :n], xres[:, t0:t0 + n].rearrange("(c p) t -> p c t", p=128), writes=[b_x])
                    for c in range(8):
                        k.op("dve", lambda e, c=c, x=x, y=y: e.scalar_tensor_tensor(x[:, c, :n], y[:, c, :n], mod[:, 16 + c, vsel:vsel + 1], x[:, c, :n],
                                                                                 ALU.mult, ALU.add),
                             reads=[b_y, b_mod, b_x], writes=[b_x])
                    k.dma("pool", xres[:, t0:t0 + n].rearrange("(c p) t -> p c t", p=128), x[:, :, :n], reads=[b_x], writes=[Buf()])
            k.barrier()

        def ffn_phase(li, ctx_out):
            TT = 512
            NJ = FH // 128
            with ExitStack() as st:
                w13, b13 = load_w(st, "w13", ffn_w13[li], 8, 2 * FH)
                w2, b2 = load_w(st, "w2", ffn_w2[li], NJ, D)
                nsc1 = norm_scratch(st, TT)
                nscs = [nsc1, nsc1]
                xt = [sb("f_x%d" % i, [128, 8, TT], F32, st) for i in range(2)]; bx = [Buf(), Buf()]
                hT1 = sb("f_h", [128, 8, TT], BF16, st); bh1 = Buf()
                hTs = [hT1, hT1]; bhs = [bh1, bh1]
                sT = sb("f_s", [128, NJ, TT], BF16, st); bs = Buf()
                sa = [sb("f_sa%d" % i, [128, TT], BF16, st) for i in range(2)]; bsa = [Buf(), Buf()]
                tl = [t for t in seg_tiles(TT) if not (t[2] == 1 and not ctx_out)]

                def prep(i):
                    t0, n, vsel = tl[i]
                    k.dma("sp", xt[i % 2][:, :, :n], xres[:, t0:t0 + n].rearrange("(c p) t -> p c t", p=128), writes=[bx[i % 2]])
                    norm_mod(nscs[i % 2], xt[i % 2], bx[i % 2], n, vsel, A2, 24, hTs[i % 2], bhs[i % 2])

                prep(0)
                for i, (t0, n, vsel) in enumerate(tl):
                    x, b_x = xt[i % 2], bx[i % 2]
                    hT, bh = hTs[i % 2], bhs[i % 2]
                    for j in range(NJ):
                        pa, pab = next_ps()
                        pg, pgb = next_ps()
                        for c in range(8):
                            k.op("pe", lambda e, pa=pa, c=c, j=j: e.matmul(pa[:, :n], w13[:, c, j * 128:(j + 1) * 128], hT[:, c, :n],
                                                                       start=(c == 0), stop=(c == 7)),
                                 reads=[b13, bh], writes=[pab], inc=(c == 7))
                        for c in range(8):
                            k.op("pe", lambda e, pg=pg, c=c, j=j: e.matmul(pg[:, :n], w13[:, c, FH + j * 128:FH + (j + 1) * 128], hT[:, c, :n],
                                                                       start=(c == 0), stop=(c == 7)),
                                 reads=[b13, bh], writes=[pgb], inc=(c == 7))
                        s_a, b_sa = sa[j % 2], bsa[j % 2]
                        k.op("act", lambda e, pa=pa, s_a=s_a: e.activation(s_a[:, :n], pa[:, :n], AF.Silu), reads=[pab], writes=[b_sa])
                        k.op("dve", lambda e, pg=pg, s_a=s_a, j=j: e.tensor_tensor(sT[:, j, :n], s_a[:, :n], pg[:, :n], ALU.mult),
                             reads=[b_sa, pgb], writes=[bs])
                    if i + 1 < len(tl):
                        prep(i + 1)
                    for c in range(8):
                        p, pb = next_ps()
                        for j in range(NJ):
                            k.op("pe", lambda e, p=p, c=c, j=j: e.matmul(p[:, :n], w2[:, j, c * 128:(c + 1) * 128], sT[:, j, :n],
                                                                     start=(j == 0), stop=(j == NJ - 1)),
                                 reads=[b2, bs], writes=[pb], inc=(j == NJ - 1))
                        k.op("dve", lambda e, p=p, c=c, x=x: e.scalar_tensor_tensor(x[:, c, :n], p[:, :n], mod[:, 40 + c, vsel:vsel + 1], x[:, c, :n],
                                                                                 ALU.mult, ALU.add),
                             reads=[pb, b_mod, b_x], writes=[b_x])
                    k.dma("sp", xres[:, t0:t0 + n].rearrange("(c p) t -> p c t", p=128), x[:, :, :n], reads=[b_x], writes=[Buf()])
            k.barrier()

        def final_phase():
            TT = 512
            with ExitStack() as st:
                fw = sb("fnw", [128, 8], F32, st); bfw = Buf()
                k.dma("sp", fw[:], final_norm_w.rearrange("(c p) -> p c", p=128), writes=[bfw], allow_slow_non_contiguous=True)
                nsc = norm_scratch(st, TT)
                xt = [sb("fn_x%d" % i, [128, 8, TT], F32, st) for i in range(2)]; bx = [Buf(), Buf()]
                for ti, (t0, n, vsel) in enumerate(seg_tiles(TT)):
                    if vsel == 1:
                        continue
                    x, b_x = xt[ti % 2], bx[ti % 2]
                    k.dma("sp", x[:, :, :n], xres[:, t0:t0 + n].rearrange("(c p) t -> p c t", p=128), writes=[b_x])
                    sq, bsq, rstd, brs = nsc["sq"], nsc["bsq"], nsc["rstd"], nsc["brs"]
                    p, pb = next_ps()
                    for c in range(8):
                        k.op("act", lambda e, c=c, x=x: e.activation(sq[:, :n], x[:, c, :n], AF.Square), reads=[b_x], writes=[bsq])
                        k.op("pe", lambda e, c=c, p=p: e.matmul(p[:, :n], ones_f[:], sq[:, :n], start=(c == 0), stop=(c == 7)),
                             reads=[b_ones, bsq], writes=[pb])
                    k.op("act", lambda e, p=p: e.activation(rstd[:, :n], p[:, :n], AF.Sqrt, bias=EPS, scale=1.0 / D), reads=[pb], writes=[brs])
                    k.op("dve", lambda e: e.reciprocal(rstd[:, :n], rstd[:, :n]), reads=[brs], writes=[brs])
                    for c in range(8):
                        k.op("dve", lambda e, c=c, x=x: e.scalar_tensor_tensor(x[:, c, :n], x[:, c, :n], fw[:, c:c + 1], rstd[:, :n], ALU.mult, ALU.mult),
                             reads=[b_x, brs, bfw], writes=[b_x])
                    k.dma("pool", outT[:, t0 - NCTX:t0 - NCTX + n].rearrange("(c p) t -> p c t", p=128), x[:, :, :n], reads=[b_x], writes=[Buf()])
            k.barrier()

        def lru_layer(li, ctx_out):
            TT = 512
            gyT = scrA
            xrT = scrB
            hfT = scrB[1280:2560]
            with ExitStack() as st:
                w, bw = load_w(st, "lin", lru_in_w[0], 8, 2560)
                nsc = norm_scratch(st, TT)
                xt = [sb("la_x%d" % i, [128, 8, TT], F32, st) for i in range(2)]; bx = [Buf(), Buf()]
                hT = sb("la_h", [128, 8, TT], BF16, st); bh = Buf()
                og = [sb("la_o%d" % i, [128, TT], F32, st) for i in range(4)]; bog = [Buf() for _ in range(4)]
                on = 0
                for ti, (t0, n, vsel) in enumerate(seg_tiles(TT)):
                    x, b_x = xt[ti % 2], bx[ti % 2]
                    k.dma("sp", x[:, :, :n], xres[:, t0:t0 + n].rearrange("(c p) t -> p c t", p=128), writes=[b_x])
                    norm_mod(nsc, x, b_x, n, vsel, A1, 0, hT, bh)
                    for m in range(20):
                        p, pb = next_ps()
                        for c in range(8):
                            k.op("pe", lambda e, p=p, c=c, m=m: e.matmul(p[:, :n], w[:, c, m * 128:(m + 1) * 128], hT[:, c, :n],
                                                                     start=(c == 0), stop=(c == 7)),
                                 reads=[bw, bh], writes=[pb], inc=(c == 7))
                        o, bo = og[on % 4], bog[on % 4]
                        on += 1
                        if m < 10:
                            k.op("act", lambda e, p=p, o=o: e.activation(o[:, :n], p[:, :n], AF.Gelu_apprx_tanh), reads=[pb], writes=[bo])
                            k.dma("sp", gyT[m * 128:(m + 1) * 128, t0:t0 + n], o[:, :n], reads=[bo], writes=[Buf()])
                        else:
                            k.op("dve", lambda e, p=p, o=o: e.tensor_copy(o[:, :n], p[:, :n]), reads=[pb], writes=[bo])
                            k.dma("pool", xrT[(m - 10) * 128:(m - 9) * 128, t0:t0 + n], o[:, :n], reads=[bo], writes=[Buf()])
            k.barrier()
            with ExitStack() as st:
                gw = sb("lgw", [128, 2, 10, 256], BF16, st); bgw = Buf()
                for d in range(2):
                    for blk in range(10):
                        k.dma("sp" if blk % 2 else "pool", gw[:, d, blk, :], lru_gate_w[0, d, blk], writes=[bgw])
                gb = sb("lgb", [128, 2, 10, 2], F32, st); bgb = Buf()
                k.dma("sp", gb[:], lru_gate_b[0].rearrange("d b (h p) -> p d b h", p=128), writes=[bgb], allow_slow_non_contiguous=True)
                cw = sb("lcw", [128, 4, 10], F32, st); cb = sb("lcb", [128, 10], F32, st); bcw = Buf()
                k.dma("sp", cw[:], lru_conv_w[0].rearrange("k (b p) -> p k b", p=128), writes=[bcw], allow_slow_non_contiguous=True)
                k.dma("sp", cb[:], lru_conv_b[0].rearrange("(b p) -> p b", p=128), writes=[bcw], allow_slow_non_contiguous=True)
                lb = sb("llb", [128, 2, 10], F32, st); lb2 = sb("llb2", [128, 2, 10], F32, st); blb = Buf()
                k.dma("sp", lb[:], lru_a_param[0].rearrange("d (b p) -> p d b", p=128), writes=[blb], allow_slow_non_contiguous=True)
                k.op("act", lambda e: e.activation(lb[:], lb[:], AF.Exp, scale=-1.0), reads=[blb], writes=[blb])
                k.op("act", lambda e: e.activation(lb[:], lb[:], AF.Ln, bias=1.0, scale=1.0), reads=[blb], writes=[blb])
                k.op("dve", lambda e: e.tensor_scalar(lb[:], lb[:], -8.0, None, ALU.mult), reads=[blb], writes=[blb])
                k.op("dve", lambda e: e.tensor_scalar(lb2[:], lb[:], 2.0, None, ALU.mult), reads=[blb], writes=[blb])
                hgb = sb("lhgb", [128, 2, 10, 2], F32, st); hlb = sb("lhlb", [128, 2, 10], F32, st); hlb2 = sb("lhlb2", [128, 2, 10], F32, st); bhl = Buf()
                k.op("dve", lambda e: e.tensor_scalar(hgb[:], gb[:], 0.5, None, ALU.mult), reads=[bgb], writes=[bhl])
                k.op("dve", lambda e: e.tensor_scalar(hlb[:], lb[:], 0.5, None, ALU.mult), reads=[blb], writes=[bhl])
                k.op("dve", lambda e: e.tensor_scalar(hlb2[:], lb2[:], 0.5, None, ALU.mult), reads=[blb], writes=[bhl])
                HB = TT + 3
                xr = [sb("lb_xr%d" % i, [128, HB], F32, st) for i in range(8)]; bxr = [Buf() for _ in range(8)]
                xc = [sb("lb_xc%d" % i, [128, TT], F32, st) for i in range(8)]; bxc = [Buf() for _ in range(8)]
                xcb = [sb("lb_xcb%d" % i, [128, TT], BF16, st) for i in range(8)]; bxcb = [Buf() for _ in range(8)]
                ta = [sb("lb_a%d" % i, [128, TT], F32, st) for i in range(8)]; bta = [Buf() for _ in range(8)]
                tm = [sb("lb_m%d" % i, [128, TT], F32, st) for i in range(8)]; btm = [Buf() for _ in range(8)]
                tu = [sb("lb_u%d" % i, [128, TT], F32, st) for i in range(8)]; btu = [Buf() for _ in range(8)]
                th = [sb("lb_h%d" % i, [128, TT], F32, st) for i in range(8)]; bth = [Buf() for _ in range(8)]
                stc = sb("lb_stc", [128, 2, 10], F32, st); bstc = [[Buf() for _ in range(10)] for _ in range(2)]
                k.op("pool", lambda e: e.memset(stc[:], 0.0), writes=[b for r in bstc for b in r])
                tg = [sb("lb_g%d" % i, [128, TT], F32, st) for i in range(8)]; btg = [Buf() for _ in range(8)]
                to = [sb("lb_o%d" % i, [128, TT], BF16, st) for i in range(8)]; bto = [Buf() for _ in range(8)]
                zero = sb("lb_z", [128, 1], F32, st); bz = Buf()
                k.op("pool", lambda e: e.memset(zero[:], 0.0), writes=[bz])
                tiles = seg_tiles(TT)
                segs = {1: (0, NCTX), 0: (NCTX, NTOK)}
                hfbufs = {}
                NBL = 8
                lunits = []

                def make_lunit(d, t0, n, vsel, blk, un):
                    s0, s1 = segs[vsel]
                    i2 = un % NBL
                    x_r, b_xr = xr[i2], bxr[i2]
                    x_c, b_xc = xc[i2], bxc[i2]
                    x_cb, b_xcb = xcb[i2], bxcb[i2]
                    a_, b_a = ta[i2], bta[i2]
                    m_, b_m = tm[i2], btm[i2]
                    u_, b_u = tu[i2], btu[i2]
                    h_, b_h = th[i2], bth[i2]
                    g_, b_g = tg[i2], btg[i2]
                    o_, b_o = to[i2], bto[i2]
                    pr, prb = ps[(un % 4) * 2], psb[(un % 4) * 2]
                    pi, pib = ps[(un % 4) * 2 + 1], psb[(un % 4) * 2 + 1]
                    lo = max(t0 - 2, s0); hi = min(t0 + n + 1, s1)
                    skip_out = (d == 1 and vsel == 1 and not ctx_out)

                    def l0():
                        if lo > t0 - 2 or hi < t0 + n + 1:
                            k.op("pool", lambda e: e.memset(x_r[:], 0.0), writes=[b_xr])
                        k.dma("sp", x_r[:, lo - (t0 - 2):hi - (t0 - 2)], xrT[blk * 128:(blk + 1) * 128, lo:hi], writes=[b_xr])
                        if d == 1 and not skip_out:
                            k.dma("sp", g_[:, :n], gyT[blk * 128:(blk + 1) * 128, t0:t0 + n], writes=[b_g])

                    def l1():
                        k.op("act", lambda e: e.activation(x_c[:, :n], x_r[:, 0:n], AF.Identity, bias=cb[:, blk:blk + 1], scale=cw[:, 0, blk:blk + 1]),
                             reads=[b_xr, bcw], writes=[b_xc])

                    def l2():
                        for kk in range(1, 4):
                            k.op("dve", lambda e, kk=kk: e.scalar_tensor_tensor(x_c[:, :n], x_r[:, kk:kk + n], cw[:, kk, blk:blk + 1], x_c[:, :n], ALU.mult, ALU.add),
                                 reads=[b_xr, bcw, b_xc], writes=[b_xc])
                        k.op("pool", lambda e: e.tensor_copy(x_cb[:, :n], x_c[:, :n]), reads=[b_xc], writes=[b_xcb])

                    def l3():
                        k.op("pe", lambda e: e.matmul(pr[:, :n], gw[:, d, blk, 0:128], x_cb[:, :n], start=True, stop=True), reads=[bgw, b_xcb], writes=[prb])
                        k.op("pe", lambda e: e.matmul(pi[:, :n], gw[:, d, blk, 128:256], x_cb[:, :n], start=True, stop=True), reads=[bgw, b_xcb], writes=[pib])

                    def l4():
                        k.op("act", lambda e: e.activation(a_[:, :n], pr[:, :n], AF.Tanh, bias=hgb[:, d, blk, 0:1], scale=0.5), reads=[prb, bhl], writes=[b_a])
                        k.op("act", lambda e: e.activation(u_[:, :n], pi[:, :n], AF.Tanh, bias=hgb[:, d, blk, 1:2], scale=0.5), reads=[pib, bhl], writes=[b_u])
                        k.op("act", lambda e: e.activation(m_[:, :n], a_[:, :n], AF.Exp, bias=hlb2[:, d, blk:blk + 1], scale=hlb2[:, d, blk:blk + 1]), reads=[b_a, bhl], writes=[b_m])
                        k.op("act", lambda e: e.activation(a_[:, :n], a_[:, :n], AF.Exp, bias=hlb[:, d, blk:blk + 1], scale=hlb[:, d, blk:blk + 1]), reads=[b_a, bhl], writes=[b_a])
                        k.op("act", lambda e: e.activation(m_[:, :n], m_[:, :n], AF.Sqrt, bias=0.25, scale=-0.25), reads=[b_m], writes=[b_m])

                    def l5():
                        k.op("dve", lambda e: e.scalar_tensor_tensor(u_[:, :n], u_[:, :n], 1.0, x_c[:, :n], ALU.add, ALU.mult), reads=[b_u, b_xc], writes=[b_u])
                        k.op("pool", lambda e: e.tensor_tensor(u_[:, :n], u_[:, :n], m_[:, :n], ALU.mult), reads=[b_u, b_m], writes=[b_u])

                    def l6():
                        init = stc[:, d, blk:blk + 1]
                        if d == 0:
                            k.op("dve", lambda e: e.tensor_tensor_scan(h_[:, :n], a_[:, :n], u_[:, :n], init, ALU.mult, ALU.add),
                                 reads=[b_a, b_u, bstc[d][blk]], writes=[b_h])
                            k.op("act", lambda e: e.copy(stc[:, d, blk:blk + 1], h_[:, n - 1:n]), reads=[b_h], writes=[bstc[d][blk]])
                            hb = Buf()
                            hfbufs[(blk, t0)] = hb
                            k.dma("sp", hfT[blk * 128:(blk + 1) * 128, t0:t0 + n], h_[:, :n], reads=[b_h], writes=[hb])
                        else:
                            k.op("dve", lambda e: e.tensor_tensor_scan(h_[:, n - 1::-1], a_[:, n - 1::-1], u_[:, n - 1::-1], init, ALU.mult, ALU.add),
                                 reads=[b_a, b_u, bstc[d][blk]], writes=[b_h])
                            k.op("act", lambda e: e.copy(stc[:, d, blk:blk + 1], h_[:, 0:1]), reads=[b_h], writes=[bstc[d][blk]])
                            if not skip_out:
                                k.dma("sp", x_c[:, :n], hfT[blk * 128:(blk + 1) * 128, t0:t0 + n], reads=[hfbufs[(blk, t0)]], writes=[b_xc])

                    def l7():
                        if d == 0 or skip_out:
                            return
                        k.op("pool", lambda e: e.tensor_tensor(x_c[:, :n], x_c[:, :n], h_[:, :n], ALU.add), reads=[b_xc, b_h], writes=[b_xc])
                        k.op("dve", lambda e: e.tensor_tensor(o_[:, :n], x_c[:, :n], g_[:, :n], ALU.mult), reads=[b_xc, b_g], writes=[b_o])
                        k.dma("sp", scrM[blk * 128:(blk + 1) * 128, t0:t0 + n], o_[:, :n], reads=[b_o], writes=[Buf()])

                    return [l0, l1, l2, l3, l4, l5, l6, l7]

                for d in range(2):
                    order = tiles if d == 0 else ([tiles[0]] + tiles[:0:-1])
                    for (t0, n, vsel) in order:
                        for blk in range(10):
                            lunits.append(make_lunit(d, t0, n, vsel, blk, len(lunits)))
                NSL = 8
                for step in range(len(lunits) + NSL - 1):
                    for j in range(NSL - 1, -1, -1):
                        u = step - j
                        if 0 <= u < len(lunits):
                            lunits[u][j]()
            k.barrier()
            outproj_phase(lru_out_w[0], 10, scrM, ctx_out)


        def ssd_layer(li, ctx_out):
            TT = 512
            xbcT = scrXB
            with ExitStack() as st:
                w, bw = load_w(st, "m2in", m2_in_w[0], 8, 5184)
                nsc = norm_scratch(st, TT)
                xt = [sb("sa_x%d" % i, [128, 8, TT], F32, st) for i in range(2)]; bx = [Buf(), Buf()]
                hT = sb("sa_h", [128, 8, TT], BF16, st); bh = Buf()
                og = [sb("sa_o%d" % i, [128, TT], F32, st) for i in range(4)]; bog = [Buf() for _ in range(4)]
                ogb = [sb("sa_ob%d" % i, [128, TT], BF16, st) for i in range(4)]; bogb = [Buf() for _ in range(4)]
                on = 0
                for ti, (t0, n, vsel) in enumerate(seg_tiles(TT)):
                    x, b_x = xt[ti % 2], bx[ti % 2]
                    k.dma("sp", x[:, :, :n], xres[:, t0:t0 + n].rearrange("(c p) t -> p c t", p=128), writes=[b_x])
                    norm_mod(nsc, x, b_x, n, vsel, A1, 0, hT, bh)
                    for m in range(24):
                        p, pb = next_ps()
                        for c in range(8):
                            k.op("pe", lambda e, p=p, c=c, m=m: e.matmul(p[:, :n], w[:, c, 2048 + m * 128:2048 + (m + 1) * 128], hT[:, c, :n],
                                                                     start=(c == 0), stop=(c == 7)),
                                 reads=[bw, bh], writes=[pb], inc=(c == 7))
                        o, bo = ogb[on % 4], bogb[on % 4]
                        on += 1
                        if m % 2 == 0:
                            k.op("act", lambda e, p=p, o=o: e.copy(o[:, :n], p[:, :n]), reads=[pb], writes=[bo])
                        else:
                            k.op("dve", lambda e, p=p, o=o: e.tensor_copy(o[:, :n], p[:, :n]), reads=[pb], writes=[bo])
                        k.dma("sp", xbcT[m * 128:(m + 1) * 128, t0:t0 + n], o[:, :n], reads=[bo], writes=[Buf()])
                    for tb in range(n // 128):
                        for cb_ in range(4):
                            p, pb = next_ps()
                            for c in range(8):
                                k.op("pe", lambda e, p=p, c=c, tb=tb, cb_=cb_: e.matmul(p[:, :], hT[:, c, tb * 128:(tb + 1) * 128], w[:, c, cb_ * 512:(cb_ + 1) * 512],
                                                                                    start=(c == 0), stop=(c == 7)),
                                     reads=[bw, bh], writes=[pb], inc=(c == 7))
                            o, bo = og[on % 4], bog[on % 4]
                            on += 1
                            if cb_ % 2 == 0:
                                k.op("act", lambda e, p=p, o=o: e.copy(o[:, :], p[:, :]), reads=[pb], writes=[bo])
                            else:
                                k.op("dve", lambda e, p=p, o=o: e.tensor_copy(o[:, :], p[:, :]), reads=[pb], writes=[bo])
                            k.dma("sp", scrZ[t0 + tb * 128:t0 + (tb + 1) * 128, cb_ * 512:(cb_ + 1) * 512], o[:, :], reads=[bo], writes=[Buf()])
                        p, pb = next_ps()
                        for c in range(8):
                            k.op("pe", lambda e, p=p, c=c, tb=tb: e.matmul(p[:, 0:64], hT[:, c, tb * 128:(tb + 1) * 128], w[:, c, 5120:5184],
                                                                      start=(c == 0), stop=(c == 7)),
                                 reads=[bw, bh], writes=[pb], inc=(c == 7))
                        o, bo = og[on % 4], bog[on % 4]
                        on += 1
                        k.op("dve", lambda e, p=p, o=o: e.tensor_copy(o[:, 0:64], p[:, 0:64]), reads=[pb], writes=[bo])
                        k.dma("sp", scrDT[t0 + tb * 128:t0 + (tb + 1) * 128, :], o[:, 0:64], reads=[bo], writes=[Buf()])
            k.barrier()
            if debug == "ssdA":
                return
            with ExitStack() as st:
                def small(name, shape, src, **kw):
                    t = sb(name, shape, F32, st); b = Buf()
                    k.dma("sp", t[:], src, writes=[b], allow_slow_non_contiguous=True)
                    return t, b
                cw, bcw = small("scw", [128, 4, 24], m2_conv_w[0].rearrange("k (m p) -> p k m", p=128))
                cb, bcb = small("scb", [128, 24], m2_conv_b[0].rearrange("(m p) -> p m", p=128))
                dtb, bdtb = small("sdtb", [128, 64], m2_dt_bias[0].rearrange("d h -> (d h)").partition_broadcast(128))
                abc, babc = small("sabc", [128, 64], m2_a_log[0].rearrange("d h -> (d h)").partition_broadcast(128))
                dsk, bdsk = small("sdsk", [128, 32], m2_d[0].partition_broadcast(128))
                k.op("dve", lambda e: e.tensor_scalar(cw[:], cw[:], 0.5, None, ALU.mult), reads=[bcw], writes=[bcw])
                k.op("dve", lambda e: e.tensor_scalar(cb[:], cb[:], 0.5, None, ALU.mult), reads=[bcb], writes=[bcb])
                k.op("act", lambda e: e.activation(abc[:], abc[:], AF.Exp), reads=[babc], writes=[babc])
                k.op("dve", lambda e: e.tensor_scalar(abc[:], abc[:], -1.0, None, ALU.mult), reads=[babc], writes=[babc])
                msk = sb("smask", [128, 4, 128], F32, st); bmsk = Buf()
                k.dma("sp", msk[:], masks_in.rearrange("m p l -> p m l"), writes=[bmsk])
                MU, ML, MSL, MSU = (msk[:, i, :] for i in range(4))
                xr = [sb("s1_xr%d" % i, [128, 24, 516], BF16, st) for i in range(2)]; bxr = [Buf(), Buf()]
                xc = sb("s1_xc", [128, 128], F32, st); bxc = Buf()
                xsT = sb("s1_xsT", [128, 24, 128], BF16, st); bxsT = Buf()
                xs_tok = [sb("s1_xst%d" % i, [128, 2048], BF16, st) for i in range(2)]; bxst = [Buf(), Buf()]
                b_tok = sb("s1_btok", [128, 512], BF16, st); bbtok = Buf()
                dtr = sb("s1_dtr", [128, 64], F32, st); bdtr = Buf()
                dtt = sb("s1_dt", [128, 64], F32, st); bdtt = Buf()
                da = sb("s1_da", [128, 64], F32, st); bda = Buf()
                tmp64 = sb("s1_t64", [128, 64], F32, st); bt64 = Buf()
                wgt = sb("s1_wgt", [128, 64], F32, st); bwgt = Buf()
                Et = [sb("s1_E%d" % i, [128, 64], F32, st) for i in range(2)]; bE = [Buf(), Buf()]
                dec = [sb("s1_dec%d" % i, [128, 64], F32, st) for i in range(2)]; bdec = [Buf(), Buf()]
                Gf = sb("s1_Gf", [128, 4, 128], F32, st); Gb = sb("s1_Gb", [128, 4, 128], F32, st); bG = Buf()
                NBH = 4
                Lf = [sb("s1_Lf%d" % i, [128, 128], F32, st) for i in range(NBH)]; bLf = [Buf() for _ in range(NBH)]
                Lb = [sb("s1_Lb%d" % i, [128, 128], F32, st) for i in range(NBH)]; bLb = [Buf() for _ in range(NBH)]
                Df = [sb("s1_Df%d" % i, [128, 128], F32, st) for i in range(NBH)]; bDf = [Buf() for _ in range(NBH)]
                Db = [sb("s1_Db%d" % i, [128, 128], F32, st) for i in range(NBH)]; bDb = [Buf() for _ in range(NBH)]
                Mt = [sb("s1_M%d" % i, [128, 128], BF16, st) for i in range(NBH)]; bMt = [Buf() for _ in range(NBH)]
                Mb = [sb("s1_Mb%d" % i, [128, 128], BF16, st) for i in range(NBH)]; bMb = [Buf() for _ in range(NBH)]
                dskI = sb("s1_dskI", [128, 32, 128], BF16, st); bdskI = Buf()
                for h_ in range(32):
                    k.op("dve", lambda e, h_=h_: e.tensor_scalar(dskI[:, h_, :], ident_f[:], dsk[:, h_:h_ + 1], None, ALU.mult), reads=[b_ident, bdsk], writes=[bdskI])
                xw = sb("s1_xw", [128, 2048], BF16, st); bxw = Buf()
                yo = [sb("s1_yo%d" % i, [128, 2048], F32, st) for i in range(2)]; byo = [Buf(), Buf()]
                so = [sb("s1_so%d" % i, [128, 512], BF16, st) for i in range(4)]; bso = [Buf() for _ in range(4)]
                dtt2 = [dtt, sb("s1_dt2", [128, 64], F32, st)]; bdtt2 = [bdtt, Buf()]
                da2 = [da, sb("s1_da2", [128, 64], F32, st)]; bda2 = [bda, Buf()]
                Gf2 = [Gf, sb("s1_Gf2", [128, 4, 128], F32, st)]; Gb2 = [Gb, sb("s1_Gb2", [128, 4, 128], F32, st)]; bG2 = [bG, Buf()]
                pspool[0] = [4]

                xc4 = [xc] + [sb("s1_xc%d" % i, [128, 128], F32, st) for i in range(3)]; bxc4 = [bxc, Buf(), Buf(), Buf()]
                th4 = [sb("s1_th%d" % i, [128, 128], F32, st) for i in range(4)]; bth4 = [Buf() for _ in range(4)]

                def pieces(ch):
                    dtt_, bdtt_ = dtt2[ch % 2], bdtt2[ch % 2]
                    da_, bda_ = da2[ch % 2], bda2[ch % 2]
                    Gf_, Gb_, bG_ = Gf2[ch % 2], Gb2[ch % 2], bG2[ch % 2]
                    t0 = ch * 128
                    s0, s1 = (0, NCTX) if t0 < NCTX else (NCTX, NTOK)
                    if t0 < NCTX:
                        big_i, big_t0, big_n = 0, 0, NCTX
                    else:
                        big_i = 1 + (t0 - NCTX) // 512
                        big_t0 = NCTX + (big_i - 1) * 512
                        big_n = 512
                    xbig, b_xr = xr[big_i % 2], bxr[big_i % 2]
                    boff = t0 - big_t0
                    x_r = xbig[:, :, boff:boff + 131]
                    blo = max(big_t0 - 2, s0); bhi = min(big_t0 + big_n + 1, s1)
                    xst, b_xst = xs_tok[ch % 2], bxst[ch % 2]
                    E_, b_E = Et[ch % 2], bE[ch % 2]
                    d_, b_d = dec[ch % 2], bdec[ch % 2]
                    lo = max(t0 - 2, s0); hi = min(t0 + 129, s1)
                    pa, pab = ps[5], psb[5]
                    P = []

                    def convA(m):
                        if m == 0 and boff == 0:
                            if blo > big_t0 - 2 or bhi < big_t0 + big_n + 1:
                                k.op("pool", lambda e: e.memset(xbig[:], 0.0), writes=[b_xr])
                            k.dma("sp", xbig[:, :, blo - (big_t0 - 2):bhi - (big_t0 - 2)], xbcT[:, blo:bhi].rearrange("(m p) t -> p m t", p=128), writes=[b_xr])
                        xc_, bxc_ = xc4[m % 4], bxc4[m % 4]
                        k.op("act", lambda e: e.activation(xc_[:], x_r[:, m, 0:128], AF.Identity, bias=cb[:, m:m + 1], scale=cw[:, 0, m:m + 1]),
                             reads=[b_xr, bcw, bcb], writes=[bxc_])

                    def convT(m):
                        xc_, bxc_ = xc4[m % 4], bxc4[m % 4]
                        for kk in range(1, 4):
                            k.op("dve", lambda e, kk=kk: e.scalar_tensor_tensor(xc_[:], x_r[:, m, kk:kk + 128], cw[:, kk, m:m + 1], xc_[:], ALU.mult, ALU.add),
                                 reads=[b_xr, bcw, bxc_], writes=[bxc_])

                    def convS(m):
                        xc_, bxc_ = xc4[m % 4], bxc4[m % 4]
                        th_, bth_ = th4[m % 4], bth4[m % 4]
                        k.op("act", lambda e: e.activation(th_[:], xc_[:], AF.Tanh), reads=[bxc_], writes=[bth_])
                        k.op("dve", lambda e: e.scalar_tensor_tensor(xsT[:, m, :], th_[:], 1.0, xc_[:], ALU.add, ALU.mult), reads=[bth_, bxc_], writes=[bxsT])

                    def dtp():
                        k.dma("sp", dtr[:], scrDT[t0:t0 + 128, :], writes=[bdtr])
                        k.op("dve", lambda e: e.tensor_tensor(dtr[:], dtr[:], dtb[:], ALU.add), reads=[bdtr, bdtb], writes=[bdtr])
                        k.op("act", lambda e: e.activation(tmp64[:], dtr[:], AF.Abs), reads=[bdtr], writes=[bt64])
                        k.op("act", lambda e: e.activation(tmp64[:], tmp64[:], AF.Exp, scale=-1.0), reads=[bt64], writes=[bt64])
                        k.op("act", lambda e: e.activation(tmp64[:], tmp64[:], AF.Ln, bias=1.0, scale=1.0), reads=[bt64], writes=[bt64])
                        k.op("dve", lambda e: e.scalar_tensor_tensor(dtt_[:], dtr[:], 0.0, tmp64[:], ALU.max, ALU.add), reads=[bdtr, bt64], writes=[bdtt_])
                        k.op("dve", lambda e: e.tensor_tensor(da_[:], dtt_[:], abc[:], ALU.mult), reads=[bdtt_, babc], writes=[bda_])

                    def convpiece(i):
                        if i - 2 >= 0 and i - 2 < 24:
                            convS(i - 2)
                        if i - 1 >= 0 and i - 1 < 24:
                            convT(i - 1)
                        if i < 24:
                            convA(i)
                        if i == 8:
                            dtp()
                        if i == 12:
                            cums()
                    for i in range(26):
                        P.append(lambda i=i: convpiece(i))

                    def tr(qs, last):
                        for q in qs:
                            p, pb = next_ps()
                            pbf = p[:].bitcast(BF16)
                            for j in range(4):
                                m = q * 4 + j
                                k.op("pe", lambda e, pbf=pbf, j=j, m=m: e.transpose(pbf[:, j * 128:(j + 1) * 128], xsT[:, m, :], ident_b[:]),
                                     reads=[bxsT, b_identb], writes=[pb], inc=(j == 3))
                            if q < 4:
                                k.op("dve", lambda e, pbf=pbf, q=q: e.tensor_copy(xst[:, q * 512:(q + 1) * 512], pbf[:, 0:512]), reads=[pb], writes=[b_xst])
                            else:
                                k.op("dve", lambda e, pbf=pbf: e.tensor_copy(b_tok[:], pbf[:, 0:512]), reads=[pb], writes=[bbtok])
                        if last:
                            k.dma("sp", scrC[ch].rearrange("p (g t) -> p g t", t=128), xsT[:, 20:24, :], reads=[bxsT], writes=[Buf()])
                    P.append(lambda: tr((0, 1), False))
                    P.append(lambda: tr((2, 3), False))
                    P.append(lambda: tr((4,), True))

                    def cums():
                        k.op("pe", lambda e: e.matmul(pa[:, 0:32], MU, da_[:, 0:32], start=True, stop=True), reads=[bmsk, bda_], writes=[pab], inc=False)
                        k.op("pe", lambda e: e.matmul(pa[:, 32:64], ML, da_[:, 32:64], start=True, stop=True), reads=[bmsk, bda_], writes=[pab], inc=False)
                        k.op("pe", lambda e: e.matmul(pa[:, 64:128], ones_f[:], da_[:, 0:64], start=True, stop=True), reads=[b_ones, bda_], writes=[pab])
                        k.op("act", lambda e: e.activation(E_[:], pa[:, 0:64], AF.Exp), reads=[pab], writes=[b_E])
                        k.op("act", lambda e: e.activation(d_[:], pa[:, 64:128], AF.Exp), reads=[pab], writes=[b_d])
                        k.dma("sp", scrE[ch], E_[:], reads=[b_E], writes=[Buf()])
                        k.dma("sp", scrDec[ch], d_[:], reads=[b_d], writes=[Buf()])
                        k.op("act", lambda e: e.copy(tmp64[:], pa[:, 0:64]), reads=[pab], writes=[bt64])
                        k.op("dve", lambda e: e.tensor_tensor(wgt[:], pa[:, 64:128], tmp64[:], ALU.subtract), reads=[pab, bt64], writes=[bwgt])
                        k.op("act", lambda e: e.activation(wgt[:], wgt[:], AF.Exp), reads=[bwgt], writes=[bwgt])
                        k.op("dve", lambda e: e.tensor_tensor(wgt[:], wgt[:], dtt_[:], ALU.mult), reads=[bwgt, bdtt_], writes=[bwgt])

                    def states(d):
                        k.op("dve", lambda e: e.tensor_tensor(xw[:].rearrange("p (h q) -> p h q", q=64), xst[:].rearrange("p (h q) -> p h q", q=64),
                                                              wgt[:, d * 32:(d + 1) * 32].unsqueeze(2).to_broadcast([128, 32, 64]), ALU.mult),
                             reads=[b_xst, bwgt], writes=[bxw])
                        for g in range(4):
                            p, pb = next_ps()
                            k.op("pe", lambda e, p=p, g=g: e.matmul(p[:, :], b_tok[:, g * 128:(g + 1) * 128], xw[:, g * 512:(g + 1) * 512], start=True, stop=True),
                                 reads=[bbtok, bxw], writes=[pb])
                            s_, b_s = so[g], bso[g]
                            k.op("act", lambda e, p=p, s_=s_: e.copy(s_[:], p[:]), reads=[pb], writes=[b_s])
                            k.dma("sp", scrS[d, ch, :, g * 512:(g + 1) * 512], s_[:], reads=[b_s], writes=[Buf()])
                    P.append(lambda: states(0))
                    P.append(lambda: states(1))

                    def gmat():
                        for g in range(4):
                            p, pb = next_ps()
                            k.op("pe", lambda e, p=p, g=g: e.matmul(p[:, 0:128], xsT[:, 16 + g, :], xsT[:, 20 + g, :], start=True, stop=True),
                                 reads=[bxsT], writes=[pb])
                            k.op("dve", lambda e, p=p, g=g: e.tensor_tensor(Gf_[:, g, :], p[:, 0:128], MU, ALU.mult), reads=[pb, bmsk], writes=[bG_])
                            k.op("dve", lambda e, p=p, g=g: e.tensor_tensor(Gb_[:, g, :], p[:, 0:128], ML, ALU.mult), reads=[pb, bmsk], writes=[bG_])
                    P.append(gmat)
                    assert len(P) == 32
                    return P

                pieces_cache = {}

                def piece(ch, i):
                    if ch >= NCH:
                        return
                    if ch not in pieces_cache:
                        pieces_cache.clear()
                        pieces_cache[ch] = pieces(ch)
                    pieces_cache[ch][i]()

                def prologue(ch):
                    if ch == 0:
                        for i in range(32):
                            piece(0, i)

                def make_head(ch, h, hc):
                    dtt_, bdtt_ = dtt2[ch % 2], bdtt2[ch % 2]
                    da_, bda_ = da2[ch % 2], bda2[ch % 2]
                    Gf_, Gb_, bG_ = Gf2[ch % 2], Gb2[ch % 2], bG2[ch % 2]
                    xst, b_xst = xs_tok[ch % 2], bxst[ch % 2]
                    g = h // 8
                    i2 = hc % NBH
                    p, pb = ps[6 + hc % 2], psb[6 + hc % 2]
                    yb = h // 8

                    def h0():
                        if h == 0:
                            prologue(ch)
                        piece(ch + 1, h)
                        k.op("act", lambda e: e.activation(Lf[i2][:], MSL, AF.Identity, scale=da_[:, h:h + 1]), reads=[bmsk, bda_], writes=[bLf[i2]])
                        k.op("act", lambda e: e.activation(Lb[i2][:], MSU, AF.Identity, scale=da_[:, 32 + h:33 + h]), reads=[bmsk, bda_], writes=[bLb[i2]])

                    def h1():
                        k.op("pe", lambda e: e.matmul(p[:, 0:128], Lf[i2][:], MU, start=True, stop=True), reads=[bLf[i2], bmsk], writes=[pb], inc=False)
                        k.op("pe", lambda e: e.matmul(p[:, 128:256], Lb[i2][:], ML, start=True, stop=True), reads=[bLb[i2], bmsk], writes=[pb])

                    def h2():
                        k.op("act", lambda e: e.activation(Df[i2][:], p[:, 0:128], AF.Exp), reads=[pb], writes=[bDf[i2]])
                        k.op("act", lambda e: e.activation(Db[i2][:], p[:, 128:256], AF.Exp), reads=[pb], writes=[bDb[i2]])

                    def h3():
                        k.op("dve", lambda e: e.scalar_tensor_tensor(Mt[i2][:], Df[i2][:], dtt_[:, h:h + 1], Gf_[:, g, :], ALU.mult, ALU.mult),
                             reads=[bDf[i2], bdtt_, bG_], writes=[bMt[i2]])
                        k.op("dve", lambda e: e.scalar_tensor_tensor(Mb[i2][:], Db[i2][:], dtt_[:, 32 + h:33 + h], Gb_[:, g, :], ALU.mult, ALU.mult),
                             reads=[bDb[i2], bdtt_, bG_], writes=[bMb[i2]])

                    def h4():
                        yout = ps[yb][:, (h % 8) * 64:(h % 8 + 1) * 64]
                        k.op("pe", lambda e: e.matmul(yout, Mt[i2][:], xst[:, h * 64:(h + 1) * 64], start=True, stop=False),
                             reads=[bMt[i2], b_xst], writes=[psb[yb]], inc=False)
                        k.op("pe", lambda e: e.matmul(yout, Mb[i2][:], xst[:, h * 64:(h + 1) * 64], start=False, stop=False),
                             reads=[bMb[i2], b_xst], writes=[psb[yb]], inc=False)
                        k.op("pe", lambda e: e.matmul(yout, dskI[:, h, :], xst[:, h * 64:(h + 1) * 64], start=False, stop=True),
                             reads=[bdskI, b_xst], writes=[psb[yb]])
                        if h != 31:
                            return
                        y_, b_y = yo[ch % 2], byo[ch % 2]
                        for yb2 in range(4):
                            if yb2 % 2 == 0:
                                k.op("act", lambda e, yb2=yb2: e.copy(y_[:, yb2 * 512:(yb2 + 1) * 512], ps[yb2][:]), reads=[psb[yb2]], writes=[b_y])
                            else:
                                k.op("dve", lambda e, yb2=yb2: e.tensor_copy(y_[:, yb2 * 512:(yb2 + 1) * 512], ps[yb2][:]), reads=[psb[yb2]], writes=[b_y])
                        k.dma("sp", scrY[ch], y_[:], reads=[b_y], writes=[Buf()])

                    return [h0, h1, h2, h3, h4]

                hunits = []
                for ch in range(NCH):
                    for h in range(32):
                        hunits.append(make_head(ch, h, len(hunits)))
                NSH = 5
                for step in range(len(hunits) + NSH - 1):
                    for j in range(NSH - 1, -1, -1):
                        u = step - j
                        if 0 <= u < len(hunits):
                            hunits[u][j]()
                pspool[0] = list(range(8))
            k.barrier()
            if debug == "ssd1":
                return
            with ExitStack() as st:
                Hs = [sb("s2_H%d" % d, [128, 2048], F32, st) for d in range(2)]; bH = [Buf(), Buf()]
                Hb = [sb("s2_Hb%d" % i, [128, 2048], BF16, st) for i in range(2)]; bHb = [Buf(), Buf()]
                St = [sb("s2_S%d" % i, [128, 2048], BF16, st) for i in range(2)]; bSt = [Buf(), Buf()]
                dc = [sb("s2_d%d" % i, [128, 64], F32, st) for i in range(2)]; bdc = [Buf(), Buf()]
                it = 0
                for d in range(2):
                    order = list(range(NCH)) if d == 0 else [1, 0] + list(range(NCH - 1, 1, -1))
                    k.op("pool", lambda e, d=d: e.memset(Hs[d][:], 0.0), writes=[bH[d]])
                    for ch in order:
                        i2 = it % 2
                        it += 1
                        k.op("act", lambda e, d=d, i2=i2: e.copy(Hb[i2][:], Hs[d][:]), reads=[bH[d]], writes=[bHb[i2]])
                        k.dma("pool", scrH[d, ch], Hb[i2][:], reads=[bHb[i2]], writes=[Buf()])
                        k.dma("sp", St[i2][:], scrS[d, ch], writes=[bSt[i2]])
                        k.dma("sp", dc[i2][:], scrDec[ch], writes=[bdc[i2]])
                        k.op("dve", lambda e, d=d, i2=i2: e.tensor_tensor(Hs[d][:].rearrange("p (h q) -> p h q", q=64), Hs[d][:].rearrange("p (h q) -> p h q", q=64),
                                                                      dc[i2][:, d * 32:(d + 1) * 32].unsqueeze(2).to_broadcast([128, 32, 64]), ALU.mult),
                             reads=[bH[d], bdc[i2]], writes=[bH[d]])
                        k.op("dve", lambda e, d=d, i2=i2: e.tensor_tensor(Hs[d][:], Hs[d][:], St[i2][:], ALU.add), reads=[bH[d], bSt[i2]], writes=[bH[d]])
            k.barrier()
            if debug == "ssd2":
                return
            with ExitStack() as st:
                ow, bow = load_w(st, "m2ow", m2_out_w[0], 16, D)
                nwb = sb("s3_nw", [128, 2048], F32, st); bnwb = Buf()
                k.dma("sp", nwb[:], m2_norm_w[0].partition_broadcast(128), writes=[bnwb])
                yt = [sb("s3_y%d" % i, [128, 2048], F32, st) for i in range(2)]; byt = [Buf(), Buf()]
                zt = [sb("s3_z%d" % i, [128, 2048], F32, st) for i in range(2)]; bzt = [Buf(), Buf()]
                Hf = [sb("s3_Hf%d" % i, [128, 2048], BF16, st) for i in range(2)]; bHf = [Buf(), Buf()]
                Hbk = [sb("s3_Hb%d" % i, [128, 2048], BF16, st) for i in range(2)]; bHbk = [Buf(), Buf()]
                Ct = [sb("s3_C%d" % i, [128, 512], BF16, st) for i in range(2)]; bCt = [Buf(), Buf()]
                Ee = [sb("s3_E%d" % i, [128, 64], F32, st) for i in range(2)]; bEe = [Buf(), Buf()]
                tmp = sb("s3_tmp", [128, 512], F32, st); btmp = Buf()
                ss = sb("s3_ss", [128, 2], F32, st); bss = Buf()
                sqj = sb("s3_sq", [128, 2048], F32, st); bsqj = Buf()
                yn = sb("s3_yn", [128, 2048], BF16, st); byn = Buf()
                ynT = sb("s3_ynT", [128, 16, 128], BF16, st); bynT = Buf()
                xt = [sb("s3_x%d" % i, [128, 8, 128], F32, st) for i in range(2)]; bx = [Buf(), Buf()]
                for ch in range(NCH):
                    t0 = ch * 128
                    vsel = 1 if t0 < NCTX else 0
                    if vsel == 1 and not ctx_out:
                        continue
                    i2 = ch % 2
                    y_, z_, b_y, b_z = yt[i2], zt[i2], byt[i2], bzt[i2]
                    k.dma("sp", y_[:], scrY[ch], writes=[b_y])
                    k.dma("sp", z_[:], scrZ[t0:t0 + 128, :], writes=[b_z])
                    k.dma("pool", Hf[i2][:], scrH[0, ch], writes=[bHf[i2]])
                    k.dma("pool", Hbk[i2][:], scrH[1, ch], writes=[bHbk[i2]])
                    k.dma("sp", Ct[i2][:], scrC[ch], writes=[bCt[i2]])
                    k.dma("sp", Ee[i2][:], scrE[ch], writes=[bEe[i2]])
                    k.dma("pool", xt[i2][:], xres[:, t0:t0 + 128].rearrange("(c p) t -> p c t", p=128), writes=[bx[i2]])
                    for d in range(2):
                        Hd, bHd = (Hf[i2], bHf[i2]) if d == 0 else (Hbk[i2], bHbk[i2])
                        for g in range(4):
                            p, pb = next_ps()
                            k.op("pe", lambda e, p=p, g=g, Hd=Hd, i2=i2: e.matmul(p[:], Ct[i2][:, g * 128:(g + 1) * 128], Hd[:, g * 512:(g + 1) * 512], start=True, stop=True),
                                 reads=[bCt[i2], bHd], writes=[pb])
                            k.op("dve", lambda e, p=p, g=g, d=d, i2=i2: e.tensor_tensor(tmp[:].rearrange("p (h q) -> p h q", q=64), p[:].rearrange("p (h q) -> p h q", q=64),
                                                                                   Ee[i2][:, d * 32 + g * 8:d * 32 + (g + 1) * 8].unsqueeze(2).to_broadcast([128, 8, 64]), ALU.mult),
                                 reads=[pb, bEe[i2]], writes=[btmp])
                            k.op("pool", lambda e, g=g, y_=y_: e.tensor_tensor(y_[:, g * 512:(g + 1) * 512], y_[:, g * 512:(g + 1) * 512], tmp[:], ALU.add),
                                 reads=[btmp, b_y], writes=[b_y])
                    k.op("act", lambda e, z_=z_: e.activation(z_[:], z_[:], AF.Silu), reads=[b_z], writes=[b_z])
                    k.op("dve", lambda e, y_=y_, z_=z_: e.tensor_tensor(y_[:], y_[:], z_[:], ALU.mult), reads=[b_y, b_z], writes=[b_y])
                    k.op("pool", lambda e: e.memset(ss[:], 0.0), writes=[bss])
                    k.op("act", lambda e, y_=y_: e.activation(sqj[:], y_[:], AF.Square, accum_out=ss[:, 0:1]), reads=[b_y], writes=[bsqj, bss])
                    k.op("act", lambda e: e.activation(ss[:, 1:2], ss[:, 0:1], AF.Sqrt, bias=EPS, scale=1.0 / 2048), reads=[bss], writes=[bss])
                    k.op("dve", lambda e: e.reciprocal(ss[:, 1:2], ss[:, 1:2]), reads=[bss], writes=[bss])
                    k.op("dve", lambda e, y_=y_: e.scalar_tensor_tensor(yn[:], y_[:], ss[:, 1:2], nwb[:], ALU.mult, ALU.mult), reads=[b_y, bss, bnwb], writes=[byn])
                    for q in range(4):
                        p, pb = next_ps()
                        pbf = p[:].bitcast(BF16)
                        for j in range(4):
                            m = q * 4 + j
                            k.op("pe", lambda e, pbf=pbf, j=j, m=m: e.transpose(pbf[:, j * 128:(j + 1) * 128], yn[:, m * 128:(m + 1) * 128], ident_b[:]),
                                 reads=[byn, b_identb], writes=[pb], inc=(j == 3))
                        k.op("act" if q % 2 else "dve", (lambda e, pbf=pbf, q=q: e.copy(ynT[:, q * 4:(q + 1) * 4, :], pbf[:, 0:512].rearrange("p (j t) -> p j t", t=128))) if q % 2 else
                             (lambda e, pbf=pbf, q=q: e.tensor_copy(ynT[:, q * 4:(q + 1) * 4, :], pbf[:, 0:512].rearrange("p (j t) -> p j t", t=128))),
                             reads=[pb], writes=[bynT])
                    for c in range(8):
                        p, pb = next_ps()
                        for j in range(16):
                            k.op("pe", lambda e, p=p, j=j, c=c: e.matmul(p[:, 0:128], ow[:, j, c * 128:(c + 1) * 128], ynT[:, j, :], start=(j == 0), stop=(j == 15)),
                                 reads=[bow, bynT], writes=[pb], inc=(j == 15))
                        k.op("dve", lambda e, p=p, c=c, i2=i2: e.scalar_tensor_tensor(xt[i2][:, c, :], p[:, 0:128], mod[:, 16 + c, vsel:vsel + 1], xt[i2][:, c, :], ALU.mult, ALU.add),
                             reads=[pb, b_mod, bx[i2]], writes=[bx[i2]])
                    k.dma("pool", xres[:, t0:t0 + 128].rearrange("(c p) t -> p c t", p=128), xt[i2][:], reads=[bx[i2]], writes=[Buf()])
            k.barrier()

        TWO_PI = 6.283185307179586

        def sincos(st, tag, phi, bphi, shape, out_sin, out_cos, bout):
            t = sb("sc_t" + tag, shape, F32, st); ki = sb("sc_k" + tag, shape, I32, st); bt = Buf()
            for (o, off) in ((out_sin, 0.0), (out_cos, 0.5 * np.pi)):
                k.op("dve", lambda e, off=off: e.tensor_scalar(t[:], phi, 1.0 / TWO_PI, off / TWO_PI, ALU.mult, ALU.add), reads=[bphi], writes=[bt])
                k.op("dve", lambda e: e.tensor_copy(ki[:], t[:]), reads=[bt], writes=[bt])
                k.op("dve", lambda e: e.tensor_copy(t[:], ki[:]), reads=[bt], writes=[bt])
                k.op("dve", lambda e: e.scalar_tensor_tensor(t[:], t[:], -TWO_PI, phi, ALU.mult, ALU.add), reads=[bt, bphi], writes=[bt])
                k.op("dve", lambda e, off=off: e.tensor_scalar(t[:], t[:], off, 3.1415925, ALU.add, ALU.min), reads=[bt], writes=[bt])
                k.op("dve", lambda e: e.tensor_scalar(t[:], t[:], -3.1415925, None, ALU.max), reads=[bt], writes=[bt])
                k.op("act", lambda e, o=o: e.activation(o, t[:], AF.Sin), reads=[bt], writes=[bout])

        def s5_layer(li, ctx_out):
            T = 128
            yfT = scrA
            tiles = seg_tiles(T)
            for d in range(2):
                with ExitStack() as st:
                    def small(name, shape, src, q="sp"):
                        t = sb(name, shape, F32, st); b = Buf()
                        k.dma(q, t[:], src, writes=[b], allow_slow_non_contiguous=True)
                        return t, b
                    ramp, bramp = small("ramp", [128, 128], ramp_in)
                    sel, bsel = small("sel", [128, 8], sel_in)
                    lrs, b1 = small("lrs", [128, 32], s5_lre[0, d].rearrange("g p -> (g p)").rearrange("(b q) -> q b", q=128))
                    lis, b2 = small("lis", [128, 32], s5_lim[0, d].rearrange("g p -> (g p)").rearrange("(b q) -> q b", q=128))
                    sts = sb("sts", [128, 32], F32, st); b3 = Buf()
                    for gl in range(2):
                        k.dma("sp", sts[gl * 64:(gl + 1) * 64, :], s5_ls[0, d].rearrange("(b gl) -> gl b", gl=2)[gl].partition_broadcast(64), writes=[b3],
                              allow_slow_non_contiguous=True)
                    rho = sb("rho", [128, 32], F32, st); theta = sb("theta", [128, 32], F32, st); bpar = Buf()
                    k.op("dve", lambda e: e.tensor_scalar(lrs[:], lrs[:], -1e-4, None, ALU.min), reads=[b1], writes=[b1])
                    k.op("act", lambda e: e.activation(sts[:], sts[:], AF.Exp), reads=[b3], writes=[b3])
                    k.op("dve", lambda e: e.tensor_tensor(rho[:], lrs[:], sts[:], ALU.mult), reads=[b1, b3], writes=[bpar])
                    k.op("act", lambda e: e.activation(rho[:], rho[:], AF.Exp), reads=[bpar], writes=[bpar])
                    k.op("dve", lambda e: e.tensor_tensor(theta[:], lis[:], sts[:], ALU.mult), reads=[b2, b3], writes=[bpar])
                    rhoT = sb("rhoT", [128, 32, 128], F32, st)
                    for b in range(32):
                        k.op("pool", lambda e, b=b: e.tensor_scalar(rhoT[:, b, :], ramp[:], 0.0, rho[:, b:b + 1], ALU.mult, ALU.add), reads=[bramp, bpar], writes=[bpar])
                    cosT = sb("cosT", [128, 32, 128], F32, st); sinT = sb("sinT", [128, 32, 128], F32, st); btab = Buf()
                    phi = sb("phi", [128, 128], F32, st); bphi = Buf()
                    with ExitStack() as st2:
                        for b in range(32):
                            k.op("dve", lambda e, b=b: e.tensor_scalar(phi[:], ramp[:], theta[:, b:b + 1], None, ALU.mult), reads=[bramp, bpar], writes=[bphi])
                            if b == 0:
                                tt_ = sb("sc_t", [128, 128], F32, st2); ki_ = sb("sc_k", [128, 128], I32, st2); bt_ = Buf()
                            for (o, off) in ((sinT[:, b, :], 0.0), (cosT[:, b, :], 0.5 * np.pi)):
                                k.op("dve", lambda e, off=off: e.tensor_scalar(tt_[:], phi[:], 1.0 / TWO_PI, off / TWO_PI, ALU.mult, ALU.add), reads=[bphi], writes=[bt_])
                                k.op("dve", lambda e: e.tensor_copy(ki_[:], tt_[:]), reads=[bt_], writes=[bt_])
                                k.op("dve", lambda e: e.tensor_copy(tt_[:], ki_[:]), reads=[bt_], writes=[bt_])
                                k.op("dve", lambda e: e.scalar_tensor_tensor(tt_[:], tt_[:], -TWO_PI, phi[:], ALU.mult, ALU.add), reads=[bt_, bphi], writes=[bt_])
                                k.op("dve", lambda e, off=off: e.tensor_scalar(tt_[:], tt_[:], off, 3.1415925, ALU.add, ALU.min), reads=[bt_], writes=[bt_])
                                k.op("dve", lambda e: e.tensor_scalar(tt_[:], tt_[:], -3.1415925, None, ALU.max), reads=[bt_], writes=[bt_])
                                k.op("act", lambda e, o=o: e.activation(o, tt_[:], AF.Sin), reads=[bt_], writes=[btab])
                    k.barrier()
                    Wre = sb("Wre", [128, 8, 4, 128], BF16, st); Wim = sb("Wim", [128, 8, 4, 128], BF16, st); bW = Buf()
                    with ExitStack() as st2:
                        def wl(name, src3):
                            t = sb(name, [128, 8, 64], F32, st2); b = Buf()
                            v = src3.rearrange("(c gl) p -> gl c p", gl=8)
                            for gl in range(8):
                                k.dma("sp" if gl % 2 else "pool", t[gl * 16:(gl + 1) * 16, :, :], v[gl].partition_broadcast(16), writes=[b], allow_slow_non_contiguous=True)
                            return t, b
                        lrw, blr = wl("lrw", s5_lre[0, d]); liw, bli = wl("liw", s5_lim[0, d])
                        stw = sb("stw", [128, 8], F32, st2); bst = Buf()
                        v = s5_ls[0, d].rearrange("(c gl) -> gl c", gl=8)
                        for gl in range(8):
                            k.dma("sp", stw[gl * 16:(gl + 1) * 16, :], v[gl].partition_broadcast(16), writes=[bst], allow_slow_non_contiguous=True)
                        brw = sb("brw", [128, 8, 64], F32, st2); biw = sb("biw", [128, 8, 64], F32, st2); bbw = Buf()
                        for (t_, src) in ((brw, s5_bre[0]), (biw, s5_bim[0])):
                            v = src.rearrange("(c gl) p j -> gl j c p", gl=8)
                            for gl in range(8):
                                for c_ in range(8):
                                    k.dma("sp" if c_ % 2 else "pool", t_[gl * 16:(gl + 1) * 16, c_, :], v[gl][:, c_, :], writes=[bbw], allow_slow_non_contiguous=True)
                        k.op("dve", lambda e: e.tensor_scalar(lrw[:], lrw[:], -1e-4, None, ALU.min), reads=[blr], writes=[blr])
                        k.op("act", lambda e: e.activation(stw[:], stw[:], AF.Exp), reads=[bst], writes=[bst])
                        stb = stw[:].unsqueeze(2).to_broadcast([128, 8, 64])
                        mag = sb("mag", [128, 8, 64], F32, st2); th = sb("thw", [128, 8, 64], F32, st2); bm_ = Buf(); bth = Buf()
                        k.op("dve", lambda e: e.tensor_tensor(mag[:], lrw[:], stb, ALU.mult), reads=[blr, bst], writes=[bm_])
                        k.op("act", lambda e: e.activation(mag[:], mag[:], AF.Exp), reads=[bm_], writes=[bm_])
                        k.op("dve", lambda e: e.tensor_tensor(th[:], liw[:], stb, ALU.mult), reads=[bli, bst], writes=[bth])
                        sn = sb("snw", [128, 8, 64], F32, st2); cs = sb("csw", [128, 8, 64], F32, st2); bsc_ = Buf()
                        sincos(st2, "w", th[:], bth, [128, 8, 64], sn[:], cs[:], bsc_)
                        k.op("dve", lambda e: e.tensor_tensor(cs[:], cs[:], mag[:], ALU.mult), reads=[bsc_, bm_], writes=[bsc_])
                        k.op("dve", lambda e: e.tensor_scalar(cs[:], cs[:], -1.0, None, ALU.add), reads=[bsc_], writes=[bsc_])
                        k.op("dve", lambda e: e.tensor_tensor(sn[:], sn[:], mag[:], ALU.mult), reads=[bsc_, bm_], writes=[bsc_])
                        den = sb("den", [128, 8, 64], F32, st2); t1 = sb("t1w", [128, 8, 64], F32, st2); bden = Buf(); bt1 = Buf()
                        zr = sb("zr", [128, 8, 64], F32, st2); zi = sb("zi", [128, 8, 64], F32, st2); bz_ = Buf()
                        k.op("dve", lambda e: e.tensor_tensor(den[:], lrw[:], lrw[:], ALU.mult), reads=[blr], writes=[bden])
                        k.op("dve", lambda e: e.tensor_tensor(t1[:], liw[:], liw[:], ALU.mult), reads=[bli], writes=[bt1])
                        k.op("dve", lambda e: e.tensor_tensor(den[:], den[:], t1[:], ALU.add), reads=[bden, bt1], writes=[bden])
                        k.op("dve", lambda e: e.reciprocal(den[:], den[:]), reads=[bden], writes=[bden])
                        k.op("dve", lambda e: e.tensor_tensor(zr[:], cs[:], lrw[:], ALU.mult), reads=[bsc_, blr], writes=[bz_])
                        k.op("dve", lambda e: e.tensor_tensor(t1[:], sn[:], liw[:], ALU.mult), reads=[bsc_, bli], writes=[bt1])
                        k.op("dve", lambda e: e.tensor_tensor(zr[:], zr[:], t1[:], ALU.add), reads=[bz_, bt1], writes=[bz_])
                        k.op("dve", lambda e: e.tensor_tensor(zr[:], zr[:], den[:], ALU.mult), reads=[bz_, bden], writes=[bz_])
                        k.op("dve", lambda e: e.tensor_tensor(zi[:], sn[:], lrw[:], ALU.mult), reads=[bsc_, blr], writes=[bz_])
                        k.op("dve", lambda e: e.tensor_tensor(t1[:], cs[:], liw[:], ALU.mult), reads=[bsc_, bli], writes=[bt1])
                        k.op("dve", lambda e: e.tensor_tensor(zi[:], zi[:], t1[:], ALU.subtract), reads=[bz_, bt1], writes=[bz_])
                        k.op("dve", lambda e: e.tensor_tensor(zi[:], zi[:], den[:], ALU.mult), reads=[bz_, bden], writes=[bz_])
                        k.op("dve", lambda e: e.tensor_tensor(mag[:], zr[:], brw[:], ALU.mult), reads=[bz_, bbw], writes=[bm_])
                        k.op("dve", lambda e: e.tensor_tensor(t1[:], zi[:], biw[:], ALU.mult), reads=[bz_, bbw], writes=[bt1])
                        k.op("dve", lambda e: e.tensor_tensor(mag[:], mag[:], t1[:], ALU.subtract), reads=[bm_, bt1], writes=[bm_])
                        k.op("dve", lambda e: e.tensor_tensor(th[:], zr[:], biw[:], ALU.mult), reads=[bz_, bbw], writes=[bth])
                        k.op("dve", lambda e: e.tensor_tensor(t1[:], zi[:], brw[:], ALU.mult), reads=[bz_, bbw], writes=[bt1])
                        k.op("dve", lambda e: e.tensor_tensor(th[:], th[:], t1[:], ALU.add), reads=[bth, bt1], writes=[bth])
                        if debug_sub == "pre":
                            for i_, (t_, b_) in enumerate(((brw, bbw), (mag, bm_), (zr, bz_), (den, bden), (sn, bsc_), (lrw, blr), (liw, bli))):
                                k.dma("sp", outT[i_ * 128:(i_ + 1) * 128, 0:512], t_[:].rearrange("p c q -> p (c q)"), reads=[b_], writes=[Buf()])
                            k.dma("sp", outT[896:1024, 0:8], stw[:], reads=[bst], writes=[Buf()])
                            k.dma("sp", outT[896:1024, 8:16], sel[:], reads=[bsel], writes=[Buf()])
                            k.barrier()
                            return
                        for q in range(4):
                            for gl2 in range(2):
                                k.op("dve", lambda e, q=q, gl2=gl2: e.tensor_scalar(Wre[:, :, q, gl2 * 64:(gl2 + 1) * 64], mag[:], sel[:, q * 2 + gl2:q * 2 + gl2 + 1], None, ALU.mult),
                                     reads=[bm_, bsel], writes=[bW])
                                k.op("dve", lambda e, q=q, gl2=gl2: e.tensor_scalar(Wim[:, :, q, gl2 * 64:(gl2 + 1) * 64], th[:], sel[:, q * 2 + gl2:q * 2 + gl2 + 1], None, ALU.mult),
                                     reads=[bth, bsel], writes=[bW])
                    k.barrier()
                    if debug_sub == "pre2":
                        for c_ in range(8):
                            k.dma("sp", outT[c_ * 128:(c_ + 1) * 128, 0:512], Wre[:, c_, :, :].rearrange("p q m -> p (q m)"), reads=[bW], writes=[Buf()])
                        k.barrier()
                        return
                    WcR = sb("WcR", [128, 32, 128], BF16, st); WcI = sb("WcI", [128, 32, 128], BF16, st); bWc = Buf()
                    for b_ in range(32):
                        k.op("pool", lambda e, b_=b_: e.memset(WcR[:, b_, :], 0.0), writes=[bWc])
                        k.op("pool", lambda e, b_=b_: e.memset(WcI[:, b_, :], 0.0), writes=[bWc])
                    if debug_sub == "p_a":
                        for c_ in range(8):
                            k.dma("sp", outT[c_ * 128:(c_ + 1) * 128, 0:512], Wre[:, c_, :, :].rearrange("p q m -> p (q m)"), reads=[bW], writes=[Buf()])
                        k.barrier()
                        return
                    with ExitStack() as st2:
                        for (Wc_, src, sgn) in ((WcR, s5_cre[0, d], 1.0), (WcI, s5_cim[0, d], -1.0)):
                            t2 = sb("t2c", [128, 32, 16], F32, st2); bt2 = Buf()
                            v = src.rearrange("(b gl) j p -> gl p b j", gl=2)
                            for gl2 in range(2):
                                for hb in range(32):
                                    k.dma("sp" if hb % 2 else "pool", t2[gl2 * 64:(gl2 + 1) * 64, hb, :], v[gl2][:, hb, :], writes=[bt2],
                                          allow_slow_non_contiguous=True)
                            for gl2 in range(2):
                                for q in range(4):
                                    col = (2 * q + gl2) * 16
                                    k.op("dve", lambda e, gl2=gl2, q=q, col=col, Wc_=Wc_, t2=t2, sgn=sgn: e.tensor_scalar(
                                        Wc_[gl2 * 64:(gl2 + 1) * 64, q::4, col:col + 16], t2[gl2 * 64:(gl2 + 1) * 64, q::4, :], sgn, None, ALU.mult),
                                        reads=[bt2], writes=[bWc])
                    k.barrier()
                    if debug_sub == "pre3":
                        for c_ in range(8):
                            k.dma("sp", outT[c_ * 128:(c_ + 1) * 128, 0:512], Wre[:, c_, :, :].rearrange("p q m -> p (q m)"), reads=[bW], writes=[Buf()])
                        k.barrier()
                        return
                    if d == 1:
                        gw_, bgw_ = load_w(st, "s5glu", s5_glu_w[0], 8, 2 * D)
                        gbias, bgb_ = small("s5gb", [128, 16], s5_glu_b[0].rearrange("(m p) -> p m", p=128))
                        dskT, bdsk_ = small("s5d", [128, 8], s5_d[0].rearrange("(c p) -> p c", p=128))
                    nscs = [norm_scratch(st, T) for _ in range(2)]
                    xt = [sb("s5_x%d" % i, [128, 8, T], F32, st) for i in range(2)]; bx = [Buf(), Buf()]
                    hTs = [sb("s5_h%d" % i, [128, 8, T], BF16, st) for i in range(2)]; bhs = [Buf(), Buf()]
                    stR = sb("s5_stR", [128, 32], F32, st); stI = sb("s5_stI", [128, 32], F32, st); bstate = [Buf() for _ in range(32)]
                    k.op("pool", lambda e: e.memset(stR[:], 0.0), writes=bstate)
                    k.op("pool", lambda e: e.memset(stI[:], 0.0), writes=bstate)
                    NB = 6
                    def mk(name, dt=F32):
                        return [sb("%s%d" % (name, i), [128, T], dt, st) for i in range(NB)], [Buf() for _ in range(NB)]
                    ur, bur = mk("s5_ur"); ui, bui = mk("s5_ui")
                    ta, bta = mk("s5_ta"); tb_, btb = mk("s5_tb"); tc, btc = mk("s5_tc"); td, btd = mk("s5_td")
                    vr, bvr = mk("s5_vr"); vi, bvi = mk("s5_vi")
                    q1, bq1 = mk("s5_q1", BF16); q2, bq2 = mk("s5_q2", BF16); q3, bq3 = mk("s5_q3", BF16); q4, bq4 = mk("s5_q4", BF16)
                    stt_ = [sb("s5_stt%d" % i, [128, 2], F32, st) for i in range(NB)]; bstt = [Buf() for _ in range(NB)]
                    yo = [sb("s5_yo%d" % i, [128, T], F32, st) for i in range(2)]; byo = [Buf(), Buf()]
                    if d == 1:
                        yf = [sb("s5_yf%d" % i, [128, T], F32, st) for i in range(2)]; byf = [Buf(), Buf()]
                        gT = sb("s5_g", [128, 8, T], BF16, st); bg = Buf()
                        ga = [sb("s5_ga%d" % i, [128, T], F32, st) for i in range(2)]; bga = [Buf(), Buf()]
                    order = tiles if d == 0 else [tiles[1], tiles[0]] + tiles[:1:-1]
                    rv = (lambda ap: ap) if d == 0 else (lambda ap: ap[:, ::-1])
                    pspool[0] = [6, 7]
                    units = []

                    def make_unit(ti, t0, n, vsel, c, q, un):
                        x, b_x = xt[ti % 2], bx[ti % 2]
                        hT, bh = hTs[ti % 2], bhs[ti % 2]
                        b = c * 4 + q
                        i3 = un % NB
                        pu, pub = ps[un % 3], psb[un % 3]
                        py, pyb = ps[3 + (un // 4) % 3], psb[3 + (un // 4) % 3]
                        cs_ = rv(cosT[:, b, :]); sn_ = rv(sinT[:, b, :])
                        rb = rhoT[:, b, :]
                        last = (n - 1) if d == 0 else 0

                        def s0():
                            if c == 0 and q == 0:
                                k.dma("sp", x[:, :, :n], xres[:, t0:t0 + n].rearrange("(c p) t -> p c t", p=128), writes=[b_x])
                                norm_mod(nscs[ti % 2], x, b_x, n, vsel, A1, 0, hT, bh)
                            k.op("pe", lambda e: e.matmul(pu[:, 0:128], Wre[:, c, q, :], hT[:, c, :], start=True, stop=True), reads=[bW, bh], writes=[pub], inc=False)
                            k.op("pe", lambda e: e.matmul(pu[:, 128:256], Wim[:, c, q, :], hT[:, c, :], start=True, stop=True), reads=[bW, bh], writes=[pub])

                        def s1():
                            k.op("act", lambda e: e.copy(ur[i3][:], pu[:, 0:128]), reads=[pub], writes=[bur[i3]])
                            k.op("act", lambda e: e.copy(ui[i3][:], pu[:, 128:256]), reads=[pub], writes=[bui[i3]])

                        def s2():
                            k.op("dve", lambda e: e.tensor_tensor(ta[i3][:], ur[i3][:], cs_, ALU.mult), reads=[bur[i3], btab], writes=[bta[i3]])
                            k.op("pool", lambda e: e.tensor_tensor(tb_[i3][:], ui[i3][:], sn_, ALU.mult), reads=[bui[i3], btab], writes=[btb[i3]])
                            k.op("dve", lambda e: e.tensor_tensor(tc[i3][:], ui[i3][:], cs_, ALU.mult), reads=[bui[i3], btab], writes=[btc[i3]])
                            k.op("pool", lambda e: e.tensor_tensor(td[i3][:], ur[i3][:], sn_, ALU.mult), reads=[bur[i3], btab], writes=[btd[i3]])

                        def s3():
                            k.op("pool", lambda e: e.tensor_tensor(ta[i3][:], ta[i3][:], tb_[i3][:], ALU.add), reads=[bta[i3], btb[i3]], writes=[bta[i3]])
                            k.op("pool", lambda e: e.tensor_tensor(tc[i3][:], tc[i3][:], td[i3][:], ALU.subtract), reads=[btc[i3], btd[i3]], writes=[btc[i3]])

                        def s4():
                            k.op("dve", lambda e: e.tensor_tensor_scan(rv(vr[i3][:]), rb, rv(ta[i3][:]), stR[:, b:b + 1], ALU.mult, ALU.add),
                                 reads=[bta[i3], bpar, bstate[b]], writes=[bvr[i3]])
                            k.op("dve", lambda e: e.tensor_tensor_scan(rv(vi[i3][:]), rb, rv(tc[i3][:]), stI[:, b:b + 1], ALU.mult, ALU.add),
                                 reads=[btc[i3], bpar, bstate[b]], writes=[bvi[i3]])

                        def s5():
                            k.op("pool", lambda e: e.tensor_tensor(q1[i3][:], vr[i3][:], cs_, ALU.mult), reads=[bvr[i3], btab], writes=[bq1[i3]])
                            k.op("dve", lambda e: e.scalar_tensor_tensor(q2[i3][:], vi[i3][:], -1.0, sn_, ALU.mult, ALU.mult), reads=[bvi[i3], btab], writes=[bq2[i3]])
                            k.op("pool", lambda e: e.tensor_tensor(q3[i3][:], vi[i3][:], cs_, ALU.mult), reads=[bvi[i3], btab], writes=[bq3[i3]])
                            k.op("dve", lambda e: e.tensor_tensor(q4[i3][:], vr[i3][:], sn_, ALU.mult), reads=[bvr[i3], btab], writes=[bq4[i3]])
                            k.op("act", lambda e: e.activation(stt_[i3][:, 0:1], vi[i3][:, last:last + 1], AF.Identity, scale=sn_[:, last:last + 1]),
                                 reads=[bvi[i3], btab], writes=[bstt[i3]])
                            k.op("act", lambda e: e.activation(stt_[i3][:, 0:1], stt_[i3][:, 0:1], AF.Identity, scale=-1.0), reads=[bstt[i3]], writes=[bstt[i3]])
                            k.op("act", lambda e: e.activation(stt_[i3][:, 1:2], vr[i3][:, last:last + 1], AF.Identity, scale=sn_[:, last:last + 1]),
                                 reads=[bvr[i3], btab], writes=[bstt[i3]])

                        def s6():
                            k.op("act", lambda e: e.activation(stR[:, b:b + 1], vr[i3][:, last:last + 1], AF.Identity, bias=stt_[i3][:, 0:1], scale=cs_[:, last:last + 1]),
                                 reads=[bvr[i3], btab, bstt[i3]], writes=[bstate[b]])
                            k.op("act", lambda e: e.activation(stI[:, b:b + 1], vi[i3][:, last:last + 1], AF.Identity, bias=stt_[i3][:, 1:2], scale=cs_[:, last:last + 1]),
                                 reads=[bvi[i3], btab, bstt[i3]], writes=[bstate[b]])
                            k.op("pe", lambda e: e.matmul(py[:, 0:128], WcR[:, b, :], q1[i3][:], start=(q == 0), stop=False), reads=[bWc, bq1[i3]], writes=[pyb], inc=False)
                            k.op("pe", lambda e: e.matmul(py[:, 0:128], WcR[:, b, :], q2[i3][:], start=False, stop=False), reads=[bWc, bq2[i3]], writes=[pyb], inc=False)
                            k.op("pe", lambda e: e.matmul(py[:, 0:128], WcI[:, b, :], q3[i3][:], start=False, stop=False), reads=[bWc, bq3[i3]], writes=[pyb], inc=False)
                            k.op("pe", lambda e: e.matmul(py[:, 0:128], WcI[:, b, :], q4[i3][:], start=False, stop=(q == 3)), reads=[bWc, bq4[i3]], writes=[pyb], inc=(q == 3))
                            if q != 3:
                                return
                            if d == 0:
                                o, bo = yo[c % 2], byo[c % 2]
                                k.op("act", lambda e: e.copy(o[:], py[:, 0:128]), reads=[pyb], writes=[bo])
                                k.dma("pool", yfT[c * 128:(c + 1) * 128, t0:t0 + n], o[:], reads=[bo], writes=[Buf()])
                                return
                            if vsel == 1 and not ctx_out:
                                return
                            f_, bf_ = yf[c % 2], byf[c % 2]
                            k.dma("sp", f_[:], yfT[c * 128:(c + 1) * 128, t0:t0 + n], writes=[bf_])
                            k.op("dve", lambda e: e.tensor_tensor(f_[:], f_[:], py[:, 0:128], ALU.add), reads=[pyb, bf_], writes=[bf_])
                            k.op("dve", lambda e: e.scalar_tensor_tensor(f_[:], hT[:, c, :], dskT[:, c:c + 1], f_[:], ALU.mult, ALU.add), reads=[bh, bdsk_, bf_], writes=[bf_])
                            k.op("act", lambda e: e.activation(gT[:, c, :], f_[:], AF.Gelu_apprx_tanh), reads=[bf_], writes=[bg])
                            if c != 7:
                                return
                            for c2 in range(8):
                                pa_, pab_ = next_ps()
                                pg_, pgb_ = next_ps()
                                for kc in range(8):
                                    k.op("pe", lambda e, kc=kc, c2=c2, pa_=pa_: e.matmul(pa_[:, 0:128], gw_[:, kc, c2 * 128:(c2 + 1) * 128], gT[:, kc, :], start=(kc == 0), stop=(kc == 7)),
                                         reads=[bgw_, bg], writes=[pab_], inc=(kc == 7))
                                for kc in range(8):
                                    k.op("pe", lambda e, kc=kc, c2=c2, pg_=pg_: e.matmul(pg_[:, 0:128], gw_[:, kc, D + c2 * 128:D + (c2 + 1) * 128], gT[:, kc, :], start=(kc == 0), stop=(kc == 7)),
                                         reads=[bgw_, bg], writes=[pgb_], inc=(kc == 7))
                                a_, ba_ = ga[c2 % 2], bga[c2 % 2]
                                k.op("act", lambda e, pg_=pg_, a_=a_, c2=c2: e.activation(a_[:], pg_[:, 0:128], AF.Sigmoid, bias=gbias[:, 8 + c2:9 + c2], scale=1.0), reads=[pgb_, bgb_], writes=[ba_])
                                k.op("dve", lambda e, pa_=pa_, a_=a_, c2=c2: e.scalar_tensor_tensor(a_[:], pa_[:, 0:128], gbias[:, c2:c2 + 1], a_[:], ALU.add, ALU.mult), reads=[pab_, bgb_, ba_], writes=[ba_])
                                k.op("dve", lambda e, a_=a_, c2=c2: e.scalar_tensor_tensor(x[:, c2, :], a_[:], mod[:, 16 + c2, vsel:vsel + 1], x[:, c2, :], ALU.mult, ALU.add),
                                     reads=[ba_, b_mod, b_x], writes=[b_x])
                            k.dma("pool", xres[:, t0:t0 + n].rearrange("(c p) t -> p c t", p=128), x[:, :, :n], reads=[b_x], writes=[Buf()])

                        return [s0, s1, s2, s3, s4, s5, s6]

                    un = 0
                    for ti, (t0, n, vsel) in enumerate(order):
                        for c in range(8):
                            for q in range(4):
                                units.append(make_unit(ti, t0, n, vsel, c, q, un))
                                un += 1
                    NS = 7
                    for step in range(len(units) + NS - 1):
                        for j in range(NS - 1, -1, -1):
                            u = step - j
                            if 0 <= u < len(units):
                                units[u][j]()
                    pspool[0] = list(range(8))
                k.barrier()
                if debug == "s5f" and d == 0:
                    return

        def mla_layer(li, ctx_out):
            TT = 512
            SCALE = 96.0 ** -0.5
            with ExitStack() as L:
                qnT = sb("qnT", [128, 3, NLAT], BF16, L); bqn = Buf()
                kvnT = sb("kvnT", [128, 2, NTOK], BF16, L); bkvn = Buf()
                KT = sb("KT", [128, NTOK], BF16, L); bKr = Buf(); bKn = Buf()
                k.op("pool", lambda e: e.memset(KT[96:97, :], 1.0), writes=[bKr])
                with ExitStack() as st:
                    w, bw = load_w(st, "mlin", mla_in_w[0], 8, 672)
                    wkr = sb("wkr", [128, 8, 96], BF16, st); wkrs = sb("wkrs", [128, 8, 96], BF16, st); bwk = Buf()
                    k.op("pool", lambda e: e.memset(wkr[:], 0.0), writes=[bwk])
                    k.op("pool", lambda e: e.memset(wkrs[:], 0.0), writes=[bwk])
                    k.dma("pool", wkr[:, :, 64:96], mla_in_w[0][:, 640:672].rearrange("(c p) n -> p c n", p=128), writes=[bwk], allow_slow_non_contiguous=True)
                    k.dma("pool", wkrs[:, :, 64:96], mla_inw_sw.rearrange("(c p) n -> p c n", p=128), writes=[bwk], allow_slow_non_contiguous=True)
                    qnw = sb("qnw", [128, 3], F32, st); kvnw = sb("kvnw", [128, 2], F32, st); bnw_ = Buf()
                    k.dma("sp", qnw[:], mla_q_norm_w[0].rearrange("(c p) -> p c", p=128), writes=[bnw_], allow_slow_non_contiguous=True)
                    k.dma("sp", kvnw[:], mla_kv_norm_w[0].rearrange("(c p) -> p c", p=128), writes=[bnw_], allow_slow_non_contiguous=True)
                    nsc = norm_scratch(st, TT)
                    xt = [sb("ma_x%d" % i, [128, 8, TT], F32, st) for i in range(2)]; bx = [Buf(), Buf()]
                    hT = sb("ma_h", [128, 8, TT], BF16, st); bh = Buf()
                    ql = sb("ma_ql", [128, 3, TT], F32, st); bql = Buf()
                    sq = sb("ma_sq", [128, TT], F32, st); bsq = Buf()
                    rs = sb("ma_rs", [128, TT], F32, st); brs = Buf()
                    rc = [sb("ma_rc%d" % i, [128, TT], F32, st) for i in range(2)]; rsn = [sb("ma_rsn%d" % i, [128, TT], F32, st) for i in range(2)]; brc = [Buf(), Buf()]
                    t1 = sb("ma_t1", [128, TT], F32, st); t2 = sb("ma_t2", [128, TT], F32, st); bt1 = Buf(); bt2 = Buf()

                    def lat_norm(ncl, col0, nw_t, dim, out_t, out_b, tsl):
                        pss, pssb = next_ps()
                        for c in range(ncl):
                            p, pb = next_ps()
                            for kc in range(8):
                                k.op("pe", lambda e, p=p, kc=kc, c=c: e.matmul(p[:, :n], w[:, kc, col0 + c * 128:col0 + (c + 1) * 128], hT[:, kc, :n], start=(kc == 0), stop=(kc == 7)),
                                     reads=[bw, bh], writes=[pb], inc=(kc == 7))
                            k.op("act", lambda e, p=p, c=c: e.copy(ql[:, c, :n], p[:, :n]), reads=[pb], writes=[bql])
                            k.op("act", lambda e, p=p: e.activation(sq[:, :n], p[:, :n], AF.Square), reads=[pb], writes=[bsq])
                            k.op("pe", lambda e, pss=pss, c=c: e.matmul(pss[:, :n], ones_f[:], sq[:, :n], start=(c == 0), stop=(c == ncl - 1)), reads=[b_ones, bsq], writes=[pssb])
                        k.op("act", lambda e, pss=pss: e.activation(rs[:, :n], pss[:, :n], AF.Sqrt, bias=EPS, scale=1.0 / dim), reads=[pssb], writes=[brs])
                        k.op("dve", lambda e: e.reciprocal(rs[:, :n], rs[:, :n]), reads=[brs], writes=[brs])
                        for c in range(ncl):
                            k.op("dve", lambda e, c=c: e.scalar_tensor_tensor(out_t[:, c, tsl], ql[:, c, :n], nw_t[:, c:c + 1], rs[:, :n], ALU.mult, ALU.mult),
                                 reads=[bql, bnw_, brs], writes=[out_b])

                    for ti, (t0, n, vsel) in enumerate(seg_tiles(TT)):
                        x, b_x = xt[ti % 2], bx[ti % 2]
                        k.dma("sp", x[:, :, :n], xres[:, t0:t0 + n].rearrange("(c p) t -> p c t", p=128), writes=[b_x])
                        norm_mod(nsc, x, b_x, n, vsel, A1, 0, hT, bh)
                        if vsel == 0:
                            lat_norm(3, 0, qnw, 384.0, qnT, bqn, slice(t0 - NCTX, t0 - NCTX + n))
                        lat_norm(2, 384, kvnw, 256.0, kvnT, bkvn, slice(t0, t0 + n))
                        pA, pAb = next_ps()
                        for kc in range(8):
                            k.op("pe", lambda e, pA=pA, kc=kc: e.matmul(pA[0:96, :n], wkr[:, kc, :], hT[:, kc, :n], start=(kc == 0), stop=(kc == 7)), reads=[bwk, bh], writes=[pAb], inc=(kc == 7))
                        if vsel == 1:
                            k.op("act", lambda e, pA=pA: e.copy(KT[64:96, t0:t0 + n], pA[64:96, :n]), reads=[pAb], writes=[bKr])
                        else:
                            pB, pBb = next_ps()
                            for kc in range(8):
                                k.op("pe", lambda e, pB=pB, kc=kc: e.matmul(pB[0:96, :n], wkrs[:, kc, :], hT[:, kc, :n], start=(kc == 0), stop=(kc == 7)), reads=[bwk, bh], writes=[pBb], inc=(kc == 7))
                            i2 = ti % 2
                            k.dma("sp", rc[i2][64:96, :n], ropeC_in[:, t0 - NCTX:t0 - NCTX + n], writes=[brc[i2]])
                            k.dma("sp", rsn[i2][64:96, :n], ropeS_in[:, t0 - NCTX:t0 - NCTX + n], writes=[brc[i2]])
                            k.op("dve", lambda e, pA=pA, i2=i2: e.tensor_tensor(t1[64:96, :n], pA[64:96, :n], rc[i2][64:96, :n], ALU.mult), reads=[pAb, brc[i2]], writes=[bt1])
                            k.op("dve", lambda e, pB=pB, i2=i2: e.tensor_tensor(t2[64:96, :n], pB[64:96, :n], rsn[i2][64:96, :n], ALU.mult), reads=[pBb, brc[i2]], writes=[bt2])
                            k.op("pool", lambda e: e.tensor_tensor(KT[64:96, t0:t0 + n], t1[64:96, :n], t2[64:96, :n], ALU.add), reads=[bt1, bt2], writes=[bKr])
                k.barrier()
                with ExitStack() as st:
                    kvbw, bkvbw = load_w(st, "kvbw", mla_kvb_w[0], 2, 2048)
                    qbw, bqbw = load_w(st, "qbw", mla_qb_w[0], 3, 1536)
                    qbws, bqbws = load_w(st, "qbws", mla_qbw_sw, 3, 512)
                    e96 = sb("e96", [128, 128], BF16, st); be96 = Buf()
                    k.dma("pool", e96[:], e96_in, writes=[be96])
                    odm = sb("odm", [128, 2, 128], BF16, st); bodm = Buf()
                    k.dma("pool", odm[:], odm_in.rearrange("a p m -> p a m"), writes=[bodm])
                    wv = [sb("wv%d" % i, [128, 2, 128], BF16, st) for i in range(2)]; bwv = [Buf(), Buf()]
                    wqr = sb("wqr", [128, 3, 96], BF16, st); wqrs = sb("wqrs", [128, 3, 96], BF16, st); bwq = Buf()
                    for t_ in (wv[0], wv[1], wqr, wqrs):
                        k.op("pool", lambda e, t_=t_: e.memset(t_[:], 0.0), writes=[bwv[0], bwv[1], bwq])
                    QT = sb("QT", [128, NLAT], BF16, st); bQ = Buf()
                    Vt = sb("Vt", [128, NCH, 128], BF16, st); bV = Buf()
                    sqb = sb("mb_sq", [128, TT], BF16, st); bsqb = Buf()
                    kk = sb("mb_kk", [128, TT], F32, st); bkk = Buf()
                    kmx = sb("mb_kmx", [128, 2], F32, st); bkmx = Buf()
                    rc = [sb("mb_rc%d" % i, [128, TT], F32, st) for i in range(2)]; rsn = [sb("mb_rsn%d" % i, [128, TT], F32, st) for i in range(2)]; brc = [Buf(), Buf()]
                    t1 = sb("mb_t1", [128, TT], F32, st); t2 = sb("mb_t2", [128, TT], F32, st); bt1 = Buf(); bt2 = Buf()
                    Pt = [sb("mb_P%d" % i, [128, TT], BF16, st) for i in range(3)]; bP = [Buf() for _ in range(3)]
                    rd = sb("mb_rd", [128, TT], F32, st); brd = Buf()
                    xs_ = sb("mb_xs", [128, TT], F32, st); bxs_ = Buf()
                    swp = sb("mb_swp", [128, 128], F32, st); bswp = Buf()
                    k.dma("sp", swp[:], swp_in, writes=[bswp])
                    ot = [sb("mb_o%d" % i, [128, TT], BF16, st) for i in range(2)]; bot = [Buf(), Buf()]
                    pspool[0] = [4, 5, 6, 7]
                    ktiles = seg_tiles(TT)
                    pn = 0
                    for h in range(16):
                        par = h % 2
                        off = par * 64
                        k.op("pool", lambda e, h=h, par=par, off=off: e.tensor_copy(wv[par][:, :, off:off + 64], kvbw[:, :, h * 128 + 64:h * 128 + 128]), reads=[bkvbw], writes=[bwv[par]])
                        k.op("pool", lambda e, h=h: e.tensor_copy(wqr[:, :, 64:96], qbw[:, :, h * 96 + 64:h * 96 + 96]), reads=[bqbw], writes=[bwq])
                        k.op("pool", lambda e, h=h: e.tensor_copy(wqrs[:, :, 64:96], qbws[:, :, h * 32:h * 32 + 32]), reads=[bqbws], writes=[bwq])
                        k.op("pool", lambda e: e.memset(kmx[:], 0.0), writes=[bkmx])
                        for (t0, n, vsel) in ktiles:
                            p, pb = next_ps()
                            for c in range(2):
                                k.op("pe", lambda e, p=p, c=c, h=h: e.matmul(p[0:64, :n], kvbw[:, c, h * 128:h * 128 + 64], kvnT[:, c, t0:t0 + n], start=(c == 0), stop=(c == 1)),
                                     reads=[bkvbw, bkvn], writes=[pb], inc=(c == 1))
                            k.op("act", lambda e, p=p: e.copy(KT[0:64, t0:t0 + n], p[0:64, :n]), reads=[pb], writes=[bKn])
                            k.op("act", lambda e: e.activation(sqb[0:96, :n], KT[0:96, t0:t0 + n], AF.Square), reads=[bKn, bKr], writes=[bsqb])
                            p2, p2b = next_ps()
                            k.op("pe", lambda e, p2=p2: e.matmul(p2[0:97, :n], e96[0:96, 0:97], sqb[0:96, :n], start=True, stop=True), reads=[be96, bsqb], writes=[p2b])
                            k.op("dve", lambda e, p2=p2: e.reduce_max(kmx[96:97, 1:2], p2[96:97, :n], AX.X), reads=[p2b], writes=[bkmx])
                            k.op("dve", lambda e: e.tensor_tensor(kmx[96:97, 0:1], kmx[96:97, 0:1], kmx[96:97, 1:2], ALU.max), reads=[bkmx], writes=[bkmx])
                        k.op("act", lambda e: e.activation(kmx[96:97, 0:1], kmx[96:97, 0:1], AF.Sqrt), reads=[bkmx], writes=[bkmx])
                        k.op("dve", lambda e: e.tensor_scalar(kmx[96:97, 0:1], kmx[96:97, 0:1], -1.0, None, ALU.mult), reads=[bkmx], writes=[bkmx])
                        for b4 in range(0, NCH, 4):
                            p, pb = next_ps()
                            nb = min(4, NCH - b4)
                            for j in range(nb):
                                blk = b4 + j
                                for c in range(2):
                                    k.op("pe", lambda e, p=p, j=j, blk=blk, c=c, par=par: e.matmul(p[:, j * 128:(j + 1) * 128], kvnT[:, c, blk * 128:(blk + 1) * 128], wv[par][:, c, :],
                                                                                                start=(c == 0), stop=(c == 1)),
                                         reads=[bkvn, bwv[par]], writes=[pb], inc=(c == 1 and j == nb - 1))
                            k.op("dve", lambda e, p=p, b4=b4, nb=nb, off=off: e.tensor_copy(Vt[:, b4:b4 + nb, off:off + 64], p[:, 0:nb * 128].rearrange("p (j m) -> p j m", m=128)[:, :, off:off + 64]),
                                 reads=[pb], writes=[bV])
                        k.op("pool", lambda e, off=off: e.memset(Vt[:, :, 64 - off:128 - off], 1.0), writes=[bV])
                        for qi in range(NLAT // TT):
                            q0 = qi * TT
                            p, pb = next_ps()
                            for c in range(3):
                                k.op("pe", lambda e, p=p, c=c, h=h: e.matmul(p[0:64, :], qbw[:, c, h * 96:h * 96 + 64], qnT[:, c, q0:q0 + TT], start=(c == 0), stop=(c == 2)),
                                     reads=[bqbw, bqn], writes=[pb], inc=(c == 2))
                            k.op("act", lambda e, p=p: e.copy(QT[0:64, q0:q0 + TT], p[0:64, :]), reads=[pb], writes=[bQ])
                            pA, pAb = next_ps()
                            pB, pBb = next_ps()
                            for c in range(3):
                                k.op("pe", lambda e, pA=pA, c=c: e.matmul(pA[0:96, :], wqr[:, c, :], qnT[:, c, q0:q0 + TT], start=(c == 0), stop=(c == 2)), reads=[bwq, bqn], writes=[pAb], inc=(c == 2))
                            for c in range(3):
                                k.op("pe", lambda e, pB=pB, c=c: e.matmul(pB[0:96, :], wqrs[:, c, :], qnT[:, c, q0:q0 + TT], start=(c == 0), stop=(c == 2)), reads=[bwq, bqn], writes=[pBb], inc=(c == 2))
                            i2 = qi % 2
                            k.dma("sp", rc[i2][64:96, :], ropeC_in[:, q0:q0 + TT], writes=[brc[i2]])
                            k.dma("sp", rsn[i2][64:96, :], ropeS_in[:, q0:q0 + TT], writes=[brc[i2]])
                            k.op("dve", lambda e, pA=pA, i2=i2: e.tensor_tensor(t1[64:96, :], pA[64:96, :], rc[i2][64:96, :], ALU.mult), reads=[pAb, brc[i2]], writes=[bt1])
                            k.op("dve", lambda e, pB=pB, i2=i2: e.tensor_tensor(t2[64:96, :], pB[64:96, :], rsn[i2][64:96, :], ALU.mult), reads=[pBb, brc[i2]], writes=[bt2])
                            k.op("pool", lambda e: e.tensor_tensor(QT[64:96, q0:q0 + TT], t1[64:96, :], t2[64:96, :], ALU.add), reads=[bt1, bt2], writes=[bQ])
                            k.op("act", lambda e: e.activation(sqb[0:96, :], QT[0:96, q0:q0 + TT], AF.Square), reads=[bQ], writes=[bsqb])
                            p2, p2b = next_ps()
                            k.op("pe", lambda e, p2=p2: e.matmul(p2[0:97, :], e96[0:96, 0:97], sqb[0:96, :], start=True, stop=True), reads=[be96, bsqb], writes=[p2b])
                            k.op("act", lambda e, p2=p2: e.activation(kk[96:97, :], p2[96:97, :], AF.Sqrt), reads=[p2b], writes=[bkk])
                            k.op("dve", lambda e: e.tensor_scalar(QT[96:97, q0:q0 + TT], kk[96:97, :], kmx[96:97, 0:1], None, ALU.mult), reads=[bkk, bkmx], writes=[bQ])
                        for qi in range(NLAT // TT):
                            q0 = qi * TT
                            pnum, pnb = ps[qi % 2], psb[qi % 2]
                            sbank = {}

                            def emit_S(blk):
                                nonlocal pn
                                p, pb = next_ps()
                                k.op("pe", lambda e, p=p, blk=blk: e.matmul(p[:, :], KT[0:97, blk * 128:(blk + 1) * 128], QT[0:97, q0:q0 + TT], start=True, stop=True),
                                     reads=[bKn, bKr, bQ], writes=[pb])
                                i3 = pn % 3
                                pn += 1
                                k.op("act", lambda e, p=p, i3=i3: e.activation(Pt[i3][:], p[:, :], AF.Exp, scale=SCALE), reads=[pb], writes=[bP[i3]])
                                sbank[blk] = i3
                            emit_S(0)
                            emit_S(1)
                            for blk in range(NCH):
                                i3 = sbank.pop(blk)
                                k.op("pe", lambda e, blk=blk, i3=i3, pnum=pnum: e.matmul(pnum[:, :], Vt[:, blk, :], Pt[i3][:], start=(blk == 0), stop=(blk == NCH - 1)),
                                     reads=[bV, bP[i3]], writes=[pnb])
                                if blk + 2 < NCH:
                                    emit_S(blk + 2)
                            k.op("act", lambda e, pnum=pnum: e.copy(xs_[:], pnum[:, :]), reads=[pnb], writes=[bxs_])
                            psw, pswb = ps[2 + qi % 2], psb[2 + qi % 2]
                            k.op("pe", lambda e, psw=psw: e.matmul(psw[:, :], swp[:], xs_[:], start=True, stop=True), reads=[bswp, bxs_], writes=[pswb])
                            k.op("dve", lambda e, psw=psw, off=off: e.reciprocal(rd[off:off + 64, :], psw[off:off + 64, :]), reads=[pswb], writes=[brd])
                            o_, b_o = ot[qi % 2], bot[qi % 2]
                            k.op("dve", lambda e, off=off, o_=o_: e.tensor_tensor(o_[off:off + 64, :], xs_[off:off + 64, :], rd[off:off + 64, :], ALU.mult),
                                 reads=[bxs_, brd], writes=[b_o])
                            k.dma("sp", scrM[h * 64:(h + 1) * 64, NCTX + q0:NCTX + q0 + TT], o_[off:off + 64, :], reads=[b_o], writes=[Buf()])
                    pspool[0] = list(range(8))
                k.barrier()
            k.barrier()
            outproj_phase(mla_out_w[0], 8, scrM, ctx_out)

        with ExitStack() as st:
            cp = [sb("cp%d" % i, [128, 8, 512], F32, st) for i in range(2)]; bcp = [Buf(), Buf()]
            for ti, (t0, n, vsel) in enumerate(seg_tiles(512)):
                k.dma("sp", cp[ti % 2][:, :, :n], xT_in[:, t0:t0 + n].rearrange("(c p) t -> p c t", p=128), writes=[bcp[ti % 2]])
                k.dma("pool", xres[:, t0:t0 + n].rearrange("(c p) t -> p c t", p=128), cp[ti % 2][:, :, :n], reads=[bcp[ti % 2]], writes=[Buf()])
        k.barrier()
        for li in layers:
            ctx_out = li < 3
            compute_mod(li)
            if debug == "mod_only":
                continue
            if debug == "ffn_only":
                ffn_phase(li, ctx_out)
                continue
            if li == 0:
                ssd_layer(li, ctx_out)
                if debug in ("ssdA", "ssd1", "ssd2"):
                    continue
            if li == 1:
                lru_layer(li, ctx_out)
            if li == 3:
                mla_layer(li, ctx_out)
            if li == 2:
                s5_layer(li, ctx_out)
                if debug == "s5f" and debug_sub in ("pre", "pre2", "pre3", "p_a"):
                    k.barrier()
                    return nc, list(di.keys())
                if debug == "s5f":
                    with ExitStack() as st:
                        cp = [sb("dq%d" % i, [128, 8, 512], F32, st) for i in range(2)]; bcp = [Buf(), Buf()]
                        for ti, (t0, n, vsel) in enumerate(seg_tiles(512)):
                            if vsel == 1:
                                continue
                            k.dma("sp", cp[ti % 2][:, :, :n], scrA[0:D, t0:t0 + n].rearrange("(c p) t -> p c t", p=128), writes=[bcp[ti % 2]])
                            k.dma("pool", outT[:, t0 - NCTX:t0 - NCTX + n].rearrange("(c p) t -> p c t", p=128), cp[ti % 2][:, :, :n], reads=[bcp[ti % 2]], writes=[Buf()])
                    k.barrier()
                    print("ninst", k.ninst)
                    return nc, list(di.keys())
            ffn_phase(li, ctx_out)
        if do_final:
            final_phase()
        else:
            with ExitStack() as st:
                cp = [sb("dp%d" % i, [128, 8, 512], F32, st) for i in range(2)]; bcp = [Buf(), Buf()]
                for ti, (t0, n, vsel) in enumerate(seg_tiles(512)):
                    if vsel == 1:
                        continue
                    k.dma("sp", cp[ti % 2][:, :, :n], xres[:, t0:t0 + n].rearrange("(c p) t -> p c t", p=128), writes=[bcp[ti % 2]])
                    k.dma("pool", outT[:, t0 - NCTX:t0 - NCTX + n].rearrange("(c p) t -> p c t", p=128), cp[ti % 2][:, :, :n], reads=[bcp[ti % 2]], writes=[Buf()])
            k.barrier()
        k.barrier()
        print("ninst", k.ninst)
    return nc, list(di.keys())


def make_in_map(inputs, b, names, x_override=None, ctx_override=None):
    xb = inputs["x"][b] if x_override is None else x_override
    cb = inputs["ctx"][b] if ctx_override is None else ctx_override
    xT = np.ascontiguousarray(np.concatenate([cb, xb], axis=0).T.astype(np.float32))
    cc = np.ascontiguousarray(np.stack([inputs["c"][b], inputs["c_ctx"]], axis=1).astype(np.float32))
    ii = np.arange(128)
    U = (ii[:, None] <= ii[None, :]).astype(np.float32)
    masks = np.stack([U, U.T.copy(), (ii[:, None] > ii[None, :]).astype(np.float32), (ii[:, None] < ii[None, :]).astype(np.float32)], axis=0)
    m = {"xT": xT, "cc": cc, "ident": np.eye(128, dtype=np.float32), "masks": np.ascontiguousarray(masks)}
    m["ramp"] = np.ascontiguousarray(np.tile(np.arange(1, 129, dtype=np.float32)[None, :], (128, 1)))
    sel = np.zeros((128, 8), np.float32)
    for gl8 in range(8):
        sel[gl8 * 16:(gl8 + 1) * 16, gl8] = 1.0
    m["sel"] = sel
    perm = np.concatenate([np.arange(8, 16), np.arange(0, 8), np.arange(24, 32), np.arange(16, 24)])
    m["mla_inw_sw"] = np.ascontiguousarray(np.asarray(inputs["mla_in_w"], np.float32)[0][:, 640:672][:, perm])
    qb = np.asarray(inputs["mla_qb_w"], np.float32)[0].reshape(384, 16, 96)
    m["mla_qbw_sw"] = np.ascontiguousarray(qb[:, :, 64:96][:, :, perm].reshape(384, 512))
    rows = NLAT // 64
    row = np.repeat(np.arange(rows, dtype=np.float32), 64)
    col = np.tile(np.arange(64, dtype=np.float32), rows)
    inv_freq = (np.float32(10000.0) ** (-np.arange(8, dtype=np.float32) / np.float32(8))).astype(np.float32)
    ang = np.stack([row[:, None] * inv_freq, col[:, None] * inv_freq], axis=1).astype(np.float32)
    cs, sn = np.cos(ang).astype(np.float32), np.sin(ang).astype(np.float32)
    C = np.zeros((32, NLAT), np.float32); S = np.zeros((32, NLAT), np.float32)
    for half in range(2):
        for part in range(2):
            r0 = half * 16 + part * 8
            C[r0:r0 + 8] = cs[:, half, :].T
            S[r0:r0 + 8] = (-sn[:, half, :].T) if part == 0 else sn[:, half, :].T
    m["ropeC"] = C; m["ropeS"] = S
    e96 = np.zeros((128, 128), np.float32); e96[0:96, 96] = 1.0
    m["e96"] = e96
    odm = np.zeros((2, 128, 128), np.float32); odm[0, :, 0:64] = 1.0; odm[1, :, 64:128] = 1.0
    m["odm"] = odm
    swp = np.zeros((128, 128), np.float32)
    swp[(np.arange(128) + 64) % 128, np.arange(128)] = 1.0
    m["swp"] = swp
    for n in names:
        if n not in m:
            m[n] = np.ascontiguousarray(np.asarray(inputs[n], dtype=np.float32))
    return m


def kernel(**inputs):
    inputs = {k_: np.asarray(v) for k_, v in inputs.items()}
    nc, names = build()
    nb = inputs["x"].shape[0]
    in_maps = [make_in_map(inputs, b, names) for b in range(nb)]
    res = run_bass_kernel_spmd(nc, in_maps, core_ids=list(range(nb)))
    out = np.stack([np.ascontiguousarray(r["outT"].T) for r in res.results], axis=0)
    return out.astype(np.float32)
```

```python
import numpy as np
from contextlib import ExitStack
import concourse.bass as bass
import concourse.mybir as mybir
from concourse.bass_utils import run_bass_kernel_spmd

F32 = mybir.dt.float32
BF16 = mybir.dt.bfloat16
I32 = mybir.dt.int32
AF = mybir.ActivationFunctionType
ALU = mybir.AluOpType
AX = mybir.AxisListType

D = 1024
NCTX = 256
NLAT = 8192
NTOK = NCTX + NLAT
FH = 2816
EPS = 1e-6


class Buf:
    __slots__ = ("name", "lw", "rd")

    def __init__(self, name=""):
        self.name = name
        self.lw = None
        self.rd = []


ATTACH_WAITS = True


class K:
    NDMA = 8

    def __init__(self, nc, es):
        self.nc = nc
        self.es = es
        self.eng = {"pe": nc.tensor, "dve": nc.vector, "act": nc.scalar, "pool": nc.gpsimd, "sp": nc.sync}
        self.sem = {}
        self.cnt = {}
        for e in self.eng:
            self.sem[e] = es.enter_context(nc.semaphore("s_" + e))
            self.cnt[e] = 0
        self.dslots = {}
        for q in ("sp", "pool", "act"):
            self.dslots[q] = []
            for j in range(self.NDMA):
                key = "d_%s_%d" % (q, j)
                self.sem[key] = es.enter_context(nc.semaphore(key))
                self.cnt[key] = 0
                self.dslots[q].append(key)
        self.dnext = {q: 0 for q in self.dslots}
        self.seen = {e: {} for e in self.eng}
        self.ninst = 0

    def _need(self, e, deps, attach=False):
        best = {}
        for (k, v) in deps:
            if k == e and e == "pe":
                continue
            if best.get(k, 0) < v:
                best[k] = v
        todo = [(k, v) for k, v in best.items() if self.seen[e].get(k, 0) < v]
        held = None
        if attach and ATTACH_WAITS and todo:
            held = todo.pop()
        for k, v in todo:
            self.eng[e].wait_ge(self.sem[k], v)
            self.seen[e][k] = v
            self.ninst += 1
        if held is not None:
            self.seen[e][held[0]] = held[1]
        return held

    def _deps(self, reads, writes):
        deps = []
        for b in reads:
            if b.lw is not None:
                deps.append(b.lw)
        for b in writes:
            if b.lw is not None:
                deps.append(b.lw)
            deps.extend(b.rd)
        return deps

    def op(self, e, fn, reads=(), writes=(), inc=True):
        held = self._need(e, self._deps(reads, writes), attach=True)
        ins = fn(self.eng[e])
        if held is not None:
            ins._wait_ge(self.sem[held[0]], held[1])
        self.ninst += 1
        if inc:
            self.cnt[e] += 1
            ins.then_inc(self.sem[e], 1)
            v = self.cnt[e]
        else:
            v = self.cnt[e] + 1
        for b in reads:
            b.rd.append((e, v))
            if len(b.rd) > 24:
                b.rd = self._compact(b.rd)
        for b in writes:
            b.lw = (e, v)
            b.rd = []
        return ins

    @staticmethod
    def _compact(rd):
        best = {}
        for (k, v) in rd:
            if best.get(k, 0) < v:
                best[k] = v
        return list(best.items())

    def dma(self, q, out, in_, reads=(), writes=(), **kw):
        if out.dtype != in_.dtype:
            q = "pool"
        slot = self.dslots[q][self.dnext[q] % self.NDMA]
        self.dnext[q] += 1
        deps = self._deps(reads, writes)
        if self.cnt[slot] > 0:
            deps.append((slot, self.cnt[slot]))
        self._need(q, deps)
        ins = self.eng[q].dma_start(out=out, in_=in_, **kw)
        self.ninst += 1
        self.cnt[slot] += 16
        ins.then_inc(self.sem[slot], 16)
        v = self.cnt[slot]
        for b in reads:
            b.rd.append((slot, v))
            if len(b.rd) > 24:
                b.rd = self._compact(b.rd)
        for b in writes:
            b.lw = (slot, v)
            b.rd = []
        return ins

    def barrier(self):
        deps = [(k, v) for k, v in self.cnt.items() if v > 0]
        for e in self.eng:
            self._need(e, deps)


def build(layers=(0, 1, 2, 3), do_final=True, debug=None, debug_sub=None):
    nc = bass.Bass("TRN2", target_bir_lowering=False)
    di = {}

    def inp(name, shape):
        di[name] = nc.dram_tensor(name, list(shape), F32, kind="ExternalInput").ap()
        return di[name]

    xT_in = inp("xT", [D, NTOK])
    cc_in = inp("cc", [D, 2])
    ident_in = inp("ident", [128, 128])
    ada_w = inp("ada_w", [4, D, 6 * D]); ada_b = inp("ada_b", [4, 6 * D])
    norm1_w = inp("norm1_w", [4, D]); norm2_w = inp("norm2_w", [4, D])
    ffn_w13 = inp("ffn_w13", [4, D, 2 * FH]); ffn_w2 = inp("ffn_w2", [4, FH, D])
    lru_in_w = inp("lru_in_w", [1, D, 2560]); lru_conv_w = inp("lru_conv_w", [1, 4, 1280]); lru_conv_b = inp("lru_conv_b", [1, 1280])
    lru_gate_w = inp("lru_gate_w", [1, 2, 10, 128, 256]); lru_gate_b = inp("lru_gate_b", [1, 2, 10, 256])
    lru_a_param = inp("lru_a_param", [1, 2, 1280]); lru_out_w = inp("lru_out_w", [1, 1280, D])
    masks_in = inp("masks", [4, 128, 128])
    ramp_in = inp("ramp", [128, 128]); sel_in = inp("sel", [128, 8])
    s5_lre = inp("s5_lambda_re", [1, 2, 64, 64]); s5_lim = inp("s5_lambda_im", [1, 2, 64, 64]); s5_ls = inp("s5_log_step", [1, 2, 64])
    s5_bre = inp("s5_b_re", [1, 64, 64, 16]); s5_bim = inp("s5_b_im", [1, 64, 64, 16])
    s5_cre = inp("s5_c_re", [1, 2, 64, 16, 64]); s5_cim = inp("s5_c_im", [1, 2, 64, 16, 64])
    s5_d = inp("s5_d", [1, D]); s5_glu_w = inp("s5_glu_w", [1, D, 2 * D]); s5_glu_b = inp("s5_glu_b", [1, 2 * D])
    m2_in_w = inp("m2_in_w", [1, D, 5184]); m2_conv_w = inp("m2_conv_w", [1, 4, 3072]); m2_conv_b = inp("m2_conv_b", [1, 3072])
    m2_dt_bias = inp("m2_dt_bias", [1, 2, 32]); m2_a_log = inp("m2_a_log", [1, 2, 32]); m2_d = inp("m2_d", [1, 32])
    m2_norm_w = inp("m2_norm_w", [1, 2048]); m2_out_w = inp("m2_out_w", [1, 2048, D])
    mla_in_w = inp("mla_in_w", [1, D, 672]); mla_q_norm_w = inp("mla_q_norm_w", [1, 384]); mla_kv_norm_w = inp("mla_kv_norm_w", [1, 256])
    mla_qb_w = inp("mla_qb_w", [1, 384, 1536]); mla_kvb_w = inp("mla_kvb_w", [1, 256, 2048]); mla_out_w = inp("mla_out_w", [1, D, D])
    mla_inw_sw = inp("mla_inw_sw", [D, 32]); mla_qbw_sw = inp("mla_qbw_sw", [384, 512])
    ropeC_in = inp("ropeC", [32, NLAT]); ropeS_in = inp("ropeS", [32, NLAT])
    e96_in = inp("e96", [128, 128]); odm_in = inp("odm", [2, 128, 128]); swp_in = inp("swp", [128, 128])
    final_norm_w = inp("final_norm_w", [D])
    outT = nc.dram_tensor("outT", [D, NLAT], F32, kind="ExternalOutput").ap()
    xres = nc.dram_tensor("xres", [D, NTOK], F32).ap()
    scrA = nc.dram_tensor("scrA", [2048, NTOK], F32).ap()
    scrB = nc.dram_tensor("scrB", [3072, NTOK], F32).ap()
    scrM = nc.dram_tensor("scrM", [2048, NTOK], BF16).ap()
    NCH = NTOK // 128
    scrZ = nc.dram_tensor("scrZ", [NTOK, 2048], F32).ap()
    scrDT = nc.dram_tensor("scrDT", [NTOK, 64], F32).ap()
    scrS = nc.dram_tensor("scrS", [2, NCH, 128, 2048], BF16).ap()
    scrXB = nc.dram_tensor("scrXB", [3072, NTOK], BF16).ap()
    scrH = nc.dram_tensor("scrH", [2, NCH, 128, 2048], BF16).ap()
    scrDec = nc.dram_tensor("scrDec", [NCH, 128, 64], F32).ap()
    scrY = nc.dram_tensor("scrY", [NCH, 128, 2048], F32).ap()
    scrC = nc.dram_tensor("scrC", [NCH, 128, 512], BF16).ap()
    scrE = nc.dram_tensor("scrE", [NCH, 128, 64], F32).ap()

    es = ExitStack()
    with es:
        k = K(nc, es)
        uniq = [0]

        def sb(name, shape, dt, st=es):
            uniq[0] += 1
            return st.enter_context(nc.sbuf_tensor("%s_%d" % (name, uniq[0]), shape, dt))

        ps = [es.enter_context(nc.psum_tensor("ps%d" % i, [128, 512], F32)) for i in range(8)]
        psb = [Buf("ps%d" % i) for i in range(8)]
        psn = [0]

        pspool = [list(range(8))]

        def next_ps():
            pl = pspool[0]
            i = pl[psn[0] % len(pl)]
            psn[0] += 1
            return ps[i], psb[i]

        ones_f = sb("ones_f", [128, 128], F32); b_ones = Buf()
        k.op("pool", lambda e: e.memset(ones_f[:], 1.0), writes=[b_ones])
        ident_f = sb("ident_f", [128, 128], F32); b_ident = Buf()
        k.dma("sp", ident_f[:], ident_in, writes=[b_ident])
        ident_b = sb("ident_b", [128, 128], BF16); b_identb = Buf()
        k.dma("sp", ident_b[:], ident_in, writes=[b_identb])
        ccT = sb("ccT", [128, 8, 2], F32); b_cc = Buf()
        k.dma("sp", ccT[:], cc_in.rearrange("(c p) v -> p c v", p=128), writes=[b_cc])
        scT = sb("scT", [128, 8, 2], BF16); b_sc = Buf()
        k.op("act", lambda e: e.activation(scT[:], ccT[:], AF.Silu), reads=[b_cc], writes=[b_sc])
        mod = sb("mod", [128, 48, 2], F32); b_mod = Buf()
        adab = sb("adab", [128, 48], F32); b_adab = Buf()
        nw1 = sb("nw1", [128, 8], F32); nw2 = sb("nw2", [128, 8], F32); b_nw = Buf()
        A1 = sb("A1", [128, 8, 2], F32); A2 = sb("A2", [128, 8, 2], F32); b_A = Buf()

        def seg_tiles(tt):
            res = []
            t = 0
            while t < NCTX:
                n = min(tt, NCTX - t)
                res.append((t, n, 1))
                t += n
            while t < NTOK:
                n = min(tt, NTOK - t)
                res.append((t, n, 0))
                t += n
            return res

        def load_w(st, name, w_ap, kc, ncols, q="sp"):
            t = sb(name, [128, kc, ncols], BF16, st)
            b = Buf(name)
            src = w_ap.rearrange("(c p) n -> p c n", p=128)
            for c in range(kc):
                k.dma(q if c % 2 == 0 else "pool", t[:, c, :], src[:, c, :], writes=[b])
            return t, b

        def compute_mod(li):
            with ExitStack() as st:
                k.dma("sp", adab[:], ada_b[li].rearrange("(j p) -> p j", p=128), writes=[b_adab], allow_slow_non_contiguous=True)
                k.dma("sp", nw1[:], norm1_w[li].rearrange("(c p) -> p c", p=128), writes=[b_nw], allow_slow_non_contiguous=True)
                k.dma("sp", nw2[:], norm2_w[li].rearrange("(c p) -> p c", p=128), writes=[b_nw], allow_slow_non_contiguous=True)
                wt = [sb("adaw%d" % i, [128, 8, 1024], BF16, st) for i in range(2)]
                wb = [Buf(), Buf()]
                for j in range(6):
                    w, b = wt[j % 2], wb[j % 2]
                    src = ada_w[li][:, j * 1024:(j + 1) * 1024].rearrange("(c p) n -> p c n", p=128)
                    for c in range(8):
                        k.dma("sp" if c % 2 == 0 else "pool", w[:, c, :], src[:, c, :], writes=[b])
                    for m in range(8):
                        p, pb = next_ps()
                        for c in range(8):
                            k.op("pe", lambda e, p=p, w=w, c=c, m=m: e.matmul(p[:, 0:2], w[:, c, m * 128:(m + 1) * 128], scT[:, c, :],
                                                                          start=(c == 0), stop=(c == 7)),
                                 reads=[b, b_sc], writes=[pb], inc=(c == 7))
                        jj = j * 8 + m
                        k.op("act", lambda e, p=p, jj=jj: e.activation(mod[:, jj, :], p[:, 0:2], AF.Identity, bias=adab[:, jj:jj + 1], scale=1.0),
                             reads=[pb, b_adab], writes=[b_mod])
                for v in range(2):
                    k.op("dve", lambda e, v=v: e.scalar_tensor_tensor(A1[:, :, v], mod[:, 8:16, v], 1.0, nw1[:], ALU.add, ALU.mult),
                         reads=[b_mod, b_nw], writes=[b_A])
                    k.op("dve", lambda e, v=v: e.scalar_tensor_tensor(A2[:, :, v], mod[:, 32:40, v], 1.0, nw2[:], ALU.add, ALU.mult),
                         reads=[b_mod, b_nw], writes=[b_A])
            k.barrier()

        def norm_mod(st_tiles, xt, bx, n, vsel, A, shoff, hT, bh):
            sq, bsq, rstd, brs = st_tiles["sq"], st_tiles["bsq"], st_tiles["rstd"], st_tiles["brs"]
            p, pb = next_ps()
            for c in range(8):
                k.op("act", lambda e, c=c: e.activation(sq[:, :n], xt[:, c, :n], AF.Square), reads=[bx], writes=[bsq])
                k.op("pe", lambda e, c=c: e.matmul(p[:, :n], ones_f[:], sq[:, :n], start=(c == 0), stop=(c == 7)),
                     reads=[b_ones, bsq], writes=[pb])
            k.op("act", lambda e: e.activation(rstd[:, :n], p[:, :n], AF.Sqrt, bias=EPS, scale=1.0 / D), reads=[pb], writes=[brs])
            k.op("dve", lambda e: e.reciprocal(rstd[:, :n], rstd[:, :n]), reads=[brs], writes=[brs])
            for c in range(8):
                k.op("dve", lambda e, c=c: e.tensor_tensor(sq[:, :n], xt[:, c, :n], rstd[:, :n], ALU.mult), reads=[bx, brs], writes=[bsq])
                k.op("act", lambda e, c=c: e.activation(hT[:, c, :n], sq[:, :n], AF.Identity, bias=mod[:, shoff + c, vsel:vsel + 1],
                                                        scale=A[:, c, vsel:vsel + 1]),
                     reads=[bsq, b_mod, b_A], writes=[bh])

        def norm_scratch(st, tt):
            return {"sq": sb("n_sq", [128, tt], F32, st), "bsq": Buf(), "rstd": sb("n_rstd", [128, tt], F32, st), "brs": Buf()}

        def outproj_phase(w_ap, kc, src_scr, ctx_out):
            TT = 512
            with ExitStack() as st:
                w, bw = load_w(st, "opw", w_ap, kc, D)
                mt = [sb("op_m%d" % i, [128, kc, TT], BF16, st) for i in range(2)]; bm = [Buf(), Buf()]
                xt = [sb("op_x%d" % i, [128, 8, TT], F32, st) for i in range(2)]; bx = [Buf(), Buf()]
                for ti, (t0, n, vsel) in enumerate(seg_tiles(TT)):
                    if vsel == 1 and not ctx_out:
                        continue
                    m, b_m, x, b_x = mt[ti % 2], bm[ti % 2], xt[ti % 2], bx[ti % 2]
                    k.dma("sp", m[:, :, :n], src_scr[0:kc * 128, t0:t0 + n].rearrange("(c p) t -> p c t", p=128), writes=[b_m])
                    k.dma("pool", x[:, :, :n], xres[:, t0:t0 + n].rearrange("(c p) t -> p c t", p=128), writes=[b_x])
                    for c in range(8):
                        p, pb = next_ps()
                        for j in range(kc):
                            k.op("pe", lambda e, p=p, j=j, c=c, m=m: e.matmul(p[:, :n], w[:, j, c * 128:(c + 1) * 128], m[:, j, :n],
                                                                          start=(j == 0), stop=(j == kc - 1)),
                                 reads=[bw, b_m], writes=[pb], inc=(j == kc - 1))
                        k.op("dve", lambda e, p=p, c=c, x=x: e.scalar_tensor_tensor(x[:, c, :n], p[:, :n], mod[:, 16 + c, vsel:vsel + 1], x[:, c, :n],
                                                                                 ALU.mult, ALU.add),
                             reads=[pb, b_mod, b_x], writes=[b_x])
                    k.dma("pool", xres[:, t0:t0 + n].rearrange("(c p) t -> p c t", p=128), x[:, :, :n], reads=[b_x], writes=[Buf()])
            k.barrier()

        def resid_phase(src_scr, ctx_out):
            TT = 512
            with ExitStack() as st:
                yt = [sb("rp_y%d" % i, [128, 8, TT], F32, st) for i in range(2)]; by = [Buf(), Buf()]
                xt = [sb("rp_x%d" % i, [128, 8, TT], F32, st) for i in range(2)]; bx = [Buf(), Buf()]
                for ti, (t0, n, vsel) in enumerate(seg_tiles(TT)):
                    if vsel == 1 and not ctx_out:
                        continue
                    y, b_y, x, b_x = yt[ti % 2], by[ti % 2], xt[ti % 2], bx[ti % 2]
                    k.dma("sp", y[:, :, :n], src_scr[0:D, t0:t0 + n].rearrange("(c p) t -> p c t", p=128), writes=[b_y])
                    k.dma("pool", x[:, :, :n], xres[:, t0:t0 + n].rearrange("(c p) t -> p c t", p=128), writes=[b_x])
                    for c in range(8):
                        k.op("dve", lambda e, c=c, x=x, y=y: e.scalar_tensor_tensor(x[:, c, :n], y[:, c, :n], mod[:, 16 + c, vsel:vsel + 1], x[:, c, :n],
                                                                                 ALU.mult, ALU.add),
                             reads=[b_y, b_mod, b_x], writes=[b_x])
                    k.dma("pool", xres[:, t0:t0 + n].rearrange("(c p) t -> p c t", p=128), x[:, :, :n], reads=[b_x], writes=[Buf()])
            k.barrier()

        def ffn_phase(li, ctx_out):
            TT = 512
            NJ = FH // 128
            with ExitStack() as st:
                w13, b13 = load_w(st, "w13", ffn_w13[li], 8, 2 * FH)
                w2, b2 = load_w(st, "w2", ffn_w2[li], NJ, D)
                nsc1 = norm_scratch(st, TT)
                nscs = [nsc1, nsc1]
                xt = [sb("f_x%d" % i, [128, 8, TT], F32, st) for i in range(2)]; bx = [Buf(), Buf()]
                hT1 = sb("f_h", [128, 8, TT], BF16, st); bh1 = Buf()
                hTs = [hT1, hT1]; bhs = [bh1, bh1]
                sT = sb("f_s", [128, NJ, TT], BF16, st); bs = Buf()
                sa = [sb("f_sa%d" % i, [128, TT], BF16, st) for i in range(2)]; bsa = [Buf(), Buf()]
                tl = [t for t in seg_tiles(TT) if not (t[2] == 1 and not ctx_out)]

                def prep(i):
                    t0, n, vsel = tl[i]
                    k.dma("sp", xt[i % 2][:, :, :n], xres[:, t0:t0 + n].rearrange("(c p) t -> p c t", p=128), writes=[bx[i % 2]])
                    norm_mod(nscs[i % 2], xt[i % 2], bx[i % 2], n, vsel, A2, 24, hTs[i % 2], bhs[i % 2])

                prep(0)
                for i, (t0, n, vsel) in enumerate(tl):
                    x, b_x = xt[i % 2], bx[i % 2]
                    hT, bh = hTs[i % 2], bhs[i % 2]
                    for j in range(NJ):
                        pa, pab = next_ps()
                        pg, pgb = next_ps()
                        for c in range(8):
                            k.op("pe", lambda e, pa=pa, c=c, j=j: e.matmul(pa[:, :n], w13[:, c, j * 128:(j + 1) * 128], hT[:, c, :n],
                                                                       start=(c == 0), stop=(c == 7)),
                                 reads=[b13, bh], writes=[pab], inc=(c == 7))
                        for c in range(8):
                            k.op("pe", lambda e, pg=pg, c=c, j=j: e.matmul(pg[:, :n], w13[:, c, FH + j * 128:FH + (j + 1) * 128], hT[:, c, :n],
                                                                       start=(c == 0), stop=(c == 7)),
                                 reads=[b13, bh], writes=[pgb], inc=(c == 7))
                        s_a, b_sa = sa[j % 2], bsa[j % 2]
                        k.op("act", lambda e, pa=pa, s_a=s_a: e.activation(s_a[:, :n], pa[:, :n], AF.Silu), reads=[pab], writes=[b_sa])
                        k.op("dve", lambda e, pg=pg, s_a=s_a, j=j: e.tensor_tensor(sT[:, j, :n], s_a[:, :n], pg[:, :n], ALU.mult),
                             reads=[b_sa, pgb], writes=[bs])
                    if i + 1 < len(tl):
                        prep(i + 1)
                    for c in range(8):
                        p, pb = next_ps()
                        for j in range(NJ):
                            k.op("pe", lambda e, p=p, c=c, j=j: e.matmul(p[:, :n], w2[:, j, c * 128:(c + 1) * 128], sT[:, j, :n],
                                                                     start=(j == 0), stop=(j == NJ - 1)),
                                 reads=[b2, bs], writes=[pb], inc=(j == NJ - 1))
                        k.op("dve", lambda e, p=p, c=c, x=x: e.scalar_tensor_tensor(x[:, c, :n], p[:, :n], mod[:, 40 + c, vsel:vsel + 1], x[:, c, :n],
                                                                                 ALU.mult, ALU.add),
                             reads=[pb, b_mod, b_x], writes=[b_x])
                    k.dma("sp", xres[:, t0:t0 + n].rearrange("(c p) t -> p c t", p=128), x[:, :, :n], reads=[b_x], writes=[Buf()])
            k.barrier()

        def final_phase():
            TT = 512
            with ExitStack() as st:
                fw = sb("fnw", [128, 8], F32, st); bfw = Buf()
                k.dma("sp", fw[:], final_norm_w.rearrange("(c p) -> p c", p=128), writes=[bfw], allow_slow_non_contiguous=True)
                nsc = norm_scratch(st, TT)
                xt = [sb("fn_x%d" % i, [128, 8, TT], F32, st) for i in range(2)]; bx = [Buf(), Buf()]
                for ti, (t0, n, vsel) in enumerate(seg_tiles(TT)):
                    if vsel == 1:
                        continue
                    x, b_x = xt[ti % 2], bx[ti % 2]
                    k.dma("sp", x[:, :, :n], xres[:, t0:t0 + n].rearrange("(c p) t -> p c t", p=128), writes=[b_x])
                    sq, bsq, rstd, brs = nsc["sq"], nsc["bsq"], nsc["rstd"], nsc["brs"]
                    p, pb = next_ps()
                    for c in range(8):
                        k.op("act", lambda e, c=c, x=x: e.activation(sq[:, :n], x[:, c, :n], AF.Square), reads=[b_x], writes=[bsq])
                        k.op("pe", lambda e, c=c, p=p: e.matmul(p[:, :n], ones_f[:], sq[:, :n], start=(c == 0), stop=(c == 7)),
                             reads=[b_ones, bsq], writes=[pb])
                    k.op("act", lambda e, p=p: e.activation(rstd[:, :n], p[:, :n], AF.Sqrt, bias=EPS, scale=1.0 / D), reads=[pb], writes=[brs])
                    k.op("dve", lambda e: e.reciprocal(rstd[:, :n], rstd[:, :n]), reads=[brs], writes=[brs])
                    for c in range(8):
                        k.op("dve", lambda e, c=c, x=x: e.scalar_tensor_tensor(x[:, c, :n], x[:, c, :n], fw[:, c:c + 1], rstd[:, :n], ALU.mult, ALU.mult),
                             reads=[b_x, brs, bfw], writes=[b_x])
                    k.dma("pool", outT[:, t0 - NCTX:t0 - NCTX + n].rearrange("(c p) t -> p c t", p=128), x[:, :, :n], reads=[b_x], writes=[Buf()])
            k.barrier()

        def lru_layer(li, ctx_out):
            TT = 512
            gyT = scrA
            xrT = scrB
            hfT = scrB[1280:2560]
            with ExitStack() as st:
                w, bw = load_w(st, "lin", lru_in_w[0], 8, 2560)
                nsc = norm_scratch(st, TT)
                xt = [sb("la_x%d" % i, [128, 8, TT], F32, st) for i in range(2)]; bx = [Buf(), Buf()]
                hT = sb("la_h", [128, 8, TT], BF16, st); bh = Buf()
                og = [sb("la_o%d" % i, [128, TT], F32, st) for i in range(4)]; bog = [Buf() for _ in range(4)]
                on = 0
                for ti, (t0, n, vsel) in enumerate(seg_tiles(TT)):
                    x, b_x = xt[ti % 2], bx[ti % 2]
                    k.dma("sp", x[:, :, :n], xres[:, t0:t0 + n].rearrange("(c p) t -> p c t", p=128), writes=[b_x])
                    norm_mod(nsc, x, b_x, n, vsel, A1, 0, hT, bh)
                    for m in range(20):
                        p, pb = next_ps()
                        for c in range(8):
                            k.op("pe", lambda e, p=p, c=c, m=m: e.matmul(p[:, :n], w[:, c, m * 128:(m + 1) * 128], hT[:, c, :n],
                                                                     start=(c == 0), stop=(c == 7)),
                                 reads=[bw, bh], writes=[pb], inc=(c == 7))
                        o, bo = og[on % 4], bog[on % 4]
                        on += 1
                        if m < 10:
                            k.op("act", lambda e, p=p, o=o: e.activation(o[:, :n], p[:, :n], AF.Gelu_apprx_tanh), reads=[pb], writes=[bo])
                            k.dma("sp", gyT[m * 128:(m + 1) * 128, t0:t0 + n], o[:, :n], reads=[bo], writes=[Buf()])
                        else:
                            k.op("dve", lambda e, p=p, o=o: e.tensor_copy(o[:, :n], p[:, :n]), reads=[pb], writes=[bo])
                            k.dma("pool", xrT[(m - 10) * 128:(m - 9) * 128, t0:t0 + n], o[:, :n], reads=[bo], writes=[Buf()])
            k.barrier()
            with ExitStack() as st:
                gw = sb("lgw", [128, 2, 10, 256], BF16, st); bgw = Buf()
                for d in range(2):
                    for blk in range(10):
                        k.dma("sp" if blk % 2 else "pool", gw[:, d, blk, :], lru_gate_w[0, d, blk], writes=[bgw])
                gb = sb("lgb", [128, 2, 10, 2], F32, st); bgb = Buf()
                k.dma("sp", gb[:], lru_gate_b[0].rearrange("d b (h p) -> p d b h", p=128), writes=[bgb], allow_slow_non_contiguous=True)
                cw = sb("lcw", [128, 4, 10], F32, st); cb = sb("lcb", [128, 10], F32, st); bcw = Buf()
                k.dma("sp", cw[:], lru_conv_w[0].rearrange("k (b p) -> p k b", p=128), writes=[bcw], allow_slow_non_contiguous=True)
                k.dma("sp", cb[:], lru_conv_b[0].rearrange("(b p) -> p b", p=128), writes=[bcw], allow_slow_non_contiguous=True)
                lb = sb("llb", [128, 2, 10], F32, st); lb2 = sb("llb2", [128, 2, 10], F32, st); blb = Buf()
                k.dma("sp", lb[:], lru_a_param[0].rearrange("d (b p) -> p d b", p=128), writes=[blb], allow_slow_non_contiguous=True)
                k.op("act", lambda e: e.activation(lb[:], lb[:], AF.Exp, scale=-1.0), reads=[blb], writes=[blb])
                k.op("act", lambda e: e.activation(lb[:], lb[:], AF.Ln, bias=1.0, scale=1.0), reads=[blb], writes=[blb])
                k.op("dve", lambda e: e.tensor_scalar(lb[:], lb[:], -8.0, None, ALU.mult), reads=[blb], writes=[blb])
                k.op("dve", lambda e: e.tensor_scalar(lb2[:], lb[:], 2.0, None, ALU.mult), reads=[blb], writes=[blb])
                hgb = sb("lhgb", [128, 2, 10, 2], F32, st); hlb = sb("lhlb", [128, 2, 10], F32, st); hlb2 = sb("lhlb2", [128, 2, 10], F32, st); bhl = Buf()
                k.op("dve", lambda e: e.tensor_scalar(hgb[:], gb[:], 0.5, None, ALU.mult), reads=[bgb], writes=[bhl])
                k.op("dve", lambda e: e.tensor_scalar(hlb[:], lb[:], 0.5, None, ALU.mult), reads=[blb], writes=[bhl])
                k.op("dve", lambda e: e.tensor_scalar(hlb2[:], lb2[:], 0.5, None, ALU.mult), reads=[blb], writes=[bhl])
                HB = TT + 3
                xr = [sb("lb_xr%d" % i, [128, HB], F32, st) for i in range(8)]; bxr = [Buf() for _ in range(8)]
                xc = [sb("lb_xc%d" % i, [128, TT], F32, st) for i in range(8)]; bxc = [Buf() for _ in range(8)]
                xcb = [sb("lb_xcb%d" % i, [128, TT], BF16, st) for i in range(8)]; bxcb = [Buf() for _ in range(8)]
                ta = [sb("lb_a%d" % i, [128, TT], F32, st) for i in range(8)]; bta = [Buf() for _ in range(8)]
                tm = [sb("lb_m%d" % i, [128, TT], F32, st) for i in range(8)]; btm = [Buf() for _ in range(8)]
                tu = [sb("lb_u%d" % i, [128, TT], F32, st) for i in range(8)]; btu = [Buf() for _ in range(8)]
                th = [sb("lb_h%d" % i, [128, TT], F32, st) for i in range(8)]; bth = [Buf() for _ in range(8)]
                stc = sb("lb_stc", [128, 2, 10], F32, st); bstc = [[Buf() for _ in range(10)] for _ in range(2)]
                k.op("pool", lambda e: e.memset(stc[:], 0.0), writes=[b for r in bstc for b in r])
                tg = [sb("lb_g%d" % i, [128, TT], F32, st) for i in range(8)]; btg = [Buf() for _ in range(8)]
                to = [sb("lb_o%d" % i, [128, TT], BF16, st) for i in range(8)]; bto = [Buf() for _ in range(8)]
                zero = sb("lb_z", [128, 1], F32, st); bz = Buf()
                k.op("pool", lambda e: e.memset(zero[:], 0.0), writes=[bz])
                tiles = seg_tiles(TT)
                segs = {1: (0, NCTX), 0: (NCTX, NTOK)}
                hfbufs = {}
                NBL = 8
                lunits = []

                def make_lunit(d, t0, n, vsel, blk, un):
                    s0, s1 = segs[vsel]
                    i2 = un % NBL
                    x_r, b_xr = xr[i2], bxr[i2]
                    x_c, b_xc = xc[i2], bxc[i2]
                    x_cb, b_xcb = xcb[i2], bxcb[i2]
                    a_, b_a = ta[i2], bta[i2]
                    m_, b_m = tm[i2], btm[i2]
                    u_, b_u = tu[i2], btu[i2]
                    h_, b_h = th[i2], bth[i2]
                    g_, b_g = tg[i2], btg[i2]
                    o_, b_o = to[i2], bto[i2]
                    pr, prb = ps[(un % 4) * 2], psb[(un % 4) * 2]
                    pi, pib = ps[(un % 4) * 2 + 1], psb[(un % 4) * 2 + 1]
                    lo = max(t0 - 2, s0); hi = min(t0 + n + 1, s1)
                    skip_out = (d == 1 and vsel == 1 and not ctx_out)

                    def l0():
                        if lo > t0 - 2 or hi < t0 + n + 1:
                            k.op("pool", lambda e: e.memset(x_r[:], 0.0), writes=[b_xr])
                        k.dma("sp", x_r[:, lo - (t0 - 2):hi - (t0 - 2)], xrT[blk * 128:(blk + 1) * 128, lo:hi], writes=[b_xr])
                        if d == 1 and not skip_out:
                            k.dma("sp", g_[:, :n], gyT[blk * 128:(blk + 1) * 128, t0:t0 + n], writes=[b_g])

                    def l1():
                        k.op("act", lambda e: e.activation(x_c[:, :n], x_r[:, 0:n], AF.Identity, bias=cb[:, blk:blk + 1], scale=cw[:, 0, blk:blk + 1]),
                             reads=[b_xr, bcw], writes=[b_xc])

                    def l2():
                        for kk in range(1, 4):
                            k.op("dve", lambda e, kk=kk: e.scalar_tensor_tensor(x_c[:, :n], x_r[:, kk:kk + n], cw[:, kk, blk:blk + 1], x_c[:, :n], ALU.mult, ALU.add),
                                 reads=[b_xr, bcw, b_xc], writes=[b_xc])
                        k.op("pool", lambda e: e.tensor_copy(x_cb[:, :n], x_c[:, :n]), reads=[b_xc], writes=[b_xcb])

                    def l3():
                        k.op("pe", lambda e: e.matmul(pr[:, :n], gw[:, d, blk, 0:128], x_cb[:, :n], start=True, stop=True), reads=[bgw, b_xcb], writes=[prb])
                        k.op("pe", lambda e: e.matmul(pi[:, :n], gw[:, d, blk, 128:256], x_cb[:, :n], start=True, stop=True), reads=[bgw, b_xcb], writes=[pib])

                    def l4():
                        k.op("act", lambda e: e.activation(a_[:, :n], pr[:, :n], AF.Tanh, bias=hgb[:, d, blk, 0:1], scale=0.5), reads=[prb, bhl], writes=[b_a])
                        k.op("act", lambda e: e.activation(u_[:, :n], pi[:, :n], AF.Tanh, bias=hgb[:, d, blk, 1:2], scale=0.5), reads=[pib, bhl], writes=[b_u])
                        k.op("act", lambda e: e.activation(m_[:, :n], a_[:, :n], AF.Exp, bias=hlb2[:, d, blk:blk + 1], scale=hlb2[:, d, blk:blk + 1]), reads=[b_a, bhl], writes=[b_m])
                        k.op("act", lambda e: e.activation(a_[:, :n], a_[:, :n], AF.Exp, bias=hlb[:, d, blk:blk + 1], scale=hlb[:, d, blk:blk + 1]), reads=[b_a, bhl], writes=[b_a])
                        k.op("act", lambda e: e.activation(m_[:, :n], m_[:, :n], AF.Sqrt, bias=0.25, scale=-0.25), reads=[b_m], writes=[b_m])

                    def l5():
                        k.op("dve", lambda e: e.scalar_tensor_tensor(u_[:, :n], u_[:, :n], 1.0, x_c[:, :n], ALU.add, ALU.mult), reads=[b_u, b_xc], writes=[b_u])
                        k.op("pool", lambda e: e.tensor_tensor(u_[:, :n], u_[:, :n], m_[:, :n], ALU.mult), reads=[b_u, b_m], writes=[b_u])

                    def l6():
                        init = stc[:, d, blk:blk + 1]
                        if d == 0:
                            k.op("dve", lambda e: e.tensor_tensor_scan(h_[:, :n], a_[:, :n], u_[:, :n], init, ALU.mult, ALU.add),
                                 reads=[b_a, b_u, bstc[d][blk]], writes=[b_h])
                            k.op("act", lambda e: e.copy(stc[:, d, blk:blk + 1], h_[:, n - 1:n]), reads=[b_h], writes=[bstc[d][blk]])
                            hb = Buf()
                            hfbufs[(blk, t0)] = hb
                            k.dma("sp", hfT[blk * 128:(blk + 1) * 128, t0:t0 + n], h_[:, :n], reads=[b_h], writes=[hb])
                        else:
                            k.op("dve", lambda e: e.tensor_tensor_scan(h_[:, n - 1::-1], a_[:, n - 1::-1], u_[:, n - 1::-1], init, ALU.mult, ALU.add),
                                 reads=[b_a, b_u, bstc[d][blk]], writes=[b_h])
                            k.op("act", lambda e: e.copy(stc[:, d, blk:blk + 1], h_[:, 0:1]), reads=[b_h], writes=[bstc[d][blk]])
                            if not skip_out:
                                k.dma("sp", x_c[:, :n], hfT[blk * 128:(blk + 1) * 128, t0:t0 + n], reads=[hfbufs[(blk, t0)]], writes=[b_xc])

                    def l7():
                        if d == 0 or skip_out:
                            return
                        k.op("pool", lambda e: e.tensor_tensor(x_c[:, :n], x_c[:, :n], h_[:, :n], ALU.add), reads=[b_xc, b_h], writes=[b_xc])
                        k.op("dve", lambda e: e.tensor_tensor(o_[:, :n], x_c[:, :n], g_[:, :n], ALU.mult), reads=[b_xc, b_g], writes=[b_o])
                        k.dma("sp", scrM[blk * 128:(blk + 1) * 128, t0:t0 + n], o_[:, :n], reads=[b_o], writes=[Buf()])

                    return [l0, l1, l2, l3, l4, l5, l6, l7]

                for d in range(2):
                    order = tiles if d == 0 else ([tiles[0]] + tiles[:0:-1])
                    for (t0, n, vsel) in order:
                        for blk in range(10):
                            lunits.append(make_lunit(d, t0, n, vsel, blk, len(lunits)))
                NSL = 8
                for step in range(len(lunits) + NSL - 1):
                    for j in range(NSL - 1, -1, -1):
                        u = step - j
                        if 0 <= u < len(lunits):
                            lunits[u][j]()
            k.barrier()
            outproj_phase(lru_out_w[0], 10, scrM, ctx_out)


        def ssd_layer(li, ctx_out):
            TT = 512
            xbcT = scrXB
            with ExitStack() as st:
                w, bw = load_w(st, "m2in", m2_in_w[0], 8, 5184)
                nsc = norm_scratch(st, TT)
                xt = [sb("sa_x%d" % i, [128, 8, TT], F32, st) for i in range(2)]; bx = [Buf(), Buf()]
                hT = sb("sa_h", [128, 8, TT], BF16, st); bh = Buf()
                og = [sb("sa_o%d" % i, [128, TT], F32, st) for i in range(4)]; bog = [Buf() for _ in range(4)]
                ogb = [sb("sa_ob%d" % i, [128, TT], BF16, st) for i in range(4)]; bogb = [Buf() for _ in range(4)]
                on = 0
                for ti, (t0, n, vsel) in enumerate(seg_tiles(TT)):
                    x, b_x = xt[ti % 2], bx[ti % 2]
                    k.dma("sp", x[:, :, :n], xres[:, t0:t0 + n].rearrange("(c p) t -> p c t", p=128), writes=[b_x])
                    norm_mod(nsc, x, b_x, n, vsel, A1, 0, hT, bh)
                    for m in range(24):
                        p, pb = next_ps()
                        for c in range(8):
                            k.op("pe", lambda e, p=p, c=c, m=m: e.matmul(p[:, :n], w[:, c, 2048 + m * 128:2048 + (m + 1) * 128], hT[:, c, :n],
                                                                     start=(c == 0), stop=(c == 7)),
                                 reads=[bw, bh], writes=[pb], inc=(c == 7))
                        o, bo = ogb[on % 4], bogb[on % 4]
                        on += 1
                        if m % 2 == 0:
                            k.op("act", lambda e, p=p, o=o: e.copy(o[:, :n], p[:, :n]), reads=[pb], writes=[bo])
                        else:
                            k.op("dve", lambda e, p=p, o=o: e.tensor_copy(o[:, :n], p[:, :n]), reads=[pb], writes=[bo])
                        k.dma("sp", xbcT[m * 128:(m + 1) * 128, t0:t0 + n], o[:, :n], reads=[bo], writes=[Buf()])
                    for tb in range(n // 128):
                        for cb_ in range(4):
                            p, pb = next_ps()
                            for c in range(8):
                                k.op("pe", lambda e, p=p, c=c, tb=tb, cb_=cb_: e.matmul(p[:, :], hT[:, c, tb * 128:(tb + 1) * 128], w[:, c, cb_ * 512:(cb_ + 1) * 512],
                                                                                    start=(c == 0), stop=(c == 7)),
                                     reads=[bw, bh], writes=[pb], inc=(c == 7))
                            o, bo = og[on % 4], bog[on % 4]
                            on += 1
                            if cb_ % 2 == 0:
                                k.op("act", lambda e, p=p, o=o: e.copy(o[:, :], p[:, :]), reads=[pb], writes=[bo])
                            else:
                                k.op("dve", lambda e, p=p, o=o: e.tensor_copy(o[:, :], p[:, :]), reads=[pb], writes=[bo])
                            k.dma("sp", scrZ[t0 + tb * 128:t0 + (tb + 1) * 128, cb_ * 512:(cb_ + 1) * 512], o[:, :], reads=[bo], writes=[Buf()])
                        p, pb = next_ps()
                        for c in range(8):
                            k.op("pe", lambda e, p=p, c=c, tb=tb: e.matmul(p[:, 0:64], hT[:, c, tb * 128:(tb + 1) * 128], w[:, c, 5120:5184],
                                                                      start=(c == 0), stop=(c == 7)),
                                 reads=[bw, bh], writes=[pb], inc=(c == 7))
                        o, bo = og[on % 4], bog[on % 4]
                        on += 1
                        k.op("dve", lambda e, p=p, o=o: e.tensor_copy(o[:, 0:64], p[:, 0:64]), reads=[pb], writes=[bo])
                        k.dma("sp", scrDT[t0 + tb * 128:t0 + (tb + 1) * 128, :], o[:, 0:64], reads=[bo], writes=[Buf()])
            k.barrier()
            if debug == "ssdA":
                return
            with ExitStack() as st:
                def small(name, shape, src, **kw):
                    t = sb(name, shape, F32, st); b = Buf()
                    k.dma("sp", t[:], src, writes=[b], allow_slow_non_contiguous=True)
                    return t, b
                cw, bcw = small("scw", [128, 4, 24], m2_conv_w[0].rearrange("k (m p) -> p k m", p=128))
                cb, bcb = small("scb", [128, 24], m2_conv_b[0].rearrange("(m p) -> p m", p=128))
                dtb, bdtb = small("sdtb", [128, 64], m2_dt_bias[0].rearrange("d h -> (d h)").partition_broadcast(128))
                abc, babc = small("sabc", [128, 64], m2_a_log[0].rearrange("d h -> (d h)").partition_broadcast(128))
                dsk, bdsk = small("sdsk", [128, 32], m2_d[0].partition_broadcast(128))
                k.op("dve", lambda e: e.tensor_scalar(cw[:], cw[:], 0.5, None, ALU.mult), reads=[bcw], writes=[bcw])
                k.op("dve", lambda e: e.tensor_scalar(cb[:], cb[:], 0.5, None, ALU.mult), reads=[bcb], writes=[bcb])
                k.op("act", lambda e: e.activation(abc[:], abc[:], AF.Exp), reads=[babc], writes=[babc])
                k.op("dve", lambda e: e.tensor_scalar(abc[:], abc[:], -1.0, None, ALU.mult), reads=[babc], writes=[babc])
                msk = sb("smask", [128, 4, 128], F32, st); bmsk = Buf()
                k.dma("sp", msk[:], masks_in.rearrange("m p l -> p m l"), writes=[bmsk])
                MU, ML, MSL, MSU = (msk[:, i, :] for i in range(4))
                xr = [sb("s1_xr%d" % i, [128, 24, 516], BF16, st) for i in range(2)]; bxr = [Buf(), Buf()]
                xc = sb("s1_xc", [128, 128], F32, st); bxc = Buf()
                xsT = sb("s1_xsT", [128, 24, 128], BF16, st); bxsT = Buf()
                xs_tok = [sb("s1_xst%d" % i, [128, 2048], BF16, st) for i in range(2)]; bxst = [Buf(), Buf()]
                b_tok = sb("s1_btok", [128, 512], BF16, st); bbtok = Buf()
                dtr = sb("s1_dtr", [128, 64], F32, st); bdtr = Buf()
                dtt = sb("s1_dt", [128, 64], F32, st); bdtt = Buf()
                da = sb("s1_da", [128, 64], F32, st); bda = Buf()
                tmp64 = sb("s1_t64", [128, 64], F32, st); bt64 = Buf()
                wgt = sb("s1_wgt", [128, 64], F32, st); bwgt = Buf()
                Et = [sb("s1_E%d" % i, [128, 64], F32, st) for i in range(2)]; bE = [Buf(), Buf()]
                dec = [sb("s1_dec%d" % i, [128, 64], F32, st) for i in range(2)]; bdec = [Buf(), Buf()]
                Gf = sb("s1_Gf", [128, 4, 128], F32, st); Gb = sb("s1_Gb", [128, 4, 128], F32, st); bG = Buf()
                NBH = 4
                Lf = [sb("s1_Lf%d" % i, [128, 128], F32, st) for i in range(NBH)]; bLf = [Buf() for _ in range(NBH)]
                Lb = [sb("s1_Lb%d" % i, [128, 128], F32, st) for i in range(NBH)]; bLb = [Buf() for _ in range(NBH)]
                Df = [sb("s1_Df%d" % i, [128, 128], F32, st) for i in range(NBH)]; bDf = [Buf() for _ in range(NBH)]
                Db = [sb("s1_Db%d" % i, [128, 128], F32, st) for i in range(NBH)]; bDb = [Buf() for _ in range(NBH)]
                Mt = [sb("s1_M%d" % i, [128, 128], BF16, st) for i in range(NBH)]; bMt = [Buf() for _ in range(NBH)]
                Mb = [sb("s1_Mb%d" % i, [128, 128], BF16, st) for i in range(NBH)]; bMb = [Buf() for _ in range(NBH)]
                dskI = sb("s1_dskI", [128, 32, 128], BF16, st); bdskI = Buf()
                for h_ in range(32):
                    k.op("dve", lambda e, h_=h_: e.tensor_scalar(dskI[:, h_, :], ident_f[:], dsk[:, h_:h_ + 1], None, ALU.mult), reads=[b_ident, bdsk], writes=[bdskI])
                xw = sb("s1_xw", [128, 2048], BF16, st); bxw = Buf()
                yo = [sb("s1_yo%d" % i, [128, 2048], F32, st) for i in range(2)]; byo = [Buf(), Buf()]
                so = [sb("s1_so%d" % i, [128, 512], BF16, st) for i in range(4)]; bso = [Buf() for _ in range(4)]
                dtt2 = [dtt, sb("s1_dt2", [128, 64], F32, st)]; bdtt2 = [bdtt, Buf()]
                da2 = [da, sb("s1_da2", [128, 64], F32, st)]; bda2 = [bda, Buf()]
                Gf2 = [Gf, sb("s1_Gf2", [128, 4, 128], F32, st)]; Gb2 = [Gb, sb("s1_Gb2", [128, 4, 128], F32, st)]; bG2 = [bG, Buf()]
                pspool[0] = [4]

                xc4 = [xc] + [sb("s1_xc%d" % i, [128, 128], F32, st) for i in range(3)]; bxc4 = [bxc, Buf(), Buf(), Buf()]
                th4 = [sb("s1_th%d" % i, [128, 128], F32, st) for i in range(4)]; bth4 = [Buf() for _ in range(4)]

                def pieces(ch):
                    dtt_, bdtt_ = dtt2[ch % 2], bdtt2[ch % 2]
                    da_, bda_ = da2[ch % 2], bda2[ch % 2]
                    Gf_, Gb_, bG_ = Gf2[ch % 2], Gb2[ch % 2], bG2[ch % 2]
                    t0 = ch * 128
                    s0, s1 = (0, NCTX) if t0 < NCTX else (NCTX, NTOK)
                    if t0 < NCTX:
                        big_i, big_t0, big_n = 0, 0, NCTX
                    else:
                        big_i = 1 + (t0 - NCTX) // 512
                        big_t0 = NCTX + (big_i - 1) * 512
                        big_n = 512
                    xbig, b_xr = xr[big_i % 2], bxr[big_i % 2]
                    boff = t0 - big_t0
                    x_r = xbig[:, :, boff:boff + 131]
                    blo = max(big_t0 - 2, s0); bhi = min(big_t0 + big_n + 1, s1)
                    xst, b_xst = xs_tok[ch % 2], bxst[ch % 2]
                    E_, b_E = Et[ch % 2], bE[ch % 2]
                    d_, b_d = dec[ch % 2], bdec[ch % 2]
                    lo = max(t0 - 2, s0); hi = min(t0 + 129, s1)
                    pa, pab = ps[5], psb[5]
                    P = []

                    def convA(m):
                        if m == 0 and boff == 0:
                            if blo > big_t0 - 2 or bhi < big_t0 + big_n + 1:
                                k.op("pool", lambda e: e.memset(xbig[:], 0.0), writes=[b_xr])
                            k.dma("sp", xbig[:, :, blo - (big_t0 - 2):bhi - (big_t0 - 2)], xbcT[:, blo:bhi].rearrange("(m p) t -> p m t", p=128), writes=[b_xr])
                        xc_, bxc_ = xc4[m % 4], bxc4[m % 4]
                        k.op("act", lambda e: e.activation(xc_[:], x_r[:, m, 0:128], AF.Identity, bias=cb[:, m:m + 1], scale=cw[:, 0, m:m + 1]),
                             reads=[b_xr, bcw, bcb], writes=[bxc_])

                    def convT(m):
                        xc_, bxc_ = xc4[m % 4], bxc4[m % 4]
                        for kk in range(1, 4):
                            k.op("dve", lambda e, kk=kk: e.scalar_tensor_tensor(xc_[:], x_r[:, m, kk:kk + 128], cw[:, kk, m:m + 1], xc_[:], ALU.mult, ALU.add),
                                 reads=[b_xr, bcw, bxc_], writes=[bxc_])

                    def convS(m):
                        xc_, bxc_ = xc4[m % 4], bxc4[m % 4]
                        th_, bth_ = th4[m % 4], bth4[m % 4]
                        k.op("act", lambda e: e.activation(th_[:], xc_[:], AF.Tanh), reads=[bxc_], writes=[bth_])
                        k.op("dve", lambda e: e.scalar_tensor_tensor(xsT[:, m, :], th_[:], 1.0, xc_[:], ALU.add, ALU.mult), reads=[bth_, bxc_], writes=[bxsT])

                    def dtp():
                        k.dma("sp", dtr[:], scrDT[t0:t0 + 128, :], writes=[bdtr])
                        k.op("dve", lambda e: e.tensor_tensor(dtr[:], dtr[:], dtb[:], ALU.add), reads=[bdtr, bdtb], writes=[bdtr])
                        k.op("act", lambda e: e.activation(tmp64[:], dtr[:], AF.Abs), reads=[bdtr], writes=[bt64])
                        k.op("act", lambda e: e.activation(tmp64[:], tmp64[:], AF.Exp, scale=-1.0), reads=[bt64], writes=[bt64])
                        k.op("act", lambda e: e.activation(tmp64[:], tmp64[:], AF.Ln, bias=1.0, scale=1.0), reads=[bt64], writes=[bt64])
                        k.op("dve", lambda e: e.scalar_tensor_tensor(dtt_[:], dtr[:], 0.0, tmp64[:], ALU.max, ALU.add), reads=[bdtr, bt64], writes=[bdtt_])
                        k.op("dve", lambda e: e.tensor_tensor(da_[:], dtt_[:], abc[:], ALU.mult), reads=[bdtt_, babc], writes=[bda_])

                    def convpiece(i):
                        if i - 2 >= 0 and i - 2 < 24:
                            convS(i - 2)
                        if i - 1 >= 0 and i - 1 < 24:
                            convT(i - 1)
                        if i < 24:
                            convA(i)
                        if i == 8:
                            dtp()
                        if i == 12:
                            cums()
                    for i in range(26):
                        P.append(lambda i=i: convpiece(i))

                    def tr(qs, last):
                        for q in qs:
                            p, pb = next_ps()
                            pbf = p[:].bitcast(BF16)
                            for j in range(4):
                                m = q * 4 + j
                                k.op("pe", lambda e, pbf=pbf, j=j, m=m: e.transpose(pbf[:, j * 128:(j + 1) * 128], xsT[:, m, :], ident_b[:]),
                                     reads=[bxsT, b_identb], writes=[pb], inc=(j == 3))
                            if q < 4:
                                k.op("dve", lambda e, pbf=pbf, q=q: e.tensor_copy(xst[:, q * 512:(q + 1) * 512], pbf[:, 0:512]), reads=[pb], writes=[b_xst])
                            else:
                                k.op("dve", lambda e, pbf=pbf: e.tensor_copy(b_tok[:], pbf[:, 0:512]), reads=[pb], writes=[bbtok])
                        if last:
                            k.dma("sp", scrC[ch].rearrange("p (g t) -> p g t", t=128), xsT[:, 20:24, :], reads=[bxsT], writes=[Buf()])
                    P.append(lambda: tr((0, 1), False))
                    P.append(lambda: tr((2, 3), False))
                    P.append(lambda: tr((4,), True))

                    def cums():
                        k.op("pe", lambda e: e.matmul(pa[:, 0:32], MU, da_[:, 0:32], start=True, stop=True), reads=[bmsk, bda_], writes=[pab], inc=False)
                        k.op("pe", lambda e: e.matmul(pa[:, 32:64], ML, da_[:, 32:64], start=True, stop=True), reads=[bmsk, bda_], writes=[pab], inc=False)
                        k.op("pe", lambda e: e.matmul(pa[:, 64:128], ones_f[:], da_[:, 0:64], start=True, stop=True), reads=[b_ones, bda_], writes=[pab])
                        k.op("act", lambda e: e.activation(E_[:], pa[:, 0:64], AF.Exp), reads=[pab], writes=[b_E])
                        k.op("act", lambda e: e.activation(d_[:], pa[:, 64:128], AF.Exp), reads=[pab], writes=[b_d])
                        k.dma("sp", scrE[ch], E_[:], reads=[b_E], writes=[Buf()])
                        k.dma("sp", scrDec[ch], d_[:], reads=[b_d], writes=[Buf()])
                        k.op("act", lambda e: e.copy(tmp64[:], pa[:, 0:64]), reads=[pab], writes=[bt64])
                        k.op("dve", lambda e: e.tensor_tensor(wgt[:], pa[:, 64:128], tmp64[:], ALU.subtract), reads=[pab, bt64], writes=[bwgt])
                        k.op("act", lambda e: e.activation(wgt[:], wgt[:], AF.Exp), reads=[bwgt], writes=[bwgt])
                        k.op("dve", lambda e: e.tensor_tensor(wgt[:], wgt[:], dtt_[:], ALU.mult), reads=[bwgt, bdtt_], writes=[bwgt])

                    def states(d):
                        k.op("dve", lambda e: e.tensor_tensor(xw[:].rearrange("p (h q) -> p h q", q=64), xst[:].rearrange("p (h q) -> p h q", q=64),
                                                              wgt[:, d * 32:(d + 1) * 32].unsqueeze(2).to_broadcast([128, 32, 64]), ALU.mult),
                             reads=[b_xst, bwgt], writes=[bxw])
                        for g in range(4):
                            p, pb = next_ps()
                            k.op("pe", lambda e, p=p, g=g: e.matmul(p[:, :], b_tok[:, g * 128:(g + 1) * 128], xw[:, g * 512:(g + 1) * 512], start=True, stop=True),
                                 reads=[bbtok, bxw], writes=[pb])
                            s_, b_s = so[g], bso[g]
                            k.op("act", lambda e, p=p, s_=s_: e.copy(s_[:], p[:]), reads=[pb], writes=[b_s])
                            k.dma("sp", scrS[d, ch, :, g * 512:(g + 1) * 512], s_[:], reads=[b_s], writes=[Buf()])
                    P.append(lambda: states(0))
                    P.append(lambda: states(1))

                    def gmat():
                        for g in range(4):
                            p, pb = next_ps()
                            k.op("pe", lambda e, p=p, g=g: e.matmul(p[:, 0:128], xsT[:, 16 + g, :], xsT[:, 20 + g, :], start=True, stop=True),
                                 reads=[bxsT], writes=[pb])
                            k.op("dve", lambda e, p=p, g=g: e.tensor_tensor(Gf_[:, g, :], p[:, 0:128], MU, ALU.mult), reads=[pb, bmsk], writes=[bG_])
                            k.op("dve", lambda e, p=p, g=g: e.tensor_tensor(Gb_[:, g, :], p[:, 0:128], ML, ALU.mult), reads=[pb, bmsk], writes=[bG_])
                    P.append(gmat)
                    assert len(P) == 32
                    return P

                pieces_cache = {}

                def piece(ch, i):
                    if ch >= NCH:
                        return
                    if ch not in pieces_cache:
                        pieces_cache.clear()
                        pieces_cache[ch] = pieces(ch)
                    pieces_cache[ch][i]()

                def prologue(ch):
                    if ch == 0:
                        for i in range(32):
                            piece(0, i)

                def make_head(ch, h, hc):
                    dtt_, bdtt_ = dtt2[ch % 2], bdtt2[ch % 2]
                    da_, bda_ = da2[ch % 2], bda2[ch % 2]
                    Gf_, Gb_, bG_ = Gf2[ch % 2], Gb2[ch % 2], bG2[ch % 2]
                    xst, b_xst = xs_tok[ch % 2], bxst[ch % 2]
                    g = h // 8
                    i2 = hc % NBH
                    p, pb = ps[6 + hc % 2], psb[6 + hc % 2]
                    yb = h // 8

                    def h0():
                        if h == 0:
                            prologue(ch)
                        piece(ch + 1, h)
                        k.op("act", lambda e: e.activation(Lf[i2][:], MSL, AF.Identity, scale=da_[:, h:h + 1]), reads=[bmsk, bda_], writes=[bLf[i2]])
                        k.op("act", lambda e: e.activation(Lb[i2][:], MSU, AF.Identity, scale=da_[:, 32 + h:33 + h]), reads=[bmsk, bda_], writes=[bLb[i2]])

                    def h1():
                        k.op("pe", lambda e: e.matmul(p[:, 0:128], Lf[i2][:], MU, start=True, stop=True), reads=[bLf[i2], bmsk], writes=[pb], inc=False)
                        k.op("pe", lambda e: e.matmul(p[:, 128:256], Lb[i2][:], ML, start=True, stop=True), reads=[bLb[i2], bmsk], writes=[pb])

                    def h2():
                        k.op("act", lambda e: e.activation(Df[i2][:], p[:, 0:128], AF.Exp), reads=[pb], writes=[bDf[i2]])
                        k.op("act", lambda e: e.activation(Db[i2][:], p[:, 128:256], AF.Exp), reads=[pb], writes=[bDb[i2]])

                    def h3():
                        k.op("dve", lambda e: e.scalar_tensor_tensor(Mt[i2][:], Df[i2][:], dtt_[:, h:h + 1], Gf_[:, g, :], ALU.mult, ALU.mult),
                             reads=[bDf[i2], bdtt_, bG_], writes=[bMt[i2]])
                        k.op("dve", lambda e: e.scalar_tensor_tensor(Mb[i2][:], Db[i2][:], dtt_[:, 32 + h:33 + h], Gb_[:, g, :], ALU.mult, ALU.mult),
                             reads=[bDb[i2], bdtt_, bG_], writes=[bMb[i2]])

                    def h4():
                        yout = ps[yb][:, (h % 8) * 64:(h % 8 + 1) * 64]
                        k.op("pe", lambda e: e.matmul(yout, Mt[i2][:], xst[:, h * 64:(h + 1) * 64], start=True, stop=False),
                             reads=[bMt[i2], b_xst], writes=[psb[yb]], inc=False)
                        k.op("pe", lambda e: e.matmul(yout, Mb[i2][:], xst[:, h * 64:(h + 1) * 64], start=False, stop=False),
                             reads=[bMb[i2], b_xst], writes=[psb[yb]], inc=False)
                        k.op("pe", lambda e: e.matmul(yout, dskI[:, h, :], xst[:, h * 64:(h + 1) * 64], start=False, stop=True),
                             reads=[bdskI, b_xst], writes=[psb[yb]])
                        if h != 31:
                            return
                        y_, b_y = yo[ch % 2], byo[ch % 2]
                        for yb2 in range(4):
                            if yb2 % 2 == 0:
                                k.op("act", lambda e, yb2=yb2: e.copy(y_[:, yb2 * 512:(yb2 + 1) * 512], ps[yb2][:]), reads=[psb[yb2]], writes=[b_y])
                            else:
                                k.op("dve", lambda e, yb2=yb2: e.tensor_copy(y_[:, yb2 * 512:(yb2 + 1) * 512], ps[yb2][:]), reads=[psb[yb2]], writes=[b_y])
                        k.dma("sp", scrY[ch], y_[:], reads=[b_y], writes=[Buf()])

                    return [h0, h1, h2, h3, h4]

                hunits = []
                for ch in range(NCH):
                    for h in range(32):
                        hunits.append(make_head(ch, h, len(hunits)))
                NSH = 5
                for step in range(len(hunits) + NSH - 1):
                    for j in range(NSH - 1, -1, -1):
                        u = step - j
                        if 0 <= u < len(hunits):
                            hunits[u][j]()
                pspool[0] = list(range(8))
            k.barrier()
            if debug == "ssd1":
                return
            with ExitStack() as st:
                Hs = [sb("s2_H%d" % d, [128, 2048], F32, st) for d in range(2)]; bH = [Buf(), Buf()]
                Hb = [sb("s2_Hb%d" % i, [128, 2048], BF16, st) for i in range(2)]; bHb = [Buf(), Buf()]
                St = [sb("s2_S%d" % i, [128, 2048], BF16, st) for i in range(2)]; bSt = [Buf(), Buf()]
                dc = [sb("s2_d%d" % i, [128, 64], F32, st) for i in range(2)]; bdc = [Buf(), Buf()]
                it = 0
                for d in range(2):
                    order = list(range(NCH)) if d == 0 else [1, 0] + list(range(NCH - 1, 1, -1))
                    k.op("pool", lambda e, d=d: e.memset(Hs[d][:], 0.0), writes=[bH[d]])
                    for ch in order:
                        i2 = it % 2
                        it += 1
                        k.op("act", lambda e, d=d, i2=i2: e.copy(Hb[i2][:], Hs[d][:]), reads=[bH[d]], writes=[bHb[i2]])
                        k.dma("pool", scrH[d, ch], Hb[i2][:], reads=[bHb[i2]], writes=[Buf()])
                        k.dma("sp", St[i2][:], scrS[d, ch], writes=[bSt[i2]])
                        k.dma("sp", dc[i2][:], scrDec[ch], writes=[bdc[i2]])
                        k.op("dve", lambda e, d=d, i2=i2: e.tensor_tensor(Hs[d][:].rearrange("p (h q) -> p h q", q=64), Hs[d][:].rearrange("p (h q) -> p h q", q=64),
                                                                      dc[i2][:, d * 32:(d + 1) * 32].unsqueeze(2).to_broadcast([128, 32, 64]), ALU.mult),
                             reads=[bH[d], bdc[i2]], writes=[bH[d]])
                        k.op("dve", lambda e, d=d, i2=i2: e.tensor_tensor(Hs[d][:], Hs[d][:], St[i2][:], ALU.add), reads=[bH[d], bSt[i2]], writes=[bH[d]])
            k.barrier()
            if debug == "ssd2":
                return
            with ExitStack() as st:
                ow, bow = load_w(st, "m2ow", m2_out_w[0], 16, D)
                nwb = sb("s3_nw", [128, 2048], F32, st); bnwb = Buf()
                k.dma("sp", nwb[:], m2_norm_w[0].partition_broadcast(128), writes=[bnwb])
                yt = [sb("s3_y%d" % i, [128, 2048], F32, st) for i in range(2)]; byt = [Buf(), Buf()]
                zt = [sb("s3_z%d" % i, [128, 2048], F32, st) for i in range(2)]; bzt = [Buf(), Buf()]
                Hf = [sb("s3_Hf%d" % i, [128, 2048], BF16, st) for i in range(2)]; bHf = [Buf(), Buf()]
                Hbk = [sb("s3_Hb%d" % i, [128, 2048], BF16, st) for i in range(2)]; bHbk = [Buf(), Buf()]
                Ct = [sb("s3_C%d" % i, [128, 512], BF16, st) for i in range(2)]; bCt = [Buf(), Buf()]
                Ee = [sb("s3_E%d" % i, [128, 64], F32, st) for i in range(2)]; bEe = [Buf(), Buf()]
                tmp = sb("s3_tmp", [128, 512], F32, st); btmp = Buf()
                ss = sb("s3_ss", [128, 2], F32, st); bss = Buf()
                sqj = sb("s3_sq", [128, 2048], F32, st); bsqj = Buf()
                yn = sb("s3_yn", [128, 2048], BF16, st); byn = Buf()
                ynT = sb("s3_ynT", [128, 16, 128], BF16, st); bynT = Buf()
                xt = [sb("s3_x%d" % i, [128, 8, 128], F32, st) for i in range(2)]; bx = [Buf(), Buf()]
                for ch in range(NCH):
                    t0 = ch * 128
                    vsel = 1 if t0 < NCTX else 0
                    if vsel == 1 and not ctx_out:
                        continue
                    i2 = ch % 2
                    y_, z_, b_y, b_z = yt[i2], zt[i2], byt[i2], bzt[i2]
                    k.dma("sp", y_[:], scrY[ch], writes=[b_y])
                    k.dma("sp", z_[:], scrZ[t0:t0 + 128, :], writes=[b_z])
                    k.dma("pool", Hf[i2][:], scrH[0, ch], writes=[bHf[i2]])
                    k.dma("pool", Hbk[i2][:], scrH[1, ch], writes=[bHbk[i2]])
                    k.dma("sp", Ct[i2][:], scrC[ch], writes=[bCt[i2]])
                    k.dma("sp", Ee[i2][:], scrE[ch], writes=[bEe[i2]])
                    k.dma("pool", xt[i2][:], xres[:, t0:t0 + 128].rearrange("(c p) t -> p c t", p=128), writes=[bx[i2]])
                    for d in range(2):
                        Hd, bHd = (Hf[i2], bHf[i2]) if d == 0 else (Hbk[i2], bHbk[i2])
                        for g in range(4):
                            p, pb = next_ps()
                            k.op("pe", lambda e, p=p, g=g, Hd=Hd, i2=i2: e.matmul(p[:], Ct[i2][:, g * 128:(g + 1) * 128], Hd[:, g * 512:(g + 1) * 512], start=True, stop=True),
                                 reads=[bCt[i2], bHd], writes=[pb])
                            k.op("dve", lambda e, p=p, g=g, d=d, i2=i2: e.tensor_tensor(tmp[:].rearrange("p (h q) -> p h q", q=64), p[:].rearrange("p (h q) -> p h q", q=64),
                                                                                   Ee[i2][:, d * 32 + g * 8:d * 32 + (g + 1) * 8].unsqueeze(2).to_broadcast([128, 8, 64]), ALU.mult),
                                 reads=[pb, bEe[i2]], writes=[btmp])
                            k.op("pool", lambda e, g=g, y_=y_: e.tensor_tensor(y_[:, g * 512:(g + 1) * 512], y_[:, g * 512:(g + 1) * 512], tmp[:], ALU.add),
                                 reads=[btmp, b_y], writes=[b_y])
                    k.op("act", lambda e, z_=z_: e.activation(z_[:], z_[:], AF.Silu), reads=[b_z], writes=[b_z])
                    k.op("dve", lambda e, y_=y_, z_=z_: e.tensor_tensor(y_[:], y_[:], z_[:], ALU.mult), reads=[b_y, b_z], writes=[b_y])
                    k.op("pool", lambda e: e.memset(ss[:], 0.0), writes=[bss])
                    k.op("act", lambda e, y_=y_: e.activation(sqj[:], y_[:], AF.Square, accum_out=ss[:, 0:1]), reads=[b_y], writes=[bsqj, bss])
                    k.op("act", lambda e: e.activation(ss[:, 1:2], ss[:, 0:1], AF.Sqrt, bias=EPS, scale=1.0 / 2048), reads=[bss], writes=[bss])
                    k.op("dve", lambda e: e.reciprocal(ss[:, 1:2], ss[:, 1:2]), reads=[bss], writes=[bss])
                    k.op("dve", lambda e, y_=y_: e.scalar_tensor_tensor(yn[:], y_[:], ss[:, 1:2], nwb[:], ALU.mult, ALU.mult), reads=[b_y, bss, bnwb], writes=[byn])
                    for q in range(4):
                        p, pb = next_ps()
                        pbf = p[:].bitcast(BF16)
                        for j in range(4):
                            m = q * 4 + j
                            k.op("pe", lambda e, pbf=pbf, j=j, m=m: e.transpose(pbf[:, j * 128:(j + 1) * 128], yn[:, m * 128:(m + 1) * 128], ident_b[:]),
                                 reads=[byn, b_identb], writes=[pb], inc=(j == 3))
                        k.op("act" if q % 2 else "dve", (lambda e, pbf=pbf, q=q: e.copy(ynT[:, q * 4:(q + 1) * 4, :], pbf[:, 0:512].rearrange("p (j t) -> p j t", t=128))) if q % 2 else
                             (lambda e, pbf=pbf, q=q: e.tensor_copy(ynT[:, q * 4:(q + 1) * 4, :], pbf[:, 0:512].rearrange("p (j t) -> p j t", t=128))),
                             reads=[pb], writes=[bynT])
                    for c in range(8):
                        p, pb = next_ps()
                        for j in range(16):
                            k.op("pe", lambda e, p=p, j=j, c=c: e.matmul(p[:, 0:128], ow[:, j, c * 128:(c + 1) * 128], ynT[:, j, :], start=(j == 0), stop=(j == 15)),
                                 reads=[bow, bynT], writes=[pb], inc=(j == 15))
                        k.op("dve", lambda e, p=p, c=c, i2=i2: e.scalar_tensor_tensor(xt[i2][:, c, :], p[:, 0:128], mod[:, 16 + c, vsel:vsel + 1], xt[i2][:, c, :], ALU.mult, ALU.add),
                             reads=[pb, b_mod, bx[i2]], writes=[bx[i2]])
                    k.dma("pool", xres[:, t0:t0 + 128].rearrange("(c p) t -> p c t", p=128), xt[i2][:], reads=[bx[i2]], writes=[Buf()])
            k.barrier()

        TWO_PI = 6.283185307179586

        def sincos(st, tag, phi, bphi, shape, out_sin, out_cos, bout):
            t = sb("sc_t" + tag, shape, F32, st); ki = sb("sc_k" + tag, shape, I32, st); bt = Buf()
            for (o, off) in ((out_sin, 0.0), (out_cos, 0.5 * np.pi)):
                k.op("dve", lambda e, off=off: e.tensor_scalar(t[:], phi, 1.0 / TWO_PI, off / TWO_PI, ALU.mult, ALU.add), reads=[bphi], writes=[bt])
                k.op("dve", lambda e: e.tensor_copy(ki[:], t[:]), reads=[bt], writes=[bt])
                k.op("dve", lambda e: e.tensor_copy(t[:], ki[:]), reads=[bt], writes=[bt])
                k.op("dve", lambda e: e.scalar_tensor_tensor(t[:], t[:], -TWO_PI, phi, ALU.mult, ALU.add), reads=[bt, bphi], writes=[bt])
                k.op("dve", lambda e, off=off: e.tensor_scalar(t[:], t[:], off, 3.1415925, ALU.add, ALU.min), reads=[bt], writes=[bt])
                k.op("dve", lambda e: e.tensor_scalar(t[:], t[:], -3.1415925, None, ALU.max), reads=[bt], writes=[bt])
                k.op("act", lambda e, o=o: e.activation(o, t[:], AF.Sin), reads=[bt], writes=[bout])

        def s5_layer(li, ctx_out):
            T = 128
            yfT = scrA
            tiles = seg_tiles(T)
            for d in range(2):
                with ExitStack() as st:
                    def small(name, shape, src, q="sp"):
                        t = sb(name, shape, F32, st); b = Buf()
                        k.dma(q, t[:], src, writes=[b], allow_slow_non_contiguous=True)
                        return t, b
                    ramp, bramp = small("ramp", [128, 128], ramp_in)
                    sel, bsel = small("sel", [128, 8], sel_in)
                    lrs, b1 = small("lrs", [128, 32], s5_lre[0, d].rearrange("g p -> (g p)").rearrange("(b q) -> q b", q=128))
                    lis, b2 = small("lis", [128, 32], s5_lim[0, d].rearrange("g p -> (g p)").rearrange("(b q) -> q b", q=128))
                    sts = sb("sts", [128, 32], F32, st); b3 = Buf()
                    for gl in range(2):
                        k.dma("sp", sts[gl * 64:(gl + 1) * 64, :], s5_ls[0, d].rearrange("(b gl) -> gl b", gl=2)[gl].partition_broadcast(64), writes=[b3],
                              allow_slow_non_contiguous=True)
                    rho = sb("rho", [128, 32], F32, st); theta = sb("theta", [128, 32], F32, st); bpar = Buf()
                    k.op("dve", lambda e: e.tensor_scalar(lrs[:], lrs[:], -1e-4, None, ALU.min), reads=[b1], writes=[b1])
                    k.op("act", lambda e: e.activation(sts[:], sts[:], AF.Exp), reads=[b3], writes=[b3])
                    k.op("dve", lambda e: e.tensor_tensor(rho[:], lrs[:], sts[:], ALU.mult), reads=[b1, b3], writes=[bpar])
                    k.op("act", lambda e: e.activation(rho[:], rho[:], AF.Exp), reads=[bpar], writes=[bpar])
                    k.op("dve", lambda e: e.tensor_tensor(theta[:], lis[:], sts[:], ALU.mult), reads=[b2, b3], writes=[bpar])
                    rhoT = sb("rhoT", [128, 32, 128], F32, st)
                    for b in range(32):
                        k.op("pool", lambda e, b=b: e.tensor_scalar(rhoT[:, b, :], ramp[:], 0.0, rho[:, b:b + 1], ALU.mult, ALU.add), reads=[bramp, bpar], writes=[bpar])
                    cosT = sb("cosT", [128, 32, 128], F32, st); sinT = sb("sinT", [128, 32, 128], F32, st); btab = Buf()
                    phi = sb("phi", [128, 128], F32, st); bphi = Buf()
                    with ExitStack() as st2:
                        for b in range(32):
                            k.op("dve", lambda e, b=b: e.tensor_scalar(phi[:], ramp[:], theta[:, b:b + 1], None, ALU.mult), reads=[bramp, bpar], writes=[bphi])
                            if b == 0:
                                tt_ = sb("sc_t", [128, 128], F32, st2); ki_ = sb("sc_k", [128, 128], I32, st2); bt_ = Buf()
                            for (o, off) in ((sinT[:, b, :], 0.0), (cosT[:, b, :], 0.5 * np.pi)):
                                k.op("dve", lambda e, off=off: e.tensor_scalar(tt_[:], phi[:], 1.0 / TWO_PI, off / TWO_PI, ALU.mult, ALU.add), reads=[bphi], writes=[bt_])
                                k.op("dve", lambda e: e.tensor_copy(ki_[:], tt_[:]), reads=[bt_], writes=[bt_])
                                k.op("dve", lambda e: e.tensor_copy(tt_[:], ki_[:]), reads=[bt_], writes=[bt_])
                                k.op("dve", lambda e: e.scalar_tensor_tensor(tt_[:], tt_[:], -TWO_PI, phi[:], ALU.mult, ALU.add), reads=[bt_, bphi], writes=[bt_])
                                k.op("dve", lambda e, off=off: e.tensor_scalar(tt_[:], tt_[:], off, 3.1415925, ALU.add, ALU.min), reads=[bt_], writes=[bt_])
                                k.op("dve", lambda e: e.tensor_scalar(tt_[:], tt_[:], -3.1415925, None, ALU.max), reads=[bt_], writes=[bt_])
                                k.op("act", lambda e, o=o: e.activation(o, tt_[:], AF.Sin), reads=[bt_], writes=[btab])
                    k.barrier()
                    Wre = sb("Wre", [128, 8, 4, 128], BF16, st); Wim = sb("Wim", [128, 8, 4, 128], BF16, st); bW = Buf()
                    with ExitStack() as st2:
                        def wl(name, src3):
                            t = sb(name, [128, 8, 64], F32, st2); b = Buf()
                            v = src3.rearrange("(c gl) p -> gl c p", gl=8)
                            for gl in range(8):
                                k.dma("sp" if gl % 2 else "pool", t[gl * 16:(gl + 1) * 16, :, :], v[gl].partition_broadcast(16), writes=[b], allow_slow_non_contiguous=True)
                            return t, b
                        lrw, blr = wl("lrw", s5_lre[0, d]); liw, bli = wl("liw", s5_lim[0, d])
                        stw = sb("stw", [128, 8], F32, st2); bst = Buf()
                        v = s5_ls[0, d].rearrange("(c gl) -> gl c", gl=8)
                        for gl in range(8):
                            k.dma("sp", stw[gl * 16:(gl + 1) * 16, :], v[gl].partition_broadcast(16), writes=[bst], allow_slow_non_contiguous=True)
                        brw = sb("brw", [128, 8, 64], F32, st2); biw = sb("biw", [128, 8, 64], F32, st2); bbw = Buf()
                        for (t_, src) in ((brw, s5_bre[0]), (biw, s5_bim[0])):
                            v = src.rearrange("(c gl) p j -> gl j c p", gl=8)
                            for gl in range(8):
                                for c_ in range(8):
                                    k.dma("sp" if c_ % 2 else "pool", t_[gl * 16:(gl + 1) * 16, c_, :], v[gl][:, c_, :], writes=[bbw], allow_slow_non_contiguous=True)
                        k.op("dve", lambda e: e.tensor_scalar(lrw[:], lrw[:], -1e-4, None, ALU.min), reads=[blr], writes=[blr])
                        k.op("act", lambda e: e.activation(stw[:], stw[:], AF.Exp), reads=[bst], writes=[bst])
                        stb = stw[:].unsqueeze(2).to_broadcast([128, 8, 64])
                        mag = sb("mag", [128, 8, 64], F32, st2); th = sb("thw", [128, 8, 64], F32, st2); bm_ = Buf(); bth = Buf()
                        k.op("dve", lambda e: e.tensor_tensor(mag[:], lrw[:], stb, ALU.mult), reads=[blr, bst], writes=[bm_])
                        k.op("act", lambda e: e.activation(mag[:], mag[:], AF.Exp), reads=[bm_], writes=[bm_])
                        k.op("dve", lambda e: e.tensor_tensor(th[:], liw[:], stb, ALU.mult), reads=[bli, bst], writes=[bth])
                        sn = sb("snw", [128, 8, 64], F32, st2); cs = sb("csw", [128, 8, 64], F32, st2); bsc_ = Buf()
                        sincos(st2, "w", th[:], bth, [128, 8, 64], sn[:], cs[:], bsc_)
                        k.op("dve", lambda e: e.tensor_tensor(cs[:], cs[:], mag[:], ALU.mult), reads=[bsc_, bm_], writes=[bsc_])
                        k.op("dve", lambda e: e.tensor_scalar(cs[:], cs[:], -1.0, None, ALU.add), reads=[bsc_], writes=[bsc_])
                        k.op("dve", lambda e: e.tensor_tensor(sn[:], sn[:], mag[:], ALU.mult), reads=[bsc_, bm_], writes=[bsc_])
                        den = sb("den", [128, 8, 64], F32, st2); t1 = sb("t1w", [128, 8, 64], F32, st2); bden = Buf(); bt1 = Buf()
                        zr = sb("zr", [128, 8, 64], F32, st2); zi = sb("zi", [128, 8, 64], F32, st2); bz_ = Buf()
                        k.op("dve", lambda e: e.tensor_tensor(den[:], lrw[:], lrw[:], ALU.mult), reads=[blr], writes=[bden])
                        k.op("dve", lambda e: e.tensor_tensor(t1[:], liw[:], liw[:], ALU.mult), reads=[bli], writes=[bt1])
                        k.op("dve", lambda e: e.tensor_tensor(den[:], den[:], t1[:], ALU.add), reads=[bden, bt1], writes=[bden])
                        k.op("dve", lambda e: e.reciprocal(den[:], den[:]), reads=[bden], writes=[bden])
                        k.op("dve", lambda e: e.tensor_tensor(zr[:], cs[:], lrw[:], ALU.mult), reads=[bsc_, blr], writes=[bz_])
                        k.op("dve", lambda e: e.tensor_tensor(t1[:], sn[:], liw[:], ALU.mult), reads=[bsc_, bli], writes=[bt1])
                        k.op("dve", lambda e: e.tensor_tensor(zr[:], zr[:], t1[:], ALU.add), reads=[bz_, bt1], writes=[bz_])
                        k.op("dve", lambda e: e.tensor_tensor(zr[:], zr[:], den[:], ALU.mult), reads=[bz_, bden], writes=[bz_])
                        k.op("dve", lambda e: e.tensor_tensor(zi[:], sn[:], lrw[:], ALU.mult), reads=[bsc_, blr], writes=[bz_])
                        k.op("dve", lambda e: e.tensor_tensor(t1[:], cs[:], liw[:], ALU.mult), reads=[bsc_, bli], writes=[bt1])
                        k.op("dve", lambda e: e.tensor_tensor(zi[:], zi[:], t1[:], ALU.subtract), reads=[bz_, bt1], writes=[bz_])
                        k.op("dve", lambda e: e.tensor_tensor(zi[:], zi[:], den[:], ALU.mult), reads=[bz_, bden], writes=[bz_])
                        k.op("dve", lambda e: e.tensor_tensor(mag[:], zr[:], brw[:], ALU.mult), reads=[bz_, bbw], writes=[bm_])
                        k.op("dve", lambda e: e.tensor_tensor(t1[:], zi[:], biw[:], ALU.mult), reads=[bz_, bbw], writes=[bt1])
                        k.op("dve", lambda e: e.tensor_tensor(mag[:], mag[:], t1[:], ALU.subtract), reads=[bm_, bt1], writes=[bm_])
                        k.op("dve", lambda e: e.tensor_tensor(th[:], zr[:], biw[:], ALU.mult), reads=[bz_, bbw], writes=[bth])
                        k.op("dve", lambda e: e.tensor_tensor(t1[:], zi[:], brw[:], ALU.mult), reads=[bz_, bbw], writes=[bt1])
                        k.op("dve", lambda e: e.tensor_tensor(th[:], th[:], t1[:], ALU.add), reads=[bth, bt1], writes=[bth])
                        if debug_sub == "pre":
                            for i_, (t_, b_) in enumerate(((brw, bbw), (mag, bm_), (zr, bz_), (den, bden), (sn, bsc_), (lrw, blr), (liw, bli))):
                                k.dma("sp", outT[i_ * 128:(i_ + 1) * 128, 0:512], t_[:].rearrange("p c q -> p (c q)"), reads=[b_], writes=[Buf()])
                            k.dma("sp", outT[896:1024, 0:8], stw[:], reads=[bst], writes=[Buf()])
                            k.dma("sp", outT[896:1024, 8:16], sel[:], reads=[bsel], writes=[Buf()])
                            k.barrier()
                            return
                        for q in range(4):
                            for gl2 in range(2):
                                k.op("dve", lambda e, q=q, gl2=gl2: e.tensor_scalar(Wre[:, :, q, gl2 * 64:(gl2 + 1) * 64], mag[:], sel[:, q * 2 + gl2:q * 2 + gl2 + 1], None, ALU.mult),
                                     reads=[bm_, bsel], writes=[bW])
                                k.op("dve", lambda e, q=q, gl2=gl2: e.tensor_scalar(Wim[:, :, q, gl2 * 64:(gl2 + 1) * 64], th[:], sel[:, q * 2 + gl2:q * 2 + gl2 + 1], None, ALU.mult),
                                     reads=[bth, bsel], writes=[bW])
                    k.barrier()
                    if debug_sub == "pre2":
                        for c_ in range(8):
                            k.dma("sp", outT[c_ * 128:(c_ + 1) * 128, 0:512], Wre[:, c_, :, :].rearrange("p q m -> p (q m)"), reads=[bW], writes=[Buf()])
                        k.barrier()
                        return
                    WcR = sb("WcR", [128, 32, 128], BF16, st); WcI = sb("WcI", [128, 32, 128], BF16, st); bWc = Buf()
                    for b_ in range(32):
                        k.op("pool", lambda e, b_=b_: e.memset(WcR[:, b_, :], 0.0), writes=[bWc])
                        k.op("pool", lambda e, b_=b_: e.memset(WcI[:, b_, :], 0.0), writes=[bWc])
                    if debug_sub == "p_a":
                        for c_ in range(8):
                            k.dma("sp", outT[c_ * 128:(c_ + 1) * 128, 0:512], Wre[:, c_, :, :].rearrange("p q m -> p (q m)"), reads=[bW], writes=[Buf()])
                        k.barrier()
                        return
                    with ExitStack() as st2:
                        for (Wc_, src, sgn) in ((WcR, s5_cre[0, d], 1.0), (WcI, s5_cim[0, d], -1.0)):
                            t2 = sb("t2c", [128, 32, 16], F32, st2); bt2 = Buf()
                            v = src.rearrange("(b gl) j p -> gl p b j", gl=2)
                            for gl2 in range(2):
                                for hb in range(32):
                                    k.dma("sp" if hb % 2 else "pool", t2[gl2 * 64:(gl2 + 1) * 64, hb, :], v[gl2][:, hb, :], writes=[bt2],
                                          allow_slow_non_contiguous=True)
                            for gl2 in range(2):
                                for q in range(4):
                                    col = (2 * q + gl2) * 16
                                    k.op("dve", lambda e, gl2=gl2, q=q, col=col, Wc_=Wc_, t2=t2, sgn=sgn: e.tensor_scalar(
                                        Wc_[gl2 * 64:(gl2 + 1) * 64, q::4, col:col + 16], t2[gl2 * 64:(gl2 + 1) * 64, q::4, :], sgn, None, ALU.mult),
                                        reads=[bt2], writes=[bWc])
                    k.barrier()
                    if debug_sub == "pre3":
                        for c_ in range(8):
                            k.dma("sp", outT[c_ * 128:(c_ + 1) * 128, 0:512], Wre[:, c_, :, :].rearrange("p q m -> p (q m)"), reads=[bW], writes=[Buf()])
                        k.barrier()
                        return
                    if d == 1:
                        gw_, bgw_ = load_w(st, "s5glu", s5_glu_w[0], 8, 2 * D)
                        gbias, bgb_ = small("s5gb", [128, 16], s5_glu_b[0].rearrange("(m p) -> p m", p=128))
                        dskT, bdsk_ = small("s5d", [128, 8], s5_d[0].rearrange("(c p) -> p c", p=128))
                    nscs = [norm_scratch(st, T) for _ in range(2)]
                    xt = [sb("s5_x%d" % i, [128, 8, T], F32, st) for i in range(2)]; bx = [Buf(), Buf()]
                    hTs = [sb("s5_h%d" % i, [128, 8, T], BF16, st) for i in range(2)]; bhs = [Buf(), Buf()]
                    stR = sb("s5_stR", [128, 32], F32, st); stI = sb("s5_stI", [128, 32], F32, st); bstate = [Buf() for _ in range(32)]
                    k.op("pool", lambda e: e.memset(stR[:], 0.0), writes=bstate)
                    k.op("pool", lambda e: e.memset(stI[:], 0.0), writes=bstate)
                    NB = 6
                    def mk(name, dt=F32):
                        return [sb("%s%d" % (name, i), [128, T], dt, st) for i in range(NB)], [Buf() for _ in range(NB)]
                    ur, bur = mk("s5_ur"); ui, bui = mk("s5_ui")
                    ta, bta = mk("s5_ta"); tb_, btb = mk("s5_tb"); tc, btc = mk("s5_tc"); td, btd = mk("s5_td")
                    vr, bvr = mk("s5_vr"); vi, bvi = mk("s5_vi")
                    q1, bq1 = mk("s5_q1", BF16); q2, bq2 = mk("s5_q2", BF16); q3, bq3 = mk("s5_q3", BF16); q4, bq4 = mk("s5_q4", BF16)
                    stt_ = [sb("s5_stt%d" % i, [128, 2], F32, st) for i in range(NB)]; bstt = [Buf() for _ in range(NB)]
                    yo = [sb("s5_yo%d" % i, [128, T], F32, st) for i in range(2)]; byo = [Buf(), Buf()]
                    if d == 1:
                        yf = [sb("s5_yf%d" % i, [128, T], F32, st) for i in range(2)]; byf = [Buf(), Buf()]
                        gT = sb("s5_g", [128, 8, T], BF16, st); bg = Buf()
                        ga = [sb("s5_ga%d" % i, [128, T], F32, st) for i in range(2)]; bga = [Buf(), Buf()]
                    order = tiles if d == 0 else [tiles[1], tiles[0]] + tiles[:1:-1]
                    rv = (lambda ap: ap) if d == 0 else (lambda ap: ap[:, ::-1])
                    pspool[0] = [6, 7]
                    units = []

                    def make_unit(ti, t0, n, vsel, c, q, un):
                        x, b_x = xt[ti % 2], bx[ti % 2]
                        hT, bh = hTs[ti % 2], bhs[ti % 2]
                        b = c * 4 + q
                        i3 = un % NB
                        pu, pub = ps[un % 3], psb[un % 3]
                        py, pyb = ps[3 + (un // 4) % 3], psb[3 + (un // 4) % 3]
                        cs_ = rv(cosT[:, b, :]); sn_ = rv(sinT[:, b, :])
                        rb = rhoT[:, b, :]
                        last = (n - 1) if d == 0 else 0

                        def s0():
                            if c == 0 and q == 0:
                                k.dma("sp", x[:, :, :n], xres[:, t0:t0 + n].rearrange("(c p) t -> p c t", p=128), writes=[b_x])
                                norm_mod(nscs[ti % 2], x, b_x, n, vsel, A1, 0, hT, bh)
                            k.op("pe", lambda e: e.matmul(pu[:, 0:128], Wre[:, c, q, :], hT[:, c, :], start=True, stop=True), reads=[bW, bh], writes=[pub], inc=False)
                            k.op("pe", lambda e: e.matmul(pu[:, 128:256], Wim[:, c, q, :], hT[:, c, :], start=True, stop=True), reads=[bW, bh], writes=[pub])

                        def s1():
                            k.op("act", lambda e: e.copy(ur[i3][:], pu[:, 0:128]), reads=[pub], writes=[bur[i3]])
                            k.op("act", lambda e: e.copy(ui[i3][:], pu[:, 128:256]), reads=[pub], writes=[bui[i3]])

                        def s2():
                            k.op("dve", lambda e: e.tensor_tensor(ta[i3][:], ur[i3][:], cs_, ALU.mult), reads=[bur[i3], btab], writes=[bta[i3]])
                            k.op("pool", lambda e: e.tensor_tensor(tb_[i3][:], ui[i3][:], sn_, ALU.mult), reads=[bui[i3], btab], writes=[btb[i3]])
                            k.op("dve", lambda e: e.tensor_tensor(tc[i3][:], ui[i3][:], cs_, ALU.mult), reads=[bui[i3], btab], writes=[btc[i3]])
                            k.op("pool", lambda e: e.tensor_tensor(td[i3][:], ur[i3][:], sn_, ALU.mult), reads=[bur[i3], btab], writes=[btd[i3]])

                        def s3():
                            k.op("pool", lambda e: e.tensor_tensor(ta[i3][:], ta[i3][:], tb_[i3][:], ALU.add), reads=[bta[i3], btb[i3]], writes=[bta[i3]])
                            k.op("pool", lambda e: e.tensor_tensor(tc[i3][:], tc[i3][:], td[i3][:], ALU.subtract), reads=[btc[i3], btd[i3]], writes=[btc[i3]])

                        def s4():
                            k.op("dve", lambda e: e.tensor_tensor_scan(rv(vr[i3][:]), rb, rv(ta[i3][:]), stR[:, b:b + 1], ALU.mult, ALU.add),
                                 reads=[bta[i3], bpar, bstate[b]], writes=[bvr[i3]])
                            k.op("dve", lambda e: e.tensor_tensor_scan(rv(vi[i3][:]), rb, rv(tc[i3][:]), stI[:, b:b + 1], ALU.mult, ALU.add),
                                 reads=[btc[i3], bpar, bstate[b]], writes=[bvi[i3]])

                        def s5():
                            k.op("pool", lambda e: e.tensor_tensor(q1[i3][:], vr[i3][:], cs_, ALU.mult), reads=[bvr[i3], btab], writes=[bq1[i3]])
                            k.op("dve", lambda e: e.scalar_tensor_tensor(q2[i3][:], vi[i3][:], -1.0, sn_, ALU.mult, ALU.mult), reads=[bvi[i3], btab], writes=[bq2[i3]])
                            k.op("pool", lambda e: e.tensor_tensor(q3[i3][:], vi[i3][:], cs_, ALU.mult), reads=[bvi[i3], btab], writes=[bq3[i3]])
                            k.op("dve", lambda e: e.tensor_tensor(q4[i3][:], vr[i3][:], sn_, ALU.mult), reads=[bvr[i3], btab], writes=[bq4[i3]])
                            k.op("act", lambda e: e.activation(stt_[i3][:, 0:1], vi[i3][:, last:last + 1], AF.Identity, scale=sn_[:, last:last + 1]),
                                 reads=[bvi[i3], btab], writes=[bstt[i3]])
                            k.op("act", lambda e: e.activation(stt_[i3][:, 0:1], stt_[i3][:, 0:1], AF.Identity, scale=-1.0), reads=[bstt[i3]], writes=[bstt[i3]])
                            k.op("act", lambda e: e.activation(stt_[i3][:, 1:2], vr[i3][:, last:last + 1], AF.Identity, scale=sn_[:, last:last + 1]),
                                 reads=[bvr[i3], btab], writes=[bstt[i3]])

                        def s6():
                            k.op("act", lambda e: e.activation(stR[:, b:b + 1], vr[i3][:, last:last + 1], AF.Identity, bias=stt_[i3][:, 0:1], scale=cs_[:, last:last + 1]),
                                 reads=[bvr[i3], btab, bstt[i3]], writes=[bstate[b]])
                            k.op("act", lambda e: e.activation(stI[:, b:b + 1], vi[i3][:, last:last + 1], AF.Identity, bias=stt_[i3][:, 1:2], scale=cs_[:, last:last + 1]),
                                 reads=[bvi[i3], btab, bstt[i3]], writes=[bstate[b]])
                            k.op("pe", lambda e: e.matmul(py[:, 0:128], WcR[:, b, :], q1[i3][:], start=(q == 0), stop=False), reads=[bWc, bq1[i3]], writes=[pyb], inc=False)
                            k.op("pe", lambda e: e.matmul(py[:, 0:128], WcR[:, b, :], q2[i3][:], start=False, stop=False), reads=[bWc, bq2[i3]], writes=[pyb], inc=False)
                            k.op("pe", lambda e: e.matmul(py[:, 0:128], WcI[:, b, :], q3[i3][:], start=False, stop=False), reads=[bWc, bq3[i3]], writes=[pyb], inc=False)
                            k.op("pe", lambda e: e.matmul(py[:, 0:128], WcI[:, b, :], q4[i3][:], start=False, stop=(q == 3)), reads=[bWc, bq4[i3]], writes=[pyb], inc=(q == 3))
                            if q != 3:
                                return
                            if d == 0:
                                o, bo = yo[c % 2], byo[c % 2]
                                k.op("act", lambda e: e.copy(o[:], py[:, 0:128]), reads=[pyb], writes=[bo])
                                k.dma("pool", yfT[c * 128:(c + 1) * 128, t0:t0 + n], o[:], reads=[bo], writes=[Buf()])
                                return
                            if vsel == 1 and not ctx_out:
                                return
                            f_, bf_ = yf[c % 2], byf[c % 2]
                            k.dma("sp", f_[:], yfT[c * 128:(c + 1) * 128, t0:t0 + n], writes=[bf_])
                            k.op("dve", lambda e: e.tensor_tensor(f_[:], f_[:], py[:, 0:128], ALU.add), reads=[pyb, bf_], writes=[bf_])
                            k.op("dve", lambda e: e.scalar_tensor_tensor(f_[:], hT[:, c, :], dskT[:, c:c + 1], f_[:], ALU.mult, ALU.add), reads=[bh, bdsk_, bf_], writes=[bf_])
                            k.op("act", lambda e: e.activation(gT[:, c, :], f_[:], AF.Gelu_apprx_tanh), reads=[bf_], writes=[bg])
                            if c != 7:
                                return
                            for c2 in range(8):
                                pa_, pab_ = next_ps()
                                pg_, pgb_ = next_ps()
                                for kc in range(8):
                                    k.op("pe", lambda e, kc=kc, c2=c2, pa_=pa_: e.matmul(pa_[:, 0:128], gw_[:, kc, c2 * 128:(c2 + 1) * 128], gT[:, kc, :], start=(kc == 0), stop=(kc == 7)),
                                         reads=[bgw_, bg], writes=[pab_], inc=(kc == 7))
                                for kc in range(8):
                                    k.op("pe", lambda e, kc=kc, c2=c2, pg_=pg_: e.matmul(pg_[:, 0:128], gw_[:, kc, D + c2 * 128:D + (c2 + 1) * 128], gT[:, kc, :], start=(kc == 0), stop=(kc == 7)),
                                         reads=[bgw_, bg], writes=[pgb_], inc=(kc == 7))
                                a_, ba_ = ga[c2 % 2], bga[c2 % 2]
                                k.op("act", lambda e, pg_=pg_, a_=a_, c2=c2: e.activation(a_[:], pg_[:, 0:128], AF.Sigmoid, bias=gbias[:, 8 + c2:9 + c2], scale=1.0), reads=[pgb_, bgb_], writes=[ba_])
                                k.op("dve", lambda e, pa_=pa_, a_=a_, c2=c2: e.scalar_tensor_tensor(a_[:], pa_[:, 0:128], gbias[:, c2:c2 + 1], a_[:], ALU.add, ALU.mult), reads=[pab_, bgb_, ba_], writes=[ba_])
                                k.op("dve", lambda e, a_=a_, c2=c2: e.scalar_tensor_tensor(x[:, c2, :], a_[:], mod[:, 16 + c2, vsel:vsel + 1], x[:, c2, :], ALU.mult, ALU.add),
                                     reads=[ba_, b_mod, b_x], writes=[b_x])
                            k.dma("pool", xres[:, t0:t0 + n].rearrange("(c p) t -> p c t", p=128), x[:, :, :n], reads=[b_x], writes=[Buf()])

                        return [s0, s1, s2, s3, s4, s5, s6]

                    un = 0
                    for ti, (t0, n, vsel) in enumerate(order):
                        for c in range(8):
                            for q in range(4):
                                units.append(make_unit(ti, t0, n, vsel, c, q, un))
                                un += 1
                    NS = 7
                    for step in range(len(units) + NS - 1):
                        for j in range(NS - 1, -1, -1):
                            u = step - j
                            if 0 <= u < len(units):
                                units[u][j]()
                    pspool[0] = list(range(8))
                k.barrier()
                if debug == "s5f" and d == 0:
                    return

        def mla_layer(li, ctx_out):
            TT = 512
            SCALE = 96.0 ** -0.5
            with ExitStack() as L:
                qnT = sb("qnT", [128, 3, NLAT], BF16, L); bqn = Buf()
                kvnT = sb("kvnT", [128, 2, NTOK], BF16, L); bkvn = Buf()
                KT = sb("KT", [128, NTOK], BF16, L); bKr = Buf(); bKn = Buf()
                k.op("pool", lambda e: e.memset(KT[96:97, :], 1.0), writes=[bKr])
                with ExitStack() as st:
                    w, bw = load_w(st, "mlin", mla_in_w[0], 8, 672)
                    wkr = sb("wkr", [128, 8, 96], BF16, st); wkrs = sb("wkrs", [128, 8, 96], BF16, st); bwk = Buf()
                    k.op("pool", lambda e: e.memset(wkr[:], 0.0), writes=[bwk])
                    k.op("pool", lambda e: e.memset(wkrs[:], 0.0), writes=[bwk])
                    k.dma("pool", wkr[:, :, 64:96], mla_in_w[0][:, 640:672].rearrange("(c p) n -> p c n", p=128), writes=[bwk], allow_slow_non_contiguous=True)
                    k.dma("pool", wkrs[:, :, 64:96], mla_inw_sw.rearrange("(c p) n -> p c n", p=128), writes=[bwk], allow_slow_non_contiguous=True)
                    qnw = sb("qnw", [128, 3], F32, st); kvnw = sb("kvnw", [128, 2], F32, st); bnw_ = Buf()
                    k.dma("sp", qnw[:], mla_q_norm_w[0].rearrange("(c p) -> p c", p=128), writes=[bnw_], allow_slow_non_contiguous=True)
                    k.dma("sp", kvnw[:], mla_kv_norm_w[0].rearrange("(c p) -> p c", p=128), writes=[bnw_], allow_slow_non_contiguous=True)
                    nsc = norm_scratch(st, TT)
                    xt = [sb("ma_x%d" % i, [128, 8, TT], F32, st) for i in range(2)]; bx = [Buf(), Buf()]
                    hT = sb("ma_h", [128, 8, TT], BF16, st); bh = Buf()
                    ql = sb("ma_ql", [128, 3, TT], F32, st); bql = Buf()
                    sq = sb("ma_sq", [128, TT], F32, st); bsq = Buf()
                    rs = sb("ma_rs", [128, TT], F32, st); brs = Buf()
                    rc = [sb("ma_rc%d" % i, [128, TT], F32, st) for i in range(2)]; rsn = [sb("ma_rsn%d" % i, [128, TT], F32, st) for i in range(2)]; brc = [Buf(), Buf()]
                    t1 = sb("ma_t1", [128, TT], F32, st); t2 = sb("ma_t2", [128, TT], F32, st); bt1 = Buf(); bt2 = Buf()

                    def lat_norm(ncl, col0, nw_t, dim, out_t, out_b, tsl):
                        pss, pssb = next_ps()
                        for c in range(ncl):
                            p, pb = next_ps()
                            for kc in range(8):
                                k.op("pe", lambda e, p=p, kc=kc, c=c: e.matmul(p[:, :n], w[:, kc, col0 + c * 128:col0 + (c + 1) * 128], hT[:, kc, :n], start=(kc == 0), stop=(kc == 7)),
                                     reads=[bw, bh], writes=[pb], inc=(kc == 7))
                            k.op("act", lambda e, p=p, c=c: e.copy(ql[:, c, :n], p[:, :n]), reads=[pb], writes=[bql])
                            k.op("act", lambda e, p=p: e.activation(sq[:, :n], p[:, :n], AF.Square), reads=[pb], writes=[bsq])
                            k.op("pe", lambda e, pss=pss, c=c: e.matmul(pss[:, :n], ones_f[:], sq[:, :n], start=(c == 0), stop=(c == ncl - 1)), reads=[b_ones, bsq], writes=[pssb])
                        k.op("act", lambda e, pss=pss: e.activation(rs[:, :n], pss[:, :n], AF.Sqrt, bias=EPS, scale=1.0 / dim), reads=[pssb], writes=[brs])
                        k.op("dve", lambda e: e.reciprocal(rs[:, :n], rs[:, :n]), reads=[brs], writes=[brs])
                        for c in range(ncl):
                            k.op("dve", lambda e, c=c: e.scalar_tensor_tensor(out_t[:, c, tsl], ql[:, c, :n], nw_t[:, c:c + 1], rs[:, :n], ALU.mult, ALU.mult),
                                 reads=[bql, bnw_, brs], writes=[out_b])

                    for ti, (t0, n, vsel) in enumerate(seg_tiles(TT)):
                        x, b_x = xt[ti % 2], bx[ti % 2]
                        k.dma("sp", x[:, :, :n], xres[:, t0:t0 + n].rearrange("(c p) t -> p c t", p=128), writes=[b_x])
                        norm_mod(nsc, x, b_x, n, vsel, A1, 0, hT, bh)
                        if vsel == 0:
                            lat_norm(3, 0, qnw, 384.0, qnT, bqn, slice(t0 - NCTX, t0 - NCTX + n))
                        lat_norm(2, 384, kvnw, 256.0, kvnT, bkvn, slice(t0, t0 + n))
                        pA, pAb = next_ps()
                        for kc in range(8):
                            k.op("pe", lambda e, pA=pA, kc=kc: e.matmul(pA[0:96, :n], wkr[:, kc, :], hT[:, kc, :n], start=(kc == 0), stop=(kc == 7)), reads=[bwk, bh], writes=[pAb], inc=(kc == 7))
                        if vsel == 1:
                            k.op("act", lambda e, pA=pA: e.copy(KT[64:96, t0:t0 + n], pA[64:96, :n]), reads=[pAb], writes=[bKr])
                        else:
                            pB, pBb = next_ps()
                            for kc in range(8):
                                k.op("pe", lambda e, pB=pB, kc=kc: e.matmul(pB[0:96, :n], wkrs[:, kc, :], hT[:, kc, :n], start=(kc == 0), stop=(kc == 7)), reads=[bwk, bh], writes=[pBb], inc=(kc == 7))
                            i2 = ti % 2
                            k.dma("sp", rc[i2][64:96, :n], ropeC_in[:, t0 - NCTX:t0 - NCTX + n], writes=[brc[i2]])
                            k.dma("sp", rsn[i2][64:96, :n], ropeS_in[:, t0 - NCTX:t0 - NCTX + n], writes=[brc[i2]])
                            k.op("dve", lambda e, pA=pA, i2=i2: e.tensor_tensor(t1[64:96, :n], pA[64:96, :n], rc[i2][64:96, :n], ALU.mult), reads=[pAb, brc[i2]], writes=[bt1])
                            k.op("dve", lambda e, pB=pB, i2=i2: e.tensor_tensor(t2[64:96, :n], pB[64:96, :n], rsn[i2][64:96, :n], ALU.mult), reads=[pBb, brc[i2]], writes=[bt2])
                            k.op("pool", lambda e: e.tensor_tensor(KT[64:96, t0:t0 + n], t1[64:96, :n], t2[64:96, :n], ALU.add), reads=[bt1, bt2], writes=[bKr])
                k.barrier()
                with ExitStack() as st:
                    kvbw, bkvbw = load_w(st, "kvbw", mla_kvb_w[0], 2, 2048)
                    qbw, bqbw = load_w(st, "qbw", mla_qb_w[0], 3, 1536)
                    qbws, bqbws = load_w(st, "qbws", mla_qbw_sw, 3, 512)
                    e96 = sb("e96", [128, 128], BF16, st); be96 = Buf()
                    k.dma("pool", e96[:], e96_in, writes=[be96])
                    odm = sb("odm", [128, 2, 128], BF16, st); bodm = Buf()
                    k.dma("pool", odm[:], odm_in.rearrange("a p m -> p a m"), writes=[bodm])
                    wv = [sb("wv%d" % i, [128, 2, 128], BF16, st) for i in range(2)]; bwv = [Buf(), Buf()]
                    wqr = sb("wqr", [128, 3, 96], BF16, st); wqrs = sb("wqrs", [128, 3, 96], BF16, st); bwq = Buf()
                    for t_ in (wv[0], wv[1], wqr, wqrs):
                        k.op("pool", lambda e, t_=t_: e.memset(t_[:], 0.0), writes=[bwv[0], bwv[1], bwq])
                    QT = sb("QT", [128, NLAT], BF16, st); bQ = Buf()
                    Vt = sb("Vt", [128, NCH, 128], BF16, st); bV = Buf()
                    sqb = sb("mb_sq", [128, TT], BF16, st); bsqb = Buf()
                    kk = sb("mb_kk", [128, TT], F32, st); bkk = Buf()
                    kmx = sb("mb_kmx", [128, 2], F32, st); bkmx = Buf()
                    rc = [sb("mb_rc%d" % i, [128, TT], F32, st) for i in range(2)]; rsn = [sb("mb_rsn%d" % i, [128, TT], F32, st) for i in range(2)]; brc = [Buf(), Buf()]
                    t1 = sb("mb_t1", [128, TT], F32, st); t2 = sb("mb_t2", [128, TT], F32, st); bt1 = Buf(); bt2 = Buf()
                    Pt = [sb("mb_P%d" % i, [128, TT], BF16, st) for i in range(4)]; bP = [Buf() for _ in range(4)]
                    rd = sb("mb_rd", [128, TT], F32, st); brd = Buf()
                    xs_ = sb("mb_xs", [128, TT], F32, st); bxs_ = Buf()
                    swp = sb("mb_swp", [128, 128], F32, st); bswp = Buf()
                    k.dma("sp", swp[:], swp_in, writes=[bswp])
                    ot = [sb("mb_o%d" % i, [128, TT], BF16, st) for i in range(2)]; bot = [Buf(), Buf()]
                    pspool[0] = [4, 5, 6, 7]
                    ktiles = seg_tiles(TT)
                    pn = 0
                    for h in range(16):
                        par = h % 2
                        off = par * 64
                        k.op("pool", lambda e, h=h, par=par, off=off: e.tensor_copy(wv[par][:, :, off:off + 64], kvbw[:, :, h * 128 + 64:h * 128 + 128]), reads=[bkvbw], writes=[bwv[par]])
                        k.op("pool", lambda e, h=h: e.tensor_copy(wqr[:, :, 64:96], qbw[:, :, h * 96 + 64:h * 96 + 96]), reads=[bqbw], writes=[bwq])
                        k.op("pool", lambda e, h=h: e.tensor_copy(wqrs[:, :, 64:96], qbws[:, :, h * 32:h * 32 + 32]), reads=[bqbws], writes=[bwq])
                        k.op("pool", lambda e: e.memset(kmx[:], 0.0), writes=[bkmx])
                        for (t0, n, vsel) in ktiles:
                            p, pb = next_ps()
                            for c in range(2):
                                k.op("pe", lambda e, p=p, c=c, h=h: e.matmul(p[0:64, :n], kvbw[:, c, h * 128:h * 128 + 64], kvnT[:, c, t0:t0 + n], start=(c == 0), stop=(c == 1)),
                                     reads=[bkvbw, bkvn], writes=[pb], inc=(c == 1))
                            k.op("act", lambda e, p=p: e.copy(KT[0:64, t0:t0 + n], p[0:64, :n]), reads=[pb], writes=[bKn])
                            k.op("act", lambda e: e.activation(sqb[0:96, :n], KT[0:96, t0:t0 + n], AF.Square), reads=[bKn, bKr], writes=[bsqb])
                            p2, p2b = next_ps()
                            k.op("pe", lambda e, p2=p2: e.matmul(p2[0:97, :n], e96[0:96, 0:97], sqb[0:96, :n], start=True, stop=True), reads=[be96, bsqb], writes=[p2b])
                            k.op("dve", lambda e, p2=p2: e.reduce_max(kmx[96:97, 1:2], p2[96:97, :n], AX.X), reads=[p2b], writes=[bkmx])
                            k.op("dve", lambda e: e.tensor_tensor(kmx[96:97, 0:1], kmx[96:97, 0:1], kmx[96:97, 1:2], ALU.max), reads=[bkmx], writes=[bkmx])
                        k.op("act", lambda e: e.activation(kmx[96:97, 0:1], kmx[96:97, 0:1], AF.Sqrt), reads=[bkmx], writes=[bkmx])
                        k.op("dve", lambda e: e.tensor_scalar(kmx[96:97, 0:1], kmx[96:97, 0:1], -1.0, None, ALU.mult), reads=[bkmx], writes=[bkmx])
                        for b4 in range(0, NCH, 4):
                            p, pb = next_ps()
                            nb = min(4, NCH - b4)
                            for j in range(nb):
                                blk = b4 + j
                                for c in range(2):
                                    k.op("pe", lambda e, p=p, j=j, blk=blk, c=c, par=par: e.matmul(p[:, j * 128:(j + 1) * 128], kvnT[:, c, blk * 128:(blk + 1) * 128], wv[par][:, c, :],
                                                                                                start=(c == 0), stop=(c == 1)),
                                         reads=[bkvn, bwv[par]], writes=[pb], inc=(c == 1 and j == nb - 1))
                            k.op("dve", lambda e, p=p, b4=b4, nb=nb, off=off: e.tensor_copy(Vt[:, b4:b4 + nb, off:off + 64], p[:, 0:nb * 128].rearrange("p (j m) -> p j m", m=128)[:, :, off:off + 64]),
                                 reads=[pb], writes=[bV])
                        k.op("pool", lambda e, off=off: e.memset(Vt[:, :, 64 - off:128 - off], 1.0), writes=[bV])
                        for qi in range(NLAT // TT):
                            q0 = qi * TT
                            p, pb = next_ps()
                            for c in range(3):
                                k.op("pe", lambda e, p=p, c=c, h=h: e.matmul(p[0:64, :], qbw[:, c, h * 96:h * 96 + 64], qnT[:, c, q0:q0 + TT], start=(c == 0), stop=(c == 2)),
                                     reads=[bqbw, bqn], writes=[pb], inc=(c == 2))
                            k.op("act", lambda e, p=p: e.copy(QT[0:64, q0:q0 + TT], p[0:64, :]), reads=[pb], writes=[bQ])
                            pA, pAb = next_ps()
                            pB, pBb = next_ps()
                            for c in range(3):
                                k.op("pe", lambda e, pA=pA, c=c: e.matmul(pA[0:96, :], wqr[:, c, :], qnT[:, c, q0:q0 + TT], start=(c == 0), stop=(c == 2)), reads=[bwq, bqn], writes=[pAb], inc=(c == 2))
                            for c in range(3):
                                k.op("pe", lambda e, pB=pB, c=c: e.matmul(pB[0:96, :], wqrs[:, c, :], qnT[:, c, q0:q0 + TT], start=(c == 0), stop=(c == 2)), reads=[bwq, bqn], writes=[pBb], inc=(c == 2))
                            i2 = qi % 2
                            k.dma("sp", rc[i2][64:96, :], ropeC_in[:, q0:q0 + TT], writes=[brc[i2]])
                            k.dma("sp", rsn[i2][64:96, :], ropeS_in[:, q0:q0 + TT], writes=[brc[i2]])
                            k.op("dve", lambda e, pA=pA, i2=i2: e.tensor_tensor(t1[64:96, :], pA[64:96, :], rc[i2][64:96, :], ALU.mult), reads=[pAb, brc[i2]], writes=[bt1])
                            k.op("dve", lambda e, pB=pB, i2=i2: e.tensor_tensor(t2[64:96, :], pB[64:96, :], rsn[i2][64:96, :], ALU.mult), reads=[pBb, brc[i2]], writes=[bt2])
                            k.op("pool", lambda e: e.tensor_tensor(QT[64:96, q0:q0 + TT], t1[64:96, :], t2[64:96, :], ALU.add), reads=[bt1, bt2], writes=[bQ])
                            k.op("act", lambda e: e.activation(sqb[0:96, :], QT[0:96, q0:q0 + TT], AF.Square), reads=[bQ], writes=[bsqb])
                            p2, p2b = next_ps()
                            k.op("pe", lambda e, p2=p2: e.matmul(p2[0:97, :], e96[0:96, 0:97], sqb[0:96, :], start=True, stop=True), reads=[be96, bsqb], writes=[p2b])
                            k.op("act", lambda e, p2=p2: e.activation(kk[96:97, :], p2[96:97, :], AF.Sqrt), reads=[p2b], writes=[bkk])
                            k.op("dve", lambda e: e.tensor_scalar(QT[96:97, q0:q0 + TT], kk[96:97, :], kmx[96:97, 0:1], None, ALU.mult), reads=[bkk, bkmx], writes=[bQ])
                        for qi in range(NLAT // TT):
                            q0 = qi * TT
                            pnum, pnb = ps[qi % 2], psb[qi % 2]
                            sbank = {}

                            def emit_S(blk):
                                nonlocal pn
                                p, pb = next_ps()
                                k.op("pe", lambda e, p=p, blk=blk: e.matmul(p[:, :], KT[0:97, blk * 128:(blk + 1) * 128], QT[0:97, q0:q0 + TT], start=True, stop=True),
                                     reads=[bKn, bKr, bQ], writes=[pb])
                                i3 = pn % 4
                                pn += 1
                                k.op("act", lambda e, p=p, i3=i3: e.activation(Pt[i3][:], p[:, :], AF.Exp, scale=SCALE), reads=[pb], writes=[bP[i3]])
                                sbank[blk] = i3
                            emit_S(0)
                            emit_S(1)
                            emit_S(2)
                            for blk in range(NCH):
                                i3 = sbank.pop(blk)
                                k.op("pe", lambda e, blk=blk, i3=i3, pnum=pnum: e.matmul(pnum[:, :], Vt[:, blk, :], Pt[i3][:], start=(blk == 0), stop=(blk == NCH - 1)),
                                     reads=[bV, bP[i3]], writes=[pnb])
                                if blk + 3 < NCH:
                                    emit_S(blk + 3)
                            k.op("act", lambda e, pnum=pnum: e.copy(xs_[:], pnum[:, :]), reads=[pnb], writes=[bxs_])
                            psw, pswb = ps[2 + qi % 2], psb[2 + qi % 2]
                            k.op("pe", lambda e, psw=psw: e.matmul(psw[:, :], swp[:], xs_[:], start=True, stop=True), reads=[bswp, bxs_], writes=[pswb])
                            k.op("dve", lambda e, psw=psw, off=off: e.reciprocal(rd[off:off + 64, :], psw[off:off + 64, :]), reads=[pswb], writes=[brd])
                            o_, b_o = ot[qi % 2], bot[qi % 2]
                            k.op("dve", lambda e, off=off, o_=o_: e.tensor_tensor(o_[off:off + 64, :], xs_[off:off + 64, :], rd[off:off + 64, :], ALU.mult),
                                 reads=[bxs_, brd], writes=[b_o])
                            k.dma("sp", scrM[h * 64:(h + 1) * 64, NCTX + q0:NCTX + q0 + TT], o_[off:off + 64, :], reads=[b_o], writes=[Buf()])
                    pspool[0] = list(range(8))
                k.barrier()
            k.barrier()
            outproj_phase(mla_out_w[0], 8, scrM, ctx_out)

        with ExitStack() as st:
            cp = [sb("cp%d" % i, [128, 8, 512], F32, st) for i in range(2)]; bcp = [Buf(), Buf()]
            for ti, (t0, n, vsel) in enumerate(seg_tiles(512)):
                k.dma("sp", cp[ti % 2][:, :, :n], xT_in[:, t0:t0 + n].rearrange("(c p) t -> p c t", p=128), writes=[bcp[ti % 2]])
                k.dma("pool", xres[:, t0:t0 + n].rearrange("(c p) t -> p c t", p=128), cp[ti % 2][:, :, :n], reads=[bcp[ti % 2]], writes=[Buf()])
        k.barrier()
        for li in layers:
            ctx_out = li < 3
            compute_mod(li)
            if debug == "mod_only":
                continue
            if debug == "ffn_only":
                ffn_phase(li, ctx_out)
                continue
            if li == 0:
                ssd_layer(li, ctx_out)
                if debug in ("ssdA", "ssd1", "ssd2"):
                    continue
            if li == 1:
                lru_layer(li, ctx_out)
            if li == 3:
                mla_layer(li, ctx_out)
            if li == 2:
                s5_layer(li, ctx_out)
                if debug == "s5f" and debug_sub in ("pre", "pre2", "pre3", "p_a"):
                    k.barrier()
                    return nc, list(di.keys())
                if debug == "s5f":
                    with ExitStack() as st:
                        cp = [sb("dq%d" % i, [128, 8, 512], F32, st) for i in range(2)]; bcp = [Buf(), Buf()]
                        for ti, (t0, n, vsel) in enumerate(seg_tiles(512)):
                            if vsel == 1:
                                continue
                            k.dma("sp", cp[ti % 2][:, :, :n], scrA[0:D, t0:t0 + n].rearrange("(c p) t -> p c t", p=128), writes=[bcp[ti % 2]])
                            k.dma("pool", outT[:, t0 - NCTX:t0 - NCTX + n].rearrange("(c p) t -> p c t", p=128), cp[ti % 2][:, :, :n], reads=[bcp[ti % 2]], writes=[Buf()])
                    k.barrier()
                    print("ninst", k.ninst)
                    return nc, list(di.keys())
            ffn_phase(li, ctx_out)
        if do_final:
            final_phase()
        else:
            with ExitStack() as st:
                cp = [sb("dp%d" % i, [128, 8, 512], F32, st) for i in range(2)]; bcp = [Buf(), Buf()]
                for ti, (t0, n, vsel) in enumerate(seg_tiles(512)):
                    if vsel == 1:
                        continue
                    k.dma("sp", cp[ti % 2][:, :, :n], xres[:, t0:t0 + n].rearrange("(c p) t -> p c t", p=128), writes=[bcp[ti % 2]])
                    k.dma("pool", outT[:, t0 - NCTX:t0 - NCTX + n].rearrange("(c p) t -> p c t", p=128), cp[ti % 2][:, :, :n], reads=[bcp[ti % 2]], writes=[Buf()])
            k.barrier()
        k.barrier()
        print("ninst", k.ninst)
    return nc, list(di.keys())


def make_in_map(inputs, b, names, x_override=None, ctx_override=None):
    xb = inputs["x"][b] if x_override is None else x_override
    cb = inputs["ctx"][b] if ctx_override is None else ctx_override
    xT = np.ascontiguousarray(np.concatenate([cb, xb], axis=0).T.astype(np.float32))
    cc = np.ascontiguousarray(np.stack([inputs["c"][b], inputs["c_ctx"]], axis=1).astype(np.float32))
    ii = np.arange(128)
    U = (ii[:, None] <= ii[None, :]).astype(np.float32)
    masks = np.stack([U, U.T.copy(), (ii[:, None] > ii[None, :]).astype(np.float32), (ii[:, None] < ii[None, :]).astype(np.float32)], axis=0)
    m = {"xT": xT, "cc": cc, "ident": np.eye(128, dtype=np.float32), "masks": np.ascontiguousarray(masks)}
    m["ramp"] = np.ascontiguousarray(np.tile(np.arange(1, 129, dtype=np.float32)[None, :], (128, 1)))
    sel = np.zeros((128, 8), np.float32)
    for gl8 in range(8):
        sel[gl8 * 16:(gl8 + 1) * 16, gl8] = 1.0
    m["sel"] = sel
    perm = np.concatenate([np.arange(8, 16), np.arange(0, 8), np.arange(24, 32), np.arange(16, 24)])
    m["mla_inw_sw"] = np.ascontiguousarray(np.asarray(inputs["mla_in_w"], np.float32)[0][:, 640:672][:, perm])
    qb = np.asarray(inputs["mla_qb_w"], np.float32)[0].reshape(384, 16, 96)
    m["mla_qbw_sw"] = np.ascontiguousarray(qb[:, :, 64:96][:, :, perm].reshape(384, 512))
    rows = NLAT // 64
    row = np.repeat(np.arange(rows, dtype=np.float32), 64)
    col = np.tile(np.arange(64, dtype=np.float32), rows)
    inv_freq = (np.float32(10000.0) ** (-np.arange(8, dtype=np.float32) / np.float32(8))).astype(np.float32)
    ang = np.stack([row[:, None] * inv_freq, col[:, None] * inv_freq], axis=1).astype(np.float32)
    cs, sn = np.cos(ang).astype(np.float32), np.sin(ang).astype(np.float32)
    C = np.zeros((32, NLAT), np.float32); S = np.zeros((32, NLAT), np.float32)
    for half in range(2):
        for part in range(2):
            r0 = half * 16 + part * 8
            C[r0:r0 + 8] = cs[:, half, :].T
            S[r0:r0 + 8] = (-sn[:, half, :].T) if part == 0 else sn[:, half, :].T
    m["ropeC"] = C; m["ropeS"] = S
    e96 = np.zeros((128, 128), np.float32); e96[0:96, 96] = 1.0
    m["e96"] = e96
    odm = np.zeros((2, 128, 128), np.float32); odm[0, :, 0:64] = 1.0; odm[1, :, 64:128] = 1.0
    m["odm"] = odm
    swp = np.zeros((128, 128), np.float32)
    swp[(np.arange(128) + 64) % 128, np.arange(128)] = 1.0
    m["swp"] = swp
    for n in names:
        if n not in m:
            m[n] = np.ascontiguousarray(np.asarray(inputs[n], dtype=np.float32))
    return m


def kernel(**inputs):
    inputs = {k_: np.asarray(v) for k_, v in inputs.items()}
    nc, names = build()
    nb = inputs["x"].shape[0]
    in_maps = [make_in_map(inputs, b, names) for b in range(nb)]
    res = run_bass_kernel_spmd(nc, in_maps, core_ids=list(range(nb)))
    out = np.stack([np.ascontiguousarray(r["outT"].T) for r in res.results], axis=0)
    return out.astype(np.float32)
```
